# Optimizing a Trainium2 kernel written in Bass

```python
import jax, jax.numpy as jnp
from jax import lax
import numpy as np

D_MODEL = 1024
BATCH = 2
SEQ = 8192
DEPTH = 2

HEAD_DIM = 64
MLA_HEADS = 6
MLA_NOPE = 64
MLA_ROPE = 32
MLA_V = 64
MLA_Q_LORA = 256
MLA_KV_LORA = 128
NSA_HEADS = 6
NSA_KV_HEADS = 2
NSA_GROUP = NSA_HEADS // NSA_KV_HEADS
NSA_BRANCHES = 3
CMP_LEN = 32
CMP_STRIDE = 16
CMP_HIDDEN = 128
SEL_LEN = 64
SEL_TOPK = 16
WINDOW = 512
SB_HEADS = 4
D_MIX = MLA_HEADS * MLA_V + NSA_HEADS * HEAD_DIM + SB_HEADS * HEAD_DIM
D_FF = 2816
ROPE_THETA = 500000.0
PARTIAL_ROT = HEAD_DIM // 4
Q_BLOCK = 128
EPS = 1e-6
NEG_INF = -1e30
FORCE_SCORE = 1e4
NSA_KV_W = NSA_KV_HEADS * HEAD_DIM
IN_WIDTHS = (MLA_Q_LORA, MLA_KV_LORA, MLA_ROPE,
             NSA_HEADS * HEAD_DIM,
             NSA_KV_W, NSA_KV_W, NSA_KV_W, NSA_KV_W, NSA_KV_W, NSA_KV_W,
             NSA_HEADS * NSA_BRANCHES,
             SB_HEADS * HEAD_DIM, SB_HEADS * HEAD_DIM, SB_HEADS * HEAD_DIM)
D_IN = sum(IN_WIDTHS)

kernel_name = 'hybrid_mla_nsa_stickbreak_macaron'


def rmsnorm(x, g):
    xf = x.astype(jnp.float32)
    y = xf * lax.rsqrt(jnp.mean(xf * xf, axis=-1, keepdims=True) + EPS)
    return (y * g.astype(jnp.float32)).astype(x.dtype)


def rope(x, pos, rot_dim):
    half = rot_dim // 2
    inv_freq = ROPE_THETA ** (-jnp.arange(half, dtype=jnp.float32) / half)
    ang = pos.astype(jnp.float32)[:, None] * inv_freq[None, :]
    cos = jnp.cos(ang)[:, None, :]
    sin = jnp.sin(ang)[:, None, :]
    xr = x[..., :rot_dim].astype(jnp.float32)
    x1, x2 = xr[..., :half], xr[..., half:]
    rot = jnp.concatenate([x1 * cos - x2 * sin, x2 * cos + x1 * sin], axis=-1)
    return jnp.concatenate([rot.astype(x.dtype), x[..., rot_dim:]], axis=-1)


def masked_softmax(s, mask):
    return jax.nn.softmax(jnp.where(mask, s.astype(jnp.float32), NEG_INF), axis=-1)


def swiglu(h, w_gate, w_up, w_down):
    return (jax.nn.silu(h @ w_gate) * (h @ w_up)) @ w_down


def split_heads(t, n_heads):
    b, s, _ = t.shape
    return t.reshape(b, s, n_heads, -1)


def sweep_query_blocks(block_fn, n_blocks):
    out = lax.map(block_fn, jnp.arange(n_blocks, dtype=jnp.int32))
    nb, b, q, h, d = out.shape
    return out.transpose(1, 0, 2, 3, 4).reshape(b, nb * q, h, d)


def mla_attention(c_q, c_kv, k_rope, q_norm, w_uq, kv_norm, w_ukv, pos):
    b, s, _ = c_q.shape
    q = (rmsnorm(c_q, q_norm) @ w_uq).reshape(b, s, MLA_HEADS, MLA_NOPE + MLA_ROPE)
    q_nope = q[..., :MLA_NOPE]
    q_pe = rope(q[..., MLA_NOPE:], pos, MLA_ROPE)
    kv = (rmsnorm(c_kv, kv_norm) @ w_ukv).reshape(b, s, MLA_HEADS, MLA_NOPE + MLA_V)
    k_nope, v = kv[..., :MLA_NOPE], kv[..., MLA_NOPE:]
    k_pe = rope(k_rope[:, :, None, :], pos, MLA_ROPE)[:, :, 0]
    scale = (MLA_NOPE + MLA_ROPE) ** -0.5

    def block(i):
        q0 = i * Q_BLOCK
        qn = lax.dynamic_slice_in_dim(q_nope, q0, Q_BLOCK, axis=1)
        qp = lax.dynamic_slice_in_dim(q_pe, q0, Q_BLOCK, axis=1)
        sc = (jnp.einsum('bqhd,bkhd->bhqk', qn, k_nope)
              + jnp.einsum('bqhr,bkr->bhqk', qp, k_pe)).astype(jnp.float32) * scale
        qpos = q0 + jnp.arange(Q_BLOCK)
        p = masked_softmax(sc, pos[None, :] <= qpos[:, None]).astype(v.dtype)
        return jnp.einsum('bhqk,bkhd->bqhd', p, v)

    return sweep_query_blocks(block, s // Q_BLOCK)


def compress_blocks(x, pos_emb, w1, w2):
    b, s, g, d = x.shape
    chunks = x.reshape(b, s // CMP_STRIDE, CMP_STRIDE, g, d)
    blocks = jnp.concatenate([chunks[:, :-1], chunks[:, 1:]], axis=2)
    blocks = blocks + pos_emb[None, None, :, None, :]
    flat = blocks.transpose(0, 1, 3, 2, 4).reshape(b, s // CMP_STRIDE - 1, g, CMP_LEN * d)
    return jax.nn.gelu(flat @ w1) @ w2


def nsa_attention(q, k_cmp, v_cmp, k_sel, v_sel, k_win, v_win, gates,
                  cmp_pos_k, cmp_w1_k, cmp_w2_k, cmp_pos_v, cmp_w1_v, cmp_w2_v, pos):
    b, s, h, d = q.shape
    q = rope(q, pos, PARTIAL_ROT)
    k_cmp = rope(k_cmp, pos, PARTIAL_ROT)
    k_sel = rope(k_sel, pos, PARTIAL_ROT)
    k_win = rope(k_win, pos, PARTIAL_ROT)
    scale = d ** -0.5
    kc = compress_blocks(k_cmp, cmp_pos_k, cmp_w1_k, cmp_w2_k)
    vc = compress_blocks(v_cmp, cmp_pos_v, cmp_w1_v, cmp_w2_v)
    n_cmp = s // CMP_STRIDE - 1
    cmp_start = jnp.arange(n_cmp) * CMP_STRIDE
    cmp_end = cmp_start + CMP_LEN - 1
    n_slc = s // SEL_LEN
    n_top = min(SEL_TOPK, n_slc)
    slc_start = jnp.arange(n_slc) * SEL_LEN
    overlap = ((cmp_start[:, None] < slc_start[None, :] + SEL_LEN)
               & (cmp_start[:, None] + CMP_LEN > slc_start[None, :])).astype(jnp.float32)
    ksb = k_sel.reshape(b, n_slc, SEL_LEN, NSA_KV_HEADS, d).transpose(0, 3, 1, 2, 4)
    vsb = v_sel.reshape(b, n_slc, SEL_LEN, NSA_KV_HEADS, d).transpose(0, 3, 1, 2, 4)
    b_ix = jnp.arange(b)[:, None, None, None]
    g_ix = jnp.arange(NSA_KV_HEADS)[None, :, None, None]
    blk = jnp.arange(n_slc)
    kw = jnp.pad(k_win, ((0, 0), (WINDOW, 0), (0, 0), (0, 0)))
    vw = jnp.pad(v_win, ((0, 0), (WINDOW, 0), (0, 0), (0, 0)))

    def block(i):
        q0 = i * Q_BLOCK
        qpos = q0 + jnp.arange(Q_BLOCK)
        qb = lax.dynamic_slice_in_dim(q, q0, Q_BLOCK, axis=1).reshape(b, Q_BLOCK, NSA_KV_HEADS, NSA_GROUP, d)
        sc = jnp.einsum('bqgrd,bcgd->bgrqc', qb, kc).astype(jnp.float32) * scale
        cmask = cmp_end[None, :] <= qpos[:, None]
        p_cmp = masked_softmax(sc, cmask)
        p_cmp = jnp.where(cmask.any(axis=-1)[:, None], p_cmp, 0.0)
        o_cmp = jnp.einsum('bgrqc,bcgd->bqgrd', p_cmp.astype(vc.dtype), vc)
        score = jnp.einsum('bgrqc,cn->bgqn', p_cmp, overlap)
        cur = qpos // SEL_LEN
        forced = (blk[None, :] == 0) | (blk[None, :] == cur[:, None]) | (blk[None, :] == cur[:, None] - 1)
        future = blk[None, :] > cur[:, None]
        score = jnp.where(forced, FORCE_SCORE, jnp.where(future, -1.0, score))
        _, idx = lax.top_k(score, n_top)
        kg = ksb[b_ix, g_ix, idx]
        vg = vsb[b_ix, g_ix, idx]
        tok = idx[..., None] * SEL_LEN + jnp.arange(SEL_LEN)
        smask = (tok <= qpos[None, None, :, None, None])[:, :, None]
        ss = jnp.einsum('bqgrd,bgqnld->bgrqnl', qb, kg).astype(jnp.float32) * scale
        ss = jnp.where(smask, ss, NEG_INF).reshape(b, NSA_KV_HEADS, NSA_GROUP, Q_BLOCK, n_top * SEL_LEN)
        p_sel = jax.nn.softmax(ss, axis=-1).astype(vg.dtype).reshape(b, NSA_KV_HEADS, NSA_GROUP, Q_BLOCK, n_top, SEL_LEN)
        o_sel = jnp.einsum('bgrqnl,bgqnld->bqgrd', p_sel, vg)
        kwb = lax.dynamic_slice_in_dim(kw, q0, WINDOW + Q_BLOCK, axis=1)
        vwb = lax.dynamic_slice_in_dim(vw, q0, WINDOW + Q_BLOCK, axis=1)
        wpos = q0 - WINDOW + jnp.arange(WINDOW + Q_BLOCK)
        wmask = ((wpos[None, :] <= qpos[:, None]) & (wpos[None, :] > qpos[:, None] - WINDOW)
                 & (wpos[None, :] >= 0))
        sw = jnp.einsum('bqgrd,bkgd->bgrqk', qb, kwb).astype(jnp.float32) * scale
        p_win = masked_softmax(sw, wmask).astype(vwb.dtype)
        o_win = jnp.einsum('bgrqk,bkgd->bqgrd', p_win, vwb)
        g = lax.dynamic_slice_in_dim(gates, q0, Q_BLOCK, axis=1).reshape(b, Q_BLOCK, NSA_KV_HEADS, NSA_GROUP, NSA_BRANCHES)
        o = g[..., 0:1] * o_cmp + g[..., 1:2] * o_sel + g[..., 2:3] * o_win
        return o.reshape(b, Q_BLOCK, h, d)

    return sweep_query_blocks(block, s // Q_BLOCK)


def stick_breaking_attention(q, k, v):
    b, s, h, d = q.shape
    scale = d ** -0.5
    kpos = jnp.arange(s)

    def block(i):
        q0 = i * Q_BLOCK
        qpos = q0 + jnp.arange(Q_BLOCK)
        qb = lax.dynamic_slice_in_dim(q, q0, Q_BLOCK, axis=1)
        z = jnp.einsum('bqhd,bkhd->bhqk', qb, k).astype(jnp.float32) * scale
        strict = kpos[None, :] < qpos[:, None]
        log_beta = jax.nn.log_sigmoid(z)
        log_rem = jnp.where(strict, jax.nn.log_sigmoid(-z), 0.0)
        suffix = lax.cumsum(log_rem, axis=3, reverse=True) - log_rem
        a = jnp.where(strict, jnp.exp(log_beta + suffix), 0.0).astype(v.dtype)
        return jnp.einsum('bhqk,bkhd->bqhd', a, v)

    return sweep_query_blocks(block, s // Q_BLOCK)


def setup_inputs(seed: int = 0) -> dict:
    key = jax.random.key(seed)
    ks = iter(jax.random.split(key, 32))
    f32 = jnp.float32
    L = DEPTH

    def nrm(shape, fan_in):
        return jax.random.normal(next(ks), shape, f32) * (fan_in ** -0.5)

    def gain(shape):
        return 1.0 + 0.02 * jax.random.normal(next(ks), shape, f32)

    return {
        'x': jax.random.normal(next(ks), (BATCH, SEQ, D_MODEL), f32),
        'ffn1_norm': gain((L, D_MODEL)),
        'ffn1_w_gate': nrm((L, D_MODEL, D_FF), D_MODEL),
        'ffn1_w_up': nrm((L, D_MODEL, D_FF), D_MODEL),
        'ffn1_w_down': nrm((L, D_FF, D_MODEL), D_FF),
        'mix_norm': gain((L, D_MODEL)),
        'w_in': nrm((L, D_MODEL, D_IN), D_MODEL),
        'mla_q_norm': gain((L, MLA_Q_LORA)),
        'mla_w_uq': nrm((L, MLA_Q_LORA, MLA_HEADS * (MLA_NOPE + MLA_ROPE)), MLA_Q_LORA),
        'mla_kv_norm': gain((L, MLA_KV_LORA)),
        'mla_w_ukv': nrm((L, MLA_KV_LORA, MLA_HEADS * (MLA_NOPE + MLA_V)), MLA_KV_LORA),
        'nsa_gate_bias': 0.1 * jax.random.normal(next(ks), (L, NSA_HEADS * NSA_BRANCHES), f32),
        'nsa_cmp_pos_k': 0.1 * jax.random.normal(next(ks), (L, CMP_LEN, HEAD_DIM), f32),
        'nsa_cmp_w1_k': nrm((L, CMP_LEN * HEAD_DIM, CMP_HIDDEN), CMP_LEN * HEAD_DIM),
        'nsa_cmp_w2_k': nrm((L, CMP_HIDDEN, HEAD_DIM), CMP_HIDDEN),
        'nsa_cmp_pos_v': 0.1 * jax.random.normal(next(ks), (L, CMP_LEN, HEAD_DIM), f32),
        'nsa_cmp_w1_v': nrm((L, CMP_LEN * HEAD_DIM, CMP_HIDDEN), CMP_LEN * HEAD_DIM),
        'nsa_cmp_w2_v': nrm((L, CMP_HIDDEN, HEAD_DIM), CMP_HIDDEN),
        'w_out': nrm((L, D_MIX, D_MODEL), D_MIX),
        'ffn2_norm': gain((L, D_MODEL)),
        'ffn2_w_gate': nrm((L, D_MODEL, D_FF), D_MODEL),
        'ffn2_w_up': nrm((L, D_MODEL, D_FF), D_MODEL),
        'ffn2_w_down': nrm((L, D_FF, D_MODEL), D_FF),
        'final_norm': gain((D_MODEL,)),
    }


def reference(x, ffn1_norm, ffn1_w_gate, ffn1_w_up, ffn1_w_down, mix_norm, w_in,
              mla_q_norm, mla_w_uq, mla_kv_norm, mla_w_ukv,
              nsa_gate_bias, nsa_cmp_pos_k, nsa_cmp_w1_k, nsa_cmp_w2_k,
              nsa_cmp_pos_v, nsa_cmp_w1_v, nsa_cmp_w2_v, w_out,
              ffn2_norm, ffn2_w_gate, ffn2_w_up, ffn2_w_down, final_norm):
    b, s, _ = x.shape
    pos = jnp.arange(s, dtype=jnp.int32)
    split_at = [int(c) for c in np.cumsum(IN_WIDTHS)[:-1]]
    for l in range(DEPTH):
        x = x + 0.5 * swiglu(rmsnorm(x, ffn1_norm[l]), ffn1_w_gate[l], ffn1_w_up[l], ffn1_w_down[l])
        h = rmsnorm(x, mix_norm[l])
        (c_q, c_kv, k_rope, n_q, n_kc, n_vc, n_ks, n_vs, n_kw, n_vw, n_gate,
         sb_q, sb_k, sb_v) = jnp.split(h @ w_in[l], split_at, axis=-1)
        o_mla = mla_attention(c_q, c_kv, k_rope, mla_q_norm[l], mla_w_uq[l],
                              mla_kv_norm[l], mla_w_ukv[l], pos)
        gates = jax.nn.sigmoid(n_gate + nsa_gate_bias[l]).reshape(b, s, NSA_HEADS, NSA_BRANCHES)
        o_nsa = nsa_attention(split_heads(n_q, NSA_HEADS),
                              split_heads(n_kc, NSA_KV_HEADS), split_heads(n_vc, NSA_KV_HEADS),
                              split_heads(n_ks, NSA_KV_HEADS), split_heads(n_vs, NSA_KV_HEADS),
                              split_heads(n_kw, NSA_KV_HEADS), split_heads(n_vw, NSA_KV_HEADS),
                              gates, nsa_cmp_pos_k[l], nsa_cmp_w1_k[l], nsa_cmp_w2_k[l],
                              nsa_cmp_pos_v[l], nsa_cmp_w1_v[l], nsa_cmp_w2_v[l], pos)
        o_sb = stick_breaking_attention(split_heads(sb_q, SB_HEADS), split_heads(sb_k, SB_HEADS),
                                        split_heads(sb_v, SB_HEADS))
        o = jnp.concatenate([o_mla.reshape(b, s, -1), o_nsa.reshape(b, s, -1),
                             o_sb.reshape(b, s, -1)], axis=-1)
        x = x + o @ w_out[l]
        x = x + 0.5 * swiglu(rmsnorm(x, ffn2_norm[l]), ffn2_w_gate[l], ffn2_w_up[l], ffn2_w_down[l])
    return rmsnorm(x, final_norm)
```

```python
import numpy as np
import ml_dtypes
from contextlib import ExitStack
import concourse.bass as bass
import concourse.mybir as mybir
from concourse.bass_utils import run_bass_kernel_spmd

F32 = mybir.dt.float32
BF16 = mybir.dt.bfloat16
AF = mybir.ActivationFunctionType
ALU = mybir.AluOpType
AX = mybir.AxisListType

D = 1024
S = 8192
B = 2
DFF = 2816
NT = 2048
NBLK = 16
EPS = 1e-6
DIN = 2354


_UN = [0]
CC_INC = 1


def U(name):
    _UN[0] += 1
    return "%s_u%d" % (name, _UN[0])


class Buf:
    __slots__ = ("name", "w", "r")

    def __init__(self, name):
        self.name = name
        self.w = None
        self.r = {}


class Tracker:
    def __init__(self, nc, stack):
        self.nc = nc
        self.stack = stack
        self.eng = {"pe": nc.tensor, "act": nc.scalar, "dve": nc.vector,
                    "pool": nc.gpsimd, "sp": nc.sync}
        self.sem = {}
        self.cnt = {}
        self.seen = {k: {} for k in self.eng}
        for k in self.eng:
            self.sem[k] = stack.enter_context(nc.semaphore("s_" + k))
            self.cnt[k] = 0
        self.ndma = 0

    def new_dma_sem(self, name):
        key = "dma_" + name + "_%d" % self.ndma
        self.ndma += 1
        self.sem[key] = self.stack.enter_context(self.nc.semaphore(key))
        self.cnt[key] = 0
        return key

    def _deps(self, e, reads, writes, ignore=None):
        deps = {}

        def add(k, c):
            if c > deps.get(k, 0):
                deps[k] = c
        for b in reads:
            if b.w is not None:
                add(*b.w)
        for b in writes:
            if b.w is not None:
                add(*b.w)
            for k, c in b.r.items():
                add(k, c)
        for k, c in deps.items():
            if (k == "pe" and e == "pe") or k == ignore:
                continue
            if k.startswith("dma_"):
                c = max(c, self.cnt[k])
            if c > self.seen[e].get(k, 0):
                self.eng[e].wait_ge(self.sem[k], c)
                self.seen[e][k] = c

    def op(self, e, fn, reads=(), writes=()):
        self._deps(e, reads, writes)
        ins = fn(self.eng[e])
        self.cnt[e] += 1
        c = self.cnt[e]
        ins.then_inc(self.sem[e], 1)
        for b in reads:
            if c > b.r.get(e, 0):
                b.r[e] = c
        for b in writes:
            b.w = (e, c)
            b.r = {}
        return ins

    def dma(self, q, dsem, out, in_, reads=(), writes=()):
        self._deps(q, reads, writes, ignore=dsem)
        ins = self.eng[q].dma_start(out=out, in_=in_)
        self.cnt[dsem] += 16
        c = self.cnt[dsem]
        ins.then_inc(self.sem[dsem], 16)
        for b in reads:
            if c > b.r.get(dsem, 0):
                b.r[dsem] = c
        for b in writes:
            b.w = (dsem, c)
            b.r = {}
        return ins

    def collective(self, dsem, src, dst, groups, reads=(), writes=()):
        self._deps("pool", reads, writes, ignore=dsem)
        ins = self.nc.gpsimd.collective_compute("AllGather", ALU.bypass, replica_groups=groups,
                                                ins=[src.opt()], outs=[dst.opt()])
        self.cnt[dsem] += CC_INC
        c = self.cnt[dsem]
        ins.then_inc(self.sem[dsem], CC_INC)
        for b in reads:
            if c > b.r.get(dsem, 0):
                b.r[dsem] = c
        for b in writes:
            b.w = (dsem, c)
            b.r = {}
        return ins

    def wait_dma_all(self, e):
        for k, c in self.cnt.items():
            if k.startswith("dma_") and c > self.seen[e].get(k, 0):
                self.eng[e].wait_ge(self.sem[k], c)
                self.seen[e][k] = c

    def barrier_all(self):
        for e in self.eng:
            for k, c in self.cnt.items():
                if k == e or c == 0:
                    continue
                if c > self.seen[e].get(k, 0):
                    self.eng[e].wait_ge(self.sem[k], c)
                    self.seen[e][k] = c


class Res:
    pass


def setup_common(nc, stack, T):
    R = Res()
    R.nc, R.T, R.stack = nc, T, stack
    R.ExitStack = ExitStack
    R.ps = []
    R.psb = []
    for i in range(8):
        R.ps.append(stack.enter_context(nc.psum_tensor("ps%d" % i, [128, 512], F32)))
        R.psb.append(Buf("ps%d" % i))
    R.xb = [[Buf("x%d_%d" % (k, t)) for t in range(4)] for k in range(8)]
    R.onesm = stack.enter_context(nc.sbuf_tensor(U("onesm"), [128, 128], BF16))
    R.onesm_b = Buf("onesm")
    T.op("pool", lambda e: e.memset(R.onesm[:], 1.0), writes=[R.onesm_b])
    return R


def alloc_xT(R, stack):
    R.xT = stack.enter_context(R.nc.sbuf_tensor(U("xT"), [128, 8, NT], F32))


def emit_norm(R, hT, hb, gam, gam_b, sq, sq_b, rstd, rstd_b, ssbank):
    T = R.T
    ss, ss_b = R.ps[ssbank], R.psb[ssbank]
    for tt in range(4):
        sl = slice(tt * 512, (tt + 1) * 512)
        for k in range(8):
            a = k % 2
            T.op("act", lambda e: e.activation(out=sq[a][:], in_=R.xT[:, k, sl], func=AF.Square),
                 reads=[R.xb[k][tt]], writes=[sq_b[a]])
            T.op("pe", lambda e: e.matmul(ss[:], lhsT=R.onesm[:], rhs=sq[a][:], start=(k == 0), stop=(k == 7)),
                 reads=[sq_b[a], R.onesm_b], writes=[ss_b])
        T.op("act", lambda e: e.activation(out=rstd[:], in_=ss[:], func=AF.Sqrt, bias=EPS, scale=1.0 / D),
             reads=[ss_b], writes=[rstd_b])
        T.op("dve", lambda e: e.reciprocal(out=rstd[:], in_=rstd[:]), reads=[rstd_b], writes=[rstd_b])
        for k in range(8):
            T.op("dve", lambda e: e.scalar_tensor_tensor(out=hT[:, k, sl], in0=R.xT[:, k, sl],
                                                         scalar=gam[:, k:k + 1], in1=rstd[:],
                                                         op0=ALU.mult, op1=ALU.mult),
                 reads=[R.xb[k][tt], rstd_b, gam_b], writes=[hb[k][tt]])


def emit_ffn(R, hT, hb, wg_d, wu_d, wd_d, W):
    T = R.T
    nfg = 6
    n_g = n_y = 0
    for fg in range(nfg):
        ncf = 4 if fg < 5 else 2
        wcols = ncf * 128
        c0 = fg * 512
        s = fg % 2
        wgs, wus, wds, wb, dsem = W["wg"][s], W["wu"][s], W["wd"][s], W["wb"][s], W["dsem"][s]
        T.dma("pool", dsem, wgs[:, :, 0:wcols],
              wg_d[:, c0:c0 + wcols].rearrange("(k p) c -> p k c", p=128), writes=[wb])
        T.dma("pool", dsem, wus[:, :, 0:wcols],
              wu_d[:, c0:c0 + wcols].rearrange("(k p) c -> p k c", p=128), writes=[wb])
        T.dma("pool", dsem, wds[:, 0:ncf, :],
              wd_d[c0:c0 + wcols, :].rearrange("(c p) m -> p c m", p=128), writes=[wb])
        for tt in range(4):
            sl = slice(tt * 512, (tt + 1) * 512)
            asl = (fg * 4 + tt) % 2
            for c in range(ncf):
                gi, ui = W["gbanks"][n_g % 2], W["ubanks"][n_g % 2]
                sgi = n_g % 2
                n_g += 1
                for k in range(8):
                    T.op("pe", lambda e: e.matmul(R.ps[gi][:], lhsT=wgs[:, k, c * 128:(c + 1) * 128],
                                                  rhs=hT[:, k, sl], start=(k == 0), stop=(k == 7)),
                         reads=[wb, hb[k][tt]], writes=[R.psb[gi]])
                for k in range(8):
                    T.op("pe", lambda e: e.matmul(R.ps[ui][:], lhsT=wus[:, k, c * 128:(c + 1) * 128],
                                                  rhs=hT[:, k, sl], start=(k == 0), stop=(k == 7)),
                         reads=[wb, hb[k][tt]], writes=[R.psb[ui]])
                T.op("act", lambda e: e.activation(out=W["sg"][sgi][:], in_=R.ps[gi][:], func=AF.Silu),
                     reads=[R.psb[gi]], writes=[W["sg_b"][sgi]])
                T.op("dve", lambda e: e.tensor_tensor(out=W["act"][asl][:, c, :], in0=W["sg"][sgi][:],
                                                      in1=R.ps[ui][:], op=ALU.mult),
                     reads=[W["sg_b"][sgi], R.psb[ui]], writes=[W["act_b"][asl][c]])
            for dmc in range(8):
                yi = W["ybanks"][n_y % 2]
                n_y += 1
                for c in range(ncf):
                    T.op("pe", lambda e: e.matmul(R.ps[yi][:], lhsT=wds[:, c, dmc * 128:(dmc + 1) * 128],
                                                  rhs=W["act"][asl][:, c, :], start=(c == 0), stop=(c == ncf - 1)),
                         reads=[wb, W["act_b"][asl][c]], writes=[R.psb[yi]])
                T.op("dve", lambda e: e.scalar_tensor_tensor(out=R.xT[:, dmc, sl], in0=R.ps[yi][:], scalar=0.5,
                                                             in1=R.xT[:, dmc, sl], op0=ALU.mult, op1=ALU.add),
                     reads=[R.psb[yi], R.xb[dmc][tt]], writes=[R.xb[dmc][tt]])


def alloc_ffn_work(R):
    nc, stack, T = R.nc, R.stack, R.T
    W = {}
    W["wg"] = [stack.enter_context(nc.sbuf_tensor(U("wg%d" % i), [128, 8, 512], BF16)) for i in range(2)]
    W["wu"] = [stack.enter_context(nc.sbuf_tensor(U("wu%d" % i), [128, 8, 512], BF16)) for i in range(2)]
    W["wd"] = [stack.enter_context(nc.sbuf_tensor(U("wd%d" % i), [128, 4, 1024], BF16)) for i in range(2)]
    W["wb"] = [Buf("wslot%d" % i) for i in range(2)]
    W["dsem"] = [T.new_dma_sem("ffnw%d" % i) for i in range(2)]
    W["sg"] = [stack.enter_context(nc.sbuf_tensor(U("sg%d" % i), [128, 512], F32)) for i in range(2)]
    W["sg_b"] = [Buf("sg%d" % i) for i in range(2)]
    W["act"] = [stack.enter_context(nc.sbuf_tensor(U("act%d" % i), [128, 4, 512], BF16)) for i in range(2)]
    W["act_b"] = [[Buf("act%d_%d" % (i, c)) for c in range(4)] for i in range(2)]
    W["gbanks"], W["ubanks"], W["ybanks"] = [0, 1], [2, 3], [4, 5]
    return W


C_CQ, C_CKV, C_KR, C_NQ, C_NKC, C_NVC, C_NKS, C_NVS, C_NKW, C_NVW, C_NG, C_SQ, C_SK, C_SV = (
    0, 256, 384, 416, 800, 928, 1056, 1184, 1312, 1440, 1568, 1586, 1842, 2098)
SW_KR, SW_NQ, SW_NKC, SW_NKS, SW_NKW = 0, 32, 416, 544, 672
NSW = 800


def emit_stage_P(R, hT, hb, A, mid_hook=None):
    nc, T = R.nc, R.T
    with R.ExitStack() as ph:
        def sb(name, shape, dt):
            return ph.enter_context(nc.sbuf_tensor(U(name), shape, dt))
        win = sb("P_win", [128, 8, DIN], BF16)
        wsw = sb("P_wsw", [128, 8, NSW], BF16)
        wuq = sb("P_wuq", [128, 2, 576], BF16)
        wuqs = sb("P_wuqs", [128, 2, 576], BF16)
        wukv = sb("P_wukv", [128, 768], BF16)
        sm = sb("P_sm", [128, 32], F32)
        wb_, smb = Buf("P_w"), Buf("P_sm")
        dw = T.new_dma_sem("Pw")
        T.dma("pool", dw, win[:], A["w_in"].rearrange("(k p) c -> p k c", p=128), writes=[wb_])
        T.dma("pool", dw, wsw[:], A["w_in_sw"].rearrange("(k p) c -> p k c", p=128), writes=[wb_])
        T.dma("pool", dw, wuq[:], A["w_uq"].rearrange("(k p) c -> p k c", p=128), writes=[wb_])
        T.dma("pool", dw, wuqs[:], A["w_uq_sw"].rearrange("(k p) c -> p k c", p=128), writes=[wb_])
        T.dma("pool", dw, wukv[:], A["w_ukv"], writes=[wb_])
        dw2 = T.new_dma_sem("Psm")
        T.dma("sp", dw2, sm[:], A["smallsP"], writes=[smb])
        tabs = []
        for i in range(2):
            tabs.append(dict(
                cosM=sb("P_cosM%d" % i, [96, 512], F32), sinM=sb("P_sinM%d" % i, [96, 512], F32),
                cosK=sb("P_cosK%d" % i, [32, 512], F32), sinK=sb("P_sinK%d" % i, [32, 512], F32),
                cosN=sb("P_cosN%d" % i, [128, 512], F32), sinN=sb("P_sinN%d" % i, [128, 512], F32),
                b=Buf("P_tab%d" % i), sem=T.new_dma_sem("Ptab%d" % i)))
        sq = [sb("P_sq%d" % i, [128, 512], BF16) for i in range(2)]
        sq_b = [Buf("P_sq0"), Buf("P_sq1")]
        rs = sb("P_rs", [128, 512], F32)
        rs_b = Buf("P_rs")
        cqn = sb("P_cqn", [128, 2, 512], BF16)
        cqn_b = Buf("P_cqn")
        ckvn = sb("P_ckvn", [128, 512], BF16)
        ckvn_b = Buf("P_ckvn")
        t1 = [sb("P_t1_%d" % i, [128, 512], F32) for i in range(2)]
        t2 = [sb("P_t2_%d" % i, [128, 512], F32) for i in range(2)]
        t_b = [Buf("P_t0"), Buf("P_t1")]
        NST = 4
        stg = [sb("P_stg%d" % i, [128, 512], BF16) for i in range(NST)]
        stg_b = [Buf("P_stg%d" % i) for i in range(NST)]
        stg_sem = [T.new_dma_sem("Pstg%d" % i) for i in range(NST)]
        vst = [sb("P_vst%d" % i, [128, 6, 65], BF16) for i in range(2)]
        vst_b = [Buf("P_vst0"), Buf("P_vst1")]
        vst_sem = [T.new_dma_sem("Pvst%d" % i) for i in range(2)]
        v2st = [sb("P_v2st%d" % i, [128, 2, 2, 65], BF16) for i in range(2)]
        v2st_b = [Buf("P_v2st0"), Buf("P_v2st1")]
        v2st_sem = [T.new_dma_sem("Pv2st%d" % i) for i in range(2)]
        vsb = [sb("P_vsb%d" % i, [128, 256], BF16) for i in range(2)]
        vsb_b = [Buf("P_vsb0"), Buf("P_vsb1")]
        vsb_sem = [T.new_dma_sem("Pvsb%d" % i) for i in range(2)]
        gst = [sb("P_gst%d" % i, [128, 18], F32) for i in range(2)]
        gst_b = [Buf("P_gst0"), Buf("P_gst1")]
        gst_sem = [T.new_dma_sem("Pgst%d" % i) for i in range(2)]
        for i in range(2):
            T.op("pool", lambda e: e.memset(vst[i][:], 1.0), writes=[vst_b[i]])
            T.op("pool", lambda e: e.memset(v2st[i][:], 1.0), writes=[v2st_b[i]])

        cnt = {"bank": 0, "stg": 0, "t": 0}

        def nbank():
            cnt["bank"] += 1
            return cnt["bank"] % 4

        def chain(bank, M, lhs_fn, rhs_fn, nk, reads, N=512, rows=None):
            o = R.ps[bank][0:M, 0:N] if rows is None else R.ps[bank][rows[0]:rows[1], 0:N]
            for k in range(nk):
                T.op("pe", lambda e: e.matmul(o, lhsT=lhs_fn(k), rhs=rhs_fn(k), start=(k == 0), stop=(k == nk - 1)),
                     reads=reads(k), writes=[R.psb[bank]])

        def store(src_ps_ap, src_bufs, M, dst_dram, use_act=False):
            i = cnt["stg"] % NST
            cnt["stg"] += 1
            eng = "act" if use_act else "dve"
            if use_act:
                T.op("act", lambda e: e.activation(out=stg[i][0:M, :], in_=src_ps_ap, func=AF.Copy),
                     reads=src_bufs, writes=[stg_b[i]])
            else:
                T.op("dve", lambda e: e.tensor_copy(out=stg[i][0:M, :], in_=src_ps_ap),
                     reads=src_bufs, writes=[stg_b[i]])
            T.dma("sp", stg_sem[i], dst_dram, stg[i][0:M, :], reads=[stg_b[i]])

        def rope_store(bankA, bankB, M, cos, sin, tb, dst_dram):
            j = cnt["t"] % 2
            cnt["t"] += 1
            i = cnt["stg"] % NST
            cnt["stg"] += 1
            T.op("dve", lambda e: e.tensor_tensor(out=t1[j][0:M, :], in0=R.ps[bankA][0:M, :], in1=cos, op=ALU.mult),
                 reads=[R.psb[bankA], tb], writes=[t_b[j]])
            T.op("dve", lambda e: e.tensor_tensor(out=t2[j][0:M, :], in0=R.ps[bankB][0:M, :], in1=sin, op=ALU.mult),
                 reads=[R.psb[bankB], tb], writes=[t_b[j]])
            T.op("pool", lambda e: e.tensor_tensor(out=stg[i][0:M, :], in0=t1[j][0:M, :], in1=t2[j][0:M, :], op=ALU.add),
                 reads=[t_b[j]], writes=[stg_b[i]])
            T.dma("sp", stg_sem[i], dst_dram, stg[i][0:M, :], reads=[stg_b[i]])

        for tt in range(4):
            sl = slice(tt * 512, (tt + 1) * 512)
            tab = tabs[tt % 2]
            for nm in ("cosM", "sinM", "cosK", "sinK"):
                T.dma("sp", tab["sem"], tab[nm][:], A[nm][:, sl], writes=[tab["b"]])
            hreads = lambda k: [wb_, hb[k][tt]]
            for c in range(2):
                chain(4 + c, 128, lambda k: win[:, k, C_CQ + c * 128:C_CQ + (c + 1) * 128],
                      lambda k: hT[:, k, sl], 8, hreads)
            chain(6, 128, lambda k: win[:, k, C_CKV:C_CKV + 128], lambda k: hT[:, k, sl], 8, hreads)
            for c in range(2):
                T.op("act", lambda e: e.activation(out=sq[c][:], in_=R.ps[4 + c][:], func=AF.Square),
                     reads=[R.psb[4 + c]], writes=[sq_b[c]])
            for c in range(2):
                T.op("pe", lambda e: e.matmul(R.ps[7][:], lhsT=R.onesm[:], rhs=sq[c][:], start=(c == 0), stop=(c == 1)),
                     reads=[sq_b[c], R.onesm_b], writes=[R.psb[7]])
            T.op("act", lambda e: e.activation(out=rs[:], in_=R.ps[7][:], func=AF.Sqrt, bias=EPS, scale=1.0 / 256),
                 reads=[R.psb[7]], writes=[rs_b])
            T.op("dve", lambda e: e.reciprocal(out=rs[:], in_=rs[:]), reads=[rs_b], writes=[rs_b])
            for c in range(2):
                T.op("dve", lambda e: e.scalar_tensor_tensor(out=cqn[:, c, :], in0=R.ps[4 + c][:], scalar=sm[:, c:c + 1],
                                                             in1=rs[:], op0=ALU.mult, op1=ALU.mult),
                     reads=[R.psb[4 + c], rs_b, smb], writes=[cqn_b])
            T.op("act", lambda e: e.activation(out=sq[0][:], in_=R.ps[6][:], func=AF.Square),
                 reads=[R.psb[6]], writes=[sq_b[0]])
            T.op("pe", lambda e: e.matmul(R.ps[7][:], lhsT=R.onesm[:], rhs=sq[0][:], start=True, stop=True),
                 reads=[sq_b[0], R.onesm_b], writes=[R.psb[7]])
            T.op("act", lambda e: e.activation(out=rs[:], in_=R.ps[7][:], func=AF.Sqrt, bias=EPS, scale=1.0 / 128),
                 reads=[R.psb[7]], writes=[rs_b])
            T.op("dve", lambda e: e.reciprocal(out=rs[:], in_=rs[:]), reads=[rs_b], writes=[rs_b])
            T.op("dve", lambda e: e.scalar_tensor_tensor(out=ckvn[:], in0=R.ps[6][:], scalar=sm[:, 2:3],
                                                         in1=rs[:], op0=ALU.mult, op1=ALU.mult),
                 reads=[R.psb[6], rs_b, smb], writes=[ckvn_b])
            for h in range(6):
                ba, bb = nbank(), 4 + (h % 2)
                chain(ba, 96, lambda c: wuq[:, c, h * 96:(h + 1) * 96], lambda c: cqn[:, c, :], 2,
                      lambda c: [wb_, cqn_b])
                chain(bb, 96, lambda c: wuqs[:, c, h * 96:(h + 1) * 96], lambda c: cqn[:, c, :], 2,
                      lambda c: [wb_, cqn_b])
                rope_store(ba, bb, 96, tab["cosM"][:], tab["sinM"][:], tab["b"], A["qT_mla"][h, :, sl])
            for h in range(6):
                ba = nbank()
                chain(ba, 64, lambda c: wukv[:, h * 128:h * 128 + 64], lambda c: ckvn[:], 1, lambda c: [wb_, ckvn_b])
                store(R.ps[ba][0:64, :], [R.psb[ba]], 64, A["kT_mla"][h][:, sl], use_act=(h % 2 == 0))
            ba, bb = nbank(), 4
            chain(ba, 32, lambda k: win[:, k, C_KR:C_KR + 32], lambda k: hT[:, k, sl], 8, hreads)
            chain(bb, 32, lambda k: wsw[:, k, SW_KR:SW_KR + 32], lambda k: hT[:, k, sl], 8, hreads)
            rope_store(ba, bb, 32, tab["cosK"][:], tab["sinK"][:], tab["b"], A["kpeT"][:, sl])
            for bl in range(4):
                tb = tt * 4 + bl
                lsl = slice(bl * 128, (bl + 1) * 128)
                i = tb % 2
                ba = nbank()
                T.op("pe", lambda e: e.matmul(R.ps[ba][:, 0:384], lhsT=ckvn[:, lsl],
                                              rhs=wukv[:].rearrange("p (h x) -> p h x", x=128)[:, :, 64:128],
                                              start=True, stop=True),
                     reads=[wb_, ckvn_b], writes=[R.psb[ba]])
                T.op("dve", lambda e: e.tensor_copy(out=vst[i][:, :, 0:64],
                                                    in_=R.ps[ba][:, 0:384].rearrange("p (h x) -> p h x", x=64)),
                     reads=[R.psb[ba]], writes=[vst_b[i]])
                T.dma("sp", vst_sem[i], A["v_mla"][tb // 8][(tb % 8) * 128:(tb % 8 + 1) * 128, :, :], vst[i][:], reads=[vst_b[i]])
        if mid_hook is not None:
            mid_hook()
        for tt in range(4):
            sl = slice(tt * 512, (tt + 1) * 512)
            tab = tabs[tt % 2]
            for nm in ("cosN", "sinN"):
                T.dma("sp", tab["sem"], tab[nm][:], A[nm][:, sl], writes=[tab["b"]])
            hreads = lambda k: [wb_, hb[k][tt]]
            ropes = [(C_NQ + 128 * c, SW_NQ + 128 * c, A["qT_nsa"][128 * c:128 * (c + 1), sl]) for c in range(3)]
            ropes += [(C_NKC, SW_NKC, A["kT_cmp"][:, sl]), (C_NKS, SW_NKS, A["kT_sel"][:, sl]),
                      (C_NKW, SW_NKW, A["kT_win"][:, sl])]
            for n, (ca, cs, dst) in enumerate(ropes):
                ba, bb = nbank(), 4 + (n % 2)
                chain(ba, 128, lambda k: win[:, k, ca:ca + 128], lambda k: hT[:, k, sl], 8, hreads)
                chain(bb, 128, lambda k: wsw[:, k, cs:cs + 128], lambda k: hT[:, k, sl], 8, hreads)
                rope_store(ba, bb, 128, tab["cosN"][:], tab["sinN"][:], tab["b"], dst)
            plain = [(C_NVC, A["vT_cmp"][:, sl]), (C_SQ, A["qT_sb"][0:128, sl]), (C_SQ + 128, A["qT_sb"][128:256, sl]),
                     (C_SK, A["kT_sb"][0:128, sl]), (C_SK + 128, A["kT_sb"][128:256, sl])]
            for n, (ca, dst) in enumerate(plain):
                ba = nbank()
                chain(ba, 128, lambda k: win[:, k, ca:ca + 128], lambda k: hT[:, k, sl], 8, hreads)
                store(R.ps[ba][:, :], [R.psb[ba]], 128, dst, use_act=(n % 2 == 0))
            for bl in range(4):
                tb = tt * 4 + bl
                tsl = slice(tb * 128, (tb + 1) * 128)
                lsl = slice(bl * 128, (bl + 1) * 128)
                i = tb % 2
                ba = nbank()
                for n, ca in enumerate((C_NVS, C_NVW)):
                    for k in range(8):
                        T.op("pe", lambda e: e.matmul(R.ps[ba][:, n * 128:(n + 1) * 128], lhsT=hT[:, k, tsl],
                                                      rhs=win[:, k, ca:ca + 128], start=(k == 0), stop=(k == 7)),
                             reads=[wb_, hb[k][tt]], writes=[R.psb[ba]])
                for k in range(8):
                    T.op("pe", lambda e: e.matmul(R.ps[ba][:, 256:274], lhsT=hT[:, k, tsl],
                                                  rhs=win[:, k, C_NG:C_NG + 18], start=(k == 0), stop=(k == 7)),
                         reads=[wb_, hb[k][tt]], writes=[R.psb[ba]])
                T.op("dve", lambda e: e.tensor_copy(out=v2st[i][:, :, :, 0:64],
                                                    in_=R.ps[ba][:, 0:256].rearrange("p (a h x) -> p a h x", a=2, x=64)),
                     reads=[R.psb[ba]], writes=[v2st_b[i]])
                T.dma("sp", v2st_sem[i], A["v_sel"][tsl, :, :], v2st[i][:, 0, :, :], reads=[v2st_b[i]])
                T.dma("sp", v2st_sem[i], A["v_win"][tsl, :, :], v2st[i][:, 1, :, :], reads=[v2st_b[i]])
                T.op("dve", lambda e: e.tensor_tensor(out=gst[i][:], in0=R.ps[ba][:, 256:274], in1=sm[:, 8:26], op=ALU.add),
                     reads=[R.psb[ba], smb], writes=[gst_b[i]])
                T.op("act", lambda e: e.activation(out=gst[i][:], in_=gst[i][:], func=AF.Sigmoid),
                     reads=[gst_b[i]], writes=[gst_b[i]])
                T.dma("sp", gst_sem[i], A["gates"][tsl, :], gst[i][:], reads=[gst_b[i]])
                ba = nbank()
                for k in range(8):
                    T.op("pe", lambda e: e.matmul(R.ps[ba][:, 0:256], lhsT=hT[:, k, tsl],
                                                  rhs=win[:, k, C_SV:C_SV + 256], start=(k == 0), stop=(k == 7)),
                         reads=[wb_, hb[k][tt]], writes=[R.psb[ba]])
                T.op("act", lambda e: e.activation(out=vsb[i][:], in_=R.ps[ba][:, 0:256], func=AF.Copy),
                     reads=[R.psb[ba]], writes=[vsb_b[i]])
                T.dma("sp", vsb_sem[i], A["v_sb"][tsl, :], vsb[i][:], reads=[vsb_b[i]])
        T.barrier_all()


THETA = 500000.0


def own_positions(core):
    j = core % 4
    return ((4 * np.arange(NBLK)[:, None] + j) * 128 + np.arange(128)[None, :]).reshape(-1)


def _cs(pos, rot):
    half = rot // 2
    inv = np.float32(THETA) ** (-(np.arange(half, dtype=np.float32) / np.float32(half)))
    ang = pos.astype(np.float32)[None, :] * inv.astype(np.float32)[:, None]
    return np.cos(ang).astype(np.float32), np.sin(ang).astype(np.float32)


def rope_tables(core):
    pos = own_positions(core)
    n = pos.shape[0]
    c16, s16 = _cs(pos, 32)
    c8, s8 = _cs(pos, 16)
    cosM = np.ones((96, n), np.float32); sinM = np.zeros((96, n), np.float32)
    cosM[64:80] = c16; cosM[80:96] = c16; sinM[64:80] = -s16; sinM[80:96] = s16
    cosK = np.concatenate([c16, c16], 0); sinK = np.concatenate([-s16, s16], 0)
    cosN = np.ones((128, n), np.float32); sinN = np.zeros((128, n), np.float32)
    for h in range(2):
        cosN[h * 64:h * 64 + 8] = c8; cosN[h * 64 + 8:h * 64 + 16] = c8
        sinN[h * 64:h * 64 + 8] = -s8; sinN[h * 64 + 8:h * 64 + 16] = s8
    return dict(cosM=cosM, sinM=sinM, cosK=cosK, sinK=sinK, cosN=cosN, sinN=sinN)


def swap_cols_rope(w, head_w, rope0, rot):
    w = np.array(w, copy=True)
    half = rot // 2
    nh = w.shape[1] // head_w
    for h in range(nh):
        a = h * head_w + rope0
        tmp = w[:, a:a + half].copy()
        w[:, a:a + half] = w[:, a + half:a + rot]
        w[:, a + half:a + rot] = tmp
    return w


def make_w_in_sw(w_in):
    parts = [swap_cols_rope(w_in[:, C_KR:C_KR + 32], 32, 0, 32),
             swap_cols_rope(w_in[:, C_NQ:C_NQ + 384], 64, 0, 16),
             swap_cols_rope(w_in[:, C_NKC:C_NKC + 128], 64, 0, 16),
             swap_cols_rope(w_in[:, C_NKS:C_NKS + 128], 64, 0, 16),
             swap_cols_rope(w_in[:, C_NKW:C_NKW + 128], 64, 0, 16)]
    return np.ascontiguousarray(np.concatenate(parts, axis=1))


def make_smallsP(q_norm, kv_norm, gate_bias):
    sm = np.zeros((128, 32), np.float32)
    sm[:, 0:2] = q_norm.reshape(2, 128).T
    sm[:, 2] = kv_norm
    sm[:, 8:26] = gate_bias[None, :]
    return sm


P_OUTS = [("qT_mla", [6, 96, NT], BF16), ("kT_mla", [6, 64, NT], BF16), ("kpeT", [32, NT], BF16),
          ("qT_nsa", [384, NT], BF16), ("kT_cmp", [128, NT], BF16), ("kT_sel", [128, NT], BF16),
          ("kT_win", [128, NT], BF16), ("vT_cmp", [128, NT], BF16), ("qT_sb", [256, NT], BF16),
          ("kT_sb", [256, NT], BF16), ("v_mla", [NT, 6, 65], BF16), ("v_sel", [NT, 2, 65], BF16),
          ("v_win", [NT, 2, 65], BF16), ("gates", [NT, 18], F32), ("v_sb", [NT, 256], BF16)]
P_INS = [("w_in", [D, DIN]), ("w_in_sw", [D, NSW]), ("w_uq", [256, 576]), ("w_uq_sw", [256, 576]),
         ("w_ukv", [128, 768]), ("smallsP", [128, 32]),
         ("cosM", [96, NT]), ("sinM", [96, NT]), ("cosK", [32, NT]), ("sinK", [32, NT]),
         ("cosN", [128, NT]), ("sinN", [128, NT])]


def make_masks(core):
    j = core % 4
    k = np.arange(128)[:, None]
    q = np.arange(128)[None, :]
    ones = np.ones((128, 128), np.float32)
    zeros = np.zeros((128, 128), np.float32)
    tri = (k <= q).astype(np.float32)
    stri = (k < q).astype(np.float32)
    gt = (k > q).astype(np.float32)
    M = np.zeros((128, 22, 128), np.float32)
    M[:, 20] = (k >= q).astype(np.float32)
    M[:, 21] = 1.0
    for d in range(4):
        M[:, d] = ones if d < j else (tri if d == j else zeros)
        M[:, 16 + d] = ones if d < j else (stri if d == j else zeros)
    for dd in range(8):
        d = dd - 4
        if d < j - 4 or d > j:
            M[:, 4 + dd] = zeros
        elif d == j - 4:
            M[:, 4 + dd] = gt
        elif d == j:
            M[:, 4 + dd] = tri
        else:
            M[:, 4 + dd] = ones
    for m4 in range(4):
        i16 = 4 * m4 + j
        M[:, 12 + m4] = (16 * k + 31 - 128 * i16 <= q).astype(np.float32)
    return M.astype(ml_dtypes.bfloat16)


def ktcol(kt):
    return (kt % 4) * NT + (kt // 4) * 128


def ktile(kt):
    return (kt % 4) * NBLK + (kt // 4)


def emit_oT_store(R, C, obank, G, ch, po, zcol=64, stride=65):
    T = R.T
    i = C["ev"] % 2
    C["ev"] += 1
    on, on_b, rz, rz_b = C["on"][i], C["on_b"][i], C["rz"][i], C["rz_b"][i]
    ov = R.ps[obank][:, 0:4 * stride].rearrange("p (m x) -> p m x", x=stride)
    T.op("dve", lambda e: e.reciprocal(out=rz[:], in_=ov[:, :, zcol:zcol + 1]), reads=[R.psb[obank]], writes=[rz_b])
    T.op("dve", lambda e: e.tensor_tensor(out=on[:], in0=ov[:, :, 0:64], in1=rz[:].broadcast_to([128, 4, 64]),
                                          op=ALU.mult),
         reads=[R.psb[obank], rz_b], writes=[on_b])
    emit_transpose_store(R, C, on, on_b, G, ch, po)


def emit_transpose_store(R, C, on, on_b, G, ch, po):
    T = R.T
    tb = C["tpbank"]
    for mm in range(4):
        T.op("pe", lambda e: e.transpose(out=R.ps[tb][0:64, mm * 128:(mm + 1) * 128], in_=on[:, mm, :],
                                         identity=C["ident"][:]),
             reads=[on_b, C["ident_b"]], writes=[R.psb[tb]])
    T.op("act", lambda e: e.activation(out=R.oT[po:po + 64, ch, G * 512:(G + 1) * 512], in_=R.ps[tb][0:64, :],
                                       func=AF.Copy),
         reads=[R.psb[tb]], writes=[R.oT_b[ch][G]])


def emit_dense_attn(R, C, KT, KT_b, V, V_b, QT, QT_b, dk, scale, obank, mask0,
                    selT=None, selT_b=None, vstride=65):
    T = R.T

    def run(G):
        steps = list(range(16 * G + 16))
        n = len(steps)
        stt = {}

        def stage_a(kt):
            mm_min = max(0, -((-(kt - 16 * G - 3)) // 4))
            c0 = mm_min * 128
            sb_ = C["sbanks"][C["sn"] % len(C["sbanks"])]
            C["sn"] += 1
            stt[kt] = [mm_min, c0, sb_, None]
            T.op("pe", lambda e: e.matmul(R.ps[sb_][:, c0:512], lhsT=KT[0:dk, ktcol(kt):ktcol(kt) + 128],
                                          rhs=QT[0:dk, G * 512 + c0:(G + 1) * 512], start=True, stop=(selT is None)),
                 reads=[KT_b, QT_b], writes=[R.psb[sb_]])
            if selT is not None:
                T.op("pe", lambda e: e.matmul(R.ps[sb_][:, c0:512], lhsT=C["Ebig"][:, kt * 128:(kt + 1) * 128],
                                              rhs=selT[:, G * 512 + c0:(G + 1) * 512], start=False, stop=True),
                     reads=[C["Ebig_b"], selT_b], writes=[R.psb[sb_]])

        def stage_b(kt):
            mm_min, c0, sb_, _ = stt[kt]
            pi = C["pn"] % len(C["PT"])
            C["pn"] += 1
            stt[kt][3] = pi
            PT, PT_b = C["PT"][pi], C["PT_b"][pi]
            T.op("act", lambda e: e.activation(out=PT[:, c0:512], in_=R.ps[sb_][:, c0:512], func=AF.Exp, scale=scale),
                 reads=[R.psb[sb_]], writes=[PT_b])
            if kt >= 16 * G:
                mmb = (kt - 16 * G) // 4
                d = (kt - 16 * G) % 4
                T.op("pool", lambda e: e.tensor_tensor(out=PT[:, mmb * 128:(mmb + 1) * 128],
                                                       in0=PT[:, mmb * 128:(mmb + 1) * 128],
                                                       in1=C["masks"][:, mask0 + d, :], op=ALU.mult),
                     reads=[PT_b, C["masks_b"]], writes=[PT_b])

        def stage_c(kt, first):
            mm_min, c0, sb_, pi = stt[kt]
            PT, PT_b = C["PT"][pi], C["PT_b"][pi]
            for mm in range(mm_min, 4):
                last = (kt == 16 * G + 4 * mm + 3)
                T.op("pe", lambda e: e.matmul(R.ps[obank][:, mm * vstride:(mm + 1) * vstride],
                                              lhsT=PT[:, mm * 128:(mm + 1) * 128], rhs=V[:, ktile(kt), :],
                                              start=(first and mm == mm_min), stop=last, skip_group_check=True),
                     reads=[PT_b, V_b], writes=[R.psb[obank]])

        for idx in range(n + 2):
            if idx < n:
                stage_a(steps[idx])
            if 1 <= idx <= n:
                stage_b(steps[idx - 1])
            if idx >= 2:
                stage_c(steps[idx - 2], idx == 2)
    return run


def alloc_attn_common(R, A, ph):
    nc, T = R.nc, R.T
    C = {"ev": 0, "sn": 0, "pn": 0, "sbanks": [0, 1, 2, 3], "tpbank": 5}

    def sb(name, shape, dt):
        return ph.enter_context(nc.sbuf_tensor(U(name), shape, dt))
    C["sb"] = sb
    C["masks"] = sb("A_masks", [128, 22, 128], BF16)
    C["masks_b"] = Buf("A_masks")
    C["ident"] = sb("A_ident", [128, 128], F32)
    C["ident_b"] = Buf("A_ident")
    ds = T.new_dma_sem("Aconst")
    T.dma("sp", ds, C["masks"][:], A["masks"], writes=[C["masks_b"]])
    T.dma("sp", ds, C["ident"][:], A["ident"], writes=[C["ident_b"]])
    C["PT"] = [sb("A_PT%d" % i, [128, 512], BF16) for i in range(6)]
    C["PT_b"] = [Buf("A_PT%d" % i) for i in range(6)]
    C["on"] = [sb("A_on%d" % i, [128, 4, 64], F32) for i in range(2)]
    C["on_b"] = [Buf("A_on%d" % i) for i in range(2)]
    C["rz"] = [sb("A_rz%d" % i, [128, 4, 1], F32) for i in range(2)]
    C["rz_b"] = [Buf("A_rz%d" % i) for i in range(2)]
    C["kcT"] = [sb("A_kcT%d" % i, [64, 512], BF16) for i in range(2)]
    C["vc"] = [sb("A_vc%d" % i, [128, 4, 65], BF16) for i in range(2)]
    C["kc_b"] = [Buf("A_kc%d" % i) for i in range(2)]
    return C


def gb(A, nm, i=None):
    d = A.get("gb")
    if d is None:
        return []
    b = d[nm]
    if isinstance(b, list):
        b = b[i]
    return [b]


def emit_mla(R, C, A, hook=None):
    nc, T = R.nc, R.T
    ngrp = 0
    hook_left = [4]
    with R.ExitStack() as ph:
        def sb(name, shape, dt):
            return ph.enter_context(nc.sbuf_tensor(U(name), shape, dt))
        KT = [sb("M_KT%d" % i, [96, 4 * NT], BF16) for i in range(2)]
        V = [sb("M_V%d" % i, [128, 64, 65], BF16) for i in range(2)]
        QT = [sb("M_QT%d" % i, [96, NT], BF16) for i in range(2)]
        bufs = [Buf("M_slot%d" % i) for i in range(2)]
        sems = [T.new_dma_sem("Mslot%d" % i) for i in range(2)]
        for h in range(6):
            s = h % 2
            for r in range(4):
                T.dma("sp", sems[s], KT[s][0:64, r * NT:(r + 1) * NT], A["kT_mla_g"][h][r, :, :], reads=gb(A, "kT_mla", h),
                      writes=[bufs[s]])
                T.dma("sp", sems[s], KT[s][64:96, r * NT:(r + 1) * NT], A["kpeT_g"][r, :, :], reads=gb(A, "kpeT"), writes=[bufs[s]])
                for hf in range(2):
                    T.dma("sp", sems[s], V[s][:, r * NBLK + hf * 8:r * NBLK + hf * 8 + 8, :],
                          A["v_mla_g"][hf][r, :, h, :].rearrange("(m p) x -> p m x", p=128), reads=gb(A, "v_mla", hf),
                          writes=[bufs[s]])
            T.dma("sp", sems[s], QT[s][:], A["qT_mla"][h, :, :], writes=[bufs[s]])
            for G in range(4):
                obank = 6 + (G % 2)
                run = emit_dense_attn(R, C, KT[s], bufs[s], V[s], bufs[s], QT[s], bufs[s], 96, 96 ** -0.5,
                                      obank, 0)
                run(G)
                emit_oT_store(R, C, obank, G, h // 2, (h % 2) * 64)
                ngrp += 1
                if hook is not None and ngrp % 3 == 1 and hook_left[0] > 0:
                    hook_left[0] -= 1
                    next(hook)
        T.barrier_all()


def make_nsa_consts(core):
    j = core % 4
    c = np.arange(512)[:, None]
    n = np.arange(128)[None, :]
    ov = ((16 * c < 64 * n + 64) & (16 * c + 32 > 64 * n)).astype(np.float32)
    ovl = np.concatenate([ov, np.ones((512, 1), np.float32)], 1).reshape(4, 128, 129).transpose(1, 0, 2)
    ql = np.arange(128)[:, None]
    w = np.arange(256)[None, :]
    rel = w - 128 - 2 * j
    cur = (ql >= 64).astype(np.int64)
    forced = (rel == cur) | (rel == cur - 1)
    future = rel > cur
    keep = (~(forced | future)).astype(np.float32)
    add = np.where(forced, 1e4, np.where(future, -1.0, 0.0)).astype(np.float32)
    ka = np.stack([keep, add], 1)
    key = np.arange(S)[None, :]
    eb = np.where(key // 64 == np.arange(128)[:, None], 1.0, 0.0).astype(np.float32)
    return dict(ovl=np.ascontiguousarray(ovl).astype(ml_dtypes.bfloat16), keepadd=np.ascontiguousarray(ka),
                Ebig=eb.astype(ml_dtypes.bfloat16))


def emit_nsa_compress(R, C, A):
    nc, T = R.nc, R.T
    with R.ExitStack() as ph:
        def sb(name, shape, dt):
            return ph.enter_context(nc.sbuf_tensor(U(name), shape, dt))
        w1 = [sb("N_w1%d" % i, [64, 32, 128], BF16) for i in range(2)]
        w2 = [sb("N_w2%d" % i, [128, 64], BF16) for i in range(2)]
        posT = [sb("N_pos%d" % i, [64, 32], BF16) for i in range(2)]
        cst_b = Buf("NC_const")
        ds = T.new_dma_sem("NconstP")
        T.dma("pool", ds, w1[0][:], A["cmp_w1_k"].rearrange("(l d) h -> d l h", d=64), writes=[cst_b])
        T.dma("pool", ds, w1[1][:], A["cmp_w1_v"].rearrange("(l d) h -> d l h", d=64), writes=[cst_b])
        T.dma("pool", ds, w2[0][:], A["cmp_w2_k"], writes=[cst_b])
        T.dma("pool", ds, w2[1][:], A["cmp_w2_v"], writes=[cst_b])
        T.dma("pool", ds, posT[0][:], A["cmp_posT_k"], writes=[cst_b])
        T.dma("pool", ds, posT[1][:], A["cmp_posT_v"], writes=[cst_b])
        bias = sb("N_bias", [128, 2], F32)
        bias_b = Buf("N_bias")
        for i in range(2):
            for l in range(32):
                T.op("pe", lambda e: e.matmul(R.ps[4][:, i:i + 1], lhsT=w1[i][:, l, :], rhs=posT[i][:, l:l + 1],
                                              start=(l == 0), stop=(l == 31)),
                     reads=[cst_b], writes=[R.psb[4]])
            T.op("dve", lambda e: e.tensor_copy(out=bias[:, i:i + 1], in_=R.ps[4][:, i:i + 1]),
                 reads=[R.psb[4]], writes=[bias_b])
        xg = [[sb("N_xg%d_%d" % (g, i), [64, S], BF16) for i in range(2)] for g in range(2)]
        xg_b = [Buf("N_xg0"), Buf("N_xg1")]
        hid = sb("N_hid", [128, 512], F32)
        tq = sb("N_tq", [128, 512], F32)
        gl = sb("N_gl", [128, 512], BF16)
        hid_b, gl_b = Buf("N_hid"), Buf("N_gl")
        for g in range(2):
            dsx = T.new_dma_sem("Nxg%d" % g)
            for i, nm in enumerate(("kT_cmp_g", "vT_cmp_g")):
                for r in range(4):
                    dst = xg[g][i][:].rearrange("d (m r p) -> d m r p", r=4, p=128)[:, :, r, :]
                    T.dma("sp", dsx, dst, A[nm][r, g * 64:(g + 1) * 64, :].rearrange("d (m p) -> d m p", p=128),
                          reads=gb(A, nm[:-2]), writes=[xg_b[g]])
        T.op("pool", lambda e: e.memset(gl[:], 0.0), writes=[gl_b])
        yield
        for g in range(2):
            kcT, vc, kc_b = C["kcT"][g], C["vc"][g], C["kc_b"][g]
            T.op("pool", lambda e: e.memset(vc[:], 1.0), reads=[], writes=[kc_b])
            T.op("pool", lambda e: e.memset(kcT[:], 0.0), reads=[], writes=[kc_b])
            for i in range(2):
                for l in range(32):
                    T.op("pe", lambda e: e.matmul(R.ps[4][:, 0:511], lhsT=w1[i][:, l, :],
                                                  rhs=xg[g][i][:, l:l + 16 * 510 + 1:16],
                                                  start=(l == 0), stop=(l == 31)),
                         reads=[cst_b, xg_b[g]], writes=[R.psb[4]])
                T.op("act", lambda e: e.activation(out=hid[:, 0:511], in_=R.ps[4][:, 0:511], func=AF.Identity,
                                                   bias=bias[:, i:i + 1]),
                     reads=[R.psb[4], bias_b], writes=[hid_b])
                T.op("dve", lambda e: e.tensor_tensor(out=tq[:, 0:511], in0=hid[:, 0:511], in1=hid[:, 0:511],
                                                      op=ALU.mult), reads=[hid_b], writes=[hid_b])
                T.op("dve", lambda e: e.tensor_scalar(out=tq[:, 0:511], in0=tq[:, 0:511], scalar1=0.044715,
                                                      scalar2=1.0, op0=ALU.mult, op1=ALU.add),
                     reads=[hid_b], writes=[hid_b])
                T.op("dve", lambda e: e.tensor_tensor(out=tq[:, 0:511], in0=tq[:, 0:511], in1=hid[:, 0:511],
                                                      op=ALU.mult), reads=[hid_b], writes=[hid_b])
                T.op("act", lambda e: e.activation(out=tq[:, 0:511], in_=tq[:, 0:511], func=AF.Sigmoid,
                                                   scale=1.5957691216057308),
                     reads=[hid_b], writes=[hid_b])
                T.op("dve", lambda e: e.tensor_tensor(out=gl[:, 0:511], in0=tq[:, 0:511], in1=hid[:, 0:511],
                                                      op=ALU.mult), reads=[hid_b, gl_b], writes=[gl_b])
                if i == 0:
                    T.op("pe", lambda e: e.matmul(R.ps[4][0:64, 0:511], lhsT=w2[0][:], rhs=gl[:, 0:511],
                                                  start=True, stop=True),
                         reads=[cst_b, gl_b], writes=[R.psb[4]])
                    T.op("dve", lambda e: e.tensor_copy(out=kcT[:, 0:511], in_=R.ps[4][0:64, 0:511]),
                         reads=[R.psb[4]], writes=[kc_b])
                else:
                    for t in range(4):
                        T.op("pe", lambda e: e.matmul(R.ps[4][:, t * 64:(t + 1) * 64], lhsT=gl[:, t * 128:(t + 1) * 128],
                                                      rhs=w2[1][:], start=True, stop=True),
                             reads=[cst_b, gl_b], writes=[R.psb[4]])
                    T.op("dve", lambda e: e.tensor_copy(out=vc[:, :, 0:64],
                                                        in_=R.ps[4][:, 0:256].rearrange("p (t x) -> p t x", x=64)),
                         reads=[R.psb[4]], writes=[kc_b])
                yield
        T.barrier_all()


def emit_nsa(R, C, A):
    nc, T = R.nc, R.T
    SC = 0.125
    with R.ExitStack() as ph:
        def sb(name, shape, dt):
            return ph.enter_context(nc.sbuf_tensor(U(name), shape, dt))
        Ebig = sb("N_Ebig", [128, S], BF16)
        ovl = sb("N_ovl", [128, 4, 129], BF16)
        ka = sb("N_ka", [128, 2, 256], F32)
        gates = sb("N_gates", [128, NBLK, 18], F32)
        cst_b = Buf("N_const")
        C["Ebig"], C["Ebig_b"] = Ebig, cst_b
        ds = T.new_dma_sem("Nconst")
        T.dma("sp", ds, Ebig[:], A["Ebig"], writes=[cst_b])
        T.dma("sp", ds, ovl[:], A["ovl"], writes=[cst_b])
        T.dma("sp", ds, ka[:], A["keepadd"], writes=[cst_b])
        T.dma("sp", ds, gates[:], A["gates"].rearrange("(m p) g -> p m g", p=128), writes=[cst_b])
        selT = sb("N_selT", [128, NT], BF16)
        selT_b = Buf("N_selT")
        QTn = sb("N_QT", [64, 3, NT], BF16)
        KTs = sb("N_KTs", [64, 4 * NT], BF16)
        Vs = sb("N_Vs", [128, 64, 65], BF16)
        KTw = sb("N_KTw", [64, 4 * NT], BF16)
        Vw = sb("N_Vw", [128, 64, 65], BF16)
        kv_b = Buf("N_kv")
        kv_sem = T.new_dma_sem("Nkv")
        oaccs = [sb("N_oacc%d" % i, [128, 3, 4, 64], F32) for i in range(2)]
        oaccs_b = [Buf("N_oacc0"), Buf("N_oacc1")]
        C["pcn"], C["pwn"] = 0, 0
        Pc = [sb("N_Pc%d" % i, [128, 3, 128], BF16) for i in range(2)]
        Pc_b = [Buf("N_Pc0"), Buf("N_Pc1")]
        rzc = sb("N_rzc", [128, 3, 1], F32)
        wc = sb("N_wc", [128, 3, 1], F32)
        sc = sb("N_sc", [128, 128], F32)
        sc2 = sb("N_sc2", [128, 128], F32)
        m8 = sb("N_m8", [128, 16], F32)
        seln = sb("N_seln", [128, 128], F32)
        sm_b = Buf("N_small")
        rz4 = sb("N_rz4", [128, 4, 1], F32)
        w4 = sb("N_w4", [128, 4, 1], F32)
        w4_b = Buf("N_w4")
        Pw = [sb("N_Pw%d" % i, [128, 512], BF16) for i in range(3)]
        Pw_b = [Buf("N_Pw%d" % i) for i in range(3)]
        for g in range(2):
            kcT, vc, kc_b = C["kcT"][g], C["vc"][g], C["kc_b"][g]
            T.dma("sp", kv_sem, QTn[:], A["qT_nsa"][g * 192:(g + 1) * 192, :].rearrange("(h d) t -> d h t", d=64),
                  writes=[kv_b])
            for r in range(4):
                T.dma("sp", kv_sem, KTs[:, r * NT:(r + 1) * NT], A["kT_sel_g"][r, g * 64:(g + 1) * 64, :], reads=gb(A, "kT_sel"), writes=[kv_b])
                T.dma("sp", kv_sem, KTw[:, r * NT:(r + 1) * NT], A["kT_win_g"][r, g * 64:(g + 1) * 64, :], reads=gb(A, "kT_win"), writes=[kv_b])
                T.dma("sp", kv_sem, Vs[:, r * NBLK:(r + 1) * NBLK, :],
                      A["v_sel_g"][r, :, g, :].rearrange("(m p) x -> p m x", p=128), reads=gb(A, "v_sel"), writes=[kv_b])
                T.dma("sp", kv_sem, Vw[:, r * NBLK:(r + 1) * NBLK, :],
                      A["v_win_g"][r, :, g, :].rearrange("(m p) x -> p m x", p=128), reads=gb(A, "v_win"), writes=[kv_b])

            def cmp_block(G, oacc, oacc_b):
                ob, sbk = 3, 4
                for mm in range(4):
                    m = 4 * G + mm
                    msl = slice(m * 128, (m + 1) * 128)
                    for t in range(G + 1):
                        sb_ = C["sbanks"][C["sn"] % len(C["sbanks"])]
                        C["sn"] += 1
                        pi = C["pcn"] % 2
                        C["pcn"] += 1
                        T.op("pe", lambda e: e.matmul(R.ps[sb_][:, 0:384], lhsT=kcT[:, t * 128:(t + 1) * 128],
                                                      rhs=QTn[:, :, msl], start=True, stop=True),
                             reads=[kc_b, kv_b], writes=[R.psb[sb_]])
                        T.op("act", lambda e: e.activation(out=Pc[pi][:].rearrange("p r q -> p (r q)"),
                                                           in_=R.ps[sb_][:, 0:384], func=AF.Exp, scale=SC),
                             reads=[R.psb[sb_]], writes=[Pc_b[pi]])
                        if t == G:
                            T.op("pool", lambda e: e.tensor_tensor(
                                out=Pc[pi][:], in0=Pc[pi][:],
                                in1=C["masks"][:, 12 + mm:13 + mm, :].broadcast_to([128, 3, 128]), op=ALU.mult),
                                reads=[Pc_b[pi], C["masks_b"]], writes=[Pc_b[pi]])
                        for r in range(3):
                            T.op("pe", lambda e: e.matmul(R.ps[ob][:, r * 65:(r + 1) * 65], lhsT=Pc[pi][:, r, :],
                                                          rhs=vc[:, t, :], start=(t == 0 and r == 0), stop=(t == G),
                                                          skip_group_check=True),
                                 reads=[Pc_b[pi], kc_b], writes=[R.psb[ob]])
                        for r in range(3):
                            T.op("pe", lambda e: e.matmul(R.ps[sbk][:, r * 129:(r + 1) * 129], lhsT=Pc[pi][:, r, :],
                                                          rhs=ovl[:, t, :], start=(t == 0 and r == 0), stop=(t == G),
                                                          skip_group_check=True),
                                 reads=[Pc_b[pi], cst_b], writes=[R.psb[sbk]])
                    scv = R.ps[sbk][:, 0:387].rearrange("p (r x) -> p r x", x=129)
                    ocv = R.ps[ob][:, 0:195].rearrange("p (r x) -> p r x", x=65)
                    T.op("dve", lambda e: e.tensor_scalar(out=rzc[:], in0=scv[:, :, 128:129], scalar1=1e-30, scalar2=None,
                                                          op0=ALU.add), reads=[R.psb[sbk]], writes=[sm_b])
                    T.op("dve", lambda e: e.reciprocal(out=rzc[:], in_=rzc[:]), reads=[sm_b], writes=[sm_b])
                    T.op("dve", lambda e: e.tensor_scalar(out=sc[:], in0=scv[:, 0, 0:128], scalar1=rzc[:, 0, :], scalar2=None,
                                                          op0=ALU.mult), reads=[R.psb[sbk], sm_b], writes=[sm_b])
                    for r in (1, 2):
                        T.op("dve", lambda e: e.scalar_tensor_tensor(out=sc[:], in0=scv[:, r, 0:128], scalar=rzc[:, r, :],
                                                                     in1=sc[:], op0=ALU.mult, op1=ALU.add),
                             reads=[R.psb[sbk], sm_b], writes=[sm_b])
                    T.op("dve", lambda e: e.tensor_tensor(
                        out=wc[:], in0=rzc[:],
                        in1=gates[:, m, 9 * g:9 * g + 9].rearrange("p (r x) -> p r x", x=3)[:, :, 0:1], op=ALU.mult),
                        reads=[sm_b, cst_b], writes=[sm_b])
                    for r in range(3):
                        T.op("dve", lambda e: e.tensor_scalar(out=oacc[:, r, mm, :], in0=ocv[:, r, 0:64],
                                                              scalar1=wc[:, r, :], scalar2=None, op0=ALU.mult),
                             reads=[R.psb[ob], sm_b], writes=[oacc_b])
                    w0 = 128 - 8 * m
                    T.op("dve", lambda e: e.tensor_tensor(out=sc[:], in0=sc[:], in1=ka[:, 0, w0:w0 + 128], op=ALU.mult),
                         reads=[sm_b, cst_b], writes=[sm_b])
                    T.op("dve", lambda e: e.tensor_tensor(out=sc[:], in0=sc[:], in1=ka[:, 1, w0:w0 + 128], op=ALU.add),
                         reads=[sm_b, cst_b], writes=[sm_b])
                    T.op("dve", lambda e: e.memset(sc[:, 0:1], 1e4), reads=[sm_b], writes=[sm_b])
                    T.op("dve", lambda e: e.max(out=m8[:, 0:8], in_=sc[:]), reads=[sm_b], writes=[sm_b])
                    T.op("dve", lambda e: e.match_replace(out=sc2[:], in_to_replace=m8[:, 0:8], in_values=sc[:],
                                                          imm_value=-1e9), reads=[sm_b], writes=[sm_b])
                    T.op("dve", lambda e: e.max(out=m8[:, 8:16], in_=sc2[:]), reads=[sm_b], writes=[sm_b])
                    T.op("dve", lambda e: e.tensor_scalar(out=seln[:], in0=sc[:], scalar1=m8[:, 15:16], scalar2=None,
                                                          op0=ALU.is_ge), reads=[sm_b], writes=[sm_b])
                    tb = C["tpbank"]
                    T.op("pe", lambda e: e.transpose(out=R.ps[tb][:, 0:128], in_=seln[:], identity=C["ident"][:]),
                         reads=[sm_b, C["ident_b"]], writes=[R.psb[tb]])
                    T.op("act", lambda e: e.activation(out=selT[:, msl], in_=R.ps[tb][:, 0:128], func=AF.Copy),
                         reads=[R.psb[tb]], writes=[selT_b])

            def win_block(r, G, oacc, oacc_b):
                h = 3 * g + r
                ob = 6
                items = []
                for mm in range(4):
                    for half in range(2):
                        kts = [16 * G + 4 * mm - 4 + half * 4 + x for x in range(4)]
                        if kts[-1] >= 0:
                            items.append((mm, half, kts))
                ni = len(items)
                stt = {}

                def wa(i):
                    mm, half, kts = items[i]
                    msl = slice((4 * G + mm) * 128, (4 * G + mm + 1) * 128)
                    sb_ = C["sbanks"][C["sn"] % len(C["sbanks"])]
                    C["sn"] += 1
                    stt[i] = [sb_, None]
                    for x, kt in enumerate(kts):
                        T.op("pe", lambda e: e.matmul(R.ps[sb_][:, x * 128:(x + 1) * 128],
                                                      lhsT=KTw[:, ktcol(kt):ktcol(kt) + 128],
                                                      rhs=QTn[:, r, msl], start=True, stop=True),
                             reads=[kv_b], writes=[R.psb[sb_]])

                def wb(i):
                    mm, half, kts = items[i]
                    sb_ = stt[i][0]
                    pi = C["pwn"] % len(Pw)
                    C["pwn"] += 1
                    stt[i][1] = pi
                    T.op("act", lambda e: e.activation(out=Pw[pi][:], in_=R.ps[sb_][:], func=AF.Exp, scale=SC),
                         reads=[R.psb[sb_]], writes=[Pw_b[pi]])
                    T.op("pool", lambda e: e.tensor_tensor(
                        out=Pw[pi][:], in0=Pw[pi][:],
                        in1=C["masks"][:, 4 + half * 4:8 + half * 4, :].rearrange("p a q -> p (a q)"),
                        op=ALU.mult), reads=[Pw_b[pi], C["masks_b"]], writes=[Pw_b[pi]])

                def wc_(i):
                    mm, half, kts = items[i]
                    pi = stt[i][1]
                    for x, kt in enumerate(kts):
                        T.op("pe", lambda e: e.matmul(R.ps[ob][:, mm * 65:(mm + 1) * 65],
                                                      lhsT=Pw[pi][:, x * 128:(x + 1) * 128], rhs=Vw[:, ktile(kt), :],
                                                      start=(i == 0 and x == 0), stop=(half == 1 and x == 3),
                                                      skip_group_check=True),
                             reads=[Pw_b[pi], kv_b], writes=[R.psb[ob]])

                for idx in range(ni + 2):
                    if idx < ni:
                        wa(idx)
                    if 1 <= idx <= ni:
                        wb(idx - 1)
                    if idx >= 2:
                        wc_(idx - 2)
                owv = R.ps[ob][:, 0:260].rearrange("p (m x) -> p m x", x=65)
                T.op("dve", lambda e: e.reciprocal(out=rz4[:], in_=owv[:, :, 64:65]), reads=[R.psb[ob]], writes=[w4_b])
                T.op("dve", lambda e: e.tensor_tensor(out=w4[:], in0=rz4[:],
                                                      in1=gates[:, 4 * G:4 * G + 4, 3 * h + 2:3 * h + 3], op=ALU.mult),
                     reads=[w4_b, cst_b], writes=[w4_b])
                for mm in range(4):
                    T.op("dve", lambda e: e.scalar_tensor_tensor(out=oacc[:, r, mm, :], in0=owv[:, mm, 0:64],
                                                                 scalar=w4[:, mm, :], in1=oacc[:, r, mm, :],
                                                                 op0=ALU.mult, op1=ALU.add),
                         reads=[R.psb[ob], w4_b, oacc_b], writes=[oacc_b])

            def sel3_block(G, oacc, oacc_b):
                obs = [4, 6, 7]
                mbanks = [2, 3]
                qbanks = [0, 1]
                nk = 16 * G + 16
                nhs = 3 * nk
                stt = {}

                def geom(kt):
                    mm_min = max(0, -((-(kt - 16 * G - 3)) // 4))
                    return mm_min, mm_min * 128

                def sa(hs):
                    kt, r = hs // 3, hs % 3
                    mm_min, c0 = geom(kt)
                    mb = mbanks[kt % 2]
                    if r == 0:
                        T.op("pe", lambda e: e.matmul(R.ps[mb][:, c0:512], lhsT=Ebig[:, kt * 128:(kt + 1) * 128],
                                                      rhs=selT[:, G * 512 + c0:(G + 1) * 512], start=True, stop=True),
                             reads=[cst_b, selT_b], writes=[R.psb[mb]])
                    qb = qbanks[hs % 2]
                    T.op("pe", lambda e: e.matmul(R.ps[qb][:, c0:512], lhsT=KTs[:, ktcol(kt):ktcol(kt) + 128],
                                                  rhs=QTn[:, r, G * 512 + c0:(G + 1) * 512], start=True, stop=True),
                         reads=[kv_b], writes=[R.psb[qb]])

                def sb_(hs):
                    kt, r = hs // 3, hs % 3
                    mm_min, c0 = geom(kt)
                    mb, qb = mbanks[kt % 2], qbanks[hs % 2]
                    pi = C["pn"] % len(C["PT"])
                    C["pn"] += 1
                    stt[hs] = pi
                    PT, PT_b = C["PT"][pi], C["PT_b"][pi]
                    T.op("act", lambda e: e.activation(out=PT[:, c0:512], in_=R.ps[qb][:, c0:512], func=AF.Exp, scale=SC),
                         reads=[R.psb[qb]], writes=[PT_b])
                    T.op("dve", lambda e: e.tensor_tensor(out=PT[:, c0:512], in0=PT[:, c0:512], in1=R.ps[mb][:, c0:512],
                                                          op=ALU.mult),
                         reads=[PT_b, R.psb[mb]], writes=[PT_b])
                    if kt >= 16 * G:
                        mmb = (kt - 16 * G) // 4
                        d = (kt - 16 * G) % 4
                        T.op("pool", lambda e: e.tensor_tensor(out=PT[:, mmb * 128:(mmb + 1) * 128],
                                                               in0=PT[:, mmb * 128:(mmb + 1) * 128],
                                                               in1=C["masks"][:, d, :], op=ALU.mult),
                             reads=[PT_b, C["masks_b"]], writes=[PT_b])

                def sc_(hs):
                    kt, r = hs // 3, hs % 3
                    mm_min, c0 = geom(kt)
                    PT, PT_b = C["PT"][stt[hs]], C["PT_b"][stt[hs]]
                    for mm in range(mm_min, 4):
                        T.op("pe", lambda e: e.matmul(R.ps[obs[r]][:, mm * 65:(mm + 1) * 65],
                                                      lhsT=PT[:, mm * 128:(mm + 1) * 128], rhs=Vs[:, ktile(kt), :],
                                                      start=(kt == 0 and mm == mm_min), stop=(kt == 16 * G + 4 * mm + 3),
                                                      skip_group_check=True),
                             reads=[PT_b, kv_b], writes=[R.psb[obs[r]]])

                for idx in range(nhs + 4):
                    if idx < nhs:
                        sa(idx)
                    if 1 <= idx <= nhs:
                        sb_(idx - 1)
                    if idx >= 4:
                        sc_(idx - 4)
                for r in range(3):
                    h = 3 * g + r
                    osv = R.ps[obs[r]][:, 0:260].rearrange("p (m x) -> p m x", x=65)
                    T.op("dve", lambda e: e.reciprocal(out=rz4[:], in_=osv[:, :, 64:65]), reads=[R.psb[obs[r]]], writes=[w4_b])
                    T.op("dve", lambda e: e.tensor_tensor(out=w4[:], in0=rz4[:],
                                                          in1=gates[:, 4 * G:4 * G + 4, 3 * h + 1:3 * h + 2], op=ALU.mult),
                         reads=[w4_b, cst_b], writes=[w4_b])
                    for mm in range(4):
                        T.op("dve", lambda e: e.scalar_tensor_tensor(out=oacc[:, r, mm, :], in0=osv[:, mm, 0:64],
                                                                     scalar=w4[:, mm, :], in1=oacc[:, r, mm, :],
                                                                     op0=ALU.mult, op1=ALU.add),
                             reads=[R.psb[obs[r]], w4_b, oacc_b], writes=[oacc_b])
                    emit_transpose_store(R, C, oacc[:, r, :, :], oacc_b, G, 3 + h // 2, (h % 2) * 64)

            C["sbanks"] = [0, 1]
            cmp_block(0, oaccs[0], oaccs_b[0])
            for G in range(4):
                if G + 1 < 4:
                    cmp_block(G + 1, oaccs[(G + 1) % 2], oaccs_b[(G + 1) % 2])
                for r in range(3):
                    win_block(r, G, oaccs[G % 2], oaccs_b[G % 2])
                sel3_block(G, oaccs[G % 2], oaccs_b[G % 2])
            C["sbanks"] = [0, 1, 2, 3]
        T.barrier_all()


def emit_sb(R, C, A):
    nc, T = R.nc, R.T
    SC = 0.125
    with R.ExitStack() as ph:
        def sb(name, shape, dt):
            return ph.enter_context(nc.sbuf_tensor(U(name), shape, dt))
        KT = [sb("S_KT%d" % i, [64, 4 * NT], BF16) for i in range(2)]
        V = [sb("S_V%d" % i, [128, 64, 64], BF16) for i in range(2)]
        QT = [sb("S_QT%d" % i, [64, NT], BF16) for i in range(2)]
        bufs = [Buf("S_slot%d" % i) for i in range(2)]
        sems = [T.new_dma_sem("Sslot%d" % i) for i in range(2)]
        NE = 4
        E = [sb("S_E%d" % i, [128, 512], F32) for i in range(NE)]
        SP = [sb("S_SP%d" % i, [128, 512], BF16) for i in range(NE)]
        X = [sb("S_X%d" % i, [128, 512], F32) for i in range(2)]
        E_b = [Buf("S_E%d" % i) for i in range(NE)]
        SP_b = [Buf("S_SP%d" % i) for i in range(NE)]
        X_b = [Buf("S_X0"), Buf("S_X1")]
        Accb = [sb("S_Accb%d" % i, [128, 512], BF16) for i in range(3)]
        Accb_b = [Buf("S_Accb%d" % i) for i in range(3)]
        tincl = C["masks"][:, 20, :]
        onesb = C["masks"][:, 21, :]
        cbanks = [3, 4]
        zbanks = [0, 1, 2]
        for h in range(4):
            s = h % 2
            for r in range(4):
                T.dma("sp", sems[s], KT[s][:, r * NT:(r + 1) * NT], A["kT_sb_g"][r, h * 64:(h + 1) * 64, :], reads=gb(A, "kT_sb"),
                      writes=[bufs[s]])
                T.dma("sp", sems[s], V[s][:, r * NBLK:(r + 1) * NBLK, :],
                      A["v_sb_g"][r, :, h * 64:(h + 1) * 64].rearrange("(m p) x -> p m x", p=128), reads=gb(A, "v_sb"),
                      writes=[bufs[s]])
            T.dma("sp", sems[s], QT[s][:], A["qT_sb"][h * 64:(h + 1) * 64, :], writes=[bufs[s]])
            for G in range(4):
                ob = 6 + (G % 2)
                for i in range(3):
                    T.op("pool", lambda e: e.memset(Accb[i][:], 0.0), writes=[Accb_b[i]])
                steps = list(range(16 * G + 15, -1, -1))
                ns = len(steps)
                stt = {}

                def geom(kt):
                    mm_min = max(0, -((-(kt - 16 * G - 3)) // 4))
                    return mm_min, mm_min * 128

                def st_a(n):
                    kt = steps[n]
                    mm_min, c0 = geom(kt)
                    zb = zbanks[n % 3]
                    T.op("pe", lambda e: e.matmul(R.ps[zb][:, c0:512], lhsT=KT[s][:, ktcol(kt):ktcol(kt) + 128],
                                                  rhs=QT[s][:, G * 512 + c0:(G + 1) * 512], start=True, stop=True),
                         reads=[bufs[s]], writes=[R.psb[zb]])

                def st_b(n):
                    kt = steps[n]
                    mm_min, c0 = geom(kt)
                    zb, ie = zbanks[n % 3], n % NE
                    T.op("act", lambda e: e.activation(out=E[ie][:, c0:512], in_=R.ps[zb][:, c0:512], func=AF.Exp, scale=SC),
                         reads=[R.psb[zb]], writes=[E_b[ie]])
                    T.op("act", lambda e: e.activation(out=SP[ie][:, c0:512], in_=E[ie][:, c0:512], func=AF.Ln, bias=1.0),
                         reads=[E_b[ie]], writes=[SP_b[ie]])
                    if kt >= 16 * G:
                        d = (kt - 16 * G) % 4
                        T.op("pool", lambda e: e.tensor_tensor(out=SP[ie][:, c0:c0 + 128], in0=SP[ie][:, c0:c0 + 128],
                                                               in1=C["masks"][:, 16 + d, :], op=ALU.mult),
                             reads=[SP_b[ie], C["masks_b"]], writes=[SP_b[ie]])
                    if n + 1 < ns:
                        T.op("pool", lambda e: e.tensor_tensor(out=Accb[(n + 1) % 3][:, c0:512], in0=Accb[n % 3][:, c0:512],
                                                               in1=SP[ie][:, c0:512], op=ALU.add),
                             reads=[Accb_b[n % 3], SP_b[ie]], writes=[Accb_b[(n + 1) % 3]])

                def st_c(n):
                    kt = steps[n]
                    mm_min, c0 = geom(kt)
                    cb, ie = cbanks[n % 2], n % NE
                    T.op("pe", lambda e: e.matmul(R.ps[cb][:, c0:512], lhsT=tincl, rhs=SP[ie][:, c0:512],
                                                  start=True, stop=False),
                         reads=[SP_b[ie], C["masks_b"]], writes=[R.psb[cb]])
                    T.op("pe", lambda e: e.matmul(R.ps[cb][:, c0:512], lhsT=onesb, rhs=Accb[n % 3][:, c0:512],
                                                  start=False, stop=True),
                         reads=[Accb_b[n % 3], C["masks_b"]], writes=[R.psb[cb]])
                    T.op("act", lambda e: e.activation(out=X[n % 2][:, c0:512], in_=R.ps[cb][:, c0:512], func=AF.Exp,
                                                       scale=-1.0),
                         reads=[R.psb[cb]], writes=[X_b[n % 2]])

                def st_e(n):
                    kt = steps[n]
                    mm_min, c0 = geom(kt)
                    ie = n % NE
                    pi = C["pn"] % len(C["PT"])
                    C["pn"] += 1
                    stt[n] = pi
                    PT, PT_b = C["PT"][pi], C["PT_b"][pi]
                    T.op("dve", lambda e: e.tensor_tensor(out=PT[:, c0:512], in0=E[ie][:, c0:512], in1=X[n % 2][:, c0:512],
                                                          op=ALU.mult),
                         reads=[E_b[ie], X_b[n % 2]], writes=[PT_b])
                    if kt >= 16 * G:
                        d = (kt - 16 * G) % 4
                        T.op("pool", lambda e: e.tensor_tensor(out=PT[:, c0:c0 + 128], in0=PT[:, c0:c0 + 128],
                                                               in1=C["masks"][:, 16 + d, :], op=ALU.mult),
                             reads=[PT_b, C["masks_b"]], writes=[PT_b])

                def st_f(n, first):
                    kt = steps[n]
                    mm_min, c0 = geom(kt)
                    PT, PT_b = C["PT"][stt[n]], C["PT_b"][stt[n]]
                    for mm in range(mm_min, 4):
                        T.op("pe", lambda e: e.matmul(R.ps[ob][:, mm * 64:(mm + 1) * 64],
                                                      lhsT=PT[:, mm * 128:(mm + 1) * 128], rhs=V[s][:, ktile(kt), :],
                                                      start=(first and mm == mm_min), stop=(kt == 0), skip_group_check=True),
                             reads=[PT_b, bufs[s]], writes=[R.psb[ob]])

                for idx in range(ns + 4):
                    if idx < ns:
                        st_a(idx)
                    if 1 <= idx <= ns:
                        st_b(idx - 1)
                    if 2 <= idx <= ns + 1:
                        st_c(idx - 2)
                    if 3 <= idx <= ns + 2:
                        st_e(idx - 3)
                    if idx >= 4:
                        st_f(idx - 4, idx == 4)
                i = C["ev"] % 2
                C["ev"] += 1
                T.op("dve", lambda e: e.tensor_copy(out=C["on"][i][:],
                                                    in_=R.ps[ob][:, 0:256].rearrange("p (m x) -> p m x", x=64)),
                     reads=[R.psb[ob]], writes=[C["on_b"][i]])
                emit_transpose_store(R, C, C["on"][i], C["on_b"][i], G, 6 + h // 2, (h % 2) * 64)
        T.barrier_all()


L = 2
GATHER = ["kT_mla", "kpeT", "v_mla", "kT_cmp", "vT_cmp", "kT_sel", "kT_win", "v_sel", "v_win", "kT_sb", "v_sb"]
LOCAL = ["qT_mla", "qT_nsa", "qT_sb", "gates"]
POUT = {nm: (shp, dt) for nm, shp, dt in P_OUTS}
W_IN = [("ffn1_w_gate", [L, D, DFF]), ("ffn1_w_up", [L, D, DFF]), ("ffn1_w_down", [L, DFF, D]),
        ("ffn2_w_gate", [L, D, DFF]), ("ffn2_w_up", [L, D, DFF]), ("ffn2_w_down", [L, DFF, D]),
        ("w_in", [L, D, DIN]), ("w_in_sw", [L, D, NSW]), ("w_uq", [L, 256, 576]), ("w_uq_sw", [L, 256, 576]),
        ("w_ukv", [L, 128, 768]), ("smallsP", [L, 128, 32]), ("w_out", [L, D, D]),
        ("cmp_w1_k", [L, 2048, 128]), ("cmp_w1_v", [L, 2048, 128]), ("cmp_w2_k", [L, 128, 64]),
        ("cmp_w2_v", [L, 128, 64]), ("cmp_posT_k", [L, 64, 32]), ("cmp_posT_v", [L, 64, 32]),
        ("gains", [128, 3 * L + 1, 8])]
C_IN = [("cosM", [96, NT], F32), ("sinM", [96, NT], F32), ("cosK", [32, NT], F32), ("sinK", [32, NT], F32),
        ("cosN", [128, NT], F32), ("sinN", [128, NT], F32), ("masks", [128, 22, 128], BF16),
        ("ident", [128, 128], F32), ("Ebig", [128, S], BF16), ("ovl", [128, 4, 129], BF16),
        ("keepadd", [128, 2, 256], F32)]


def emit_layer_X(R, A, l, do_post, do_pre, final, xin, xout, mid_hook=None):
    nc, T = R.nc, R.T
    with ExitStack() as px:
        alloc_xT(R, px)
        hT = px.enter_context(nc.sbuf_tensor(U("hT"), [128, 8, NT], BF16))
        hb = [[Buf("h%d_%d" % (k, t)) for t in range(4)] for k in range(8)]
        gam = px.enter_context(nc.sbuf_tensor(U("gam"), [128, 3 * L + 1, 8], F32))
        gam_b = Buf("gam")
        sq = [px.enter_context(nc.sbuf_tensor(U("sq%d" % i), [128, 512], BF16)) for i in range(2)]
        sq_b = [Buf("sq0"), Buf("sq1")]
        rstd = px.enter_context(nc.sbuf_tensor(U("rstd"), [128, 512], F32))
        rstd_b = Buf("rstd")
        ld = T.new_dma_sem("ldx")
        for k in range(8):
            T.dma("sp", ld, R.xT[:, k, :], xin[k * 128:(k + 1) * 128, :], writes=[R.xb[k][t] for t in range(4)])
        T.dma("sp", ld, gam[:], A["gains"], writes=[gam_b])

        def norm(gi):
            emit_norm(R, hT, hb, gam[:, gi, :], gam_b, sq, sq_b, rstd, rstd_b, 6)

        lp = l
        with ExitStack() as pf:
            R.stack = pf
            W = alloc_ffn_work(R)

            def ffn(pref, ll):
                emit_ffn(R, hT, hb, A[pref + "_w_gate"][ll], A[pref + "_w_up"][ll], A[pref + "_w_down"][ll], W)

            if do_post:
                oT2 = pf.enter_context(nc.sbuf_tensor(U("oT2"), [128, 8, NT], BF16))
                o_b = Buf("oT2")
                d1 = T.new_dma_sem("oT2")
                T.dma("sp", d1, oT2[:], A["oT_d"], writes=[o_b])
                for hf in range(2):
                    T.dma("pool", W["dsem"][hf], W["wd"][hf][:],
                          A["w_out"][l][hf * 512:(hf + 1) * 512, :].rearrange("(k p) c -> p k c", p=128),
                          writes=[W["wb"][hf]])
                n = 0
                for tt in range(4):
                    sl = slice(tt * 512, (tt + 1) * 512)
                    for dmc in range(8):
                        bk = n % 4
                        n += 1
                        for k in range(8):
                            T.op("pe", lambda e: e.matmul(R.ps[bk][:], lhsT=W["wd"][k // 4][:, k % 4, dmc * 128:(dmc + 1) * 128],
                                                          rhs=oT2[:, k, sl], start=(k == 0), stop=(k == 7)),
                                 reads=[o_b, W["wb"][k // 4]], writes=[R.psb[bk]])
                        T.op("dve", lambda e: e.tensor_tensor(out=R.xT[:, dmc, sl], in0=R.ps[bk][:], in1=R.xT[:, dmc, sl],
                                                              op=ALU.add),
                             reads=[R.psb[bk], R.xb[dmc][tt]], writes=[R.xb[dmc][tt]])
                norm(3 * l + 2)
                ffn("ffn2", l)
                lp = l + 1
            if do_pre:
                norm(3 * lp + 0)
                ffn("ffn1", lp)
            T.barrier_all()
        if do_pre:
            norm(3 * lp + 1)
            AP_ = dict(A)
            for nm in ("w_in", "w_in_sw", "w_uq", "w_uq_sw", "w_ukv", "smallsP"):
                AP_[nm] = A[nm][lp]
            emit_stage_P(R, hT, hb, AP_, mid_hook=mid_hook)
        st = T.new_dma_sem("stx")
        if final:
            ysq = [px.enter_context(nc.sbuf_tensor(U("ystg%d" % i), [128, 512], F32)) for i in range(2)]
            y_b = [Buf("y0"), Buf("y1")]
            ss, ss_b = R.ps[6], R.psb[6]
            n = 0
            for tt in range(4):
                sl = slice(tt * 512, (tt + 1) * 512)
                for k in range(8):
                    a = k % 2
                    T.op("act", lambda e: e.activation(out=sq[a][:], in_=R.xT[:, k, sl], func=AF.Square),
                         reads=[R.xb[k][tt]], writes=[sq_b[a]])
                    T.op("pe", lambda e: e.matmul(ss[:], lhsT=R.onesm[:], rhs=sq[a][:], start=(k == 0), stop=(k == 7)),
                         reads=[sq_b[a], R.onesm_b], writes=[ss_b])
                T.op("act", lambda e: e.activation(out=rstd[:], in_=ss[:], func=AF.Sqrt, bias=EPS, scale=1.0 / D),
                     reads=[ss_b], writes=[rstd_b])
                T.op("dve", lambda e: e.reciprocal(out=rstd[:], in_=rstd[:]), reads=[rstd_b], writes=[rstd_b])
                for k in range(8):
                    i = n % 2
                    n += 1
                    T.op("dve", lambda e: e.scalar_tensor_tensor(out=ysq[i][:], in0=R.xT[:, k, sl],
                                                                 scalar=gam[:, 3 * L, k:k + 1], in1=rstd[:],
                                                                 op0=ALU.mult, op1=ALU.mult),
                         reads=[R.xb[k][tt], rstd_b, gam_b], writes=[y_b[i]])
                    T.dma("sp", st, xout[k * 128:(k + 1) * 128, sl], ysq[i][:], reads=[y_b[i]])
        else:
            for k in range(8):
                T.dma("sp", st, xout[k * 128:(k + 1) * 128, :], R.xT[:, k, :], reads=[R.xb[k][t] for t in range(4)])
        T.barrier_all()
        return st


def emit_layer_A(R, A, l):
    nc, T = R.nc, R.T
    with ExitStack() as pa:
        R.oT = pa.enter_context(nc.sbuf_tensor(U("oT"), [128, 8, NT], BF16))
        R.oT_b = [[Buf("oT%d_%d" % (c, g)) for g in range(4)] for c in range(8)]
        AL = dict(A)
        for nm in ("cmp_w1_k", "cmp_w1_v", "cmp_w2_k", "cmp_w2_v", "cmp_posT_k", "cmp_posT_v"):
            AL[nm] = A[nm][l]
        with ExitStack() as ph:
            C = alloc_attn_common(R, AL, ph)
            gen = emit_nsa_compress(R, C, AL)
            next(gen)
            emit_mla(R, C, AL, hook=gen)
            for _ in gen:
                pass
            emit_nsa(R, C, AL)
            emit_sb(R, C, AL)
        st = T.new_dma_sem("stoT")
        T.dma("sp", st, A["oT_d"], R.oT[:], reads=[b for ll in R.oT_b for b in ll])
        T.barrier_all()


def build_launch(kind, l):
    nc = bass.Bass("TRN2", target_bir_lowering=False)
    A = {}

    def din(nm, shp, dt=F32):
        A[nm] = nc.dram_tensor(nm, shp, dt, kind="ExternalInput").ap()

    def dout(nm, shp, dt=F32):
        A[nm] = nc.dram_tensor(nm, shp, dt, kind="ExternalOutput").ap()
    for nm, shp in W_IN:
        din(nm, shp)
    for nm, shp, dt in C_IN:
        din(nm, shp, dt)
    din("xT_in", [D, NT])
    dout("xT_out", [D, NT])
    if kind != "first":
        for nm in GATHER:
            shp, dt = POUT[nm]
            din(nm + "_g", [4] + shp, dt)
        for nm in LOCAL:
            shp, dt = POUT[nm]
            din(nm, shp, dt)
        A["oT_d"] = nc.dram_tensor("oT_d", [128, 8, NT], BF16, kind="Internal").ap()
    if kind != "last":
        for nm, shp, dt in P_OUTS:
            if kind == "first" or nm not in LOCAL:
                dout(nm, shp, dt)
            else:
                A[nm + "_o"] = nc.dram_tensor(nm + "_o", shp, dt, kind="ExternalOutput").ap()
    with ExitStack() as stack:
        T = Tracker(nc, stack)
        R = setup_common(nc, stack, T)
        if kind != "first":
            emit_layer_A(R, A, l)
        AX = dict(A)
        if kind == "mid":
            for nm in LOCAL:
                AX[nm] = A[nm + "_o"]
        st = emit_layer_X(R, AX, l, do_post=(kind != "first"), do_pre=(kind != "last"), final=(kind == "last"),
                          xin=A["xT_in"], xout=A["xT_out"])
        nc.sync.wait_ge(T.sem[st], T.cnt[st])
    return nc


def _host_weights(inp):
    f = lambda a: np.ascontiguousarray(np.asarray(a, dtype=np.float32))
    W = {}
    for nm in ("ffn1_w_gate", "ffn1_w_up", "ffn1_w_down", "ffn2_w_gate", "ffn2_w_up", "ffn2_w_down", "w_in", "w_out"):
        W[nm] = f(inp[nm])
    W["w_uq"] = f(inp["mla_w_uq"])
    W["w_ukv"] = f(inp["mla_w_ukv"])
    W["w_in_sw"] = np.stack([make_w_in_sw(W["w_in"][l]) for l in range(L)])
    W["w_uq_sw"] = np.stack([swap_cols_rope(W["w_uq"][l], 96, 64, 32) for l in range(L)])
    W["smallsP"] = np.stack([make_smallsP(f(inp["mla_q_norm"])[l], f(inp["mla_kv_norm"])[l], f(inp["nsa_gate_bias"])[l])
                             for l in range(L)])
    W["cmp_w1_k"] = f(inp["nsa_cmp_w1_k"])
    W["cmp_w1_v"] = f(inp["nsa_cmp_w1_v"])
    W["cmp_w2_k"] = f(inp["nsa_cmp_w2_k"])
    W["cmp_w2_v"] = f(inp["nsa_cmp_w2_v"])
    W["cmp_posT_k"] = np.ascontiguousarray(f(inp["nsa_cmp_pos_k"]).transpose(0, 2, 1))
    W["cmp_posT_v"] = np.ascontiguousarray(f(inp["nsa_cmp_pos_v"]).transpose(0, 2, 1))
    g = np.zeros((128, 3 * L + 1, 8), np.float32)
    for l in range(L):
        for i, nm in enumerate(("ffn1_norm", "mix_norm", "ffn2_norm")):
            g[:, 3 * l + i, :] = f(inp[nm])[l].reshape(8, 128).T
    g[:, 3 * L, :] = f(inp["final_norm"]).reshape(8, 128).T
    W["gains"] = g
    return W


def _core_consts(core):
    c = {}
    c.update(rope_tables(core))
    c["masks"] = make_masks(core)
    c["ident"] = np.eye(128, dtype=np.float32)
    c.update(make_nsa_consts(core))
    return c


def _kernel_unfused_impl(**inp):
    x = np.asarray(inp["x"], dtype=np.float32)
    W = _host_weights(inp)
    consts = [_core_consts(c) for c in range(8)]
    xT = [np.ascontiguousarray(x[c // 4][own_positions(c)].T) for c in range(8)]
    cores = list(range(8))
    nc = build_launch("first", 0)
    ims = [dict(W, **consts[c], xT_in=xT[c]) for c in cores]
    res = run_bass_kernel_spmd(nc, ims, core_ids=cores).results
    for l in range(L):
        kind = "mid" if l < L - 1 else "last"
        nc = build_launch(kind, l)
        ims = []
        for c in cores:
            b = c // 4
            im = dict(W, **consts[c], xT_in=np.asarray(res[c]["xT_out"]))
            for nm in GATHER:
                im[nm + "_g"] = np.stack([np.asarray(res[4 * b + r][nm]) for r in range(4)])
            for nm in LOCAL:
                key = nm if l == 0 else nm + "_o"
                im[nm] = np.asarray(res[c][key])
            ims.append(im)
        res = run_bass_kernel_spmd(nc, ims, core_ids=cores).results
    out = np.zeros((B, S, D), np.float32)
    for c in cores:
        out[c // 4][own_positions(c)] = np.asarray(res[c]["xT_out"]).T
    return out


PIECES = [
    ([192, NT], [("kT_mla", "heads", 0, 3)]),
    ([192, NT], [("kT_mla", "heads", 3, 6)]),
    ([32, NT], [("kpeT", "rows", 0, 32)]),
    ([256, NT], [("vT_cmp", "rows", 0, 128), ("kT_sel", "rows", 128, 256)]),
    ([256, NT], [("kT_cmp", "rows", 0, 128), ("kT_win", "rows", 128, 256)]),
    ([256, NT], [("kT_sb", "rows", 0, 256)]),
    ([1024, 6, 65], [("v_mla", "half", 0, 0)]),
    ([1024, 6, 65], [("v_mla", "half", 1, 1)]),
    ([NT, 2, 65], [("v_sel", "all", 0, 0)]),
    ([NT, 2, 65], [("v_win", "all", 0, 0)]),
    ([NT, 256], [("v_sb", "all", 0, 0)]),
]


def make_pieces(nc, l):
    V = {"kT_mla": [None] * 6, "kT_mla_g": [None] * 6, "v_mla": [None] * 2, "v_mla_g": [None] * 2}
    GB = {"kT_mla": [None] * 6, "v_mla": [None] * 2}
    cc = []
    for k, (shp, members) in enumerate(PIECES):
        n = int(np.prod(shp))
        w = n // 128
        gs = nc.dram_tensor("gs%d_%d" % (l, k), [128, w], BF16, kind="Internal").ap()
        gd = nc.dram_tensor("gd%d_%d" % (l, k), [512, w], BF16, kind="Internal").ap()
        pb = Buf("piece%d_%d" % (l, k))
        cc.append((gs, gd, pb))
        fs = gs.rearrange("p w -> (p w)")
        fd = gd.rearrange("(r p) w -> r (p w)", r=4)
        if len(shp) == 2:
            ns = fs.rearrange("(a c) -> a c", c=shp[1])
            nd = fd.rearrange("r (a c) -> r a c", c=shp[1])
        else:
            ns = fs.rearrange("(t h x) -> t h x", h=shp[1], x=shp[2])
            nd = fd.rearrange("r (t h x) -> r t h x", h=shp[1], x=shp[2])
        for nm, kind, lo, hi in members:
            if kind == "heads":
                for h in range(lo, hi):
                    V[nm][h] = ns[(h - lo) * 64:(h - lo + 1) * 64, :]
                    V[nm + "_g"][h] = nd[:, (h - lo) * 64:(h - lo + 1) * 64, :]
                    GB[nm][h] = pb
            elif kind == "rows":
                V[nm] = ns[lo:hi, :]
                V[nm + "_g"] = nd[:, lo:hi, :]
                GB[nm] = pb
            elif kind == "half":
                V[nm][lo] = ns
                V[nm + "_g"][lo] = nd
                GB[nm][lo] = pb
            else:
                V[nm] = ns
                V[nm + "_g"] = nd
                GB[nm] = pb
    V["cc"] = cc
    V["gb"] = GB
    return V


def build_fused():
    nc = bass.Bass("TRN2", target_bir_lowering=False)
    A = {}
    for nm, shp in W_IN:
        A[nm] = nc.dram_tensor(nm, shp, F32, kind="ExternalInput").ap()
    for nm, shp, dt in C_IN:
        A[nm] = nc.dram_tensor(nm, shp, dt, kind="ExternalInput").ap()
    A["xT_in"] = nc.dram_tensor("xT_in", [D, NT], F32, kind="ExternalInput").ap()
    A["yT_out"] = nc.dram_tensor("yT_out", [D, NT], F32, kind="ExternalOutput").ap()
    A["xT_d"] = nc.dram_tensor("xT_d", [D, NT], F32, kind="Internal").ap()
    A["oT_d"] = nc.dram_tensor("oT_d", [128, 8, NT], BF16, kind="Internal").ap()
    LA = []
    for l in range(L):
        V = make_pieces(nc, l)
        for nm in LOCAL:
            shp, dt = POUT[nm]
            V[nm] = nc.dram_tensor("%s_l%d" % (nm, l), shp, dt, kind="Internal").ap()
        LA.append(V)
    with ExitStack() as stack:
        T = Tracker(nc, stack)
        R = setup_common(nc, stack, T)
        st = None
        GRP = [[0, 1, 2, 3], [4, 5, 6, 7]]

        def mk_hook(ll):
            def hook():
                T.wait_dma_all("pool")
                for k in (2, 0, 6, 7, 1):
                    gs, gd, pb = LA[ll]["cc"][k]
                    T.collective(T.new_dma_sem("cc%d_%d" % (ll, k)), gs, gd, GRP, writes=[pb])
            return hook

        for l in range(L):
            if l == 0:
                emit_layer_X(R, dict(A, **LA[0]), 0, do_post=False, do_pre=True, final=False,
                             xin=A["xT_in"], xout=A["xT_d"], mid_hook=mk_hook(0))
            T.barrier_all()
            for k in (4, 3, 8, 9, 5, 10):
                gs, gd, pb = LA[l]["cc"][k]
                T.collective(T.new_dma_sem("cc%d_%d" % (l, k)), gs, gd, GRP, writes=[pb])
            emit_layer_A(R, dict(A, **LA[l]), l)
            last = (l == L - 1)
            AX = dict(A, **(LA[l + 1] if not last else {}))
            st = emit_layer_X(R, AX, l, do_post=True, do_pre=not last, final=last,
                              xin=A["xT_d"], xout=(A["yT_out"] if last else A["xT_d"]),
                              mid_hook=(None if last else mk_hook(l + 1)))
        nc.sync.wait_ge(T.sem[st], T.cnt[st])
    return nc


def kernel_unfused(**inp):
    return _kernel_unfused_impl(**inp)


def kernel_fused(**inp):
    x = np.asarray(inp["x"], dtype=np.float32)
    W = _host_weights(inp)
    cores = list(range(8))
    nc = build_fused()
    ims = []
    for c in cores:
        xT = np.ascontiguousarray(x[c // 4][own_positions(c)].T)
        ims.append(dict(W, **_core_consts(c), xT_in=xT))
    res = run_bass_kernel_spmd(nc, ims, core_ids=cores).results
    out = np.zeros((B, S, D), np.float32)
    for c in cores:
        out[c // 4][own_positions(c)] = np.asarray(res[c]["yT_out"]).T
    return out


def kernel(**inp):
    return kernel_fused(**inp)
```

```python
import numpy as np
import ml_dtypes
from contextlib import ExitStack
import concourse.bass as bass
import concourse.mybir as mybir
from concourse.bass_utils import run_bass_kernel_spmd

F32 = mybir.dt.float32
BF16 = mybir.dt.bfloat16
AF = mybir.ActivationFunctionType
ALU = mybir.AluOpType
AX = mybir.AxisListType

D = 1024
S = 8192
B = 2
DFF = 2816
NT = 2048
NBLK = 16
EPS = 1e-6
DIN = 2354


_UN = [0]
CC_INC = 1


def U(name):
    _UN[0] += 1
    return "%s_u%d" % (name, _UN[0])


class Buf:
    __slots__ = ("name", "w", "r")

    def __init__(self, name):
        self.name = name
        self.w = None
        self.r = {}


class Tracker:
    def __init__(self, nc, stack):
        self.nc = nc
        self.stack = stack
        self.eng = {"pe": nc.tensor, "act": nc.scalar, "dve": nc.vector,
                    "pool": nc.gpsimd, "sp": nc.sync}
        self.sem = {}
        self.cnt = {}
        self.seen = {k: {} for k in self.eng}
        for k in self.eng:
            self.sem[k] = stack.enter_context(nc.semaphore("s_" + k))
            self.cnt[k] = 0
        self.ndma = 0

    def new_dma_sem(self, name):
        key = "dma_" + name + "_%d" % self.ndma
        self.ndma += 1
        self.sem[key] = self.stack.enter_context(self.nc.semaphore(key))
        self.cnt[key] = 0
        return key

    def _deps(self, e, reads, writes, ignore=None):
        deps = {}

        def add(k, c):
            if c > deps.get(k, 0):
                deps[k] = c
        for b in reads:
            if b.w is not None:
                add(*b.w)
        for b in writes:
            if b.w is not None:
                add(*b.w)
            for k, c in b.r.items():
                add(k, c)
        for k, c in deps.items():
            if (k == "pe" and e == "pe") or k == ignore:
                continue
            if k.startswith("dma_"):
                c = max(c, self.cnt[k])
            if c > self.seen[e].get(k, 0):
                self.eng[e].wait_ge(self.sem[k], c)
                self.seen[e][k] = c

    def op(self, e, fn, reads=(), writes=()):
        self._deps(e, reads, writes)
        ins = fn(self.eng[e])
        self.cnt[e] += 1
        c = self.cnt[e]
        ins.then_inc(self.sem[e], 1)
        for b in reads:
            if c > b.r.get(e, 0):
                b.r[e] = c
        for b in writes:
            b.w = (e, c)
            b.r = {}
        return ins

    def dma(self, q, dsem, out, in_, reads=(), writes=()):
        self._deps(q, reads, writes, ignore=dsem)
        ins = self.eng[q].dma_start(out=out, in_=in_)
        self.cnt[dsem] += 16
        c = self.cnt[dsem]
        ins.then_inc(self.sem[dsem], 16)
        for b in reads:
            if c > b.r.get(dsem, 0):
                b.r[dsem] = c
        for b in writes:
            b.w = (dsem, c)
            b.r = {}
        return ins

    def collective(self, dsem, src, dst, groups, reads=(), writes=()):
        self._deps("pool", reads, writes, ignore=dsem)
        ins = self.nc.gpsimd.collective_compute("AllGather", ALU.bypass, replica_groups=groups,
                                                ins=[src.opt()], outs=[dst.opt()])
        self.cnt[dsem] += CC_INC
        c = self.cnt[dsem]
        ins.then_inc(self.sem[dsem], CC_INC)
        for b in reads:
            if c > b.r.get(dsem, 0):
                b.r[dsem] = c
        for b in writes:
            b.w = (dsem, c)
            b.r = {}
        return ins

    def wait_dma_all(self, e):
        for k, c in self.cnt.items():
            if k.startswith("dma_") and c > self.seen[e].get(k, 0):
                self.eng[e].wait_ge(self.sem[k], c)
                self.seen[e][k] = c

    def barrier_all(self):
        for e in self.eng:
            for k, c in self.cnt.items():
                if k == e or c == 0:
                    continue
                if c > self.seen[e].get(k, 0):
                    self.eng[e].wait_ge(self.sem[k], c)
                    self.seen[e][k] = c


class Res:
    pass


def setup_common(nc, stack, T):
    R = Res()
    R.nc, R.T, R.stack = nc, T, stack
    R.ExitStack = ExitStack
    R.ps = []
    R.psb = []
    for i in range(8):
        R.ps.append(stack.enter_context(nc.psum_tensor("ps%d" % i, [128, 512], F32)))
        R.psb.append(Buf("ps%d" % i))
    R.xb = [[Buf("x%d_%d" % (k, t)) for t in range(4)] for k in range(8)]
    R.onesm = stack.enter_context(nc.sbuf_tensor(U("onesm"), [128, 128], BF16))
    R.onesm_b = Buf("onesm")
    T.op("pool", lambda e: e.memset(R.onesm[:], 1.0), writes=[R.onesm_b])
    return R


def alloc_xT(R, stack):
    R.xT = stack.enter_context(R.nc.sbuf_tensor(U("xT"), [128, 8, NT], F32))


def emit_norm(R, hT, hb, gam, gam_b, sq, sq_b, rstd, rstd_b, ssbank):
    T = R.T
    ss, ss_b = R.ps[ssbank], R.psb[ssbank]
    for tt in range(4):
        sl = slice(tt * 512, (tt + 1) * 512)
        for k in range(8):
            a = k % 2
            T.op("act", lambda e: e.activation(out=sq[a][:], in_=R.xT[:, k, sl], func=AF.Square),
                 reads=[R.xb[k][tt]], writes=[sq_b[a]])
            T.op("pe", lambda e: e.matmul(ss[:], lhsT=R.onesm[:], rhs=sq[a][:], start=(k == 0), stop=(k == 7)),
                 reads=[sq_b[a], R.onesm_b], writes=[ss_b])
        T.op("act", lambda e: e.activation(out=rstd[:], in_=ss[:], func=AF.Sqrt, bias=EPS, scale=1.0 / D),
             reads=[ss_b], writes=[rstd_b])
        T.op("dve", lambda e: e.reciprocal(out=rstd[:], in_=rstd[:]), reads=[rstd_b], writes=[rstd_b])
        for k in range(8):
            T.op("dve", lambda e: e.scalar_tensor_tensor(out=hT[:, k, sl], in0=R.xT[:, k, sl],
                                                         scalar=gam[:, k:k + 1], in1=rstd[:],
                                                         op0=ALU.mult, op1=ALU.mult),
                 reads=[R.xb[k][tt], rstd_b, gam_b], writes=[hb[k][tt]])


def emit_ffn(R, hT, hb, wg_d, wu_d, wd_d, W):
    T = R.T
    nfg = 6
    n_g = n_y = 0
    for fg in range(nfg):
        ncf = 4 if fg < 5 else 2
        wcols = ncf * 128
        c0 = fg * 512
        s = fg % 2
        wgs, wus, wds, wb, dsem = W["wg"][s], W["wu"][s], W["wd"][s], W["wb"][s], W["dsem"][s]
        T.dma("pool", dsem, wgs[:, :, 0:wcols],
              wg_d[:, c0:c0 + wcols].rearrange("(k p) c -> p k c", p=128), writes=[wb])
        T.dma("pool", dsem, wus[:, :, 0:wcols],
              wu_d[:, c0:c0 + wcols].rearrange("(k p) c -> p k c", p=128), writes=[wb])
        T.dma("pool", dsem, wds[:, 0:ncf, :],
              wd_d[c0:c0 + wcols, :].rearrange("(c p) m -> p c m", p=128), writes=[wb])
        for tt in range(4):
            sl = slice(tt * 512, (tt + 1) * 512)
            asl = (fg * 4 + tt) % 2
            for c in range(ncf):
                gi, ui = W["gbanks"][n_g % 2], W["ubanks"][n_g % 2]
                sgi = n_g % 2
                n_g += 1
                for k in range(8):
                    T.op("pe", lambda e: e.matmul(R.ps[gi][:], lhsT=wgs[:, k, c * 128:(c + 1) * 128],
                                                  rhs=hT[:, k, sl], start=(k == 0), stop=(k == 7)),
                         reads=[wb, hb[k][tt]], writes=[R.psb[gi]])
                for k in range(8):
                    T.op("pe", lambda e: e.matmul(R.ps[ui][:], lhsT=wus[:, k, c * 128:(c + 1) * 128],
                                                  rhs=hT[:, k, sl], start=(k == 0), stop=(k == 7)),
                         reads=[wb, hb[k][tt]], writes=[R.psb[ui]])
                T.op("act", lambda e: e.activation(out=W["sg"][sgi][:], in_=R.ps[gi][:], func=AF.Silu),
                     reads=[R.psb[gi]], writes=[W["sg_b"][sgi]])
                T.op("dve", lambda e: e.tensor_tensor(out=W["act"][asl][:, c, :], in0=W["sg"][sgi][:],
                                                      in1=R.ps[ui][:], op=ALU.mult),
                     reads=[W["sg_b"][sgi], R.psb[ui]], writes=[W["act_b"][asl][c]])
            for dmc in range(8):
                yi = W["ybanks"][n_y % 2]
                n_y += 1
                for c in range(ncf):
                    T.op("pe", lambda e: e.matmul(R.ps[yi][:], lhsT=wds[:, c, dmc * 128:(dmc + 1) * 128],
                                                  rhs=W["act"][asl][:, c, :], start=(c == 0), stop=(c == ncf - 1)),
                         reads=[wb, W["act_b"][asl][c]], writes=[R.psb[yi]])
                T.op("dve", lambda e: e.scalar_tensor_tensor(out=R.xT[:, dmc, sl], in0=R.ps[yi][:], scalar=0.5,
                                                             in1=R.xT[:, dmc, sl], op0=ALU.mult, op1=ALU.add),
                     reads=[R.psb[yi], R.xb[dmc][tt]], writes=[R.xb[dmc][tt]])


def alloc_ffn_work(R):
    nc, stack, T = R.nc, R.stack, R.T
    W = {}
    W["wg"] = [stack.enter_context(nc.sbuf_tensor(U("wg%d" % i), [128, 8, 512], BF16)) for i in range(2)]
    W["wu"] = [stack.enter_context(nc.sbuf_tensor(U("wu%d" % i), [128, 8, 512], BF16)) for i in range(2)]
    W["wd"] = [stack.enter_context(nc.sbuf_tensor(U("wd%d" % i), [128, 4, 1024], BF16)) for i in range(2)]
    W["wb"] = [Buf("wslot%d" % i) for i in range(2)]
    W["dsem"] = [T.new_dma_sem("ffnw%d" % i) for i in range(2)]
    W["sg"] = [stack.enter_context(nc.sbuf_tensor(U("sg%d" % i), [128, 512], F32)) for i in range(2)]
    W["sg_b"] = [Buf("sg%d" % i) for i in range(2)]
    W["act"] = [stack.enter_context(nc.sbuf_tensor(U("act%d" % i), [128, 4, 512], BF16)) for i in range(2)]
    W["act_b"] = [[Buf("act%d_%d" % (i, c)) for c in range(4)] for i in range(2)]
    W["gbanks"], W["ubanks"], W["ybanks"] = [0, 1], [2, 3], [4, 5]
    return W


C_CQ, C_CKV, C_KR, C_NQ, C_NKC, C_NVC, C_NKS, C_NVS, C_NKW, C_NVW, C_NG, C_SQ, C_SK, C_SV = (
    0, 256, 384, 416, 800, 928, 1056, 1184, 1312, 1440, 1568, 1586, 1842, 2098)
SW_KR, SW_NQ, SW_NKC, SW_NKS, SW_NKW = 0, 32, 416, 544, 672
NSW = 800


def emit_stage_P(R, hT, hb, A, mid_hook=None):
    nc, T = R.nc, R.T
    with R.ExitStack() as ph:
        def sb(name, shape, dt):
            return ph.enter_context(nc.sbuf_tensor(U(name), shape, dt))
        win = sb("P_win", [128, 8, DIN], BF16)
        wsw = sb("P_wsw", [128, 8, NSW], BF16)
        wuq = sb("P_wuq", [128, 2, 576], BF16)
        wuqs = sb("P_wuqs", [128, 2, 576], BF16)
        wukv = sb("P_wukv", [128, 768], BF16)
        sm = sb("P_sm", [128, 32], F32)
        wb_, smb = Buf("P_w"), Buf("P_sm")
        dw = T.new_dma_sem("Pw")
        T.dma("pool", dw, win[:], A["w_in"].rearrange("(k p) c -> p k c", p=128), writes=[wb_])
        T.dma("pool", dw, wsw[:], A["w_in_sw"].rearrange("(k p) c -> p k c", p=128), writes=[wb_])
        T.dma("pool", dw, wuq[:], A["w_uq"].rearrange("(k p) c -> p k c", p=128), writes=[wb_])
        T.dma("pool", dw, wuqs[:], A["w_uq_sw"].rearrange("(k p) c -> p k c", p=128), writes=[wb_])
        T.dma("pool", dw, wukv[:], A["w_ukv"], writes=[wb_])
        dw2 = T.new_dma_sem("Psm")
        T.dma("sp", dw2, sm[:], A["smallsP"], writes=[smb])
        tabs = []
        for i in range(2):
            tabs.append(dict(
                cosM=sb("P_cosM%d" % i, [96, 512], F32), sinM=sb("P_sinM%d" % i, [96, 512], F32),
                cosK=sb("P_cosK%d" % i, [32, 512], F32), sinK=sb("P_sinK%d" % i, [32, 512], F32),
                cosN=sb("P_cosN%d" % i, [128, 512], F32), sinN=sb("P_sinN%d" % i, [128, 512], F32),
                b=Buf("P_tab%d" % i), sem=T.new_dma_sem("Ptab%d" % i)))
        sq = [sb("P_sq%d" % i, [128, 512], BF16) for i in range(2)]
        sq_b = [Buf("P_sq0"), Buf("P_sq1")]
        rs = sb("P_rs", [128, 512], F32)
        rs_b = Buf("P_rs")
        cqn = sb("P_cqn", [128, 2, 512], BF16)
        cqn_b = Buf("P_cqn")
        ckvn = sb("P_ckvn", [128, 512], BF16)
        ckvn_b = Buf("P_ckvn")
        t1 = [sb("P_t1_%d" % i, [128, 512], F32) for i in range(2)]
        t2 = [sb("P_t2_%d" % i, [128, 512], F32) for i in range(2)]
        t_b = [Buf("P_t0"), Buf("P_t1")]
        NST = 4
        stg = [sb("P_stg%d" % i, [128, 512], BF16) for i in range(NST)]
        stg_b = [Buf("P_stg%d" % i) for i in range(NST)]
        stg_sem = [T.new_dma_sem("Pstg%d" % i) for i in range(NST)]
        vst = [sb("P_vst%d" % i, [128, 6, 65], BF16) for i in range(2)]
        vst_b = [Buf("P_vst0"), Buf("P_vst1")]
        vst_sem = [T.new_dma_sem("Pvst%d" % i) for i in range(2)]
        v2st = [sb("P_v2st%d" % i, [128, 2, 2, 65], BF16) for i in range(2)]
        v2st_b = [Buf("P_v2st0"), Buf("P_v2st1")]
        v2st_sem = [T.new_dma_sem("Pv2st%d" % i) for i in range(2)]
        vsb = [sb("P_vsb%d" % i, [128, 256], BF16) for i in range(2)]
        vsb_b = [Buf("P_vsb0"), Buf("P_vsb1")]
        vsb_sem = [T.new_dma_sem("Pvsb%d" % i) for i in range(2)]
        gst = [sb("P_gst%d" % i, [128, 18], F32) for i in range(2)]
        gst_b = [Buf("P_gst0"), Buf("P_gst1")]
        gst_sem = [T.new_dma_sem("Pgst%d" % i) for i in range(2)]
        for i in range(2):
            T.op("pool", lambda e: e.memset(vst[i][:], 1.0), writes=[vst_b[i]])
            T.op("pool", lambda e: e.memset(v2st[i][:], 1.0), writes=[v2st_b[i]])

        cnt = {"bank": 0, "stg": 0, "t": 0}

        def nbank():
            cnt["bank"] += 1
            return cnt["bank"] % 4

        def chain(bank, M, lhs_fn, rhs_fn, nk, reads, N=512, rows=None):
            o = R.ps[bank][0:M, 0:N] if rows is None else R.ps[bank][rows[0]:rows[1], 0:N]
            for k in range(nk):
                T.op("pe", lambda e: e.matmul(o, lhsT=lhs_fn(k), rhs=rhs_fn(k), start=(k == 0), stop=(k == nk - 1)),
                     reads=reads(k), writes=[R.psb[bank]])

        def store(src_ps_ap, src_bufs, M, dst_dram, use_act=False):
            i = cnt["stg"] % NST
            cnt["stg"] += 1
            eng = "act" if use_act else "dve"
            if use_act:
                T.op("act", lambda e: e.activation(out=stg[i][0:M, :], in_=src_ps_ap, func=AF.Copy),
                     reads=src_bufs, writes=[stg_b[i]])
            else:
                T.op("dve", lambda e: e.tensor_copy(out=stg[i][0:M, :], in_=src_ps_ap),
                     reads=src_bufs, writes=[stg_b[i]])
            T.dma("sp", stg_sem[i], dst_dram, stg[i][0:M, :], reads=[stg_b[i]])

        def rope_store(bankA, bankB, M, cos, sin, tb, dst_dram):
            j = cnt["t"] % 2
            cnt["t"] += 1
            i = cnt["stg"] % NST
            cnt["stg"] += 1
            T.op("dve", lambda e: e.tensor_tensor(out=t1[j][0:M, :], in0=R.ps[bankA][0:M, :], in1=cos, op=ALU.mult),
                 reads=[R.psb[bankA], tb], writes=[t_b[j]])
            T.op("dve", lambda e: e.tensor_tensor(out=t2[j][0:M, :], in0=R.ps[bankB][0:M, :], in1=sin, op=ALU.mult),
                 reads=[R.psb[bankB], tb], writes=[t_b[j]])
            T.op("pool", lambda e: e.tensor_tensor(out=stg[i][0:M, :], in0=t1[j][0:M, :], in1=t2[j][0:M, :], op=ALU.add),
                 reads=[t_b[j]], writes=[stg_b[i]])
            T.dma("sp", stg_sem[i], dst_dram, stg[i][0:M, :], reads=[stg_b[i]])

        for tt in range(4):
            sl = slice(tt * 512, (tt + 1) * 512)
            tab = tabs[tt % 2]
            for nm in ("cosM", "sinM", "cosK", "sinK"):
                T.dma("sp", tab["sem"], tab[nm][:], A[nm][:, sl], writes=[tab["b"]])
            hreads = lambda k: [wb_, hb[k][tt]]
            for c in range(2):
                chain(4 + c, 128, lambda k: win[:, k, C_CQ + c * 128:C_CQ + (c + 1) * 128],
                      lambda k: hT[:, k, sl], 8, hreads)
            chain(6, 128, lambda k: win[:, k, C_CKV:C_CKV + 128], lambda k: hT[:, k, sl], 8, hreads)
            for c in range(2):
                T.op("act", lambda e: e.activation(out=sq[c][:], in_=R.ps[4 + c][:], func=AF.Square),
                     reads=[R.psb[4 + c]], writes=[sq_b[c]])
            for c in range(2):
                T.op("pe", lambda e: e.matmul(R.ps[7][:], lhsT=R.onesm[:], rhs=sq[c][:], start=(c == 0), stop=(c == 1)),
                     reads=[sq_b[c], R.onesm_b], writes=[R.psb[7]])
            T.op("act", lambda e: e.activation(out=rs[:], in_=R.ps[7][:], func=AF.Sqrt, bias=EPS, scale=1.0 / 256),
                 reads=[R.psb[7]], writes=[rs_b])
            T.op("dve", lambda e: e.reciprocal(out=rs[:], in_=rs[:]), reads=[rs_b], writes=[rs_b])
            for c in range(2):
                T.op("dve", lambda e: e.scalar_tensor_tensor(out=cqn[:, c, :], in0=R.ps[4 + c][:], scalar=sm[:, c:c + 1],
                                                             in1=rs[:], op0=ALU.mult, op1=ALU.mult),
                     reads=[R.psb[4 + c], rs_b, smb], writes=[cqn_b])
            T.op("act", lambda e: e.activation(out=sq[0][:], in_=R.ps[6][:], func=AF.Square),
                 reads=[R.psb[6]], writes=[sq_b[0]])
            T.op("pe", lambda e: e.matmul(R.ps[7][:], lhsT=R.onesm[:], rhs=sq[0][:], start=True, stop=True),
                 reads=[sq_b[0], R.onesm_b], writes=[R.psb[7]])
            T.op("act", lambda e: e.activation(out=rs[:], in_=R.ps[7][:], func=AF.Sqrt, bias=EPS, scale=1.0 / 128),
                 reads=[R.psb[7]], writes=[rs_b])
            T.op("dve", lambda e: e.reciprocal(out=rs[:], in_=rs[:]), reads=[rs_b], writes=[rs_b])
            T.op("dve", lambda e: e.scalar_tensor_tensor(out=ckvn[:], in0=R.ps[6][:], scalar=sm[:, 2:3],
                                                         in1=rs[:], op0=ALU.mult, op1=ALU.mult),
                 reads=[R.psb[6], rs_b, smb], writes=[ckvn_b])
            for h in range(6):
                ba, bb = nbank(), 4 + (h % 2)
                chain(ba, 96, lambda c: wuq[:, c, h * 96:(h + 1) * 96], lambda c: cqn[:, c, :], 2,
                      lambda c: [wb_, cqn_b])
                chain(bb, 96, lambda c: wuqs[:, c, h * 96:(h + 1) * 96], lambda c: cqn[:, c, :], 2,
                      lambda c: [wb_, cqn_b])
                rope_store(ba, bb, 96, tab["cosM"][:], tab["sinM"][:], tab["b"], A["qT_mla"][h, :, sl])
            for h in range(6):
                ba = nbank()
                chain(ba, 64, lambda c: wukv[:, h * 128:h * 128 + 64], lambda c: ckvn[:], 1, lambda c: [wb_, ckvn_b])
                store(R.ps[ba][0:64, :], [R.psb[ba]], 64, A["kT_mla"][h][:, sl], use_act=(h % 2 == 0))
            ba, bb = nbank(), 4
            chain(ba, 32, lambda k: win[:, k, C_KR:C_KR + 32], lambda k: hT[:, k, sl], 8, hreads)
            chain(bb, 32, lambda k: wsw[:, k, SW_KR:SW_KR + 32], lambda k: hT[:, k, sl], 8, hreads)
            rope_store(ba, bb, 32, tab["cosK"][:], tab["sinK"][:], tab["b"], A["kpeT"][:, sl])
            for bl in range(4):
                tb = tt * 4 + bl
                lsl = slice(bl * 128, (bl + 1) * 128)
                i = tb % 2
                ba = nbank()
                T.op("pe", lambda e: e.matmul(R.ps[ba][:, 0:384], lhsT=ckvn[:, lsl],
                                              rhs=wukv[:].rearrange("p (h x) -> p h x", x=128)[:, :, 64:128],
                                              start=True, stop=True),
                     reads=[wb_, ckvn_b], writes=[R.psb[ba]])
                T.op("dve", lambda e: e.tensor_copy(out=vst[i][:, :, 0:64],
                                                    in_=R.ps[ba][:, 0:384].rearrange("p (h x) -> p h x", x=64)),
                     reads=[R.psb[ba]], writes=[vst_b[i]])
                T.dma("sp", vst_sem[i], A["v_mla"][tb // 8][(tb % 8) * 128:(tb % 8 + 1) * 128, :, :], vst[i][:], reads=[vst_b[i]])
        if mid_hook is not None:
            mid_hook()
        for tt in range(4):
            sl = slice(tt * 512, (tt + 1) * 512)
            tab = tabs[tt % 2]
            for nm in ("cosN", "sinN"):
                T.dma("sp", tab["sem"], tab[nm][:], A[nm][:, sl], writes=[tab["b"]])
            hreads = lambda k: [wb_, hb[k][tt]]
            ropes = [(C_NQ + 128 * c, SW_NQ + 128 * c, A["qT_nsa"][128 * c:128 * (c + 1), sl]) for c in range(3)]
            ropes += [(C_NKC, SW_NKC, A["kT_cmp"][:, sl]), (C_NKS, SW_NKS, A["kT_sel"][:, sl]),
                      (C_NKW, SW_NKW, A["kT_win"][:, sl])]
            for n, (ca, cs, dst) in enumerate(ropes):
                ba, bb = nbank(), 4 + (n % 2)
                chain(ba, 128, lambda k: win[:, k, ca:ca + 128], lambda k: hT[:, k, sl], 8, hreads)
                chain(bb, 128, lambda k: wsw[:, k, cs:cs + 128], lambda k: hT[:, k, sl], 8, hreads)
                rope_store(ba, bb, 128, tab["cosN"][:], tab["sinN"][:], tab["b"], dst)
            plain = [(C_NVC, A["vT_cmp"][:, sl]), (C_SQ, A["qT_sb"][0:128, sl]), (C_SQ + 128, A["qT_sb"][128:256, sl]),
                     (C_SK, A["kT_sb"][0:128, sl]), (C_SK + 128, A["kT_sb"][128:256, sl])]
            for n, (ca, dst) in enumerate(plain):
                ba = nbank()
                chain(ba, 128, lambda k: win[:, k, ca:ca + 128], lambda k: hT[:, k, sl], 8, hreads)
                store(R.ps[ba][:, :], [R.psb[ba]], 128, dst, use_act=(n % 2 == 0))
            for bl in range(4):
                tb = tt * 4 + bl
                tsl = slice(tb * 128, (tb + 1) * 128)
                lsl = slice(bl * 128, (bl + 1) * 128)
                i = tb % 2
                ba = nbank()
                for n, ca in enumerate((C_NVS, C_NVW)):
                    for k in range(8):
                        T.op("pe", lambda e: e.matmul(R.ps[ba][:, n * 128:(n + 1) * 128], lhsT=hT[:, k, tsl],
                                                      rhs=win[:, k, ca:ca + 128], start=(k == 0), stop=(k == 7)),
                             reads=[wb_, hb[k][tt]], writes=[R.psb[ba]])
                for k in range(8):
                    T.op("pe", lambda e: e.matmul(R.ps[ba][:, 256:274], lhsT=hT[:, k, tsl],
                                                  rhs=win[:, k, C_NG:C_NG + 18], start=(k == 0), stop=(k == 7)),
                         reads=[wb_, hb[k][tt]], writes=[R.psb[ba]])
                T.op("dve", lambda e: e.tensor_copy(out=v2st[i][:, :, :, 0:64],
                                                    in_=R.ps[ba][:, 0:256].rearrange("p (a h x) -> p a h x", a=2, x=64)),
                     reads=[R.psb[ba]], writes=[v2st_b[i]])
                T.dma("sp", v2st_sem[i], A["v_sel"][tsl, :, :], v2st[i][:, 0, :, :], reads=[v2st_b[i]])
                T.dma("sp", v2st_sem[i], A["v_win"][tsl, :, :], v2st[i][:, 1, :, :], reads=[v2st_b[i]])
                T.op("dve", lambda e: e.tensor_tensor(out=gst[i][:], in0=R.ps[ba][:, 256:274], in1=sm[:, 8:26], op=ALU.add),
                     reads=[R.psb[ba], smb], writes=[gst_b[i]])
                T.op("act", lambda e: e.activation(out=gst[i][:], in_=gst[i][:], func=AF.Sigmoid),
                     reads=[gst_b[i]], writes=[gst_b[i]])
                T.dma("sp", gst_sem[i], A["gates"][tsl, :], gst[i][:], reads=[gst_b[i]])
                ba = nbank()
                for k in range(8):
                    T.op("pe", lambda e: e.matmul(R.ps[ba][:, 0:256], lhsT=hT[:, k, tsl],
                                                  rhs=win[:, k, C_SV:C_SV + 256], start=(k == 0), stop=(k == 7)),
                         reads=[wb_, hb[k][tt]], writes=[R.psb[ba]])
                T.op("act", lambda e: e.activation(out=vsb[i][:], in_=R.ps[ba][:, 0:256], func=AF.Copy),
                     reads=[R.psb[ba]], writes=[vsb_b[i]])
                T.dma("sp", vsb_sem[i], A["v_sb"][tsl, :], vsb[i][:], reads=[vsb_b[i]])
        T.barrier_all()


THETA = 500000.0


def own_positions(core):
    j = core % 4
    return ((4 * np.arange(NBLK)[:, None] + j) * 128 + np.arange(128)[None, :]).reshape(-1)


def _cs(pos, rot):
    half = rot // 2
    inv = np.float32(THETA) ** (-(np.arange(half, dtype=np.float32) / np.float32(half)))
    ang = pos.astype(np.float32)[None, :] * inv.astype(np.float32)[:, None]
    return np.cos(ang).astype(np.float32), np.sin(ang).astype(np.float32)


def rope_tables(core):
    pos = own_positions(core)
    n = pos.shape[0]
    c16, s16 = _cs(pos, 32)
    c8, s8 = _cs(pos, 16)
    cosM = np.ones((96, n), np.float32); sinM = np.zeros((96, n), np.float32)
    cosM[64:80] = c16; cosM[80:96] = c16; sinM[64:80] = -s16; sinM[80:96] = s16
    cosK = np.concatenate([c16, c16], 0); sinK = np.concatenate([-s16, s16], 0)
    cosN = np.ones((128, n), np.float32); sinN = np.zeros((128, n), np.float32)
    for h in range(2):
        cosN[h * 64:h * 64 + 8] = c8; cosN[h * 64 + 8:h * 64 + 16] = c8
        sinN[h * 64:h * 64 + 8] = -s8; sinN[h * 64 + 8:h * 64 + 16] = s8
    return dict(cosM=cosM, sinM=sinM, cosK=cosK, sinK=sinK, cosN=cosN, sinN=sinN)


def swap_cols_rope(w, head_w, rope0, rot):
    w = np.array(w, copy=True)
    half = rot // 2
    nh = w.shape[1] // head_w
    for h in range(nh):
        a = h * head_w + rope0
        tmp = w[:, a:a + half].copy()
        w[:, a:a + half] = w[:, a + half:a + rot]
        w[:, a + half:a + rot] = tmp
    return w


def make_w_in_sw(w_in):
    parts = [swap_cols_rope(w_in[:, C_KR:C_KR + 32], 32, 0, 32),
             swap_cols_rope(w_in[:, C_NQ:C_NQ + 384], 64, 0, 16),
             swap_cols_rope(w_in[:, C_NKC:C_NKC + 128], 64, 0, 16),
             swap_cols_rope(w_in[:, C_NKS:C_NKS + 128], 64, 0, 16),
             swap_cols_rope(w_in[:, C_NKW:C_NKW + 128], 64, 0, 16)]
    return np.ascontiguousarray(np.concatenate(parts, axis=1))


def make_smallsP(q_norm, kv_norm, gate_bias):
    sm = np.zeros((128, 32), np.float32)
    sm[:, 0:2] = q_norm.reshape(2, 128).T
    sm[:, 2] = kv_norm
    sm[:, 8:26] = gate_bias[None, :]
    return sm


P_OUTS = [("qT_mla", [6, 96, NT], BF16), ("kT_mla", [6, 64, NT], BF16), ("kpeT", [32, NT], BF16),
          ("qT_nsa", [384, NT], BF16), ("kT_cmp", [128, NT], BF16), ("kT_sel", [128, NT], BF16),
          ("kT_win", [128, NT], BF16), ("vT_cmp", [128, NT], BF16), ("qT_sb", [256, NT], BF16),
          ("kT_sb", [256, NT], BF16), ("v_mla", [NT, 6, 65], BF16), ("v_sel", [NT, 2, 65], BF16),
          ("v_win", [NT, 2, 65], BF16), ("gates", [NT, 18], F32), ("v_sb", [NT, 256], BF16)]
P_INS = [("w_in", [D, DIN]), ("w_in_sw", [D, NSW]), ("w_uq", [256, 576]), ("w_uq_sw", [256, 576]),
         ("w_ukv", [128, 768]), ("smallsP", [128, 32]),
         ("cosM", [96, NT]), ("sinM", [96, NT]), ("cosK", [32, NT]), ("sinK", [32, NT]),
         ("cosN", [128, NT]), ("sinN", [128, NT])]


def make_masks(core):
    j = core % 4
    k = np.arange(128)[:, None]
    q = np.arange(128)[None, :]
    ones = np.ones((128, 128), np.float32)
    zeros = np.zeros((128, 128), np.float32)
    tri = (k <= q).astype(np.float32)
    stri = (k < q).astype(np.float32)
    gt = (k > q).astype(np.float32)
    M = np.zeros((128, 22, 128), np.float32)
    M[:, 20] = (k >= q).astype(np.float32)
    M[:, 21] = 1.0
    for d in range(4):
        M[:, d] = ones if d < j else (tri if d == j else zeros)
        M[:, 16 + d] = ones if d < j else (stri if d == j else zeros)
    for dd in range(8):
        d = dd - 4
        if d < j - 4 or d > j:
            M[:, 4 + dd] = zeros
        elif d == j - 4:
            M[:, 4 + dd] = gt
        elif d == j:
            M[:, 4 + dd] = tri
        else:
            M[:, 4 + dd] = ones
    for m4 in range(4):
        i16 = 4 * m4 + j
        M[:, 12 + m4] = (16 * k + 31 - 128 * i16 <= q).astype(np.float32)
    return M.astype(ml_dtypes.bfloat16)


def ktcol(kt):
    return (kt % 4) * NT + (kt // 4) * 128


def ktile(kt):
    return (kt % 4) * NBLK + (kt // 4)


def emit_oT_store(R, C, obank, G, ch, po, zcol=64, stride=65):
    T = R.T
    i = C["ev"] % 2
    C["ev"] += 1
    on, on_b, rz, rz_b = C["on"][i], C["on_b"][i], C["rz"][i], C["rz_b"][i]
    ov = R.ps[obank][:, 0:4 * stride].rearrange("p (m x) -> p m x", x=stride)
    T.op("dve", lambda e: e.reciprocal(out=rz[:], in_=ov[:, :, zcol:zcol + 1]), reads=[R.psb[obank]], writes=[rz_b])
    T.op("dve", lambda e: e.tensor_tensor(out=on[:], in0=ov[:, :, 0:64], in1=rz[:].broadcast_to([128, 4, 64]),
                                          op=ALU.mult),
         reads=[R.psb[obank], rz_b], writes=[on_b])
    emit_transpose_store(R, C, on, on_b, G, ch, po)


def emit_transpose_store(R, C, on, on_b, G, ch, po):
    T = R.T
    tb = C["tpbank"]
    for mm in range(4):
        T.op("pe", lambda e: e.transpose(out=R.ps[tb][0:64, mm * 128:(mm + 1) * 128], in_=on[:, mm, :],
                                         identity=C["ident"][:]),
             reads=[on_b, C["ident_b"]], writes=[R.psb[tb]])
    T.op("act", lambda e: e.activation(out=R.oT[po:po + 64, ch, G * 512:(G + 1) * 512], in_=R.ps[tb][0:64, :],
                                       func=AF.Copy),
         reads=[R.psb[tb]], writes=[R.oT_b[ch][G]])


def emit_dense_attn(R, C, KT, KT_b, V, V_b, QT, QT_b, dk, scale, obank, mask0,
                    selT=None, selT_b=None, vstride=65):
    T = R.T

    def run(G):
        steps = list(range(16 * G + 16))
        n = len(steps)
        stt = {}

        def stage_a(kt):
            mm_min = max(0, -((-(kt - 16 * G - 3)) // 4))
            c0 = mm_min * 128
            sb_ = C["sbanks"][C["sn"] % len(C["sbanks"])]
            C["sn"] += 1
            stt[kt] = [mm_min, c0, sb_, None]
            T.op("pe", lambda e: e.matmul(R.ps[sb_][:, c0:512], lhsT=KT[0:dk, ktcol(kt):ktcol(kt) + 128],
                                          rhs=QT[0:dk, G * 512 + c0:(G + 1) * 512], start=True, stop=(selT is None)),
                 reads=[KT_b, QT_b], writes=[R.psb[sb_]])
            if selT is not None:
                T.op("pe", lambda e: e.matmul(R.ps[sb_][:, c0:512], lhsT=C["Ebig"][:, kt * 128:(kt + 1) * 128],
                                              rhs=selT[:, G * 512 + c0:(G + 1) * 512], start=False, stop=True),
                     reads=[C["Ebig_b"], selT_b], writes=[R.psb[sb_]])

        def stage_b(kt):
            mm_min, c0, sb_, _ = stt[kt]
            pi = C["pn"] % len(C["PT"])
            C["pn"] += 1
            stt[kt][3] = pi
            PT, PT_b = C["PT"][pi], C["PT_b"][pi]
            T.op("act", lambda e: e.activation(out=PT[:, c0:512], in_=R.ps[sb_][:, c0:512], func=AF.Exp, scale=scale),
                 reads=[R.psb[sb_]], writes=[PT_b])
            if kt >= 16 * G:
                mmb = (kt - 16 * G) // 4
                d = (kt - 16 * G) % 4
                T.op("pool", lambda e: e.tensor_tensor(out=PT[:, mmb * 128:(mmb + 1) * 128],
                                                       in0=PT[:, mmb * 128:(mmb + 1) * 128],
                                                       in1=C["masks"][:, mask0 + d, :], op=ALU.mult),
                     reads=[PT_b, C["masks_b"]], writes=[PT_b])

        def stage_c(kt, first):
            mm_min, c0, sb_, pi = stt[kt]
            PT, PT_b = C["PT"][pi], C["PT_b"][pi]
            for mm in range(mm_min, 4):
                last = (kt == 16 * G + 4 * mm + 3)
                T.op("pe", lambda e: e.matmul(R.ps[obank][:, mm * vstride:(mm + 1) * vstride],
                                              lhsT=PT[:, mm * 128:(mm + 1) * 128], rhs=V[:, ktile(kt), :],
                                              start=(first and mm == mm_min), stop=last, skip_group_check=True),
                     reads=[PT_b, V_b], writes=[R.psb[obank]])

        for idx in range(n + 2):
            if idx < n:
                stage_a(steps[idx])
            if 1 <= idx <= n:
                stage_b(steps[idx - 1])
            if idx >= 2:
                stage_c(steps[idx - 2], idx == 2)
    return run


def alloc_attn_common(R, A, ph):
    nc, T = R.nc, R.T
    C = {"ev": 0, "sn": 0, "pn": 0, "sbanks": [0, 1, 2, 3], "tpbank": 5}

    def sb(name, shape, dt):
        return ph.enter_context(nc.sbuf_tensor(U(name), shape, dt))
    C["sb"] = sb
    C["masks"] = sb("A_masks", [128, 22, 128], BF16)
    C["masks_b"] = Buf("A_masks")
    C["ident"] = sb("A_ident", [128, 128], F32)
    C["ident_b"] = Buf("A_ident")
    ds = T.new_dma_sem("Aconst")
    T.dma("sp", ds, C["masks"][:], A["masks"], writes=[C["masks_b"]])
    T.dma("sp", ds, C["ident"][:], A["ident"], writes=[C["ident_b"]])
    C["PT"] = [sb("A_PT%d" % i, [128, 512], BF16) for i in range(6)]
    C["PT_b"] = [Buf("A_PT%d" % i) for i in range(6)]
    C["on"] = [sb("A_on%d" % i, [128, 4, 64], F32) for i in range(2)]
    C["on_b"] = [Buf("A_on%d" % i) for i in range(2)]
    C["rz"] = [sb("A_rz%d" % i, [128, 4, 1], F32) for i in range(2)]
    C["rz_b"] = [Buf("A_rz%d" % i) for i in range(2)]
    C["kcT"] = [sb("A_kcT%d" % i, [64, 512], BF16) for i in range(2)]
    C["vc"] = [sb("A_vc%d" % i, [128, 4, 65], BF16) for i in range(2)]
    C["kc_b"] = [Buf("A_kc%d" % i) for i in range(2)]
    return C


def gb(A, nm, i=None):
    d = A.get("gb")
    if d is None:
        return []
    b = d[nm]
    if isinstance(b, list):
        b = b[i]
    return [b]


def emit_mla(R, C, A, hook=None):
    nc, T = R.nc, R.T
    ngrp = 0
    hook_left = [4]
    with R.ExitStack() as ph:
        def sb(name, shape, dt):
            return ph.enter_context(nc.sbuf_tensor(U(name), shape, dt))
        KT = [sb("M_KT%d" % i, [96, 4 * NT], BF16) for i in range(2)]
        V = [sb("M_V%d" % i, [128, 64, 65], BF16) for i in range(2)]
        QT = [sb("M_QT%d" % i, [96, NT], BF16) for i in range(2)]
        bufs = [Buf("M_slot%d" % i) for i in range(2)]
        sems = [T.new_dma_sem("Mslot%d" % i) for i in range(2)]
        for h in range(6):
            s = h % 2
            for r in range(4):
                T.dma("sp", sems[s], KT[s][0:64, r * NT:(r + 1) * NT], A["kT_mla_g"][h][r, :, :], reads=gb(A, "kT_mla", h),
                      writes=[bufs[s]])
                T.dma("sp", sems[s], KT[s][64:96, r * NT:(r + 1) * NT], A["kpeT_g"][r, :, :], reads=gb(A, "kpeT"), writes=[bufs[s]])
                for hf in range(2):
                    T.dma("sp", sems[s], V[s][:, r * NBLK + hf * 8:r * NBLK + hf * 8 + 8, :],
                          A["v_mla_g"][hf][r, :, h, :].rearrange("(m p) x -> p m x", p=128), reads=gb(A, "v_mla", hf),
                          writes=[bufs[s]])
            T.dma("sp", sems[s], QT[s][:], A["qT_mla"][h, :, :], writes=[bufs[s]])
            for G in range(4):
                obank = 6 + (G % 2)
                run = emit_dense_attn(R, C, KT[s], bufs[s], V[s], bufs[s], QT[s], bufs[s], 96, 96 ** -0.5,
                                      obank, 0)
                run(G)
                emit_oT_store(R, C, obank, G, h // 2, (h % 2) * 64)
                ngrp += 1
                if hook is not None and ngrp % 3 == 1 and hook_left[0] > 0:
                    hook_left[0] -= 1
                    next(hook)
        T.barrier_all()


def make_nsa_consts(core):
    j = core % 4
    c = np.arange(512)[:, None]
    n = np.arange(128)[None, :]
    ov = ((16 * c < 64 * n + 64) & (16 * c + 32 > 64 * n)).astype(np.float32)
    ovl = np.concatenate([ov, np.ones((512, 1), np.float32)], 1).reshape(4, 128, 129).transpose(1, 0, 2)
    ql = np.arange(128)[:, None]
    w = np.arange(256)[None, :]
    rel = w - 128 - 2 * j
    cur = (ql >= 64).astype(np.int64)
    forced = (rel == cur) | (rel == cur - 1)
    future = rel > cur
    keep = (~(forced | future)).astype(np.float32)
    add = np.where(forced, 1e4, np.where(future, -1.0, 0.0)).astype(np.float32)
    ka = np.stack([keep, add], 1)
    key = np.arange(S)[None, :]
    eb = np.where(key // 64 == np.arange(128)[:, None], 1.0, 0.0).astype(np.float32)
    return dict(ovl=np.ascontiguousarray(ovl).astype(ml_dtypes.bfloat16), keepadd=np.ascontiguousarray(ka),
                Ebig=eb.astype(ml_dtypes.bfloat16))


def emit_nsa_compress(R, C, A):
    nc, T = R.nc, R.T
    with R.ExitStack() as ph:
        def sb(name, shape, dt):
            return ph.enter_context(nc.sbuf_tensor(U(name), shape, dt))
        w1 = [sb("N_w1%d" % i, [64, 32, 128], BF16) for i in range(2)]
        w2 = [sb("N_w2%d" % i, [128, 64], BF16) for i in range(2)]
        posT = [sb("N_pos%d" % i, [64, 32], BF16) for i in range(2)]
        cst_b = Buf("NC_const")
        ds = T.new_dma_sem("NconstP")
        T.dma("pool", ds, w1[0][:], A["cmp_w1_k"].rearrange("(l d) h -> d l h", d=64), writes=[cst_b])
        T.dma("pool", ds, w1[1][:], A["cmp_w1_v"].rearrange("(l d) h -> d l h", d=64), writes=[cst_b])
        T.dma("pool", ds, w2[0][:], A["cmp_w2_k"], writes=[cst_b])
        T.dma("pool", ds, w2[1][:], A["cmp_w2_v"], writes=[cst_b])
        T.dma("pool", ds, posT[0][:], A["cmp_posT_k"], writes=[cst_b])
        T.dma("pool", ds, posT[1][:], A["cmp_posT_v"], writes=[cst_b])
        bias = sb("N_bias", [128, 2], F32)
        bias_b = Buf("N_bias")
        for i in range(2):
            for l in range(32):
                T.op("pe", lambda e: e.matmul(R.ps[4][:, i:i + 1], lhsT=w1[i][:, l, :], rhs=posT[i][:, l:l + 1],
                                              start=(l == 0), stop=(l == 31)),
                     reads=[cst_b], writes=[R.psb[4]])
            T.op("dve", lambda e: e.tensor_copy(out=bias[:, i:i + 1], in_=R.ps[4][:, i:i + 1]),
                 reads=[R.psb[4]], writes=[bias_b])
        xg = [[sb("N_xg%d_%d" % (g, i), [64, S], BF16) for i in range(2)] for g in range(2)]
        xg_b = [[Buf("N_xg%d_%d" % (g, i)) for i in range(2)] for g in range(2)]
        hid = sb("N_hid", [128, 512], F32)
        tq = sb("N_tq", [128, 512], F32)
        gl = sb("N_gl", [128, 512], BF16)
        hid_b, gl_b = Buf("N_hid"), Buf("N_gl")
        T.op("pool", lambda e: e.memset(gl[:], 0.0), writes=[gl_b])
        yield
        for g in range(2):
            kcT, vc, kc_b = C["kcT"][g], C["vc"][g], C["kc_b"][g]
            T.op("pool", lambda e: e.memset(vc[:], 1.0), reads=[], writes=[kc_b])
            T.op("pool", lambda e: e.memset(kcT[:], 0.0), reads=[], writes=[kc_b])
            for i in range(2):
                nm = ("kT_cmp_g", "vT_cmp_g")[i]
                dsx = T.new_dma_sem("Nxg%d_%d" % (g, i))
                for r in range(4):
                    dst = xg[g][i][:].rearrange("d (m r p) -> d m r p", r=4, p=128)[:, :, r, :]
                    T.dma("sp", dsx, dst, A[nm][r, g * 64:(g + 1) * 64, :].rearrange("d (m p) -> d m p", p=128),
                          reads=gb(A, nm[:-2]), writes=[xg_b[g][i]])
                for l in range(32):
                    T.op("pe", lambda e: e.matmul(R.ps[4][:, 0:511], lhsT=w1[i][:, l, :],
                                                  rhs=xg[g][i][:, l:l + 16 * 510 + 1:16],
                                                  start=(l == 0), stop=(l == 31)),
                         reads=[cst_b, xg_b[g][i]], writes=[R.psb[4]])
                T.op("act", lambda e: e.activation(out=hid[:, 0:511], in_=R.ps[4][:, 0:511], func=AF.Identity,
                                                   bias=bias[:, i:i + 1]),
                     reads=[R.psb[4], bias_b], writes=[hid_b])
                T.op("dve", lambda e: e.tensor_tensor(out=tq[:, 0:511], in0=hid[:, 0:511], in1=hid[:, 0:511],
                                                      op=ALU.mult), reads=[hid_b], writes=[hid_b])
                T.op("dve", lambda e: e.tensor_scalar(out=tq[:, 0:511], in0=tq[:, 0:511], scalar1=0.044715,
                                                      scalar2=1.0, op0=ALU.mult, op1=ALU.add),
                     reads=[hid_b], writes=[hid_b])
                T.op("dve", lambda e: e.tensor_tensor(out=tq[:, 0:511], in0=tq[:, 0:511], in1=hid[:, 0:511],
                                                      op=ALU.mult), reads=[hid_b], writes=[hid_b])
                T.op("act", lambda e: e.activation(out=tq[:, 0:511], in_=tq[:, 0:511], func=AF.Sigmoid,
                                                   scale=1.5957691216057308),
                     reads=[hid_b], writes=[hid_b])
                T.op("dve", lambda e: e.tensor_tensor(out=gl[:, 0:511], in0=tq[:, 0:511], in1=hid[:, 0:511],
                                                      op=ALU.mult), reads=[hid_b, gl_b], writes=[gl_b])
                if i == 0:
                    T.op("pe", lambda e: e.matmul(R.ps[4][0:64, 0:511], lhsT=w2[0][:], rhs=gl[:, 0:511],
                                                  start=True, stop=True),
                         reads=[cst_b, gl_b], writes=[R.psb[4]])
                    T.op("dve", lambda e: e.tensor_copy(out=kcT[:, 0:511], in_=R.ps[4][0:64, 0:511]),
                         reads=[R.psb[4]], writes=[kc_b])
                else:
                    for t in range(4):
                        T.op("pe", lambda e: e.matmul(R.ps[4][:, t * 64:(t + 1) * 64], lhsT=gl[:, t * 128:(t + 1) * 128],
                                                      rhs=w2[1][:], start=True, stop=True),
                             reads=[cst_b, gl_b], writes=[R.psb[4]])
                    T.op("dve", lambda e: e.tensor_copy(out=vc[:, :, 0:64],
                                                        in_=R.ps[4][:, 0:256].rearrange("p (t x) -> p t x", x=64)),
                         reads=[R.psb[4]], writes=[kc_b])
                yield
        T.barrier_all()


def emit_nsa(R, C, A):
    nc, T = R.nc, R.T
    SC = 0.125
    with R.ExitStack() as ph:
        def sb(name, shape, dt):
            return ph.enter_context(nc.sbuf_tensor(U(name), shape, dt))
        Ebig = sb("N_Ebig", [128, S], BF16)
        ovl = sb("N_ovl", [128, 4, 129], BF16)
        ka = sb("N_ka", [128, 2, 256], F32)
        gates = sb("N_gates", [128, NBLK, 18], F32)
        cst_b = Buf("N_const")
        C["Ebig"], C["Ebig_b"] = Ebig, cst_b
        ds = T.new_dma_sem("Nconst")
        T.dma("sp", ds, Ebig[:], A["Ebig"], writes=[cst_b])
        T.dma("sp", ds, ovl[:], A["ovl"], writes=[cst_b])
        T.dma("sp", ds, ka[:], A["keepadd"], writes=[cst_b])
        T.dma("sp", ds, gates[:], A["gates"].rearrange("(m p) g -> p m g", p=128), writes=[cst_b])
        selT = sb("N_selT", [128, NT], BF16)
        selT_b = Buf("N_selT")
        QTn = sb("N_QT", [64, 3, NT], BF16)
        KTs = sb("N_KTs", [64, 4 * NT], BF16)
        Vs = sb("N_Vs", [128, 64, 65], BF16)
        KTw = sb("N_KTw", [64, 4 * NT], BF16)
        Vw = sb("N_Vw", [128, 64, 65], BF16)
        kv_b = Buf("N_kv")
        kv_sem = T.new_dma_sem("Nkv")
        oaccs = [sb("N_oacc%d" % i, [128, 3, 4, 64], F32) for i in range(2)]
        oaccs_b = [Buf("N_oacc0"), Buf("N_oacc1")]
        C["pcn"], C["pwn"] = 0, 0
        Pc = [sb("N_Pc%d" % i, [128, 3, 128], BF16) for i in range(2)]
        Pc_b = [Buf("N_Pc0"), Buf("N_Pc1")]
        rzc = sb("N_rzc", [128, 3, 1], F32)
        wc = sb("N_wc", [128, 3, 1], F32)
        sc = sb("N_sc", [128, 128], F32)
        sc2 = sb("N_sc2", [128, 128], F32)
        m8 = sb("N_m8", [128, 16], F32)
        seln = sb("N_seln", [128, 128], F32)
        sm_b = Buf("N_small")
        rz4 = sb("N_rz4", [128, 4, 1], F32)
        w4 = sb("N_w4", [128, 4, 1], F32)
        w4_b = Buf("N_w4")
        Pw = [sb("N_Pw%d" % i, [128, 512], BF16) for i in range(3)]
        Pw_b = [Buf("N_Pw%d" % i) for i in range(3)]
        for g in range(2):
            kcT, vc, kc_b = C["kcT"][g], C["vc"][g], C["kc_b"][g]
            T.dma("sp", kv_sem, QTn[:], A["qT_nsa"][g * 192:(g + 1) * 192, :].rearrange("(h d) t -> d h t", d=64),
                  writes=[kv_b])
            for r in range(4):
                T.dma("sp", kv_sem, KTs[:, r * NT:(r + 1) * NT], A["kT_sel_g"][r, g * 64:(g + 1) * 64, :], reads=gb(A, "kT_sel"), writes=[kv_b])
                T.dma("sp", kv_sem, KTw[:, r * NT:(r + 1) * NT], A["kT_win_g"][r, g * 64:(g + 1) * 64, :], reads=gb(A, "kT_win"), writes=[kv_b])
                T.dma("sp", kv_sem, Vs[:, r * NBLK:(r + 1) * NBLK, :],
                      A["v_sel_g"][r, :, g, :].rearrange("(m p) x -> p m x", p=128), reads=gb(A, "v_sel"), writes=[kv_b])
                T.dma("sp", kv_sem, Vw[:, r * NBLK:(r + 1) * NBLK, :],
                      A["v_win_g"][r, :, g, :].rearrange("(m p) x -> p m x", p=128), reads=gb(A, "v_win"), writes=[kv_b])

            def cmp_block(G, oacc, oacc_b):
                ob, sbk = 3, 4
                for mm in range(4):
                    m = 4 * G + mm
                    msl = slice(m * 128, (m + 1) * 128)
                    for t in range(G + 1):
                        sb_ = C["sbanks"][C["sn"] % len(C["sbanks"])]
                        C["sn"] += 1
                        pi = C["pcn"] % 2
                        C["pcn"] += 1
                        T.op("pe", lambda e: e.matmul(R.ps[sb_][:, 0:384], lhsT=kcT[:, t * 128:(t + 1) * 128],
                                                      rhs=QTn[:, :, msl], start=True, stop=True),
                             reads=[kc_b, kv_b], writes=[R.psb[sb_]])
                        T.op("act", lambda e: e.activation(out=Pc[pi][:].rearrange("p r q -> p (r q)"),
                                                           in_=R.ps[sb_][:, 0:384], func=AF.Exp, scale=SC),
                             reads=[R.psb[sb_]], writes=[Pc_b[pi]])
                        if t == G:
                            T.op("pool", lambda e: e.tensor_tensor(
                                out=Pc[pi][:], in0=Pc[pi][:],
                                in1=C["masks"][:, 12 + mm:13 + mm, :].broadcast_to([128, 3, 128]), op=ALU.mult),
                                reads=[Pc_b[pi], C["masks_b"]], writes=[Pc_b[pi]])
                        for r in range(3):
                            T.op("pe", lambda e: e.matmul(R.ps[ob][:, r * 65:(r + 1) * 65], lhsT=Pc[pi][:, r, :],
                                                          rhs=vc[:, t, :], start=(t == 0 and r == 0), stop=(t == G),
                                                          skip_group_check=True),
                                 reads=[Pc_b[pi], kc_b], writes=[R.psb[ob]])
                        for r in range(3):
                            T.op("pe", lambda e: e.matmul(R.ps[sbk][:, r * 129:(r + 1) * 129], lhsT=Pc[pi][:, r, :],
                                                          rhs=ovl[:, t, :], start=(t == 0 and r == 0), stop=(t == G),
                                                          skip_group_check=True),
                                 reads=[Pc_b[pi], cst_b], writes=[R.psb[sbk]])
                    scv = R.ps[sbk][:, 0:387].rearrange("p (r x) -> p r x", x=129)
                    ocv = R.ps[ob][:, 0:195].rearrange("p (r x) -> p r x", x=65)
                    T.op("dve", lambda e: e.tensor_scalar(out=rzc[:], in0=scv[:, :, 128:129], scalar1=1e-30, scalar2=None,
                                                          op0=ALU.add), reads=[R.psb[sbk]], writes=[sm_b])
                    T.op("dve", lambda e: e.reciprocal(out=rzc[:], in_=rzc[:]), reads=[sm_b], writes=[sm_b])
                    T.op("dve", lambda e: e.tensor_scalar(out=sc[:], in0=scv[:, 0, 0:128], scalar1=rzc[:, 0, :], scalar2=None,
                                                          op0=ALU.mult), reads=[R.psb[sbk], sm_b], writes=[sm_b])
                    for r in (1, 2):
                        T.op("dve", lambda e: e.scalar_tensor_tensor(out=sc[:], in0=scv[:, r, 0:128], scalar=rzc[:, r, :],
                                                                     in1=sc[:], op0=ALU.mult, op1=ALU.add),
                             reads=[R.psb[sbk], sm_b], writes=[sm_b])
                    T.op("dve", lambda e: e.tensor_tensor(
                        out=wc[:], in0=rzc[:],
                        in1=gates[:, m, 9 * g:9 * g + 9].rearrange("p (r x) -> p r x", x=3)[:, :, 0:1], op=ALU.mult),
                        reads=[sm_b, cst_b], writes=[sm_b])
                    for r in range(3):
                        T.op("dve", lambda e: e.tensor_scalar(out=oacc[:, r, mm, :], in0=ocv[:, r, 0:64],
                                                              scalar1=wc[:, r, :], scalar2=None, op0=ALU.mult),
                             reads=[R.psb[ob], sm_b], writes=[oacc_b])
                    w0 = 128 - 8 * m
                    T.op("dve", lambda e: e.tensor_tensor(out=sc[:], in0=sc[:], in1=ka[:, 0, w0:w0 + 128], op=ALU.mult),
                         reads=[sm_b, cst_b], writes=[sm_b])
                    T.op("dve", lambda e: e.tensor_tensor(out=sc[:], in0=sc[:], in1=ka[:, 1, w0:w0 + 128], op=ALU.add),
                         reads=[sm_b, cst_b], writes=[sm_b])
                    T.op("dve", lambda e: e.memset(sc[:, 0:1], 1e4), reads=[sm_b], writes=[sm_b])
                    T.op("dve", lambda e: e.max(out=m8[:, 0:8], in_=sc[:]), reads=[sm_b], writes=[sm_b])
                    T.op("dve", lambda e: e.match_replace(out=sc2[:], in_to_replace=m8[:, 0:8], in_values=sc[:],
                                                          imm_value=-1e9), reads=[sm_b], writes=[sm_b])
                    T.op("dve", lambda e: e.max(out=m8[:, 8:16], in_=sc2[:]), reads=[sm_b], writes=[sm_b])
                    T.op("dve", lambda e: e.tensor_scalar(out=seln[:], in0=sc[:], scalar1=m8[:, 15:16], scalar2=None,
                                                          op0=ALU.is_ge), reads=[sm_b], writes=[sm_b])
                    tb = C["tpbank"]
                    T.op("pe", lambda e: e.transpose(out=R.ps[tb][:, 0:128], in_=seln[:], identity=C["ident"][:]),
                         reads=[sm_b, C["ident_b"]], writes=[R.psb[tb]])
                    T.op("act", lambda e: e.activation(out=selT[:, msl], in_=R.ps[tb][:, 0:128], func=AF.Copy),
                         reads=[R.psb[tb]], writes=[selT_b])

            def win_block(r, G, oacc, oacc_b):
                h = 3 * g + r
                ob = 6
                items = []
                for mm in range(4):
                    for half in range(2):
                        kts = [16 * G + 4 * mm - 4 + half * 4 + x for x in range(4)]
                        if kts[-1] >= 0:
                            items.append((mm, half, kts))
                ni = len(items)
                stt = {}

                def wa(i):
                    mm, half, kts = items[i]
                    msl = slice((4 * G + mm) * 128, (4 * G + mm + 1) * 128)
                    sb_ = C["sbanks"][C["sn"] % len(C["sbanks"])]
                    C["sn"] += 1
                    stt[i] = [sb_, None]
                    for x, kt in enumerate(kts):
                        T.op("pe", lambda e: e.matmul(R.ps[sb_][:, x * 128:(x + 1) * 128],
                                                      lhsT=KTw[:, ktcol(kt):ktcol(kt) + 128],
                                                      rhs=QTn[:, r, msl], start=True, stop=True),
                             reads=[kv_b], writes=[R.psb[sb_]])

                def wb(i):
                    mm, half, kts = items[i]
                    sb_ = stt[i][0]
                    pi = C["pwn"] % len(Pw)
                    C["pwn"] += 1
                    stt[i][1] = pi
                    T.op("act", lambda e: e.activation(out=Pw[pi][:], in_=R.ps[sb_][:], func=AF.Exp, scale=SC),
                         reads=[R.psb[sb_]], writes=[Pw_b[pi]])
                    T.op("pool", lambda e: e.tensor_tensor(
                        out=Pw[pi][:], in0=Pw[pi][:],
                        in1=C["masks"][:, 4 + half * 4:8 + half * 4, :].rearrange("p a q -> p (a q)"),
                        op=ALU.mult), reads=[Pw_b[pi], C["masks_b"]], writes=[Pw_b[pi]])

                def wc_(i):
                    mm, half, kts = items[i]
                    pi = stt[i][1]
                    for x, kt in enumerate(kts):
                        T.op("pe", lambda e: e.matmul(R.ps[ob][:, mm * 65:(mm + 1) * 65],
                                                      lhsT=Pw[pi][:, x * 128:(x + 1) * 128], rhs=Vw[:, ktile(kt), :],
                                                      start=(i == 0 and x == 0), stop=(half == 1 and x == 3),
                                                      skip_group_check=True),
                             reads=[Pw_b[pi], kv_b], writes=[R.psb[ob]])

                for idx in range(ni + 2):
                    if idx < ni:
                        wa(idx)
                    if 1 <= idx <= ni:
                        wb(idx - 1)
                    if idx >= 2:
                        wc_(idx - 2)
                owv = R.ps[ob][:, 0:260].rearrange("p (m x) -> p m x", x=65)
                T.op("dve", lambda e: e.reciprocal(out=rz4[:], in_=owv[:, :, 64:65]), reads=[R.psb[ob]], writes=[w4_b])
                T.op("dve", lambda e: e.tensor_tensor(out=w4[:], in0=rz4[:],
                                                      in1=gates[:, 4 * G:4 * G + 4, 3 * h + 2:3 * h + 3], op=ALU.mult),
                     reads=[w4_b, cst_b], writes=[w4_b])
                for mm in range(4):
                    T.op("dve", lambda e: e.scalar_tensor_tensor(out=oacc[:, r, mm, :], in0=owv[:, mm, 0:64],
                                                                 scalar=w4[:, mm, :], in1=oacc[:, r, mm, :],
                                                                 op0=ALU.mult, op1=ALU.add),
                         reads=[R.psb[ob], w4_b, oacc_b], writes=[oacc_b])

            def sel3_block(G, oacc, oacc_b):
                obs = [4, 6, 7]
                mbanks = [2, 3]
                qbanks = [0, 1]
                nk = 16 * G + 16
                nhs = 3 * nk
                stt = {}

                def geom(kt):
                    mm_min = max(0, -((-(kt - 16 * G - 3)) // 4))
                    return mm_min, mm_min * 128

                def sa(hs):
                    kt, r = hs // 3, hs % 3
                    mm_min, c0 = geom(kt)
                    mb = mbanks[kt % 2]
                    if r == 0:
                        T.op("pe", lambda e: e.matmul(R.ps[mb][:, c0:512], lhsT=Ebig[:, kt * 128:(kt + 1) * 128],
                                                      rhs=selT[:, G * 512 + c0:(G + 1) * 512], start=True, stop=True),
                             reads=[cst_b, selT_b], writes=[R.psb[mb]])
                    qb = qbanks[hs % 2]
                    T.op("pe", lambda e: e.matmul(R.ps[qb][:, c0:512], lhsT=KTs[:, ktcol(kt):ktcol(kt) + 128],
                                                  rhs=QTn[:, r, G * 512 + c0:(G + 1) * 512], start=True, stop=True),
                         reads=[kv_b], writes=[R.psb[qb]])

                def sb_(hs):
                    kt, r = hs // 3, hs % 3
                    mm_min, c0 = geom(kt)
                    mb, qb = mbanks[kt % 2], qbanks[hs % 2]
                    pi = C["pn"] % len(C["PT"])
                    C["pn"] += 1
                    stt[hs] = pi
                    PT, PT_b = C["PT"][pi], C["PT_b"][pi]
                    T.op("act", lambda e: e.activation(out=PT[:, c0:512], in_=R.ps[qb][:, c0:512], func=AF.Exp, scale=SC),
                         reads=[R.psb[qb]], writes=[PT_b])
                    T.op("dve", lambda e: e.tensor_tensor(out=PT[:, c0:512], in0=PT[:, c0:512], in1=R.ps[mb][:, c0:512],
                                                          op=ALU.mult),
                         reads=[PT_b, R.psb[mb]], writes=[PT_b])
                    if kt >= 16 * G:
                        mmb = (kt - 16 * G) // 4
                        d = (kt - 16 * G) % 4
                        T.op("pool", lambda e: e.tensor_tensor(out=PT[:, mmb * 128:(mmb + 1) * 128],
                                                               in0=PT[:, mmb * 128:(mmb + 1) * 128],
                                                               in1=C["masks"][:, d, :], op=ALU.mult),
                             reads=[PT_b, C["masks_b"]], writes=[PT_b])

                def sc_(hs):
                    kt, r = hs // 3, hs % 3
                    mm_min, c0 = geom(kt)
                    PT, PT_b = C["PT"][stt[hs]], C["PT_b"][stt[hs]]
                    for mm in range(mm_min, 4):
                        T.op("pe", lambda e: e.matmul(R.ps[obs[r]][:, mm * 65:(mm + 1) * 65],
                                                      lhsT=PT[:, mm * 128:(mm + 1) * 128], rhs=Vs[:, ktile(kt), :],
                                                      start=(kt == 0 and mm == mm_min), stop=(kt == 16 * G + 4 * mm + 3),
                                                      skip_group_check=True),
                             reads=[PT_b, kv_b], writes=[R.psb[obs[r]]])

                for idx in range(nhs + 4):
                    if idx < nhs:
                        sa(idx)
                    if 1 <= idx <= nhs:
                        sb_(idx - 1)
                    if idx >= 4:
                        sc_(idx - 4)
                for r in range(3):
                    h = 3 * g + r
                    osv = R.ps[obs[r]][:, 0:260].rearrange("p (m x) -> p m x", x=65)
                    T.op("dve", lambda e: e.reciprocal(out=rz4[:], in_=osv[:, :, 64:65]), reads=[R.psb[obs[r]]], writes=[w4_b])
                    T.op("dve", lambda e: e.tensor_tensor(out=w4[:], in0=rz4[:],
                                                          in1=gates[:, 4 * G:4 * G + 4, 3 * h + 1:3 * h + 2], op=ALU.mult),
                         reads=[w4_b, cst_b], writes=[w4_b])
                    for mm in range(4):
                        T.op("dve", lambda e: e.scalar_tensor_tensor(out=oacc[:, r, mm, :], in0=osv[:, mm, 0:64],
                                                                     scalar=w4[:, mm, :], in1=oacc[:, r, mm, :],
                                                                     op0=ALU.mult, op1=ALU.add),
                             reads=[R.psb[obs[r]], w4_b, oacc_b], writes=[oacc_b])
                    emit_transpose_store(R, C, oacc[:, r, :, :], oacc_b, G, 3 + h // 2, (h % 2) * 64)

            C["sbanks"] = [0, 1]
            cmp_block(0, oaccs[0], oaccs_b[0])
            for G in range(4):
                if G + 1 < 4:
                    cmp_block(G + 1, oaccs[(G + 1) % 2], oaccs_b[(G + 1) % 2])
                for r in range(3):
                    win_block(r, G, oaccs[G % 2], oaccs_b[G % 2])
                sel3_block(G, oaccs[G % 2], oaccs_b[G % 2])
            C["sbanks"] = [0, 1, 2, 3]
        T.barrier_all()


def emit_sb(R, C, A):
    nc, T = R.nc, R.T
    SC = 0.125
    with R.ExitStack() as ph:
        def sb(name, shape, dt):
            return ph.enter_context(nc.sbuf_tensor(U(name), shape, dt))
        KT = [sb("S_KT%d" % i, [64, 4 * NT], BF16) for i in range(2)]
        V = [sb("S_V%d" % i, [128, 64, 64], BF16) for i in range(2)]
        QT = [sb("S_QT%d" % i, [64, NT], BF16) for i in range(2)]
        bufs = [Buf("S_slot%d" % i) for i in range(2)]
        sems = [T.new_dma_sem("Sslot%d" % i) for i in range(2)]
        NE = 4
        E = [sb("S_E%d" % i, [128, 512], F32) for i in range(NE)]
        SP = [sb("S_SP%d" % i, [128, 512], BF16) for i in range(NE)]
        X = [sb("S_X%d" % i, [128, 512], F32) for i in range(2)]
        E_b = [Buf("S_E%d" % i) for i in range(NE)]
        SP_b = [Buf("S_SP%d" % i) for i in range(NE)]
        X_b = [Buf("S_X0"), Buf("S_X1")]
        Accb = [sb("S_Accb%d" % i, [128, 512], BF16) for i in range(3)]
        Accb_b = [Buf("S_Accb%d" % i) for i in range(3)]
        tincl = C["masks"][:, 20, :]
        onesb = C["masks"][:, 21, :]
        cbanks = [3, 4]
        zbanks = [0, 1, 2]
        for h in range(4):
            s = h % 2
            for r in range(4):
                T.dma("sp", sems[s], KT[s][:, r * NT:(r + 1) * NT], A["kT_sb_g"][r, h * 64:(h + 1) * 64, :], reads=gb(A, "kT_sb"),
                      writes=[bufs[s]])
                T.dma("sp", sems[s], V[s][:, r * NBLK:(r + 1) * NBLK, :],
                      A["v_sb_g"][r, :, h * 64:(h + 1) * 64].rearrange("(m p) x -> p m x", p=128), reads=gb(A, "v_sb"),
                      writes=[bufs[s]])
            T.dma("sp", sems[s], QT[s][:], A["qT_sb"][h * 64:(h + 1) * 64, :], writes=[bufs[s]])
            for G in range(4):
                ob = 6 + (G % 2)
                for i in range(3):
                    T.op("pool", lambda e: e.memset(Accb[i][:], 0.0), writes=[Accb_b[i]])
                steps = list(range(16 * G + 15, -1, -1))
                ns = len(steps)
                stt = {}

                def geom(kt):
                    mm_min = max(0, -((-(kt - 16 * G - 3)) // 4))
                    return mm_min, mm_min * 128

                def st_a(n):
                    kt = steps[n]
                    mm_min, c0 = geom(kt)
                    zb = zbanks[n % 3]
                    T.op("pe", lambda e: e.matmul(R.ps[zb][:, c0:512], lhsT=KT[s][:, ktcol(kt):ktcol(kt) + 128],
                                                  rhs=QT[s][:, G * 512 + c0:(G + 1) * 512], start=True, stop=True),
                         reads=[bufs[s]], writes=[R.psb[zb]])

                def st_b(n):
                    kt = steps[n]
                    mm_min, c0 = geom(kt)
                    zb, ie = zbanks[n % 3], n % NE
                    T.op("act", lambda e: e.activation(out=E[ie][:, c0:512], in_=R.ps[zb][:, c0:512], func=AF.Exp, scale=SC),
                         reads=[R.psb[zb]], writes=[E_b[ie]])
                    T.op("act", lambda e: e.activation(out=SP[ie][:, c0:512], in_=E[ie][:, c0:512], func=AF.Ln, bias=1.0),
                         reads=[E_b[ie]], writes=[SP_b[ie]])
                    if kt >= 16 * G:
                        d = (kt - 16 * G) % 4
                        T.op("pool", lambda e: e.tensor_tensor(out=SP[ie][:, c0:c0 + 128], in0=SP[ie][:, c0:c0 + 128],
                                                               in1=C["masks"][:, 16 + d, :], op=ALU.mult),
                             reads=[SP_b[ie], C["masks_b"]], writes=[SP_b[ie]])
                    if n + 1 < ns:
                        T.op("pool", lambda e: e.tensor_tensor(out=Accb[(n + 1) % 3][:, c0:512], in0=Accb[n % 3][:, c0:512],
                                                               in1=SP[ie][:, c0:512], op=ALU.add),
                             reads=[Accb_b[n % 3], SP_b[ie]], writes=[Accb_b[(n + 1) % 3]])

                def st_c(n):
                    kt = steps[n]
                    mm_min, c0 = geom(kt)
                    cb, ie = cbanks[n % 2], n % NE
                    T.op("pe", lambda e: e.matmul(R.ps[cb][:, c0:512], lhsT=tincl, rhs=SP[ie][:, c0:512],
                                                  start=True, stop=False),
                         reads=[SP_b[ie], C["masks_b"]], writes=[R.psb[cb]])
                    T.op("pe", lambda e: e.matmul(R.ps[cb][:, c0:512], lhsT=onesb, rhs=Accb[n % 3][:, c0:512],
                                                  start=False, stop=True),
                         reads=[Accb_b[n % 3], C["masks_b"]], writes=[R.psb[cb]])
                    T.op("act", lambda e: e.activation(out=X[n % 2][:, c0:512], in_=R.ps[cb][:, c0:512], func=AF.Exp,
                                                       scale=-1.0),
                         reads=[R.psb[cb]], writes=[X_b[n % 2]])

                def st_e(n):
                    kt = steps[n]
                    mm_min, c0 = geom(kt)
                    ie = n % NE
                    pi = C["pn"] % len(C["PT"])
                    C["pn"] += 1
                    stt[n] = pi
                    PT, PT_b = C["PT"][pi], C["PT_b"][pi]
                    T.op("dve", lambda e: e.tensor_tensor(out=PT[:, c0:512], in0=E[ie][:, c0:512], in1=X[n % 2][:, c0:512],
                                                          op=ALU.mult),
                         reads=[E_b[ie], X_b[n % 2]], writes=[PT_b])
                    if kt >= 16 * G:
                        d = (kt - 16 * G) % 4
                        T.op("pool", lambda e: e.tensor_tensor(out=PT[:, c0:c0 + 128], in0=PT[:, c0:c0 + 128],
                                                               in1=C["masks"][:, 16 + d, :], op=ALU.mult),
                             reads=[PT_b, C["masks_b"]], writes=[PT_b])

                def st_f(n, first):
                    kt = steps[n]
                    mm_min, c0 = geom(kt)
                    PT, PT_b = C["PT"][stt[n]], C["PT_b"][stt[n]]
                    for mm in range(mm_min, 4):
                        T.op("pe", lambda e: e.matmul(R.ps[ob][:, mm * 64:(mm + 1) * 64],
                                                      lhsT=PT[:, mm * 128:(mm + 1) * 128], rhs=V[s][:, ktile(kt), :],
                                                      start=(first and mm == mm_min), stop=(kt == 0), skip_group_check=True),
                             reads=[PT_b, bufs[s]], writes=[R.psb[ob]])

                for idx in range(ns + 4):
                    if idx < ns:
                        st_a(idx)
                    if 1 <= idx <= ns:
                        st_b(idx - 1)
                    if 2 <= idx <= ns + 1:
                        st_c(idx - 2)
                    if 3 <= idx <= ns + 2:
                        st_e(idx - 3)
                    if idx >= 4:
                        st_f(idx - 4, idx == 4)
                i = C["ev"] % 2
                C["ev"] += 1
                T.op("dve", lambda e: e.tensor_copy(out=C["on"][i][:],
                                                    in_=R.ps[ob][:, 0:256].rearrange("p (m x) -> p m x", x=64)),
                     reads=[R.psb[ob]], writes=[C["on_b"][i]])
                emit_transpose_store(R, C, C["on"][i], C["on_b"][i], G, 6 + h // 2, (h % 2) * 64)
        T.barrier_all()


L = 2
GATHER = ["kT_mla", "kpeT", "v_mla", "kT_cmp", "vT_cmp", "kT_sel", "kT_win", "v_sel", "v_win", "kT_sb", "v_sb"]
LOCAL = ["qT_mla", "qT_nsa", "qT_sb", "gates"]
POUT = {nm: (shp, dt) for nm, shp, dt in P_OUTS}
W_IN = [("ffn1_w_gate", [L, D, DFF]), ("ffn1_w_up", [L, D, DFF]), ("ffn1_w_down", [L, DFF, D]),
        ("ffn2_w_gate", [L, D, DFF]), ("ffn2_w_up", [L, D, DFF]), ("ffn2_w_down", [L, DFF, D]),
        ("w_in", [L, D, DIN]), ("w_in_sw", [L, D, NSW]), ("w_uq", [L, 256, 576]), ("w_uq_sw", [L, 256, 576]),
        ("w_ukv", [L, 128, 768]), ("smallsP", [L, 128, 32]), ("w_out", [L, D, D]),
        ("cmp_w1_k", [L, 2048, 128]), ("cmp_w1_v", [L, 2048, 128]), ("cmp_w2_k", [L, 128, 64]),
        ("cmp_w2_v", [L, 128, 64]), ("cmp_posT_k", [L, 64, 32]), ("cmp_posT_v", [L, 64, 32]),
        ("gains", [128, 3 * L + 1, 8])]
C_IN = [("cosM", [96, NT], F32), ("sinM", [96, NT], F32), ("cosK", [32, NT], F32), ("sinK", [32, NT], F32),
        ("cosN", [128, NT], F32), ("sinN", [128, NT], F32), ("masks", [128, 22, 128], BF16),
        ("ident", [128, 128], F32), ("Ebig", [128, S], BF16), ("ovl", [128, 4, 129], BF16),
        ("keepadd", [128, 2, 256], F32)]


def emit_layer_X(R, A, l, do_post, do_pre, final, xin, xout, mid_hook=None):
    nc, T = R.nc, R.T
    with ExitStack() as px:
        alloc_xT(R, px)
        hT = px.enter_context(nc.sbuf_tensor(U("hT"), [128, 8, NT], BF16))
        hb = [[Buf("h%d_%d" % (k, t)) for t in range(4)] for k in range(8)]
        gam = px.enter_context(nc.sbuf_tensor(U("gam"), [128, 3 * L + 1, 8], F32))
        gam_b = Buf("gam")
        sq = [px.enter_context(nc.sbuf_tensor(U("sq%d" % i), [128, 512], BF16)) for i in range(2)]
        sq_b = [Buf("sq0"), Buf("sq1")]
        rstd = px.enter_context(nc.sbuf_tensor(U("rstd"), [128, 512], F32))
        rstd_b = Buf("rstd")
        ld = T.new_dma_sem("ldx")
        for k in range(8):
            T.dma("sp", ld, R.xT[:, k, :], xin[k * 128:(k + 1) * 128, :], writes=[R.xb[k][t] for t in range(4)])
        T.dma("sp", ld, gam[:], A["gains"], writes=[gam_b])

        def norm(gi):
            emit_norm(R, hT, hb, gam[:, gi, :], gam_b, sq, sq_b, rstd, rstd_b, 6)

        lp = l
        with ExitStack() as pf:
            R.stack = pf
            W = alloc_ffn_work(R)

            def ffn(pref, ll):
                emit_ffn(R, hT, hb, A[pref + "_w_gate"][ll], A[pref + "_w_up"][ll], A[pref + "_w_down"][ll], W)

            if do_post:
                oT2 = pf.enter_context(nc.sbuf_tensor(U("oT2"), [128, 8, NT], BF16))
                o_b = Buf("oT2")
                d1 = T.new_dma_sem("oT2")
                T.dma("sp", d1, oT2[:], A["oT_d"], writes=[o_b])
                for hf in range(2):
                    T.dma("pool", W["dsem"][hf], W["wd"][hf][:],
                          A["w_out"][l][hf * 512:(hf + 1) * 512, :].rearrange("(k p) c -> p k c", p=128),
                          writes=[W["wb"][hf]])
                n = 0
                for tt in range(4):
                    sl = slice(tt * 512, (tt + 1) * 512)
                    for dmc in range(8):
                        bk = n % 4
                        n += 1
                        for k in range(8):
                            T.op("pe", lambda e: e.matmul(R.ps[bk][:], lhsT=W["wd"][k // 4][:, k % 4, dmc * 128:(dmc + 1) * 128],
                                                          rhs=oT2[:, k, sl], start=(k == 0), stop=(k == 7)),
                                 reads=[o_b, W["wb"][k // 4]], writes=[R.psb[bk]])
                        T.op("dve", lambda e: e.tensor_tensor(out=R.xT[:, dmc, sl], in0=R.ps[bk][:], in1=R.xT[:, dmc, sl],
                                                              op=ALU.add),
                             reads=[R.psb[bk], R.xb[dmc][tt]], writes=[R.xb[dmc][tt]])
                norm(3 * l + 2)
                ffn("ffn2", l)
                lp = l + 1
            if do_pre:
                norm(3 * lp + 0)
                ffn("ffn1", lp)
            T.barrier_all()
        if do_pre:
            norm(3 * lp + 1)
            AP_ = dict(A)
            for nm in ("w_in", "w_in_sw", "w_uq", "w_uq_sw", "w_ukv", "smallsP"):
                AP_[nm] = A[nm][lp]
            emit_stage_P(R, hT, hb, AP_, mid_hook=mid_hook)
        st = T.new_dma_sem("stx")
        if final:
            ysq = [px.enter_context(nc.sbuf_tensor(U("ystg%d" % i), [128, 512], F32)) for i in range(2)]
            y_b = [Buf("y0"), Buf("y1")]
            ss, ss_b = R.ps[6], R.psb[6]
            n = 0
            for tt in range(4):
                sl = slice(tt * 512, (tt + 1) * 512)
                for k in range(8):
                    a = k % 2
                    T.op("act", lambda e: e.activation(out=sq[a][:], in_=R.xT[:, k, sl], func=AF.Square),
                         reads=[R.xb[k][tt]], writes=[sq_b[a]])
                    T.op("pe", lambda e: e.matmul(ss[:], lhsT=R.onesm[:], rhs=sq[a][:], start=(k == 0), stop=(k == 7)),
                         reads=[sq_b[a], R.onesm_b], writes=[ss_b])
                T.op("act", lambda e: e.activation(out=rstd[:], in_=ss[:], func=AF.Sqrt, bias=EPS, scale=1.0 / D),
                     reads=[ss_b], writes=[rstd_b])
                T.op("dve", lambda e: e.reciprocal(out=rstd[:], in_=rstd[:]), reads=[rstd_b], writes=[rstd_b])
                for k in range(8):
                    i = n % 2
                    n += 1
                    T.op("dve", lambda e: e.scalar_tensor_tensor(out=ysq[i][:], in0=R.xT[:, k, sl],
                                                                 scalar=gam[:, 3 * L, k:k + 1], in1=rstd[:],
                                                                 op0=ALU.mult, op1=ALU.mult),
                         reads=[R.xb[k][tt], rstd_b, gam_b], writes=[y_b[i]])
                    T.dma("sp", st, xout[k * 128:(k + 1) * 128, sl], ysq[i][:], reads=[y_b[i]])
        else:
            for k in range(8):
                T.dma("sp", st, xout[k * 128:(k + 1) * 128, :], R.xT[:, k, :], reads=[R.xb[k][t] for t in range(4)])
        T.barrier_all()
        return st


def emit_layer_A(R, A, l):
    nc, T = R.nc, R.T
    with ExitStack() as pa:
        R.oT = pa.enter_context(nc.sbuf_tensor(U("oT"), [128, 8, NT], BF16))
        R.oT_b = [[Buf("oT%d_%d" % (c, g)) for g in range(4)] for c in range(8)]
        AL = dict(A)
        for nm in ("cmp_w1_k", "cmp_w1_v", "cmp_w2_k", "cmp_w2_v", "cmp_posT_k", "cmp_posT_v"):
            AL[nm] = A[nm][l]
        with ExitStack() as ph:
            C = alloc_attn_common(R, AL, ph)
            gen = emit_nsa_compress(R, C, AL)
            next(gen)
            emit_mla(R, C, AL, hook=gen)
            for _ in gen:
                pass
            emit_nsa(R, C, AL)
            emit_sb(R, C, AL)
        st = T.new_dma_sem("stoT")
        T.dma("sp", st, A["oT_d"], R.oT[:], reads=[b for ll in R.oT_b for b in ll])
        T.barrier_all()


def build_launch(kind, l):
    nc = bass.Bass("TRN2", target_bir_lowering=False)
    A = {}

    def din(nm, shp, dt=F32):
        A[nm] = nc.dram_tensor(nm, shp, dt, kind="ExternalInput").ap()

    def dout(nm, shp, dt=F32):
        A[nm] = nc.dram_tensor(nm, shp, dt, kind="ExternalOutput").ap()
    for nm, shp in W_IN:
        din(nm, shp)
    for nm, shp, dt in C_IN:
        din(nm, shp, dt)
    din("xT_in", [D, NT])
    dout("xT_out", [D, NT])
    if kind != "first":
        for nm in GATHER:
            shp, dt = POUT[nm]
            din(nm + "_g", [4] + shp, dt)
        for nm in LOCAL:
            shp, dt = POUT[nm]
            din(nm, shp, dt)
        A["oT_d"] = nc.dram_tensor("oT_d", [128, 8, NT], BF16, kind="Internal").ap()
    if kind != "last":
        for nm, shp, dt in P_OUTS:
            if kind == "first" or nm not in LOCAL:
                dout(nm, shp, dt)
            else:
                A[nm + "_o"] = nc.dram_tensor(nm + "_o", shp, dt, kind="ExternalOutput").ap()
    with ExitStack() as stack:
        T = Tracker(nc, stack)
        R = setup_common(nc, stack, T)
        if kind != "first":
            emit_layer_A(R, A, l)
        AX = dict(A)
        if kind == "mid":
            for nm in LOCAL:
                AX[nm] = A[nm + "_o"]
        st = emit_layer_X(R, AX, l, do_post=(kind != "first"), do_pre=(kind != "last"), final=(kind == "last"),
                          xin=A["xT_in"], xout=A["xT_out"])
        nc.sync.wait_ge(T.sem[st], T.cnt[st])
    return nc


def _host_weights(inp):
    f = lambda a: np.ascontiguousarray(np.asarray(a, dtype=np.float32))
    W = {}
    for nm in ("ffn1_w_gate", "ffn1_w_up", "ffn1_w_down", "ffn2_w_gate", "ffn2_w_up", "ffn2_w_down", "w_in", "w_out"):
        W[nm] = f(inp[nm])
    W["w_uq"] = f(inp["mla_w_uq"])
    W["w_ukv"] = f(inp["mla_w_ukv"])
    W["w_in_sw"] = np.stack([make_w_in_sw(W["w_in"][l]) for l in range(L)])
    W["w_uq_sw"] = np.stack([swap_cols_rope(W["w_uq"][l], 96, 64, 32) for l in range(L)])
    W["smallsP"] = np.stack([make_smallsP(f(inp["mla_q_norm"])[l], f(inp["mla_kv_norm"])[l], f(inp["nsa_gate_bias"])[l])
                             for l in range(L)])
    W["cmp_w1_k"] = f(inp["nsa_cmp_w1_k"])
    W["cmp_w1_v"] = f(inp["nsa_cmp_w1_v"])
    W["cmp_w2_k"] = f(inp["nsa_cmp_w2_k"])
    W["cmp_w2_v"] = f(inp["nsa_cmp_w2_v"])
    W["cmp_posT_k"] = np.ascontiguousarray(f(inp["nsa_cmp_pos_k"]).transpose(0, 2, 1))
    W["cmp_posT_v"] = np.ascontiguousarray(f(inp["nsa_cmp_pos_v"]).transpose(0, 2, 1))
    g = np.zeros((128, 3 * L + 1, 8), np.float32)
    for l in range(L):
        for i, nm in enumerate(("ffn1_norm", "mix_norm", "ffn2_norm")):
            g[:, 3 * l + i, :] = f(inp[nm])[l].reshape(8, 128).T
    g[:, 3 * L, :] = f(inp["final_norm"]).reshape(8, 128).T
    W["gains"] = g
    return W


def _core_consts(core):
    c = {}
    c.update(rope_tables(core))
    c["masks"] = make_masks(core)
    c["ident"] = np.eye(128, dtype=np.float32)
    c.update(make_nsa_consts(core))
    return c


def _kernel_unfused_impl(**inp):
    x = np.asarray(inp["x"], dtype=np.float32)
    W = _host_weights(inp)
    consts = [_core_consts(c) for c in range(8)]
    xT = [np.ascontiguousarray(x[c // 4][own_positions(c)].T) for c in range(8)]
    cores = list(range(8))
    nc = build_launch("first", 0)
    ims = [dict(W, **consts[c], xT_in=xT[c]) for c in cores]
    res = run_bass_kernel_spmd(nc, ims, core_ids=cores).results
    for l in range(L):
        kind = "mid" if l < L - 1 else "last"
        nc = build_launch(kind, l)
        ims = []
        for c in cores:
            b = c // 4
            im = dict(W, **consts[c], xT_in=np.asarray(res[c]["xT_out"]))
            for nm in GATHER:
                im[nm + "_g"] = np.stack([np.asarray(res[4 * b + r][nm]) for r in range(4)])
            for nm in LOCAL:
                key = nm if l == 0 else nm + "_o"
                im[nm] = np.asarray(res[c][key])
            ims.append(im)
        res = run_bass_kernel_spmd(nc, ims, core_ids=cores).results
    out = np.zeros((B, S, D), np.float32)
    for c in cores:
        out[c // 4][own_positions(c)] = np.asarray(res[c]["xT_out"]).T
    return out


PIECES = [
    ([192, NT], [("kT_mla", "heads", 0, 3)]),
    ([192, NT], [("kT_mla", "heads", 3, 6)]),
    ([32, NT], [("kpeT", "rows", 0, 32)]),
    ([256, NT], [("vT_cmp", "rows", 0, 128), ("kT_sel", "rows", 128, 256)]),
    ([256, NT], [("kT_cmp", "rows", 0, 128), ("kT_win", "rows", 128, 256)]),
    ([256, NT], [("kT_sb", "rows", 0, 256)]),
    ([1024, 6, 65], [("v_mla", "half", 0, 0)]),
    ([1024, 6, 65], [("v_mla", "half", 1, 1)]),
    ([NT, 2, 65], [("v_sel", "all", 0, 0)]),
    ([NT, 2, 65], [("v_win", "all", 0, 0)]),
    ([NT, 256], [("v_sb", "all", 0, 0)]),
]


def make_pieces(nc, l):
    V = {"kT_mla": [None] * 6, "kT_mla_g": [None] * 6, "v_mla": [None] * 2, "v_mla_g": [None] * 2}
    GB = {"kT_mla": [None] * 6, "v_mla": [None] * 2}
    cc = []
    for k, (shp, members) in enumerate(PIECES):
        n = int(np.prod(shp))
        w = n // 128
        gs = nc.dram_tensor("gs%d_%d" % (l, k), [128, w], BF16, kind="Internal").ap()
        gd = nc.dram_tensor("gd%d_%d" % (l, k), [512, w], BF16, kind="Internal").ap()
        pb = Buf("piece%d_%d" % (l, k))
        cc.append((gs, gd, pb))
        fs = gs.rearrange("p w -> (p w)")
        fd = gd.rearrange("(r p) w -> r (p w)", r=4)
        if len(shp) == 2:
            ns = fs.rearrange("(a c) -> a c", c=shp[1])
            nd = fd.rearrange("r (a c) -> r a c", c=shp[1])
        else:
            ns = fs.rearrange("(t h x) -> t h x", h=shp[1], x=shp[2])
            nd = fd.rearrange("r (t h x) -> r t h x", h=shp[1], x=shp[2])
        for nm, kind, lo, hi in members:
            if kind == "heads":
                for h in range(lo, hi):
                    V[nm][h] = ns[(h - lo) * 64:(h - lo + 1) * 64, :]
                    V[nm + "_g"][h] = nd[:, (h - lo) * 64:(h - lo + 1) * 64, :]
                    GB[nm][h] = pb
            elif kind == "rows":
                V[nm] = ns[lo:hi, :]
                V[nm + "_g"] = nd[:, lo:hi, :]
                GB[nm] = pb
            elif kind == "half":
                V[nm][lo] = ns
                V[nm + "_g"][lo] = nd
                GB[nm][lo] = pb
            else:
                V[nm] = ns
                V[nm + "_g"] = nd
                GB[nm] = pb
    V["cc"] = cc
    V["gb"] = GB
    return V


def build_fused():
    nc = bass.Bass("TRN2", target_bir_lowering=False)
    A = {}
    for nm, shp in W_IN:
        A[nm] = nc.dram_tensor(nm, shp, F32, kind="ExternalInput").ap()
    for nm, shp, dt in C_IN:
        A[nm] = nc.dram_tensor(nm, shp, dt, kind="ExternalInput").ap()
    A["xT_in"] = nc.dram_tensor("xT_in", [D, NT], F32, kind="ExternalInput").ap()
    A["yT_out"] = nc.dram_tensor("yT_out", [D, NT], F32, kind="ExternalOutput").ap()
    A["xT_d"] = nc.dram_tensor("xT_d", [D, NT], F32, kind="Internal").ap()
    A["oT_d"] = nc.dram_tensor("oT_d", [128, 8, NT], BF16, kind="Internal").ap()
    LA = []
    for l in range(L):
        V = make_pieces(nc, l)
        for nm in LOCAL:
            shp, dt = POUT[nm]
            V[nm] = nc.dram_tensor("%s_l%d" % (nm, l), shp, dt, kind="Internal").ap()
        LA.append(V)
    with ExitStack() as stack:
        T = Tracker(nc, stack)
        R = setup_common(nc, stack, T)
        st = None
        GRP = [[0, 1, 2, 3], [4, 5, 6, 7]]

        def mk_hook(ll):
            def hook():
                T.wait_dma_all("pool")
                for k in (2, 0, 6, 7, 1):
                    gs, gd, pb = LA[ll]["cc"][k]
                    T.collective(T.new_dma_sem("cc%d_%d" % (ll, k)), gs, gd, GRP, writes=[pb])
            return hook

        for l in range(L):
            if l == 0:
                emit_layer_X(R, dict(A, **LA[0]), 0, do_post=False, do_pre=True, final=False,
                             xin=A["xT_in"], xout=A["xT_d"], mid_hook=mk_hook(0))
            T.barrier_all()
            for k in (4, 3, 8, 9, 5, 10):
                gs, gd, pb = LA[l]["cc"][k]
                T.collective(T.new_dma_sem("cc%d_%d" % (l, k)), gs, gd, GRP, writes=[pb])
            emit_layer_A(R, dict(A, **LA[l]), l)
            last = (l == L - 1)
            AX = dict(A, **(LA[l + 1] if not last else {}))
            st = emit_layer_X(R, AX, l, do_post=True, do_pre=not last, final=last,
                              xin=A["xT_d"], xout=(A["yT_out"] if last else A["xT_d"]),
                              mid_hook=(None if last else mk_hook(l + 1)))
        nc.sync.wait_ge(T.sem[st], T.cnt[st])
    return nc


def kernel_unfused(**inp):
    return _kernel_unfused_impl(**inp)


def kernel_fused(**inp):
    x = np.asarray(inp["x"], dtype=np.float32)
    W = _host_weights(inp)
    cores = list(range(8))
    nc = build_fused()
    ims = []
    for c in cores:
        xT = np.ascontiguousarray(x[c // 4][own_positions(c)].T)
        ims.append(dict(W, **_core_consts(c), xT_in=xT))
    res = run_bass_kernel_spmd(nc, ims, core_ids=cores).results
    out = np.zeros((B, S, D), np.float32)
    for c in cores:
        out[c // 4][own_positions(c)] = np.asarray(res[c]["yT_out"]).T
    return out


def kernel(**inp):
    return kernel_fused(**inp)
```

```python
import numpy as np
import ml_dtypes
from contextlib import ExitStack
import concourse.bass as bass
import concourse.mybir as mybir
from concourse.bass_utils import run_bass_kernel_spmd

F32 = mybir.dt.float32
BF16 = mybir.dt.bfloat16
AF = mybir.ActivationFunctionType
ALU = mybir.AluOpType
AX = mybir.AxisListType

D = 1024
S = 8192
B = 2
DFF = 2816
NT = 2048
NBLK = 16
EPS = 1e-6
DIN = 2354


_UN = [0]
CC_INC = 1


def U(name):
    _UN[0] += 1
    return "%s_u%d" % (name, _UN[0])


class Buf:
    __slots__ = ("name", "w", "r")

    def __init__(self, name):
        self.name = name
        self.w = None
        self.r = {}


class Tracker:
    def __init__(self, nc, stack):
        self.nc = nc
        self.stack = stack
        self.eng = {"pe": nc.tensor, "act": nc.scalar, "dve": nc.vector,
                    "pool": nc.gpsimd, "sp": nc.sync}
        self.sem = {}
        self.cnt = {}
        self.seen = {k: {} for k in self.eng}
        for k in self.eng:
            self.sem[k] = stack.enter_context(nc.semaphore("s_" + k))
            self.cnt[k] = 0
        self.ndma = 0

    def new_dma_sem(self, name):
        key = "dma_" + name + "_%d" % self.ndma
        self.ndma += 1
        self.sem[key] = self.stack.enter_context(self.nc.semaphore(key))
        self.cnt[key] = 0
        return key

    def _deps(self, e, reads, writes, ignore=None):
        deps = {}

        def add(k, c):
            if c > deps.get(k, 0):
                deps[k] = c
        for b in reads:
            if b.w is not None:
                add(*b.w)
        for b in writes:
            if b.w is not None:
                add(*b.w)
            for k, c in b.r.items():
                add(k, c)
        for k, c in deps.items():
            if (k == "pe" and e == "pe") or k == ignore:
                continue
            if k.startswith("dma_"):
                c = max(c, self.cnt[k])
            if c > self.seen[e].get(k, 0):
                self.eng[e].wait_ge(self.sem[k], c)
                self.seen[e][k] = c

    def op(self, e, fn, reads=(), writes=()):
        self._deps(e, reads, writes)
        ins = fn(self.eng[e])
        self.cnt[e] += 1
        c = self.cnt[e]
        ins.then_inc(self.sem[e], 1)
        for b in reads:
            if c > b.r.get(e, 0):
                b.r[e] = c
        for b in writes:
            b.w = (e, c)
            b.r = {}
        return ins

    def dma(self, q, dsem, out, in_, reads=(), writes=()):
        self._deps(q, reads, writes, ignore=dsem)
        ins = self.eng[q].dma_start(out=out, in_=in_)
        self.cnt[dsem] += 16
        c = self.cnt[dsem]
        ins.then_inc(self.sem[dsem], 16)
        for b in reads:
            if c > b.r.get(dsem, 0):
                b.r[dsem] = c
        for b in writes:
            b.w = (dsem, c)
            b.r = {}
        return ins

    def collective(self, dsem, src, dst, groups, reads=(), writes=()):
        self._deps("pool", reads, writes, ignore=dsem)
        ins = self.nc.gpsimd.collective_compute("AllGather", ALU.bypass, replica_groups=groups,
                                                ins=[src.opt()], outs=[dst.opt()])
        self.cnt[dsem] += CC_INC
        c = self.cnt[dsem]
        ins.then_inc(self.sem[dsem], CC_INC)
        for b in reads:
            if c > b.r.get(dsem, 0):
                b.r[dsem] = c
        for b in writes:
            b.w = (dsem, c)
            b.r = {}
        return ins

    def wait_dma_all(self, e):
        for k, c in self.cnt.items():
            if k.startswith("dma_") and c > self.seen[e].get(k, 0):
                self.eng[e].wait_ge(self.sem[k], c)
                self.seen[e][k] = c

    def barrier_all(self):
        for e in self.eng:
            for k, c in self.cnt.items():
                if k == e or c == 0:
                    continue
                if c > self.seen[e].get(k, 0):
                    self.eng[e].wait_ge(self.sem[k], c)
                    self.seen[e][k] = c


class Res:
    pass


def setup_common(nc, stack, T):
    R = Res()
    R.nc, R.T, R.stack = nc, T, stack
    R.ExitStack = ExitStack
    R.ps = []
    R.psb = []
    for i in range(8):
        R.ps.append(stack.enter_context(nc.psum_tensor("ps%d" % i, [128, 512], F32)))
        R.psb.append(Buf("ps%d" % i))
    R.xb = [[Buf("x%d_%d" % (k, t)) for t in range(4)] for k in range(8)]
    R.onesm = stack.enter_context(nc.sbuf_tensor(U("onesm"), [128, 128], BF16))
    R.onesm_b = Buf("onesm")
    T.op("pool", lambda e: e.memset(R.onesm[:], 1.0), writes=[R.onesm_b])
    return R


def alloc_xT(R, stack):
    R.xT = stack.enter_context(R.nc.sbuf_tensor(U("xT"), [128, 8, NT], F32))


def emit_norm(R, hT, hb, gam, gam_b, sq, sq_b, rstd, rstd_b, ssbank):
    T = R.T
    ss, ss_b = R.ps[ssbank], R.psb[ssbank]
    for tt in range(4):
        sl = slice(tt * 512, (tt + 1) * 512)
        for k in range(8):
            a = k % 2
            T.op("act", lambda e: e.activation(out=sq[a][:], in_=R.xT[:, k, sl], func=AF.Square),
                 reads=[R.xb[k][tt]], writes=[sq_b[a]])
            T.op("pe", lambda e: e.matmul(ss[:], lhsT=R.onesm[:], rhs=sq[a][:], start=(k == 0), stop=(k == 7)),
                 reads=[sq_b[a], R.onesm_b], writes=[ss_b])
        T.op("act", lambda e: e.activation(out=rstd[:], in_=ss[:], func=AF.Sqrt, bias=EPS, scale=1.0 / D),
             reads=[ss_b], writes=[rstd_b])
        T.op("dve", lambda e: e.reciprocal(out=rstd[:], in_=rstd[:]), reads=[rstd_b], writes=[rstd_b])
        for k in range(8):
            T.op("dve", lambda e: e.scalar_tensor_tensor(out=hT[:, k, sl], in0=R.xT[:, k, sl],
                                                         scalar=gam[:, k:k + 1], in1=rstd[:],
                                                         op0=ALU.mult, op1=ALU.mult),
                 reads=[R.xb[k][tt], rstd_b, gam_b], writes=[hb[k][tt]])


def emit_ffn(R, hT, hb, wg_d, wu_d, wd_d, W):
    T = R.T
    nfg = 6
    n_g = n_y = 0
    for fg in range(nfg):
        ncf = 4 if fg < 5 else 2
        wcols = ncf * 128
        c0 = fg * 512
        s = fg % 2
        wgs, wus, wds, wb, dsem = W["wg"][s], W["wu"][s], W["wd"][s], W["wb"][s], W["dsem"][s]
        T.dma("pool", dsem, wgs[:, :, 0:wcols],
              wg_d[:, c0:c0 + wcols].rearrange("(k p) c -> p k c", p=128), writes=[wb])
        T.dma("pool", dsem, wus[:, :, 0:wcols],
              wu_d[:, c0:c0 + wcols].rearrange("(k p) c -> p k c", p=128), writes=[wb])
        T.dma("pool", dsem, wds[:, 0:ncf, :],
              wd_d[c0:c0 + wcols, :].rearrange("(c p) m -> p c m", p=128), writes=[wb])
        for tt in range(4):
            sl = slice(tt * 512, (tt + 1) * 512)
            asl = (fg * 4 + tt) % 2
            for c in range(ncf):
                gi, ui = W["gbanks"][n_g % 2], W["ubanks"][n_g % 2]
                sgi = n_g % 2
                n_g += 1
                for k in range(8):
                    T.op("pe", lambda e: e.matmul(R.ps[gi][:], lhsT=wgs[:, k, c * 128:(c + 1) * 128],
                                                  rhs=hT[:, k, sl], start=(k == 0), stop=(k == 7)),
                         reads=[wb, hb[k][tt]], writes=[R.psb[gi]])
                for k in range(8):
                    T.op("pe", lambda e: e.matmul(R.ps[ui][:], lhsT=wus[:, k, c * 128:(c + 1) * 128],
                                                  rhs=hT[:, k, sl], start=(k == 0), stop=(k == 7)),
                         reads=[wb, hb[k][tt]], writes=[R.psb[ui]])
                T.op("act", lambda e: e.activation(out=W["sg"][sgi][:], in_=R.ps[gi][:], func=AF.Silu),
                     reads=[R.psb[gi]], writes=[W["sg_b"][sgi]])
                T.op("dve", lambda e: e.tensor_tensor(out=W["act"][asl][:, c, :], in0=W["sg"][sgi][:],
                                                      in1=R.ps[ui][:], op=ALU.mult),
                     reads=[W["sg_b"][sgi], R.psb[ui]], writes=[W["act_b"][asl][c]])
            for dmc in range(8):
                yi = W["ybanks"][n_y % 2]
                n_y += 1
                for c in range(ncf):
                    T.op("pe", lambda e: e.matmul(R.ps[yi][:], lhsT=wds[:, c, dmc * 128:(dmc + 1) * 128],
                                                  rhs=W["act"][asl][:, c, :], start=(c == 0), stop=(c == ncf - 1)),
                         reads=[wb, W["act_b"][asl][c]], writes=[R.psb[yi]])
                T.op("dve", lambda e: e.scalar_tensor_tensor(out=R.xT[:, dmc, sl], in0=R.ps[yi][:], scalar=0.5,
                                                             in1=R.xT[:, dmc, sl], op0=ALU.mult, op1=ALU.add),
                     reads=[R.psb[yi], R.xb[dmc][tt]], writes=[R.xb[dmc][tt]])


def alloc_ffn_work(R):
    nc, stack, T = R.nc, R.stack, R.T
    W = {}
    W["wg"] = [stack.enter_context(nc.sbuf_tensor(U("wg%d" % i), [128, 8, 512], BF16)) for i in range(2)]
    W["wu"] = [stack.enter_context(nc.sbuf_tensor(U("wu%d" % i), [128, 8, 512], BF16)) for i in range(2)]
    W["wd"] = [stack.enter_context(nc.sbuf_tensor(U("wd%d" % i), [128, 4, 1024], BF16)) for i in range(2)]
    W["wb"] = [Buf("wslot%d" % i) for i in range(2)]
    W["dsem"] = [T.new_dma_sem("ffnw%d" % i) for i in range(2)]
    W["sg"] = [stack.enter_context(nc.sbuf_tensor(U("sg%d" % i), [128, 512], F32)) for i in range(2)]
    W["sg_b"] = [Buf("sg%d" % i) for i in range(2)]
    W["act"] = [stack.enter_context(nc.sbuf_tensor(U("act%d" % i), [128, 4, 512], BF16)) for i in range(2)]
    W["act_b"] = [[Buf("act%d_%d" % (i, c)) for c in range(4)] for i in range(2)]
    W["gbanks"], W["ubanks"], W["ybanks"] = [0, 1], [2, 3], [4, 5]
    return W


C_CQ, C_CKV, C_KR, C_NQ, C_NKC, C_NVC, C_NKS, C_NVS, C_NKW, C_NVW, C_NG, C_SQ, C_SK, C_SV = (
    0, 256, 384, 416, 800, 928, 1056, 1184, 1312, 1440, 1568, 1586, 1842, 2098)
SW_KR, SW_NQ, SW_NKC, SW_NKS, SW_NKW = 0, 32, 416, 544, 672
NSW = 800


def emit_stage_P(R, hT, hb, A, mid_hook=None):
    nc, T = R.nc, R.T
    with R.ExitStack() as ph:
        def sb(name, shape, dt):
            return ph.enter_context(nc.sbuf_tensor(U(name), shape, dt))
        win = sb("P_win", [128, 8, DIN], BF16)
        wsw = sb("P_wsw", [128, 8, NSW], BF16)
        wuq = sb("P_wuq", [128, 2, 576], BF16)
        wuqs = sb("P_wuqs", [128, 2, 576], BF16)
        wukv = sb("P_wukv", [128, 768], BF16)
        sm = sb("P_sm", [128, 32], F32)
        wb_, smb = Buf("P_w"), Buf("P_sm")
        dw = T.new_dma_sem("Pw")
        T.dma("pool", dw, win[:], A["w_in"].rearrange("(k p) c -> p k c", p=128), writes=[wb_])
        T.dma("pool", dw, wsw[:], A["w_in_sw"].rearrange("(k p) c -> p k c", p=128), writes=[wb_])
        T.dma("pool", dw, wuq[:], A["w_uq"].rearrange("(k p) c -> p k c", p=128), writes=[wb_])
        T.dma("pool", dw, wuqs[:], A["w_uq_sw"].rearrange("(k p) c -> p k c", p=128), writes=[wb_])
        T.dma("pool", dw, wukv[:], A["w_ukv"], writes=[wb_])
        dw2 = T.new_dma_sem("Psm")
        T.dma("sp", dw2, sm[:], A["smallsP"], writes=[smb])
        tabs = []
        for i in range(2):
            tabs.append(dict(
                cosM=sb("P_cosM%d" % i, [96, 512], F32), sinM=sb("P_sinM%d" % i, [96, 512], F32),
                cosK=sb("P_cosK%d" % i, [32, 512], F32), sinK=sb("P_sinK%d" % i, [32, 512], F32),
                cosN=sb("P_cosN%d" % i, [128, 512], F32), sinN=sb("P_sinN%d" % i, [128, 512], F32),
                b=Buf("P_tab%d" % i), sem=T.new_dma_sem("Ptab%d" % i)))
        sq = [sb("P_sq%d" % i, [128, 512], BF16) for i in range(2)]
        sq_b = [Buf("P_sq0"), Buf("P_sq1")]
        rs = sb("P_rs", [128, 512], F32)
        rs_b = Buf("P_rs")
        cqn = sb("P_cqn", [128, 2, 512], BF16)
        cqn_b = Buf("P_cqn")
        ckvn = sb("P_ckvn", [128, 512], BF16)
        ckvn_b = Buf("P_ckvn")
        t1 = [sb("P_t1_%d" % i, [128, 512], F32) for i in range(2)]
        t2 = [sb("P_t2_%d" % i, [128, 512], F32) for i in range(2)]
        t_b = [Buf("P_t0"), Buf("P_t1")]
        NST = 4
        stg = [sb("P_stg%d" % i, [128, 512], BF16) for i in range(NST)]
        stg_b = [Buf("P_stg%d" % i) for i in range(NST)]
        stg_sem = [T.new_dma_sem("Pstg%d" % i) for i in range(NST)]
        vst = [sb("P_vst%d" % i, [128, 6, 65], BF16) for i in range(2)]
        vst_b = [Buf("P_vst0"), Buf("P_vst1")]
        vst_sem = [T.new_dma_sem("Pvst%d" % i) for i in range(2)]
        v2st = [sb("P_v2st%d" % i, [128, 2, 2, 65], BF16) for i in range(2)]
        v2st_b = [Buf("P_v2st0"), Buf("P_v2st1")]
        v2st_sem = [T.new_dma_sem("Pv2st%d" % i) for i in range(2)]
        vsb = [sb("P_vsb%d" % i, [128, 256], BF16) for i in range(2)]
        vsb_b = [Buf("P_vsb0"), Buf("P_vsb1")]
        vsb_sem = [T.new_dma_sem("Pvsb%d" % i) for i in range(2)]
        gst = [sb("P_gst%d" % i, [128, 18], F32) for i in range(2)]
        gst_b = [Buf("P_gst0"), Buf("P_gst1")]
        gst_sem = [T.new_dma_sem("Pgst%d" % i) for i in range(2)]
        for i in range(2):
            T.op("pool", lambda e: e.memset(vst[i][:], 1.0), writes=[vst_b[i]])
            T.op("pool", lambda e: e.memset(v2st[i][:], 1.0), writes=[v2st_b[i]])

        cnt = {"bank": 0, "stg": 0, "t": 0}

        def nbank():
            cnt["bank"] += 1
            return cnt["bank"] % 4

        def chain(bank, M, lhs_fn, rhs_fn, nk, reads, N=512, rows=None):
            o = R.ps[bank][0:M, 0:N] if rows is None else R.ps[bank][rows[0]:rows[1], 0:N]
            for k in range(nk):
                T.op("pe", lambda e: e.matmul(o, lhsT=lhs_fn(k), rhs=rhs_fn(k), start=(k == 0), stop=(k == nk - 1)),
                     reads=reads(k), writes=[R.psb[bank]])

        def store(src_ps_ap, src_bufs, M, dst_dram, use_act=False):
            i = cnt["stg"] % NST
            cnt["stg"] += 1
            eng = "act" if use_act else "dve"
            if use_act:
                T.op("act", lambda e: e.activation(out=stg[i][0:M, :], in_=src_ps_ap, func=AF.Copy),
                     reads=src_bufs, writes=[stg_b[i]])
            else:
                T.op("dve", lambda e: e.tensor_copy(out=stg[i][0:M, :], in_=src_ps_ap),
                     reads=src_bufs, writes=[stg_b[i]])
            T.dma("sp", stg_sem[i], dst_dram, stg[i][0:M, :], reads=[stg_b[i]])

        def rope_store(bankA, bankB, M, cos, sin, tb, dst_dram):
            j = cnt["t"] % 2
            cnt["t"] += 1
            i = cnt["stg"] % NST
            cnt["stg"] += 1
            T.op("dve", lambda e: e.tensor_tensor(out=t1[j][0:M, :], in0=R.ps[bankA][0:M, :], in1=cos, op=ALU.mult),
                 reads=[R.psb[bankA], tb], writes=[t_b[j]])
            T.op("dve", lambda e: e.tensor_tensor(out=t2[j][0:M, :], in0=R.ps[bankB][0:M, :], in1=sin, op=ALU.mult),
                 reads=[R.psb[bankB], tb], writes=[t_b[j]])
            T.op("pool", lambda e: e.tensor_tensor(out=stg[i][0:M, :], in0=t1[j][0:M, :], in1=t2[j][0:M, :], op=ALU.add),
                 reads=[t_b[j]], writes=[stg_b[i]])
            T.dma("sp", stg_sem[i], dst_dram, stg[i][0:M, :], reads=[stg_b[i]])

        for tt in range(4):
            sl = slice(tt * 512, (tt + 1) * 512)
            tab = tabs[tt % 2]
            for nm in ("cosM", "sinM", "cosK", "sinK"):
                T.dma("sp", tab["sem"], tab[nm][:], A[nm][:, sl], writes=[tab["b"]])
            hreads = lambda k: [wb_, hb[k][tt]]
            for c in range(2):
                chain(4 + c, 128, lambda k: win[:, k, C_CQ + c * 128:C_CQ + (c + 1) * 128],
                      lambda k: hT[:, k, sl], 8, hreads)
            chain(6, 128, lambda k: win[:, k, C_CKV:C_CKV + 128], lambda k: hT[:, k, sl], 8, hreads)
            for c in range(2):
                T.op("act", lambda e: e.activation(out=sq[c][:], in_=R.ps[4 + c][:], func=AF.Square),
                     reads=[R.psb[4 + c]], writes=[sq_b[c]])
            for c in range(2):
                T.op("pe", lambda e: e.matmul(R.ps[7][:], lhsT=R.onesm[:], rhs=sq[c][:], start=(c == 0), stop=(c == 1)),
                     reads=[sq_b[c], R.onesm_b], writes=[R.psb[7]])
            T.op("act", lambda e: e.activation(out=rs[:], in_=R.ps[7][:], func=AF.Sqrt, bias=EPS, scale=1.0 / 256),
                 reads=[R.psb[7]], writes=[rs_b])
            T.op("dve", lambda e: e.reciprocal(out=rs[:], in_=rs[:]), reads=[rs_b], writes=[rs_b])
            for c in range(2):
                T.op("dve", lambda e: e.scalar_tensor_tensor(out=cqn[:, c, :], in0=R.ps[4 + c][:], scalar=sm[:, c:c + 1],
                                                             in1=rs[:], op0=ALU.mult, op1=ALU.mult),
                     reads=[R.psb[4 + c], rs_b, smb], writes=[cqn_b])
            T.op("act", lambda e: e.activation(out=sq[0][:], in_=R.ps[6][:], func=AF.Square),
                 reads=[R.psb[6]], writes=[sq_b[0]])
            T.op("pe", lambda e: e.matmul(R.ps[7][:], lhsT=R.onesm[:], rhs=sq[0][:], start=True, stop=True),
                 reads=[sq_b[0], R.onesm_b], writes=[R.psb[7]])
            T.op("act", lambda e: e.activation(out=rs[:], in_=R.ps[7][:], func=AF.Sqrt, bias=EPS, scale=1.0 / 128),
                 reads=[R.psb[7]], writes=[rs_b])
            T.op("dve", lambda e: e.reciprocal(out=rs[:], in_=rs[:]), reads=[rs_b], writes=[rs_b])
            T.op("dve", lambda e: e.scalar_tensor_tensor(out=ckvn[:], in0=R.ps[6][:], scalar=sm[:, 2:3],
                                                         in1=rs[:], op0=ALU.mult, op1=ALU.mult),
                 reads=[R.psb[6], rs_b, smb], writes=[ckvn_b])
            for h in range(6):
                ba, bb = nbank(), 4 + (h % 2)
                chain(ba, 96, lambda c: wuq[:, c, h * 96:(h + 1) * 96], lambda c: cqn[:, c, :], 2,
                      lambda c: [wb_, cqn_b])
                chain(bb, 96, lambda c: wuqs[:, c, h * 96:(h + 1) * 96], lambda c: cqn[:, c, :], 2,
                      lambda c: [wb_, cqn_b])
                rope_store(ba, bb, 96, tab["cosM"][:], tab["sinM"][:], tab["b"], A["qT_mla"][h, :, sl])
            for h in range(6):
                ba = nbank()
                chain(ba, 64, lambda c: wukv[:, h * 128:h * 128 + 64], lambda c: ckvn[:], 1, lambda c: [wb_, ckvn_b])
                store(R.ps[ba][0:64, :], [R.psb[ba]], 64, A["kT_mla"][h][:, sl], use_act=(h % 2 == 0))
            ba, bb = nbank(), 4
            chain(ba, 32, lambda k: win[:, k, C_KR:C_KR + 32], lambda k: hT[:, k, sl], 8, hreads)
            chain(bb, 32, lambda k: wsw[:, k, SW_KR:SW_KR + 32], lambda k: hT[:, k, sl], 8, hreads)
            rope_store(ba, bb, 32, tab["cosK"][:], tab["sinK"][:], tab["b"], A["kpeT"][:, sl])
            for bl in range(4):
                tb = tt * 4 + bl
                lsl = slice(bl * 128, (bl + 1) * 128)
                i = tb % 2
                ba = nbank()
                T.op("pe", lambda e: e.matmul(R.ps[ba][:, 0:384], lhsT=ckvn[:, lsl],
                                              rhs=wukv[:].rearrange("p (h x) -> p h x", x=128)[:, :, 64:128],
                                              start=True, stop=True),
                     reads=[wb_, ckvn_b], writes=[R.psb[ba]])
                T.op("dve", lambda e: e.tensor_copy(out=vst[i][:, :, 0:64],
                                                    in_=R.ps[ba][:, 0:384].rearrange("p (h x) -> p h x", x=64)),
                     reads=[R.psb[ba]], writes=[vst_b[i]])
                T.dma("sp", vst_sem[i], A["v_mla"][tb // 8][(tb % 8) * 128:(tb % 8 + 1) * 128, :, :], vst[i][:], reads=[vst_b[i]])
        if mid_hook is not None:
            mid_hook()
        for tt in range(4):
            sl = slice(tt * 512, (tt + 1) * 512)
            tab = tabs[tt % 2]
            for nm in ("cosN", "sinN"):
                T.dma("sp", tab["sem"], tab[nm][:], A[nm][:, sl], writes=[tab["b"]])
            hreads = lambda k: [wb_, hb[k][tt]]
            ropes = [(C_NQ + 128 * c, SW_NQ + 128 * c, A["qT_nsa"][128 * c:128 * (c + 1), sl]) for c in range(3)]
            ropes += [(C_NKC, SW_NKC, A["kT_cmp"][:, sl]), (C_NKS, SW_NKS, A["kT_sel"][:, sl]),
                      (C_NKW, SW_NKW, A["kT_win"][:, sl])]
            for n, (ca, cs, dst) in enumerate(ropes):
                ba, bb = nbank(), 4 + (n % 2)
                chain(ba, 128, lambda k: win[:, k, ca:ca + 128], lambda k: hT[:, k, sl], 8, hreads)
                chain(bb, 128, lambda k: wsw[:, k, cs:cs + 128], lambda k: hT[:, k, sl], 8, hreads)
                rope_store(ba, bb, 128, tab["cosN"][:], tab["sinN"][:], tab["b"], dst)
            plain = [(C_NVC, A["vT_cmp"][:, sl]), (C_SQ, A["qT_sb"][0:128, sl]), (C_SQ + 128, A["qT_sb"][128:256, sl]),
                     (C_SK, A["kT_sb"][0:128, sl]), (C_SK + 128, A["kT_sb"][128:256, sl])]
            for n, (ca, dst) in enumerate(plain):
                ba = nbank()
                chain(ba, 128, lambda k: win[:, k, ca:ca + 128], lambda k: hT[:, k, sl], 8, hreads)
                store(R.ps[ba][:, :], [R.psb[ba]], 128, dst, use_act=(n % 2 == 0))
            for bl in range(4):
                tb = tt * 4 + bl
                tsl = slice(tb * 128, (tb + 1) * 128)
                lsl = slice(bl * 128, (bl + 1) * 128)
                i = tb % 2
                ba = nbank()
                for n, ca in enumerate((C_NVS, C_NVW)):
                    for k in range(8):
                        T.op("pe", lambda e: e.matmul(R.ps[ba][:, n * 128:(n + 1) * 128], lhsT=hT[:, k, tsl],
                                                      rhs=win[:, k, ca:ca + 128], start=(k == 0), stop=(k == 7)),
                             reads=[wb_, hb[k][tt]], writes=[R.psb[ba]])
                for k in range(8):
                    T.op("pe", lambda e: e.matmul(R.ps[ba][:, 256:274], lhsT=hT[:, k, tsl],
                                                  rhs=win[:, k, C_NG:C_NG + 18], start=(k == 0), stop=(k == 7)),
                         reads=[wb_, hb[k][tt]], writes=[R.psb[ba]])
                T.op("dve", lambda e: e.tensor_copy(out=v2st[i][:, :, :, 0:64],
                                                    in_=R.ps[ba][:, 0:256].rearrange("p (a h x) -> p a h x", a=2, x=64)),
                     reads=[R.psb[ba]], writes=[v2st_b[i]])
                T.dma("sp", v2st_sem[i], A["v_sel"][tsl, :, :], v2st[i][:, 0, :, :], reads=[v2st_b[i]])
                T.dma("sp", v2st_sem[i], A["v_win"][tsl, :, :], v2st[i][:, 1, :, :], reads=[v2st_b[i]])
                T.op("dve", lambda e: e.tensor_tensor(out=gst[i][:], in0=R.ps[ba][:, 256:274], in1=sm[:, 8:26], op=ALU.add),
                     reads=[R.psb[ba], smb], writes=[gst_b[i]])
                T.op("act", lambda e: e.activation(out=gst[i][:], in_=gst[i][:], func=AF.Sigmoid),
                     reads=[gst_b[i]], writes=[gst_b[i]])
                T.dma("sp", gst_sem[i], A["gates"][tsl, :], gst[i][:], reads=[gst_b[i]])
                ba = nbank()
                for k in range(8):
                    T.op("pe", lambda e: e.matmul(R.ps[ba][:, 0:256], lhsT=hT[:, k, tsl],
                                                  rhs=win[:, k, C_SV:C_SV + 256], start=(k == 0), stop=(k == 7)),
                         reads=[wb_, hb[k][tt]], writes=[R.psb[ba]])
                T.op("act", lambda e: e.activation(out=vsb[i][:], in_=R.ps[ba][:, 0:256], func=AF.Copy),
                     reads=[R.psb[ba]], writes=[vsb_b[i]])
                T.dma("sp", vsb_sem[i], A["v_sb"][tsl, :], vsb[i][:], reads=[vsb_b[i]])
        T.barrier_all()


THETA = 500000.0


def own_positions(core):
    j = core % 4
    return ((4 * np.arange(NBLK)[:, None] + j) * 128 + np.arange(128)[None, :]).reshape(-1)


def _cs(pos, rot):
    half = rot // 2
    inv = np.float32(THETA) ** (-(np.arange(half, dtype=np.float32) / np.float32(half)))
    ang = pos.astype(np.float32)[None, :] * inv.astype(np.float32)[:, None]
    return np.cos(ang).astype(np.float32), np.sin(ang).astype(np.float32)


def rope_tables(core):
    pos = own_positions(core)
    n = pos.shape[0]
    c16, s16 = _cs(pos, 32)
    c8, s8 = _cs(pos, 16)
    cosM = np.ones((96, n), np.float32); sinM = np.zeros((96, n), np.float32)
    cosM[64:80] = c16; cosM[80:96] = c16; sinM[64:80] = -s16; sinM[80:96] = s16
    cosK = np.concatenate([c16, c16], 0); sinK = np.concatenate([-s16, s16], 0)
    cosN = np.ones((128, n), np.float32); sinN = np.zeros((128, n), np.float32)
    for h in range(2):
        cosN[h * 64:h * 64 + 8] = c8; cosN[h * 64 + 8:h * 64 + 16] = c8
        sinN[h * 64:h * 64 + 8] = -s8; sinN[h * 64 + 8:h * 64 + 16] = s8
    return dict(cosM=cosM, sinM=sinM, cosK=cosK, sinK=sinK, cosN=cosN, sinN=sinN)


def swap_cols_rope(w, head_w, rope0, rot):
    w = np.array(w, copy=True)
    half = rot // 2
    nh = w.shape[1] // head_w
    for h in range(nh):
        a = h * head_w + rope0
        tmp = w[:, a:a + half].copy()
        w[:, a:a + half] = w[:, a + half:a + rot]
        w[:, a + half:a + rot] = tmp
    return w


def make_w_in_sw(w_in):
    parts = [swap_cols_rope(w_in[:, C_KR:C_KR + 32], 32, 0, 32),
             swap_cols_rope(w_in[:, C_NQ:C_NQ + 384], 64, 0, 16),
             swap_cols_rope(w_in[:, C_NKC:C_NKC + 128], 64, 0, 16),
             swap_cols_rope(w_in[:, C_NKS:C_NKS + 128], 64, 0, 16),
             swap_cols_rope(w_in[:, C_NKW:C_NKW + 128], 64, 0, 16)]
    return np.ascontiguousarray(np.concatenate(parts, axis=1))


def make_smallsP(q_norm, kv_norm, gate_bias):
    sm = np.zeros((128, 32), np.float32)
    sm[:, 0:2] = q_norm.reshape(2, 128).T
    sm[:, 2] = kv_norm
    sm[:, 8:26] = gate_bias[None, :]
    return sm


P_OUTS = [("qT_mla", [6, 96, NT], BF16), ("kT_mla", [6, 64, NT], BF16), ("kpeT", [32, NT], BF16),
          ("qT_nsa", [384, NT], BF16), ("kT_cmp", [128, NT], BF16), ("kT_sel", [128, NT], BF16),
          ("kT_win", [128, NT], BF16), ("vT_cmp", [128, NT], BF16), ("qT_sb", [256, NT], BF16),
          ("kT_sb", [256, NT], BF16), ("v_mla", [NT, 6, 65], BF16), ("v_sel", [NT, 2, 65], BF16),
          ("v_win", [NT, 2, 65], BF16), ("gates", [NT, 18], F32), ("v_sb", [NT, 256], BF16)]
P_INS = [("w_in", [D, DIN]), ("w_in_sw", [D, NSW]), ("w_uq", [256, 576]), ("w_uq_sw", [256, 576]),
         ("w_ukv", [128, 768]), ("smallsP", [128, 32]),
         ("cosM", [96, NT]), ("sinM", [96, NT]), ("cosK", [32, NT]), ("sinK", [32, NT]),
         ("cosN", [128, NT]), ("sinN", [128, NT])]


def make_masks(core):
    j = core % 4
    k = np.arange(128)[:, None]
    q = np.arange(128)[None, :]
    ones = np.ones((128, 128), np.float32)
    zeros = np.zeros((128, 128), np.float32)
    tri = (k <= q).astype(np.float32)
    stri = (k < q).astype(np.float32)
    gt = (k > q).astype(np.float32)
    M = np.zeros((128, 22, 128), np.float32)
    M[:, 20] = (k >= q).astype(np.float32)
    M[:, 21] = 1.0
    for d in range(4):
        M[:, d] = ones if d < j else (tri if d == j else zeros)
        M[:, 16 + d] = ones if d < j else (stri if d == j else zeros)
    for dd in range(8):
        d = dd - 4
        if d < j - 4 or d > j:
            M[:, 4 + dd] = zeros
        elif d == j - 4:
            M[:, 4 + dd] = gt
        elif d == j:
            M[:, 4 + dd] = tri
        else:
            M[:, 4 + dd] = ones
    for m4 in range(4):
        i16 = 4 * m4 + j
        M[:, 12 + m4] = (16 * k + 31 - 128 * i16 <= q).astype(np.float32)
    return M.astype(ml_dtypes.bfloat16)


def ktcol(kt):
    return (kt % 4) * NT + (kt // 4) * 128


def ktile(kt):
    return (kt % 4) * NBLK + (kt // 4)


def emit_oT_store(R, C, obank, G, ch, po, zcol=64, stride=65):
    T = R.T
    i = C["ev"] % 2
    C["ev"] += 1
    on, on_b, rz, rz_b = C["on"][i], C["on_b"][i], C["rz"][i], C["rz_b"][i]
    ov = R.ps[obank][:, 0:4 * stride].rearrange("p (m x) -> p m x", x=stride)
    T.op("dve", lambda e: e.reciprocal(out=rz[:], in_=ov[:, :, zcol:zcol + 1]), reads=[R.psb[obank]], writes=[rz_b])
    T.op("dve", lambda e: e.tensor_tensor(out=on[:], in0=ov[:, :, 0:64], in1=rz[:].broadcast_to([128, 4, 64]),
                                          op=ALU.mult),
         reads=[R.psb[obank], rz_b], writes=[on_b])
    emit_transpose_store(R, C, on, on_b, G, ch, po)


def emit_transpose_store(R, C, on, on_b, G, ch, po):
    T = R.T
    tb = C["tpbank"]
    for mm in range(4):
        T.op("pe", lambda e: e.transpose(out=R.ps[tb][0:64, mm * 128:(mm + 1) * 128], in_=on[:, mm, :],
                                         identity=C["ident"][:]),
             reads=[on_b, C["ident_b"]], writes=[R.psb[tb]])
    T.op("act", lambda e: e.activation(out=R.oT[po:po + 64, ch, G * 512:(G + 1) * 512], in_=R.ps[tb][0:64, :],
                                       func=AF.Copy),
         reads=[R.psb[tb]], writes=[R.oT_b[ch][G]])


def emit_dense_attn(R, C, KT, KT_b, V, V_b, QT, QT_b, dk, scale, obank, mask0,
                    selT=None, selT_b=None, vstride=65):
    T = R.T

    def run(G):
        steps = list(range(16 * G + 16))
        n = len(steps)
        stt = {}

        def stage_a(kt):
            mm_min = max(0, -((-(kt - 16 * G - 3)) // 4))
            c0 = mm_min * 128
            sb_ = C["sbanks"][C["sn"] % len(C["sbanks"])]
            C["sn"] += 1
            stt[kt] = [mm_min, c0, sb_, None]
            T.op("pe", lambda e: e.matmul(R.ps[sb_][:, c0:512], lhsT=KT[0:dk, ktcol(kt):ktcol(kt) + 128],
                                          rhs=QT[0:dk, G * 512 + c0:(G + 1) * 512], start=True, stop=(selT is None)),
                 reads=[KT_b, QT_b], writes=[R.psb[sb_]])
            if selT is not None:
                T.op("pe", lambda e: e.matmul(R.ps[sb_][:, c0:512], lhsT=C["Ebig"][:, kt * 128:(kt + 1) * 128],
                                              rhs=selT[:, G * 512 + c0:(G + 1) * 512], start=False, stop=True),
                     reads=[C["Ebig_b"], selT_b], writes=[R.psb[sb_]])

        def stage_b(kt):
            mm_min, c0, sb_, _ = stt[kt]
            pi = C["pn"] % len(C["PT"])
            C["pn"] += 1
            stt[kt][3] = pi
            PT, PT_b = C["PT"][pi], C["PT_b"][pi]
            T.op("act", lambda e: e.activation(out=PT[:, c0:512], in_=R.ps[sb_][:, c0:512], func=AF.Exp, scale=scale),
                 reads=[R.psb[sb_]], writes=[PT_b])
            if kt >= 16 * G:
                mmb = (kt - 16 * G) // 4
                d = (kt - 16 * G) % 4
                T.op("pool", lambda e: e.tensor_tensor(out=PT[:, mmb * 128:(mmb + 1) * 128],
                                                       in0=PT[:, mmb * 128:(mmb + 1) * 128],
                                                       in1=C["masks"][:, mask0 + d, :], op=ALU.mult),
                     reads=[PT_b, C["masks_b"]], writes=[PT_b])

        def stage_c(kt, first):
            mm_min, c0, sb_, pi = stt[kt]
            PT, PT_b = C["PT"][pi], C["PT_b"][pi]
            for mm in range(mm_min, 4):
                last = (kt == 16 * G + 4 * mm + 3)
                T.op("pe", lambda e: e.matmul(R.ps[obank][:, mm * vstride:(mm + 1) * vstride],
                                              lhsT=PT[:, mm * 128:(mm + 1) * 128], rhs=V[:, ktile(kt), :],
                                              start=(first and mm == mm_min), stop=last, skip_group_check=True),
                     reads=[PT_b, V_b], writes=[R.psb[obank]])

        for idx in range(n + 2):
            if idx < n:
                stage_a(steps[idx])
            if 1 <= idx <= n:
                stage_b(steps[idx - 1])
            if idx >= 2:
                stage_c(steps[idx - 2], idx == 2)
    return run


def alloc_attn_common(R, A, ph):
    nc, T = R.nc, R.T
    C = {"ev": 0, "sn": 0, "pn": 0, "sbanks": [0, 1, 2, 3], "tpbank": 5}

    def sb(name, shape, dt):
        return ph.enter_context(nc.sbuf_tensor(U(name), shape, dt))
    C["sb"] = sb
    C["masks"] = sb("A_masks", [128, 22, 128], BF16)
    C["masks_b"] = Buf("A_masks")
    C["ident"] = sb("A_ident", [128, 128], F32)
    C["ident_b"] = Buf("A_ident")
    ds = T.new_dma_sem("Aconst")
    T.dma("sp", ds, C["masks"][:], A["masks"], writes=[C["masks_b"]])
    T.dma("sp", ds, C["ident"][:], A["ident"], writes=[C["ident_b"]])
    C["PT"] = [sb("A_PT%d" % i, [128, 512], BF16) for i in range(6)]
    C["PT_b"] = [Buf("A_PT%d" % i) for i in range(6)]
    C["on"] = [sb("A_on%d" % i, [128, 4, 64], F32) for i in range(2)]
    C["on_b"] = [Buf("A_on%d" % i) for i in range(2)]
    C["rz"] = [sb("A_rz%d" % i, [128, 4, 1], F32) for i in range(2)]
    C["rz_b"] = [Buf("A_rz%d" % i) for i in range(2)]
    C["kcT"] = [sb("A_kcT%d" % i, [64, 512], BF16) for i in range(2)]
    C["vc"] = [sb("A_vc%d" % i, [128, 4, 65], BF16) for i in range(2)]
    C["kc_b"] = [Buf("A_kc%d" % i) for i in range(2)]
    return C


def gb(A, nm, i=None):
    d = A.get("gb")
    if d is None:
        return []
    b = d[nm]
    if isinstance(b, list):
        b = b[i]
    return [b]


def emit_mla(R, C, A, hook=None):
    nc, T = R.nc, R.T
    ngrp = 0
    hook_left = [4]
    with R.ExitStack() as ph:
        def sb(name, shape, dt):
            return ph.enter_context(nc.sbuf_tensor(U(name), shape, dt))
        KT = [sb("M_KT%d" % i, [96, 4 * NT], BF16) for i in range(2)]
        V = [sb("M_V%d" % i, [128, 64, 65], BF16) for i in range(2)]
        QT = [sb("M_QT%d" % i, [96, NT], BF16) for i in range(2)]
        bufs = [Buf("M_slot%d" % i) for i in range(2)]
        sems = [T.new_dma_sem("Mslot%d" % i) for i in range(2)]
        for h in range(6):
            s = h % 2
            for r in range(4):
                T.dma("sp", sems[s], KT[s][0:64, r * NT:(r + 1) * NT], A["kT_mla_g"][h][r, :, :], reads=gb(A, "kT_mla", h),
                      writes=[bufs[s]])
                T.dma("sp", sems[s], KT[s][64:96, r * NT:(r + 1) * NT], A["kpeT_g"][r, :, :], reads=gb(A, "kpeT"), writes=[bufs[s]])
                for hf in range(2):
                    T.dma("sp", sems[s], V[s][:, r * NBLK + hf * 8:r * NBLK + hf * 8 + 8, :],
                          A["v_mla_g"][hf][r, :, h, :].rearrange("(m p) x -> p m x", p=128), reads=gb(A, "v_mla", hf),
                          writes=[bufs[s]])
            T.dma("sp", sems[s], QT[s][:], A["qT_mla"][h, :, :], writes=[bufs[s]])
            for G in range(4):
                obank = 6 + (G % 2)
                run = emit_dense_attn(R, C, KT[s], bufs[s], V[s], bufs[s], QT[s], bufs[s], 96, 96 ** -0.5,
                                      obank, 0)
                run(G)
                emit_oT_store(R, C, obank, G, h // 2, (h % 2) * 64)
                ngrp += 1
                if hook is not None and ngrp % 3 == 1 and hook_left[0] > 0:
                    hook_left[0] -= 1
                    next(hook)
        T.barrier_all()


def make_nsa_consts(core):
    j = core % 4
    c = np.arange(512)[:, None]
    n = np.arange(128)[None, :]
    ov = ((16 * c < 64 * n + 64) & (16 * c + 32 > 64 * n)).astype(np.float32)
    ovl = np.concatenate([ov, np.ones((512, 1), np.float32)], 1).reshape(4, 128, 129).transpose(1, 0, 2)
    ql = np.arange(128)[:, None]
    w = np.arange(256)[None, :]
    rel = w - 128 - 2 * j
    cur = (ql >= 64).astype(np.int64)
    forced = (rel == cur) | (rel == cur - 1)
    future = rel > cur
    keep = (~(forced | future)).astype(np.float32)
    add = np.where(forced, 1e4, np.where(future, -1.0, 0.0)).astype(np.float32)
    ka = np.stack([keep, add], 1)
    key = np.arange(S)[None, :]
    eb = np.where(key // 64 == np.arange(128)[:, None], 1.0, 0.0).astype(np.float32)
    return dict(ovl=np.ascontiguousarray(ovl).astype(ml_dtypes.bfloat16), keepadd=np.ascontiguousarray(ka),
                Ebig=eb.astype(ml_dtypes.bfloat16))


def emit_nsa_compress(R, C, A):
    nc, T = R.nc, R.T
    with R.ExitStack() as ph:
        def sb(name, shape, dt):
            return ph.enter_context(nc.sbuf_tensor(U(name), shape, dt))
        w1 = [sb("N_w1%d" % i, [64, 32, 128], BF16) for i in range(2)]
        w2 = [sb("N_w2%d" % i, [128, 64], BF16) for i in range(2)]
        posT = [sb("N_pos%d" % i, [64, 32], BF16) for i in range(2)]
        cst_b = Buf("NC_const")
        ds = T.new_dma_sem("NconstP")
        T.dma("pool", ds, w1[0][:], A["cmp_w1_k"].rearrange("(l d) h -> d l h", d=64), writes=[cst_b])
        T.dma("pool", ds, w1[1][:], A["cmp_w1_v"].rearrange("(l d) h -> d l h", d=64), writes=[cst_b])
        T.dma("pool", ds, w2[0][:], A["cmp_w2_k"], writes=[cst_b])
        T.dma("pool", ds, w2[1][:], A["cmp_w2_v"], writes=[cst_b])
        T.dma("pool", ds, posT[0][:], A["cmp_posT_k"], writes=[cst_b])
        T.dma("pool", ds, posT[1][:], A["cmp_posT_v"], writes=[cst_b])
        bias = sb("N_bias", [128, 2], F32)
        bias_b = Buf("N_bias")
        for i in range(2):
            for l in range(32):
                T.op("pe", lambda e: e.matmul(R.ps[4][:, i:i + 1], lhsT=w1[i][:, l, :], rhs=posT[i][:, l:l + 1],
                                              start=(l == 0), stop=(l == 31)),
                     reads=[cst_b], writes=[R.psb[4]])
            T.op("dve", lambda e: e.tensor_copy(out=bias[:, i:i + 1], in_=R.ps[4][:, i:i + 1]),
                 reads=[R.psb[4]], writes=[bias_b])
        xg = [[sb("N_xg%d_%d" % (g, i), [64, S], BF16) for i in range(2)] for g in range(2)]
        xg_b = [[Buf("N_xg%d_%d" % (g, i)) for i in range(2)] for g in range(2)]
        hid = sb("N_hid", [128, 512], F32)
        tq = sb("N_tq", [128, 512], F32)
        gl = sb("N_gl", [128, 512], BF16)
        hid_b, gl_b = Buf("N_hid"), Buf("N_gl")
        T.op("pool", lambda e: e.memset(gl[:], 0.0), writes=[gl_b])
        yield
        for g in range(2):
            kcT, vc, kc_b = C["kcT"][g], C["vc"][g], C["kc_b"][g]
            T.op("pool", lambda e: e.memset(vc[:], 1.0), reads=[], writes=[kc_b])
            T.op("pool", lambda e: e.memset(kcT[:], 0.0), reads=[], writes=[kc_b])
            for i in range(2):
                nm = ("kT_cmp_g", "vT_cmp_g")[i]
                dsx = T.new_dma_sem("Nxg%d_%d" % (g, i))
                for r in range(4):
                    dst = xg[g][i][:].rearrange("d (m r p) -> d m r p", r=4, p=128)[:, :, r, :]
                    T.dma("sp", dsx, dst, A[nm][r, g * 64:(g + 1) * 64, :].rearrange("d (m p) -> d m p", p=128),
                          reads=gb(A, nm[:-2]), writes=[xg_b[g][i]])
                for l in range(32):
                    T.op("pe", lambda e: e.matmul(R.ps[4][:, 0:511], lhsT=w1[i][:, l, :],
                                                  rhs=xg[g][i][:, l:l + 16 * 510 + 1:16],
                                                  start=(l == 0), stop=(l == 31)),
                         reads=[cst_b, xg_b[g][i]], writes=[R.psb[4]])
                T.op("act", lambda e: e.activation(out=hid[:, 0:511], in_=R.ps[4][:, 0:511], func=AF.Identity,
                                                   bias=bias[:, i:i + 1]),
                     reads=[R.psb[4], bias_b], writes=[hid_b])
                T.op("dve", lambda e: e.tensor_tensor(out=tq[:, 0:511], in0=hid[:, 0:511], in1=hid[:, 0:511],
                                                      op=ALU.mult), reads=[hid_b], writes=[hid_b])
                T.op("dve", lambda e: e.tensor_scalar(out=tq[:, 0:511], in0=tq[:, 0:511], scalar1=0.044715,
                                                      scalar2=1.0, op0=ALU.mult, op1=ALU.add),
                     reads=[hid_b], writes=[hid_b])
                T.op("dve", lambda e: e.tensor_tensor(out=tq[:, 0:511], in0=tq[:, 0:511], in1=hid[:, 0:511],
                                                      op=ALU.mult), reads=[hid_b], writes=[hid_b])
                T.op("act", lambda e: e.activation(out=tq[:, 0:511], in_=tq[:, 0:511], func=AF.Sigmoid,
                                                   scale=1.5957691216057308),
                     reads=[hid_b], writes=[hid_b])
                T.op("dve", lambda e: e.tensor_tensor(out=gl[:, 0:511], in0=tq[:, 0:511], in1=hid[:, 0:511],
                                                      op=ALU.mult), reads=[hid_b, gl_b], writes=[gl_b])
                if i == 0:
                    T.op("pe", lambda e: e.matmul(R.ps[4][0:64, 0:511], lhsT=w2[0][:], rhs=gl[:, 0:511],
                                                  start=True, stop=True),
                         reads=[cst_b, gl_b], writes=[R.psb[4]])
                    T.op("dve", lambda e: e.tensor_copy(out=kcT[:, 0:511], in_=R.ps[4][0:64, 0:511]),
                         reads=[R.psb[4]], writes=[kc_b])
                else:
                    for t in range(4):
                        T.op("pe", lambda e: e.matmul(R.ps[4][:, t * 64:(t + 1) * 64], lhsT=gl[:, t * 128:(t + 1) * 128],
                                                      rhs=w2[1][:], start=True, stop=True),
                             reads=[cst_b, gl_b], writes=[R.psb[4]])
                    T.op("dve", lambda e: e.tensor_copy(out=vc[:, :, 0:64],
                                                        in_=R.ps[4][:, 0:256].rearrange("p (t x) -> p t x", x=64)),
                         reads=[R.psb[4]], writes=[kc_b])
                yield
        T.barrier_all()


def emit_nsa(R, C, A):
    nc, T = R.nc, R.T
    SC = 0.125
    with R.ExitStack() as ph:
        def sb(name, shape, dt):
            return ph.enter_context(nc.sbuf_tensor(U(name), shape, dt))
        Ebig = sb("N_Ebig", [128, S], BF16)
        ovl = sb("N_ovl", [128, 4, 129], BF16)
        ka = sb("N_ka", [128, 2, 256], F32)
        gates = sb("N_gates", [128, NBLK, 18], F32)
        cst_b = Buf("N_const")
        C["Ebig"], C["Ebig_b"] = Ebig, cst_b
        ds = T.new_dma_sem("Nconst")
        T.dma("sp", ds, Ebig[:], A["Ebig"], writes=[cst_b])
        T.dma("sp", ds, ovl[:], A["ovl"], writes=[cst_b])
        T.dma("sp", ds, ka[:], A["keepadd"], writes=[cst_b])
        T.dma("sp", ds, gates[:], A["gates"].rearrange("(m p) g -> p m g", p=128), writes=[cst_b])
        selT = sb("N_selT", [128, NT], BF16)
        selT_b = Buf("N_selT")
        QTn = sb("N_QT", [64, 3, NT], BF16)
        KTs = sb("N_KTs", [64, 4 * NT], BF16)
        Vs = sb("N_Vs", [128, 64, 65], BF16)
        KTw = sb("N_KTw", [64, 4 * NT], BF16)
        Vw = sb("N_Vw", [128, 64, 65], BF16)
        kv_b = Buf("N_kv")
        kv_sem = T.new_dma_sem("Nkv")
        oaccs = [sb("N_oacc%d" % i, [128, 3, 4, 64], F32) for i in range(2)]
        oaccs_b = [Buf("N_oacc0"), Buf("N_oacc1")]
        C["pcn"], C["pwn"] = 0, 0
        Pc = [sb("N_Pc%d" % i, [128, 3, 128], BF16) for i in range(2)]
        Pc_b = [Buf("N_Pc0"), Buf("N_Pc1")]
        rzc = sb("N_rzc", [128, 3, 1], F32)
        wc = sb("N_wc", [128, 3, 1], F32)
        sc = sb("N_sc", [128, 128], F32)
        sc2 = sb("N_sc2", [128, 128], F32)
        m8 = sb("N_m8", [128, 16], F32)
        selns = [sb("N_seln%d" % i, [128, 128], F32) for i in range(4)]
        selns_b = [Buf("N_seln%d" % i) for i in range(4)]
        sm_b = Buf("N_small")
        rz4 = sb("N_rz4", [128, 4, 1], F32)
        w4 = sb("N_w4", [128, 4, 1], F32)
        w4_b = Buf("N_w4")
        Pw = [sb("N_Pw%d" % i, [128, 512], BF16) for i in range(3)]
        Pw_b = [Buf("N_Pw%d" % i) for i in range(3)]
        for g in range(2):
            kcT, vc, kc_b = C["kcT"][g], C["vc"][g], C["kc_b"][g]
            T.dma("sp", kv_sem, QTn[:], A["qT_nsa"][g * 192:(g + 1) * 192, :].rearrange("(h d) t -> d h t", d=64),
                  writes=[kv_b])
            for r in range(4):
                T.dma("sp", kv_sem, KTs[:, r * NT:(r + 1) * NT], A["kT_sel_g"][r, g * 64:(g + 1) * 64, :], reads=gb(A, "kT_sel"), writes=[kv_b])
                T.dma("sp", kv_sem, KTw[:, r * NT:(r + 1) * NT], A["kT_win_g"][r, g * 64:(g + 1) * 64, :], reads=gb(A, "kT_win"), writes=[kv_b])
                T.dma("sp", kv_sem, Vs[:, r * NBLK:(r + 1) * NBLK, :],
                      A["v_sel_g"][r, :, g, :].rearrange("(m p) x -> p m x", p=128), reads=gb(A, "v_sel"), writes=[kv_b])
                T.dma("sp", kv_sem, Vw[:, r * NBLK:(r + 1) * NBLK, :],
                      A["v_win_g"][r, :, g, :].rearrange("(m p) x -> p m x", p=128), reads=gb(A, "v_win"), writes=[kv_b])

            def cmp_block(G, oacc, oacc_b):
                for mm in range(4):
                    ob, sbk = (3, 4) if mm % 2 == 0 else (2, 7)
                    m = 4 * G + mm
                    msl = slice(m * 128, (m + 1) * 128)
                    for t in range(G + 1):
                        sb_ = C["sbanks"][C["sn"] % len(C["sbanks"])]
                        C["sn"] += 1
                        pi = C["pcn"] % 2
                        C["pcn"] += 1
                        T.op("pe", lambda e: e.matmul(R.ps[sb_][:, 0:384], lhsT=kcT[:, t * 128:(t + 1) * 128],
                                                      rhs=QTn[:, :, msl], start=True, stop=True),
                             reads=[kc_b, kv_b], writes=[R.psb[sb_]])
                        T.op("act", lambda e: e.activation(out=Pc[pi][:].rearrange("p r q -> p (r q)"),
                                                           in_=R.ps[sb_][:, 0:384], func=AF.Exp, scale=SC),
                             reads=[R.psb[sb_]], writes=[Pc_b[pi]])
                        if t == G:
                            T.op("pool", lambda e: e.tensor_tensor(
                                out=Pc[pi][:], in0=Pc[pi][:],
                                in1=C["masks"][:, 12 + mm:13 + mm, :].broadcast_to([128, 3, 128]), op=ALU.mult),
                                reads=[Pc_b[pi], C["masks_b"]], writes=[Pc_b[pi]])
                        for r in range(3):
                            T.op("pe", lambda e: e.matmul(R.ps[ob][:, r * 65:(r + 1) * 65], lhsT=Pc[pi][:, r, :],
                                                          rhs=vc[:, t, :], start=(t == 0 and r == 0), stop=(t == G),
                                                          skip_group_check=True),
                                 reads=[Pc_b[pi], kc_b], writes=[R.psb[ob]])
                        for r in range(3):
                            T.op("pe", lambda e: e.matmul(R.ps[sbk][:, r * 129:(r + 1) * 129], lhsT=Pc[pi][:, r, :],
                                                          rhs=ovl[:, t, :], start=(t == 0 and r == 0), stop=(t == G),
                                                          skip_group_check=True),
                                 reads=[Pc_b[pi], cst_b], writes=[R.psb[sbk]])
                    scv = R.ps[sbk][:, 0:387].rearrange("p (r x) -> p r x", x=129)
                    ocv = R.ps[ob][:, 0:195].rearrange("p (r x) -> p r x", x=65)
                    T.op("dve", lambda e: e.tensor_scalar(out=rzc[:], in0=scv[:, :, 128:129], scalar1=1e-30, scalar2=None,
                                                          op0=ALU.add), reads=[R.psb[sbk]], writes=[sm_b])
                    T.op("dve", lambda e: e.reciprocal(out=rzc[:], in_=rzc[:]), reads=[sm_b], writes=[sm_b])
                    T.op("dve", lambda e: e.tensor_scalar(out=sc[:], in0=scv[:, 0, 0:128], scalar1=rzc[:, 0, :], scalar2=None,
                                                          op0=ALU.mult), reads=[R.psb[sbk], sm_b], writes=[sm_b])
                    for r in (1, 2):
                        T.op("dve", lambda e: e.scalar_tensor_tensor(out=sc[:], in0=scv[:, r, 0:128], scalar=rzc[:, r, :],
                                                                     in1=sc[:], op0=ALU.mult, op1=ALU.add),
                             reads=[R.psb[sbk], sm_b], writes=[sm_b])
                    T.op("dve", lambda e: e.tensor_tensor(
                        out=wc[:], in0=rzc[:],
                        in1=gates[:, m, 9 * g:9 * g + 9].rearrange("p (r x) -> p r x", x=3)[:, :, 0:1], op=ALU.mult),
                        reads=[sm_b, cst_b], writes=[sm_b])
                    for r in range(3):
                        T.op("dve", lambda e: e.tensor_scalar(out=oacc[:, r, mm, :], in0=ocv[:, r, 0:64],
                                                              scalar1=wc[:, r, :], scalar2=None, op0=ALU.mult),
                             reads=[R.psb[ob], sm_b], writes=[oacc_b])
                    w0 = 128 - 8 * m
                    T.op("dve", lambda e: e.tensor_tensor(out=sc[:], in0=sc[:], in1=ka[:, 0, w0:w0 + 128], op=ALU.mult),
                         reads=[sm_b, cst_b], writes=[sm_b])
                    T.op("dve", lambda e: e.tensor_tensor(out=sc[:], in0=sc[:], in1=ka[:, 1, w0:w0 + 128], op=ALU.add),
                         reads=[sm_b, cst_b], writes=[sm_b])
                    T.op("dve", lambda e: e.memset(sc[:, 0:1], 1e4), reads=[sm_b], writes=[sm_b])
                    T.op("dve", lambda e: e.max(out=m8[:, 0:8], in_=sc[:]), reads=[sm_b], writes=[sm_b])
                    T.op("dve", lambda e: e.match_replace(out=sc2[:], in_to_replace=m8[:, 0:8], in_values=sc[:],
                                                          imm_value=-1e9), reads=[sm_b], writes=[sm_b])
                    T.op("dve", lambda e: e.max(out=m8[:, 8:16], in_=sc2[:]), reads=[sm_b], writes=[sm_b])
                    T.op("dve", lambda e: e.tensor_scalar(out=selns[mm][:], in0=sc[:], scalar1=m8[:, 15:16], scalar2=None,
                                                          op0=ALU.is_ge), reads=[sm_b], writes=[selns_b[mm]])

            def cmp_finish(G):
                tb = C["tpbank"]
                for mm in range(4):
                    T.op("pe", lambda e: e.transpose(out=R.ps[tb][:, mm * 128:(mm + 1) * 128], in_=selns[mm][:],
                                                     identity=C["ident"][:]),
                         reads=[selns_b[mm], C["ident_b"]], writes=[R.psb[tb]])
                T.op("act", lambda e: e.activation(out=selT[:, G * 512:(G + 1) * 512], in_=R.ps[tb][:, :], func=AF.Copy),
                     reads=[R.psb[tb]], writes=[selT_b])

            def win_block(r, G, oacc, oacc_b):
                h = 3 * g + r
                ob = 6
                items = []
                for mm in range(4):
                    for half in range(2):
                        kts = [16 * G + 4 * mm - 4 + half * 4 + x for x in range(4)]
                        if kts[-1] >= 0:
                            items.append((mm, half, kts))
                ni = len(items)
                stt = {}

                def wa(i):
                    mm, half, kts = items[i]
                    msl = slice((4 * G + mm) * 128, (4 * G + mm + 1) * 128)
                    sb_ = C["sbanks"][C["sn"] % len(C["sbanks"])]
                    C["sn"] += 1
                    stt[i] = [sb_, None]
                    for x, kt in enumerate(kts):
                        T.op("pe", lambda e: e.matmul(R.ps[sb_][:, x * 128:(x + 1) * 128],
                                                      lhsT=KTw[:, ktcol(kt):ktcol(kt) + 128],
                                                      rhs=QTn[:, r, msl], start=True, stop=True),
                             reads=[kv_b], writes=[R.psb[sb_]])

                def wb(i):
                    mm, half, kts = items[i]
                    sb_ = stt[i][0]
                    pi = C["pwn"] % len(Pw)
                    C["pwn"] += 1
                    stt[i][1] = pi
                    T.op("act", lambda e: e.activation(out=Pw[pi][:], in_=R.ps[sb_][:], func=AF.Exp, scale=SC),
                         reads=[R.psb[sb_]], writes=[Pw_b[pi]])
                    T.op("pool", lambda e: e.tensor_tensor(
                        out=Pw[pi][:], in0=Pw[pi][:],
                        in1=C["masks"][:, 4 + half * 4:8 + half * 4, :].rearrange("p a q -> p (a q)"),
                        op=ALU.mult), reads=[Pw_b[pi], C["masks_b"]], writes=[Pw_b[pi]])

                def wc_(i):
                    mm, half, kts = items[i]
                    pi = stt[i][1]
                    for x, kt in enumerate(kts):
                        T.op("pe", lambda e: e.matmul(R.ps[ob][:, mm * 65:(mm + 1) * 65],
                                                      lhsT=Pw[pi][:, x * 128:(x + 1) * 128], rhs=Vw[:, ktile(kt), :],
                                                      start=(i == 0 and x == 0), stop=(half == 1 and x == 3),
                                                      skip_group_check=True),
                             reads=[Pw_b[pi], kv_b], writes=[R.psb[ob]])

                for idx in range(ni + 2):
                    if idx < ni:
                        wa(idx)
                    if 1 <= idx <= ni:
                        wb(idx - 1)
                    if idx >= 2:
                        wc_(idx - 2)
                owv = R.ps[ob][:, 0:260].rearrange("p (m x) -> p m x", x=65)
                T.op("dve", lambda e: e.reciprocal(out=rz4[:], in_=owv[:, :, 64:65]), reads=[R.psb[ob]], writes=[w4_b])
                T.op("dve", lambda e: e.tensor_tensor(out=w4[:], in0=rz4[:],
                                                      in1=gates[:, 4 * G:4 * G + 4, 3 * h + 2:3 * h + 3], op=ALU.mult),
                     reads=[w4_b, cst_b], writes=[w4_b])
                for mm in range(4):
                    T.op("dve", lambda e: e.scalar_tensor_tensor(out=oacc[:, r, mm, :], in0=owv[:, mm, 0:64],
                                                                 scalar=w4[:, mm, :], in1=oacc[:, r, mm, :],
                                                                 op0=ALU.mult, op1=ALU.add),
                         reads=[R.psb[ob], w4_b, oacc_b], writes=[oacc_b])

            def sel3_block(G, oacc, oacc_b):
                obs = [4, 6, 7]
                mbanks = [2, 3]
                qbanks = [0, 1]
                nk = 16 * G + 16
                nhs = 3 * nk
                stt = {}

                def geom(kt):
                    mm_min = max(0, -((-(kt - 16 * G - 3)) // 4))
                    return mm_min, mm_min * 128

                def sa(hs):
                    kt, r = hs // 3, hs % 3
                    mm_min, c0 = geom(kt)
                    mb = mbanks[kt % 2]
                    if r == 0:
                        T.op("pe", lambda e: e.matmul(R.ps[mb][:, c0:512], lhsT=Ebig[:, kt * 128:(kt + 1) * 128],
                                                      rhs=selT[:, G * 512 + c0:(G + 1) * 512], start=True, stop=True),
                             reads=[cst_b, selT_b], writes=[R.psb[mb]])
                    qb = qbanks[hs % 2]
                    T.op("pe", lambda e: e.matmul(R.ps[qb][:, c0:512], lhsT=KTs[:, ktcol(kt):ktcol(kt) + 128],
                                                  rhs=QTn[:, r, G * 512 + c0:(G + 1) * 512], start=True, stop=True),
                         reads=[kv_b], writes=[R.psb[qb]])

                def sb_(hs):
                    kt, r = hs // 3, hs % 3
                    mm_min, c0 = geom(kt)
                    mb, qb = mbanks[kt % 2], qbanks[hs % 2]
                    pi = C["pn"] % len(C["PT"])
                    C["pn"] += 1
                    stt[hs] = pi
                    PT, PT_b = C["PT"][pi], C["PT_b"][pi]
                    T.op("act", lambda e: e.activation(out=PT[:, c0:512], in_=R.ps[qb][:, c0:512], func=AF.Exp, scale=SC),
                         reads=[R.psb[qb]], writes=[PT_b])
                    T.op("dve", lambda e: e.tensor_tensor(out=PT[:, c0:512], in0=PT[:, c0:512], in1=R.ps[mb][:, c0:512],
                                                          op=ALU.mult),
                         reads=[PT_b, R.psb[mb]], writes=[PT_b])
                    if kt >= 16 * G:
                        mmb = (kt - 16 * G) // 4
                        d = (kt - 16 * G) % 4
                        T.op("pool", lambda e: e.tensor_tensor(out=PT[:, mmb * 128:(mmb + 1) * 128],
                                                               in0=PT[:, mmb * 128:(mmb + 1) * 128],
                                                               in1=C["masks"][:, d, :], op=ALU.mult),
                             reads=[PT_b, C["masks_b"]], writes=[PT_b])

                def sc_(hs):
                    kt, r = hs // 3, hs % 3
                    mm_min, c0 = geom(kt)
                    PT, PT_b = C["PT"][stt[hs]], C["PT_b"][stt[hs]]
                    for mm in range(mm_min, 4):
                        T.op("pe", lambda e: e.matmul(R.ps[obs[r]][:, mm * 65:(mm + 1) * 65],
                                                      lhsT=PT[:, mm * 128:(mm + 1) * 128], rhs=Vs[:, ktile(kt), :],
                                                      start=(kt == 0 and mm == mm_min), stop=(kt == 16 * G + 4 * mm + 3),
                                                      skip_group_check=True),
                             reads=[PT_b, kv_b], writes=[R.psb[obs[r]]])

                for idx in range(nhs + 4):
                    if idx < nhs:
                        sa(idx)
                    if 1 <= idx <= nhs:
                        sb_(idx - 1)
                    if idx >= 4:
                        sc_(idx - 4)
                for r in range(3):
                    h = 3 * g + r
                    osv = R.ps[obs[r]][:, 0:260].rearrange("p (m x) -> p m x", x=65)
                    T.op("dve", lambda e: e.reciprocal(out=rz4[:], in_=osv[:, :, 64:65]), reads=[R.psb[obs[r]]], writes=[w4_b])
                    T.op("dve", lambda e: e.tensor_tensor(out=w4[:], in0=rz4[:],
                                                          in1=gates[:, 4 * G:4 * G + 4, 3 * h + 1:3 * h + 2], op=ALU.mult),
                         reads=[w4_b, cst_b], writes=[w4_b])
                    for mm in range(4):
                        T.op("dve", lambda e: e.scalar_tensor_tensor(out=oacc[:, r, mm, :], in0=osv[:, mm, 0:64],
                                                                     scalar=w4[:, mm, :], in1=oacc[:, r, mm, :],
                                                                     op0=ALU.mult, op1=ALU.add),
                             reads=[R.psb[obs[r]], w4_b, oacc_b], writes=[oacc_b])
                    emit_transpose_store(R, C, oacc[:, r, :, :], oacc_b, G, 3 + h // 2, (h % 2) * 64)

            C["sbanks"] = [0, 1]
            cmp_block(0, oaccs[0], oaccs_b[0])
            for G in range(4):
                cmp_finish(G)
                if G + 1 < 4:
                    cmp_block(G + 1, oaccs[(G + 1) % 2], oaccs_b[(G + 1) % 2])
                for r in range(3):
                    win_block(r, G, oaccs[G % 2], oaccs_b[G % 2])
                sel3_block(G, oaccs[G % 2], oaccs_b[G % 2])
            C["sbanks"] = [0, 1, 2, 3]
        T.barrier_all()


def emit_sb(R, C, A):
    nc, T = R.nc, R.T
    SC = 0.125
    with R.ExitStack() as ph:
        def sb(name, shape, dt):
            return ph.enter_context(nc.sbuf_tensor(U(name), shape, dt))
        KT = [sb("S_KT%d" % i, [64, 4 * NT], BF16) for i in range(2)]
        V = [sb("S_V%d" % i, [128, 64, 64], BF16) for i in range(2)]
        QT = [sb("S_QT%d" % i, [64, NT], BF16) for i in range(2)]
        bufs = [Buf("S_slot%d" % i) for i in range(2)]
        sems = [T.new_dma_sem("Sslot%d" % i) for i in range(2)]
        NE = 4
        E = [sb("S_E%d" % i, [128, 512], F32) for i in range(NE)]
        SP = [sb("S_SP%d" % i, [128, 512], BF16) for i in range(NE)]
        X = [sb("S_X%d" % i, [128, 512], F32) for i in range(2)]
        E_b = [Buf("S_E%d" % i) for i in range(NE)]
        SP_b = [Buf("S_SP%d" % i) for i in range(NE)]
        X_b = [Buf("S_X0"), Buf("S_X1")]
        Accb = [sb("S_Accb%d" % i, [128, 512], BF16) for i in range(3)]
        Accb_b = [Buf("S_Accb%d" % i) for i in range(3)]
        tincl = C["masks"][:, 20, :]
        onesb = C["masks"][:, 21, :]
        cbanks = [3, 4]
        zbanks = [0, 1, 2]
        for h in range(4):
            s = h % 2
            for r in range(4):
                T.dma("sp", sems[s], KT[s][:, r * NT:(r + 1) * NT], A["kT_sb_g"][r, h * 64:(h + 1) * 64, :], reads=gb(A, "kT_sb"),
                      writes=[bufs[s]])
                T.dma("sp", sems[s], V[s][:, r * NBLK:(r + 1) * NBLK, :],
                      A["v_sb_g"][r, :, h * 64:(h + 1) * 64].rearrange("(m p) x -> p m x", p=128), reads=gb(A, "v_sb"),
                      writes=[bufs[s]])
            T.dma("sp", sems[s], QT[s][:], A["qT_sb"][h * 64:(h + 1) * 64, :], writes=[bufs[s]])
            for G in range(4):
                ob = 6 + (G % 2)
                for i in range(3):
                    T.op("pool", lambda e: e.memset(Accb[i][:], 0.0), writes=[Accb_b[i]])
                steps = list(range(16 * G + 15, -1, -1))
                ns = len(steps)
                stt = {}

                def geom(kt):
                    mm_min = max(0, -((-(kt - 16 * G - 3)) // 4))
                    return mm_min, mm_min * 128

                def st_a(n):
                    kt = steps[n]
                    mm_min, c0 = geom(kt)
                    zb = zbanks[n % 3]
                    T.op("pe", lambda e: e.matmul(R.ps[zb][:, c0:512], lhsT=KT[s][:, ktcol(kt):ktcol(kt) + 128],
                                                  rhs=QT[s][:, G * 512 + c0:(G + 1) * 512], start=True, stop=True),
                         reads=[bufs[s]], writes=[R.psb[zb]])

                def st_b(n):
                    kt = steps[n]
                    mm_min, c0 = geom(kt)
                    zb, ie = zbanks[n % 3], n % NE
                    T.op("act", lambda e: e.activation(out=E[ie][:, c0:512], in_=R.ps[zb][:, c0:512], func=AF.Exp, scale=SC),
                         reads=[R.psb[zb]], writes=[E_b[ie]])
                    T.op("act", lambda e: e.activation(out=SP[ie][:, c0:512], in_=E[ie][:, c0:512], func=AF.Ln, bias=1.0),
                         reads=[E_b[ie]], writes=[SP_b[ie]])
                    if kt >= 16 * G:
                        d = (kt - 16 * G) % 4
                        T.op("pool", lambda e: e.tensor_tensor(out=SP[ie][:, c0:c0 + 128], in0=SP[ie][:, c0:c0 + 128],
                                                               in1=C["masks"][:, 16 + d, :], op=ALU.mult),
                             reads=[SP_b[ie], C["masks_b"]], writes=[SP_b[ie]])
                    if n + 1 < ns:
                        T.op("pool", lambda e: e.tensor_tensor(out=Accb[(n + 1) % 3][:, c0:512], in0=Accb[n % 3][:, c0:512],
                                                               in1=SP[ie][:, c0:512], op=ALU.add),
                             reads=[Accb_b[n % 3], SP_b[ie]], writes=[Accb_b[(n + 1) % 3]])

                def st_c(n):
                    kt = steps[n]
                    mm_min, c0 = geom(kt)
                    cb, ie = cbanks[n % 2], n % NE
                    T.op("pe", lambda e: e.matmul(R.ps[cb][:, c0:512], lhsT=tincl, rhs=SP[ie][:, c0:512],
                                                  start=True, stop=False),
                         reads=[SP_b[ie], C["masks_b"]], writes=[R.psb[cb]])
                    T.op("pe", lambda e: e.matmul(R.ps[cb][:, c0:512], lhsT=onesb, rhs=Accb[n % 3][:, c0:512],
                                                  start=False, stop=True),
                         reads=[Accb_b[n % 3], C["masks_b"]], writes=[R.psb[cb]])
                    T.op("act", lambda e: e.activation(out=X[n % 2][:, c0:512], in_=R.ps[cb][:, c0:512], func=AF.Exp,
                                                       scale=-1.0),
                         reads=[R.psb[cb]], writes=[X_b[n % 2]])

                def st_e(n):
                    kt = steps[n]
                    mm_min, c0 = geom(kt)
                    ie = n % NE
                    pi = C["pn"] % len(C["PT"])
                    C["pn"] += 1
                    stt[n] = pi
                    PT, PT_b = C["PT"][pi], C["PT_b"][pi]
                    T.op("dve", lambda e: e.tensor_tensor(out=PT[:, c0:512], in0=E[ie][:, c0:512], in1=X[n % 2][:, c0:512],
                                                          op=ALU.mult),
                         reads=[E_b[ie], X_b[n % 2]], writes=[PT_b])
                    if kt >= 16 * G:
                        d = (kt - 16 * G) % 4
                        T.op("pool", lambda e: e.tensor_tensor(out=PT[:, c0:c0 + 128], in0=PT[:, c0:c0 + 128],
                                                               in1=C["masks"][:, 16 + d, :], op=ALU.mult),
                             reads=[PT_b, C["masks_b"]], writes=[PT_b])

                def st_f(n, first):
                    kt = steps[n]
                    mm_min, c0 = geom(kt)
                    PT, PT_b = C["PT"][stt[n]], C["PT_b"][stt[n]]
                    for mm in range(mm_min, 4):
                        T.op("pe", lambda e: e.matmul(R.ps[ob][:, mm * 64:(mm + 1) * 64],
                                                      lhsT=PT[:, mm * 128:(mm + 1) * 128], rhs=V[s][:, ktile(kt), :],
                                                      start=(first and mm == mm_min), stop=(kt == 0), skip_group_check=True),
                             reads=[PT_b, bufs[s]], writes=[R.psb[ob]])

                for idx in range(ns + 4):
                    if idx < ns:
                        st_a(idx)
                    if 1 <= idx <= ns:
                        st_b(idx - 1)
                    if 2 <= idx <= ns + 1:
                        st_c(idx - 2)
                    if 3 <= idx <= ns + 2:
                        st_e(idx - 3)
                    if idx >= 4:
                        st_f(idx - 4, idx == 4)
                i = C["ev"] % 2
                C["ev"] += 1
                T.op("dve", lambda e: e.tensor_copy(out=C["on"][i][:],
                                                    in_=R.ps[ob][:, 0:256].rearrange("p (m x) -> p m x", x=64)),
                     reads=[R.psb[ob]], writes=[C["on_b"][i]])
                emit_transpose_store(R, C, C["on"][i], C["on_b"][i], G, 6 + h // 2, (h % 2) * 64)
        T.barrier_all()


L = 2
GATHER = ["kT_mla", "kpeT", "v_mla", "kT_cmp", "vT_cmp", "kT_sel", "kT_win", "v_sel", "v_win", "kT_sb", "v_sb"]
LOCAL = ["qT_mla", "qT_nsa", "qT_sb", "gates"]
POUT = {nm: (shp, dt) for nm, shp, dt in P_OUTS}
W_IN = [("ffn1_w_gate", [L, D, DFF]), ("ffn1_w_up", [L, D, DFF]), ("ffn1_w_down", [L, DFF, D]),
        ("ffn2_w_gate", [L, D, DFF]), ("ffn2_w_up", [L, D, DFF]), ("ffn2_w_down", [L, DFF, D]),
        ("w_in", [L, D, DIN]), ("w_in_sw", [L, D, NSW]), ("w_uq", [L, 256, 576]), ("w_uq_sw", [L, 256, 576]),
        ("w_ukv", [L, 128, 768]), ("smallsP", [L, 128, 32]), ("w_out", [L, D, D]),
        ("cmp_w1_k", [L, 2048, 128]), ("cmp_w1_v", [L, 2048, 128]), ("cmp_w2_k", [L, 128, 64]),
        ("cmp_w2_v", [L, 128, 64]), ("cmp_posT_k", [L, 64, 32]), ("cmp_posT_v", [L, 64, 32]),
        ("gains", [128, 3 * L + 1, 8])]
C_IN = [("cosM", [96, NT], F32), ("sinM", [96, NT], F32), ("cosK", [32, NT], F32), ("sinK", [32, NT], F32),
        ("cosN", [128, NT], F32), ("sinN", [128, NT], F32), ("masks", [128, 22, 128], BF16),
        ("ident", [128, 128], F32), ("Ebig", [128, S], BF16), ("ovl", [128, 4, 129], BF16),
        ("keepadd", [128, 2, 256], F32)]


def emit_layer_X(R, A, l, do_post, do_pre, final, xin, xout, mid_hook=None):
    nc, T = R.nc, R.T
    with ExitStack() as px:
        alloc_xT(R, px)
        hT = px.enter_context(nc.sbuf_tensor(U("hT"), [128, 8, NT], BF16))
        hb = [[Buf("h%d_%d" % (k, t)) for t in range(4)] for k in range(8)]
        gam = px.enter_context(nc.sbuf_tensor(U("gam"), [128, 3 * L + 1, 8], F32))
        gam_b = Buf("gam")
        sq = [px.enter_context(nc.sbuf_tensor(U("sq%d" % i), [128, 512], BF16)) for i in range(2)]
        sq_b = [Buf("sq0"), Buf("sq1")]
        rstd = px.enter_context(nc.sbuf_tensor(U("rstd"), [128, 512], F32))
        rstd_b = Buf("rstd")
        ld = T.new_dma_sem("ldx")
        for k in range(8):
            T.dma("sp", ld, R.xT[:, k, :], xin[k * 128:(k + 1) * 128, :], writes=[R.xb[k][t] for t in range(4)])
        T.dma("sp", ld, gam[:], A["gains"], writes=[gam_b])

        def norm(gi):
            emit_norm(R, hT, hb, gam[:, gi, :], gam_b, sq, sq_b, rstd, rstd_b, 6)

        lp = l
        with ExitStack() as pf:
            R.stack = pf
            W = alloc_ffn_work(R)

            def ffn(pref, ll):
                emit_ffn(R, hT, hb, A[pref + "_w_gate"][ll], A[pref + "_w_up"][ll], A[pref + "_w_down"][ll], W)

            if do_post:
                oT2 = pf.enter_context(nc.sbuf_tensor(U("oT2"), [128, 8, NT], BF16))
                o_b = Buf("oT2")
                d1 = T.new_dma_sem("oT2")
                T.dma("sp", d1, oT2[:], A["oT_d"], writes=[o_b])
                for hf in range(2):
                    T.dma("pool", W["dsem"][hf], W["wd"][hf][:],
                          A["w_out"][l][hf * 512:(hf + 1) * 512, :].rearrange("(k p) c -> p k c", p=128),
                          writes=[W["wb"][hf]])
                n = 0
                for tt in range(4):
                    sl = slice(tt * 512, (tt + 1) * 512)
                    for dmc in range(8):
                        bk = n % 4
                        n += 1
                        for k in range(8):
                            T.op("pe", lambda e: e.matmul(R.ps[bk][:], lhsT=W["wd"][k // 4][:, k % 4, dmc * 128:(dmc + 1) * 128],
                                                          rhs=oT2[:, k, sl], start=(k == 0), stop=(k == 7)),
                                 reads=[o_b, W["wb"][k // 4]], writes=[R.psb[bk]])
                        T.op("dve", lambda e: e.tensor_tensor(out=R.xT[:, dmc, sl], in0=R.ps[bk][:], in1=R.xT[:, dmc, sl],
                                                              op=ALU.add),
                             reads=[R.psb[bk], R.xb[dmc][tt]], writes=[R.xb[dmc][tt]])
                norm(3 * l + 2)
                ffn("ffn2", l)
                lp = l + 1
            if do_pre:
                norm(3 * lp + 0)
                ffn("ffn1", lp)
            T.barrier_all()
        if do_pre:
            norm(3 * lp + 1)
            AP_ = dict(A)
            for nm in ("w_in", "w_in_sw", "w_uq", "w_uq_sw", "w_ukv", "smallsP"):
                AP_[nm] = A[nm][lp]
            emit_stage_P(R, hT, hb, AP_, mid_hook=mid_hook)
        st = T.new_dma_sem("stx")
        if final:
            ysq = [px.enter_context(nc.sbuf_tensor(U("ystg%d" % i), [128, 512], F32)) for i in range(2)]
            y_b = [Buf("y0"), Buf("y1")]
            ss, ss_b = R.ps[6], R.psb[6]
            n = 0
            for tt in range(4):
                sl = slice(tt * 512, (tt + 1) * 512)
                for k in range(8):
                    a = k % 2
                    T.op("act", lambda e: e.activation(out=sq[a][:], in_=R.xT[:, k, sl], func=AF.Square),
                         reads=[R.xb[k][tt]], writes=[sq_b[a]])
                    T.op("pe", lambda e: e.matmul(ss[:], lhsT=R.onesm[:], rhs=sq[a][:], start=(k == 0), stop=(k == 7)),
                         reads=[sq_b[a], R.onesm_b], writes=[ss_b])
                T.op("act", lambda e: e.activation(out=rstd[:], in_=ss[:], func=AF.Sqrt, bias=EPS, scale=1.0 / D),
                     reads=[ss_b], writes=[rstd_b])
                T.op("dve", lambda e: e.reciprocal(out=rstd[:], in_=rstd[:]), reads=[rstd_b], writes=[rstd_b])
                for k in range(8):
                    i = n % 2
                    n += 1
                    T.op("dve", lambda e: e.scalar_tensor_tensor(out=ysq[i][:], in0=R.xT[:, k, sl],
                                                                 scalar=gam[:, 3 * L, k:k + 1], in1=rstd[:],
                                                                 op0=ALU.mult, op1=ALU.mult),
                         reads=[R.xb[k][tt], rstd_b, gam_b], writes=[y_b[i]])
                    T.dma("sp", st, xout[k * 128:(k + 1) * 128, sl], ysq[i][:], reads=[y_b[i]])
        else:
            for k in range(8):
                T.dma("sp", st, xout[k * 128:(k + 1) * 128, :], R.xT[:, k, :], reads=[R.xb[k][t] for t in range(4)])
        T.barrier_all()
        return st


def emit_layer_A(R, A, l):
    nc, T = R.nc, R.T
    with ExitStack() as pa:
        R.oT = pa.enter_context(nc.sbuf_tensor(U("oT"), [128, 8, NT], BF16))
        R.oT_b = [[Buf("oT%d_%d" % (c, g)) for g in range(4)] for c in range(8)]
        AL = dict(A)
        for nm in ("cmp_w1_k", "cmp_w1_v", "cmp_w2_k", "cmp_w2_v", "cmp_posT_k", "cmp_posT_v"):
            AL[nm] = A[nm][l]
        with ExitStack() as ph:
            C = alloc_attn_common(R, AL, ph)
            gen = emit_nsa_compress(R, C, AL)
            next(gen)
            emit_mla(R, C, AL, hook=gen)
            for _ in gen:
                pass
            emit_nsa(R, C, AL)
            emit_sb(R, C, AL)
        st = T.new_dma_sem("stoT")
        T.dma("sp", st, A["oT_d"], R.oT[:], reads=[b for ll in R.oT_b for b in ll])
        T.barrier_all()


def build_launch(kind, l):
    nc = bass.Bass("TRN2", target_bir_lowering=False)
    A = {}

    def din(nm, shp, dt=F32):
        A[nm] = nc.dram_tensor(nm, shp, dt, kind="ExternalInput").ap()

    def dout(nm, shp, dt=F32):
        A[nm] = nc.dram_tensor(nm, shp, dt, kind="ExternalOutput").ap()
    for nm, shp in W_IN:
        din(nm, shp)
    for nm, shp, dt in C_IN:
        din(nm, shp, dt)
    din("xT_in", [D, NT])
    dout("xT_out", [D, NT])
    if kind != "first":
        for nm in GATHER:
            shp, dt = POUT[nm]
            din(nm + "_g", [4] + shp, dt)
        for nm in LOCAL:
            shp, dt = POUT[nm]
            din(nm, shp, dt)
        A["oT_d"] = nc.dram_tensor("oT_d", [128, 8, NT], BF16, kind="Internal").ap()
    if kind != "last":
        for nm, shp, dt in P_OUTS:
            if kind == "first" or nm not in LOCAL:
                dout(nm, shp, dt)
            else:
                A[nm + "_o"] = nc.dram_tensor(nm + "_o", shp, dt, kind="ExternalOutput").ap()
    with ExitStack() as stack:
        T = Tracker(nc, stack)
        R = setup_common(nc, stack, T)
        if kind != "first":
            emit_layer_A(R, A, l)
        AX = dict(A)
        if kind == "mid":
            for nm in LOCAL:
                AX[nm] = A[nm + "_o"]
        st = emit_layer_X(R, AX, l, do_post=(kind != "first"), do_pre=(kind != "last"), final=(kind == "last"),
                          xin=A["xT_in"], xout=A["xT_out"])
        nc.sync.wait_ge(T.sem[st], T.cnt[st])
    return nc


def _host_weights(inp):
    f = lambda a: np.ascontiguousarray(np.asarray(a, dtype=np.float32))
    W = {}
    for nm in ("ffn1_w_gate", "ffn1_w_up", "ffn1_w_down", "ffn2_w_gate", "ffn2_w_up", "ffn2_w_down", "w_in", "w_out"):
        W[nm] = f(inp[nm])
    W["w_uq"] = f(inp["mla_w_uq"])
    W["w_ukv"] = f(inp["mla_w_ukv"])
    W["w_in_sw"] = np.stack([make_w_in_sw(W["w_in"][l]) for l in range(L)])
    W["w_uq_sw"] = np.stack([swap_cols_rope(W["w_uq"][l], 96, 64, 32) for l in range(L)])
    W["smallsP"] = np.stack([make_smallsP(f(inp["mla_q_norm"])[l], f(inp["mla_kv_norm"])[l], f(inp["nsa_gate_bias"])[l])
                             for l in range(L)])
    W["cmp_w1_k"] = f(inp["nsa_cmp_w1_k"])
    W["cmp_w1_v"] = f(inp["nsa_cmp_w1_v"])
    W["cmp_w2_k"] = f(inp["nsa_cmp_w2_k"])
    W["cmp_w2_v"] = f(inp["nsa_cmp_w2_v"])
    W["cmp_posT_k"] = np.ascontiguousarray(f(inp["nsa_cmp_pos_k"]).transpose(0, 2, 1))
    W["cmp_posT_v"] = np.ascontiguousarray(f(inp["nsa_cmp_pos_v"]).transpose(0, 2, 1))
    g = np.zeros((128, 3 * L + 1, 8), np.float32)
    for l in range(L):
        for i, nm in enumerate(("ffn1_norm", "mix_norm", "ffn2_norm")):
            g[:, 3 * l + i, :] = f(inp[nm])[l].reshape(8, 128).T
    g[:, 3 * L, :] = f(inp["final_norm"]).reshape(8, 128).T
    W["gains"] = g
    return W


def _core_consts(core):
    c = {}
    c.update(rope_tables(core))
    c["masks"] = make_masks(core)
    c["ident"] = np.eye(128, dtype=np.float32)
    c.update(make_nsa_consts(core))
    return c


def _kernel_unfused_impl(**inp):
    x = np.asarray(inp["x"], dtype=np.float32)
    W = _host_weights(inp)
    consts = [_core_consts(c) for c in range(8)]
    xT = [np.ascontiguousarray(x[c // 4][own_positions(c)].T) for c in range(8)]
    cores = list(range(8))
    nc = build_launch("first", 0)
    ims = [dict(W, **consts[c], xT_in=xT[c]) for c in cores]
    res = run_bass_kernel_spmd(nc, ims, core_ids=cores).results
    for l in range(L):
        kind = "mid" if l < L - 1 else "last"
        nc = build_launch(kind, l)
        ims = []
        for c in cores:
            b = c // 4
            im = dict(W, **consts[c], xT_in=np.asarray(res[c]["xT_out"]))
            for nm in GATHER:
                im[nm + "_g"] = np.stack([np.asarray(res[4 * b + r][nm]) for r in range(4)])
            for nm in LOCAL:
                key = nm if l == 0 else nm + "_o"
                im[nm] = np.asarray(res[c][key])
            ims.append(im)
        res = run_bass_kernel_spmd(nc, ims, core_ids=cores).results
    out = np.zeros((B, S, D), np.float32)
    for c in cores:
        out[c // 4][own_positions(c)] = np.asarray(res[c]["xT_out"]).T
    return out


PIECES = [
    ([192, NT], [("kT_mla", "heads", 0, 3)]),
    ([192, NT], [("kT_mla", "heads", 3, 6)]),
    ([32, NT], [("kpeT", "rows", 0, 32)]),
    ([256, NT], [("vT_cmp", "rows", 0, 128), ("kT_sel", "rows", 128, 256)]),
    ([256, NT], [("kT_cmp", "rows", 0, 128), ("kT_win", "rows", 128, 256)]),
    ([256, NT], [("kT_sb", "rows", 0, 256)]),
    ([1024, 6, 65], [("v_mla", "half", 0, 0)]),
    ([1024, 6, 65], [("v_mla", "half", 1, 1)]),
    ([NT, 2, 65], [("v_sel", "all", 0, 0)]),
    ([NT, 2, 65], [("v_win", "all", 0, 0)]),
    ([NT, 256], [("v_sb", "all", 0, 0)]),
]


def make_pieces(nc, l):
    V = {"kT_mla": [None] * 6, "kT_mla_g": [None] * 6, "v_mla": [None] * 2, "v_mla_g": [None] * 2}
    GB = {"kT_mla": [None] * 6, "v_mla": [None] * 2}
    cc = []
    for k, (shp, members) in enumerate(PIECES):
        n = int(np.prod(shp))
        w = n // 128
        gs = nc.dram_tensor("gs%d_%d" % (l, k), [128, w], BF16, kind="Internal").ap()
        gd = nc.dram_tensor("gd%d_%d" % (l, k), [512, w], BF16, kind="Internal").ap()
        pb = Buf("piece%d_%d" % (l, k))
        cc.append((gs, gd, pb))
        fs = gs.rearrange("p w -> (p w)")
        fd = gd.rearrange("(r p) w -> r (p w)", r=4)
        if len(shp) == 2:
            ns = fs.rearrange("(a c) -> a c", c=shp[1])
            nd = fd.rearrange("r (a c) -> r a c", c=shp[1])
        else:
            ns = fs.rearrange("(t h x) -> t h x", h=shp[1], x=shp[2])
            nd = fd.rearrange("r (t h x) -> r t h x", h=shp[1], x=shp[2])
        for nm, kind, lo, hi in members:
            if kind == "heads":
                for h in range(lo, hi):
                    V[nm][h] = ns[(h - lo) * 64:(h - lo + 1) * 64, :]
                    V[nm + "_g"][h] = nd[:, (h - lo) * 64:(h - lo + 1) * 64, :]
                    GB[nm][h] = pb
            elif kind == "rows":
                V[nm] = ns[lo:hi, :]
                V[nm + "_g"] = nd[:, lo:hi, :]
                GB[nm] = pb
            elif kind == "half":
                V[nm][lo] = ns
                V[nm + "_g"][lo] = nd
                GB[nm][lo] = pb
            else:
                V[nm] = ns
                V[nm + "_g"] = nd
                GB[nm] = pb
    V["cc"] = cc
    V["gb"] = GB
    return V


def build_fused():
    nc = bass.Bass("TRN2", target_bir_lowering=False)
    A = {}
    for nm, shp in W_IN:
        A[nm] = nc.dram_tensor(nm, shp, F32, kind="ExternalInput").ap()
    for nm, shp, dt in C_IN:
        A[nm] = nc.dram_tensor(nm, shp, dt, kind="ExternalInput").ap()
    A["xT_in"] = nc.dram_tensor("xT_in", [D, NT], F32, kind="ExternalInput").ap()
    A["yT_out"] = nc.dram_tensor("yT_out", [D, NT], F32, kind="ExternalOutput").ap()
    A["xT_d"] = nc.dram_tensor("xT_d", [D, NT], F32, kind="Internal").ap()
    A["oT_d"] = nc.dram_tensor("oT_d", [128, 8, NT], BF16, kind="Internal").ap()
    LA = []
    for l in range(L):
        V = make_pieces(nc, l)
        for nm in LOCAL:
            shp, dt = POUT[nm]
            V[nm] = nc.dram_tensor("%s_l%d" % (nm, l), shp, dt, kind="Internal").ap()
        LA.append(V)
    with ExitStack() as stack:
        T = Tracker(nc, stack)
        R = setup_common(nc, stack, T)
        st = None
        GRP = [[0, 1, 2, 3], [4, 5, 6, 7]]

        def mk_hook(ll):
            def hook():
                T.wait_dma_all("pool")
                for k in (2, 0, 6, 7, 1):
                    gs, gd, pb = LA[ll]["cc"][k]
                    T.collective(T.new_dma_sem("cc%d_%d" % (ll, k)), gs, gd, GRP, writes=[pb])
            return hook

        for l in range(L):
            if l == 0:
                emit_layer_X(R, dict(A, **LA[0]), 0, do_post=False, do_pre=True, final=False,
                             xin=A["xT_in"], xout=A["xT_d"], mid_hook=mk_hook(0))
            T.barrier_all()
            for k in (4, 3, 8, 9, 5, 10):
                gs, gd, pb = LA[l]["cc"][k]
                T.collective(T.new_dma_sem("cc%d_%d" % (l, k)), gs, gd, GRP, writes=[pb])
            emit_layer_A(R, dict(A, **LA[l]), l)
            last = (l == L - 1)
            AX = dict(A, **(LA[l + 1] if not last else {}))
            st = emit_layer_X(R, AX, l, do_post=True, do_pre=not last, final=last,
                              xin=A["xT_d"], xout=(A["yT_out"] if last else A["xT_d"]),
                              mid_hook=(None if last else mk_hook(l + 1)))
        nc.sync.wait_ge(T.sem[st], T.cnt[st])
    return nc


def kernel_unfused(**inp):
    return _kernel_unfused_impl(**inp)


def kernel_fused(**inp):
    x = np.asarray(inp["x"], dtype=np.float32)
    W = _host_weights(inp)
    cores = list(range(8))
    nc = build_fused()
    ims = []
    for c in cores:
        xT = np.ascontiguousarray(x[c // 4][own_positions(c)].T)
        ims.append(dict(W, **_core_consts(c), xT_in=xT))
    res = run_bass_kernel_spmd(nc, ims, core_ids=cores).results
    out = np.zeros((B, S, D), np.float32)
    for c in cores:
        out[c // 4][own_positions(c)] = np.asarray(res[c]["yT_out"]).T
    return out


def kernel(**inp):
    return kernel_fused(**inp)
```

```python
import numpy as np
import ml_dtypes
from contextlib import ExitStack
import concourse.bass as bass
import concourse.mybir as mybir
from concourse.bass_utils import run_bass_kernel_spmd

F32 = mybir.dt.float32
BF16 = mybir.dt.bfloat16
AF = mybir.ActivationFunctionType
ALU = mybir.AluOpType
AX = mybir.AxisListType

D = 1024
S = 8192
B = 2
DFF = 2816
NT = 2048
NBLK = 16
EPS = 1e-6
DIN = 2354


_UN = [0]
CC_INC = 1


def U(name):
    _UN[0] += 1
    return "%s_u%d" % (name, _UN[0])


class Buf:
    __slots__ = ("name", "w", "r")

    def __init__(self, name):
        self.name = name
        self.w = None
        self.r = {}


class Tracker:
    def __init__(self, nc, stack):
        self.nc = nc
        self.stack = stack
        self.eng = {"pe": nc.tensor, "act": nc.scalar, "dve": nc.vector,
                    "pool": nc.gpsimd, "sp": nc.sync}
        self.sem = {}
        self.cnt = {}
        self.seen = {k: {} for k in self.eng}
        for k in self.eng:
            self.sem[k] = stack.enter_context(nc.semaphore("s_" + k))
            self.cnt[k] = 0
        self.ndma = 0

    def new_dma_sem(self, name):
        key = "dma_" + name + "_%d" % self.ndma
        self.ndma += 1
        self.sem[key] = self.stack.enter_context(self.nc.semaphore(key))
        self.cnt[key] = 0
        return key

    def _deps(self, e, reads, writes, ignore=None):
        deps = {}

        def add(k, c):
            if c > deps.get(k, 0):
                deps[k] = c
        for b in reads:
            if b.w is not None:
                add(*b.w)
        for b in writes:
            if b.w is not None:
                add(*b.w)
            for k, c in b.r.items():
                add(k, c)
        for k, c in deps.items():
            if (k == "pe" and e == "pe") or k == ignore:
                continue
            if k.startswith("dma_"):
                c = max(c, self.cnt[k])
            if c > self.seen[e].get(k, 0):
                self.eng[e].wait_ge(self.sem[k], c)
                self.seen[e][k] = c

    def op(self, e, fn, reads=(), writes=()):
        self._deps(e, reads, writes)
        ins = fn(self.eng[e])
        self.cnt[e] += 1
        c = self.cnt[e]
        ins.then_inc(self.sem[e], 1)
        for b in reads:
            if c > b.r.get(e, 0):
                b.r[e] = c
        for b in writes:
            b.w = (e, c)
            b.r = {}
        return ins

    def dma(self, q, dsem, out, in_, reads=(), writes=()):
        self._deps(q, reads, writes, ignore=dsem)
        ins = self.eng[q].dma_start(out=out, in_=in_)
        self.cnt[dsem] += 16
        c = self.cnt[dsem]
        ins.then_inc(self.sem[dsem], 16)
        for b in reads:
            if c > b.r.get(dsem, 0):
                b.r[dsem] = c
        for b in writes:
            b.w = (dsem, c)
            b.r = {}
        return ins

    def collective(self, dsem, src, dst, groups, reads=(), writes=()):
        self._deps("pool", reads, writes, ignore=dsem)
        ins = self.nc.gpsimd.collective_compute("AllGather", ALU.bypass, replica_groups=groups,
                                                ins=[src.opt()], outs=[dst.opt()])
        self.cnt[dsem] += CC_INC
        c = self.cnt[dsem]
        ins.then_inc(self.sem[dsem], CC_INC)
        for b in reads:
            if c > b.r.get(dsem, 0):
                b.r[dsem] = c
        for b in writes:
            b.w = (dsem, c)
            b.r = {}
        return ins

    def wait_dma_all(self, e):
        for k, c in self.cnt.items():
            if k.startswith("dma_") and c > self.seen[e].get(k, 0):
                self.eng[e].wait_ge(self.sem[k], c)
                self.seen[e][k] = c

    def barrier_all(self):
        for e in self.eng:
            for k, c in self.cnt.items():
                if k == e or c == 0:
                    continue
                if c > self.seen[e].get(k, 0):
                    self.eng[e].wait_ge(self.sem[k], c)
                    self.seen[e][k] = c


class Res:
    pass


def setup_common(nc, stack, T):
    R = Res()
    R.nc, R.T, R.stack = nc, T, stack
    R.ExitStack = ExitStack
    R.ps = []
    R.psb = []
    for i in range(8):
        R.ps.append(stack.enter_context(nc.psum_tensor("ps%d" % i, [128, 512], F32)))
        R.psb.append(Buf("ps%d" % i))
    R.xb = [[Buf("x%d_%d" % (k, t)) for t in range(4)] for k in range(8)]
    R.onesm = stack.enter_context(nc.sbuf_tensor(U("onesm"), [128, 128], BF16))
    R.onesm_b = Buf("onesm")
    T.op("pool", lambda e: e.memset(R.onesm[:], 1.0), writes=[R.onesm_b])
    return R


def alloc_xT(R, stack):
    R.xT = stack.enter_context(R.nc.sbuf_tensor(U("xT"), [128, 8, NT], F32))


def emit_norm(R, hT, hb, gam, gam_b, sq, sq_b, rstd, rstd_b, ssbank):
    T = R.T
    ss, ss_b = R.ps[ssbank], R.psb[ssbank]
    for tt in range(4):
        sl = slice(tt * 512, (tt + 1) * 512)
        for k in range(8):
            a = k % 2
            T.op("act", lambda e: e.activation(out=sq[a][:], in_=R.xT[:, k, sl], func=AF.Square),
                 reads=[R.xb[k][tt]], writes=[sq_b[a]])
            T.op("pe", lambda e: e.matmul(ss[:], lhsT=R.onesm[:], rhs=sq[a][:], start=(k == 0), stop=(k == 7)),
                 reads=[sq_b[a], R.onesm_b], writes=[ss_b])
        T.op("act", lambda e: e.activation(out=rstd[:], in_=ss[:], func=AF.Sqrt, bias=EPS, scale=1.0 / D),
             reads=[ss_b], writes=[rstd_b])
        T.op("dve", lambda e: e.reciprocal(out=rstd[:], in_=rstd[:]), reads=[rstd_b], writes=[rstd_b])
        for k in range(8):
            T.op("dve", lambda e: e.scalar_tensor_tensor(out=hT[:, k, sl], in0=R.xT[:, k, sl],
                                                         scalar=gam[:, k:k + 1], in1=rstd[:],
                                                         op0=ALU.mult, op1=ALU.mult),
                 reads=[R.xb[k][tt], rstd_b, gam_b], writes=[hb[k][tt]])


def emit_ffn(R, hT, hb, wg_d, wu_d, wd_d, W):
    T = R.T
    nfg = 6
    n_g = n_y = 0
    for fg in range(nfg):
        ncf = 4 if fg < 5 else 2
        wcols = ncf * 128
        c0 = fg * 512
        s = fg % 2
        wgs, wus, wds, wb, dsem = W["wg"][s], W["wu"][s], W["wd"][s], W["wb"][s], W["dsem"][s]
        T.dma("pool", dsem, wgs[:, :, 0:wcols],
              wg_d[:, c0:c0 + wcols].rearrange("(k p) c -> p k c", p=128), writes=[wb])
        T.dma("pool", dsem, wus[:, :, 0:wcols],
              wu_d[:, c0:c0 + wcols].rearrange("(k p) c -> p k c", p=128), writes=[wb])
        T.dma("pool", dsem, wds[:, 0:ncf, :],
              wd_d[c0:c0 + wcols, :].rearrange("(c p) m -> p c m", p=128), writes=[wb])
        for tt in range(4):
            sl = slice(tt * 512, (tt + 1) * 512)
            asl = (fg * 4 + tt) % 2
            for c in range(ncf):
                gi, ui = W["gbanks"][n_g % 2], W["ubanks"][n_g % 2]
                sgi = n_g % 2
                n_g += 1
                for k in range(8):
                    T.op("pe", lambda e: e.matmul(R.ps[gi][:], lhsT=wgs[:, k, c * 128:(c + 1) * 128],
                                                  rhs=hT[:, k, sl], start=(k == 0), stop=(k == 7)),
                         reads=[wb, hb[k][tt]], writes=[R.psb[gi]])
                for k in range(8):
                    T.op("pe", lambda e: e.matmul(R.ps[ui][:], lhsT=wus[:, k, c * 128:(c + 1) * 128],
                                                  rhs=hT[:, k, sl], start=(k == 0), stop=(k == 7)),
                         reads=[wb, hb[k][tt]], writes=[R.psb[ui]])
                T.op("act", lambda e: e.activation(out=W["sg"][sgi][:], in_=R.ps[gi][:], func=AF.Silu),
                     reads=[R.psb[gi]], writes=[W["sg_b"][sgi]])
                T.op("dve", lambda e: e.tensor_tensor(out=W["act"][asl][:, c, :], in0=W["sg"][sgi][:],
                                                      in1=R.ps[ui][:], op=ALU.mult),
                     reads=[W["sg_b"][sgi], R.psb[ui]], writes=[W["act_b"][asl][c]])
            for dmc in range(8):
                yi = W["ybanks"][n_y % 2]
                n_y += 1
                for c in range(ncf):
                    T.op("pe", lambda e: e.matmul(R.ps[yi][:], lhsT=wds[:, c, dmc * 128:(dmc + 1) * 128],
                                                  rhs=W["act"][asl][:, c, :], start=(c == 0), stop=(c == ncf - 1)),
                         reads=[wb, W["act_b"][asl][c]], writes=[R.psb[yi]])
                T.op("dve", lambda e: e.scalar_tensor_tensor(out=R.xT[:, dmc, sl], in0=R.ps[yi][:], scalar=0.5,
                                                             in1=R.xT[:, dmc, sl], op0=ALU.mult, op1=ALU.add),
                     reads=[R.psb[yi], R.xb[dmc][tt]], writes=[R.xb[dmc][tt]])


def alloc_ffn_work(R):
    nc, stack, T = R.nc, R.stack, R.T
    W = {}
    W["wg"] = [stack.enter_context(nc.sbuf_tensor(U("wg%d" % i), [128, 8, 512], BF16)) for i in range(2)]
    W["wu"] = [stack.enter_context(nc.sbuf_tensor(U("wu%d" % i), [128, 8, 512], BF16)) for i in range(2)]
    W["wd"] = [stack.enter_context(nc.sbuf_tensor(U("wd%d" % i), [128, 4, 1024], BF16)) for i in range(2)]
    W["wb"] = [Buf("wslot%d" % i) for i in range(2)]
    W["dsem"] = [T.new_dma_sem("ffnw%d" % i) for i in range(2)]
    W["sg"] = [stack.enter_context(nc.sbuf_tensor(U("sg%d" % i), [128, 512], F32)) for i in range(2)]
    W["sg_b"] = [Buf("sg%d" % i) for i in range(2)]
    W["act"] = [stack.enter_context(nc.sbuf_tensor(U("act%d" % i), [128, 4, 512], BF16)) for i in range(2)]
    W["act_b"] = [[Buf("act%d_%d" % (i, c)) for c in range(4)] for i in range(2)]
    W["gbanks"], W["ubanks"], W["ybanks"] = [0, 1], [2, 3], [4, 5]
    return W


C_CQ, C_CKV, C_KR, C_NQ, C_NKC, C_NVC, C_NKS, C_NVS, C_NKW, C_NVW, C_NG, C_SQ, C_SK, C_SV = (
    0, 256, 384, 416, 800, 928, 1056, 1184, 1312, 1440, 1568, 1586, 1842, 2098)
SW_KR, SW_NQ, SW_NKC, SW_NKS, SW_NKW = 0, 32, 416, 544, 672
NSW = 800


def emit_stage_P(R, hT, hb, A, mid_hook=None):
    nc, T = R.nc, R.T
    with R.ExitStack() as ph:
        def sb(name, shape, dt):
            return ph.enter_context(nc.sbuf_tensor(U(name), shape, dt))
        win = sb("P_win", [128, 8, DIN], BF16)
        wsw = sb("P_wsw", [128, 8, NSW], BF16)
        wuq = sb("P_wuq", [128, 2, 576], BF16)
        wuqs = sb("P_wuqs", [128, 2, 576], BF16)
        wukv = sb("P_wukv", [128, 768], BF16)
        sm = sb("P_sm", [128, 32], F32)
        wb_, smb = Buf("P_w"), Buf("P_sm")
        dw = T.new_dma_sem("Pw")
        T.dma("pool", dw, win[:], A["w_in"].rearrange("(k p) c -> p k c", p=128), writes=[wb_])
        T.dma("pool", dw, wsw[:], A["w_in_sw"].rearrange("(k p) c -> p k c", p=128), writes=[wb_])
        T.dma("pool", dw, wuq[:], A["w_uq"].rearrange("(k p) c -> p k c", p=128), writes=[wb_])
        T.dma("pool", dw, wuqs[:], A["w_uq_sw"].rearrange("(k p) c -> p k c", p=128), writes=[wb_])
        T.dma("pool", dw, wukv[:], A["w_ukv"], writes=[wb_])
        dw2 = T.new_dma_sem("Psm")
        T.dma("sp", dw2, sm[:], A["smallsP"], writes=[smb])
        tabs = []
        for i in range(2):
            tabs.append(dict(
                cosM=sb("P_cosM%d" % i, [96, 512], F32), sinM=sb("P_sinM%d" % i, [96, 512], F32),
                cosK=sb("P_cosK%d" % i, [32, 512], F32), sinK=sb("P_sinK%d" % i, [32, 512], F32),
                cosN=sb("P_cosN%d" % i, [128, 512], F32), sinN=sb("P_sinN%d" % i, [128, 512], F32),
                b=Buf("P_tab%d" % i), sem=T.new_dma_sem("Ptab%d" % i)))
        sq = [sb("P_sq%d" % i, [128, 512], BF16) for i in range(2)]
        sq_b = [Buf("P_sq0"), Buf("P_sq1")]
        rs = sb("P_rs", [128, 512], F32)
        rs_b = Buf("P_rs")
        cqn = sb("P_cqn", [128, 2, 512], BF16)
        cqn_b = Buf("P_cqn")
        ckvn = sb("P_ckvn", [128, 512], BF16)
        ckvn_b = Buf("P_ckvn")
        t1 = [sb("P_t1_%d" % i, [128, 512], F32) for i in range(2)]
        t2 = [sb("P_t2_%d" % i, [128, 512], F32) for i in range(2)]
        t_b = [Buf("P_t0"), Buf("P_t1")]
        NST = 4
        stg = [sb("P_stg%d" % i, [128, 512], BF16) for i in range(NST)]
        stg_b = [Buf("P_stg%d" % i) for i in range(NST)]
        stg_sem = [T.new_dma_sem("Pstg%d" % i) for i in range(NST)]
        vst = [sb("P_vst%d" % i, [128, 6, 65], BF16) for i in range(2)]
        vst_b = [Buf("P_vst0"), Buf("P_vst1")]
        vst_sem = [T.new_dma_sem("Pvst%d" % i) for i in range(2)]
        v2st = [sb("P_v2st%d" % i, [128, 2, 2, 65], BF16) for i in range(2)]
        v2st_b = [Buf("P_v2st0"), Buf("P_v2st1")]
        v2st_sem = [T.new_dma_sem("Pv2st%d" % i) for i in range(2)]
        vsb = [sb("P_vsb%d" % i, [128, 256], BF16) for i in range(2)]
        vsb_b = [Buf("P_vsb0"), Buf("P_vsb1")]
        vsb_sem = [T.new_dma_sem("Pvsb%d" % i) for i in range(2)]
        gst = [sb("P_gst%d" % i, [128, 18], F32) for i in range(2)]
        gst_b = [Buf("P_gst0"), Buf("P_gst1")]
        gst_sem = [T.new_dma_sem("Pgst%d" % i) for i in range(2)]
        for i in range(2):
            T.op("pool", lambda e: e.memset(vst[i][:], 1.0), writes=[vst_b[i]])
            T.op("pool", lambda e: e.memset(v2st[i][:], 1.0), writes=[v2st_b[i]])

        cnt = {"bank": 0, "stg": 0, "t": 0}

        def nbank():
            cnt["bank"] += 1
            return cnt["bank"] % 4

        def chain(bank, M, lhs_fn, rhs_fn, nk, reads, N=512, rows=None):
            o = R.ps[bank][0:M, 0:N] if rows is None else R.ps[bank][rows[0]:rows[1], 0:N]
            for k in range(nk):
                T.op("pe", lambda e: e.matmul(o, lhsT=lhs_fn(k), rhs=rhs_fn(k), start=(k == 0), stop=(k == nk - 1)),
                     reads=reads(k), writes=[R.psb[bank]])

        def store(src_ps_ap, src_bufs, M, dst_dram, use_act=False):
            i = cnt["stg"] % NST
            cnt["stg"] += 1
            eng = "act" if use_act else "dve"
            if use_act:
                T.op("act", lambda e: e.activation(out=stg[i][0:M, :], in_=src_ps_ap, func=AF.Copy),
                     reads=src_bufs, writes=[stg_b[i]])
            else:
                T.op("dve", lambda e: e.tensor_copy(out=stg[i][0:M, :], in_=src_ps_ap),
                     reads=src_bufs, writes=[stg_b[i]])
            T.dma("sp", stg_sem[i], dst_dram, stg[i][0:M, :], reads=[stg_b[i]])

        def rope_store(bankA, bankB, M, cos, sin, tb, dst_dram):
            j = cnt["t"] % 2
            cnt["t"] += 1
            i = cnt["stg"] % NST
            cnt["stg"] += 1
            T.op("dve", lambda e: e.tensor_tensor(out=t1[j][0:M, :], in0=R.ps[bankA][0:M, :], in1=cos, op=ALU.mult),
                 reads=[R.psb[bankA], tb], writes=[t_b[j]])
            T.op("dve", lambda e: e.tensor_tensor(out=t2[j][0:M, :], in0=R.ps[bankB][0:M, :], in1=sin, op=ALU.mult),
                 reads=[R.psb[bankB], tb], writes=[t_b[j]])
            T.op("pool", lambda e: e.tensor_tensor(out=stg[i][0:M, :], in0=t1[j][0:M, :], in1=t2[j][0:M, :], op=ALU.add),
                 reads=[t_b[j]], writes=[stg_b[i]])
            T.dma("sp", stg_sem[i], dst_dram, stg[i][0:M, :], reads=[stg_b[i]])

        for tt in range(4):
            sl = slice(tt * 512, (tt + 1) * 512)
            tab = tabs[tt % 2]
            for nm in ("cosM", "sinM", "cosK", "sinK"):
                T.dma("sp", tab["sem"], tab[nm][:], A[nm][:, sl], writes=[tab["b"]])
            hreads = lambda k: [wb_, hb[k][tt]]
            for c in range(2):
                chain(4 + c, 128, lambda k: win[:, k, C_CQ + c * 128:C_CQ + (c + 1) * 128],
                      lambda k: hT[:, k, sl], 8, hreads)
            chain(6, 128, lambda k: win[:, k, C_CKV:C_CKV + 128], lambda k: hT[:, k, sl], 8, hreads)
            for c in range(2):
                T.op("act", lambda e: e.activation(out=sq[c][:], in_=R.ps[4 + c][:], func=AF.Square),
                     reads=[R.psb[4 + c]], writes=[sq_b[c]])
            for c in range(2):
                T.op("pe", lambda e: e.matmul(R.ps[7][:], lhsT=R.onesm[:], rhs=sq[c][:], start=(c == 0), stop=(c == 1)),
                     reads=[sq_b[c], R.onesm_b], writes=[R.psb[7]])
            T.op("act", lambda e: e.activation(out=rs[:], in_=R.ps[7][:], func=AF.Sqrt, bias=EPS, scale=1.0 / 256),
                 reads=[R.psb[7]], writes=[rs_b])
            T.op("dve", lambda e: e.reciprocal(out=rs[:], in_=rs[:]), reads=[rs_b], writes=[rs_b])
            for c in range(2):
                T.op("dve", lambda e: e.scalar_tensor_tensor(out=cqn[:, c, :], in0=R.ps[4 + c][:], scalar=sm[:, c:c + 1],
                                                             in1=rs[:], op0=ALU.mult, op1=ALU.mult),
                     reads=[R.psb[4 + c], rs_b, smb], writes=[cqn_b])
            T.op("act", lambda e: e.activation(out=sq[0][:], in_=R.ps[6][:], func=AF.Square),
                 reads=[R.psb[6]], writes=[sq_b[0]])
            T.op("pe", lambda e: e.matmul(R.ps[7][:], lhsT=R.onesm[:], rhs=sq[0][:], start=True, stop=True),
                 reads=[sq_b[0], R.onesm_b], writes=[R.psb[7]])
            T.op("act", lambda e: e.activation(out=rs[:], in_=R.ps[7][:], func=AF.Sqrt, bias=EPS, scale=1.0 / 128),
                 reads=[R.psb[7]], writes=[rs_b])
            T.op("dve", lambda e: e.reciprocal(out=rs[:], in_=rs[:]), reads=[rs_b], writes=[rs_b])
            T.op("dve", lambda e: e.scalar_tensor_tensor(out=ckvn[:], in0=R.ps[6][:], scalar=sm[:, 2:3],
                                                         in1=rs[:], op0=ALU.mult, op1=ALU.mult),
                 reads=[R.psb[6], rs_b, smb], writes=[ckvn_b])
            for h in range(6):
                ba, bb = nbank(), 4 + (h % 2)
                chain(ba, 96, lambda c: wuq[:, c, h * 96:(h + 1) * 96], lambda c: cqn[:, c, :], 2,
                      lambda c: [wb_, cqn_b])
                chain(bb, 96, lambda c: wuqs[:, c, h * 96:(h + 1) * 96], lambda c: cqn[:, c, :], 2,
                      lambda c: [wb_, cqn_b])
                rope_store(ba, bb, 96, tab["cosM"][:], tab["sinM"][:], tab["b"], A["qT_mla"][h, :, sl])
            for h in range(6):
                ba = nbank()
                chain(ba, 64, lambda c: wukv[:, h * 128:h * 128 + 64], lambda c: ckvn[:], 1, lambda c: [wb_, ckvn_b])
                store(R.ps[ba][0:64, :], [R.psb[ba]], 64, A["kT_mla"][h][:, sl], use_act=(h % 2 == 0))
            ba, bb = nbank(), 4
            chain(ba, 32, lambda k: win[:, k, C_KR:C_KR + 32], lambda k: hT[:, k, sl], 8, hreads)
            chain(bb, 32, lambda k: wsw[:, k, SW_KR:SW_KR + 32], lambda k: hT[:, k, sl], 8, hreads)
            rope_store(ba, bb, 32, tab["cosK"][:], tab["sinK"][:], tab["b"], A["kpeT"][:, sl])
            for bl in range(4):
                tb = tt * 4 + bl
                lsl = slice(bl * 128, (bl + 1) * 128)
                i = tb % 2
                ba = nbank()
                T.op("pe", lambda e: e.matmul(R.ps[ba][:, 0:384], lhsT=ckvn[:, lsl],
                                              rhs=wukv[:].rearrange("p (h x) -> p h x", x=128)[:, :, 64:128],
                                              start=True, stop=True),
                     reads=[wb_, ckvn_b], writes=[R.psb[ba]])
                T.op("dve", lambda e: e.tensor_copy(out=vst[i][:, :, 0:64],
                                                    in_=R.ps[ba][:, 0:384].rearrange("p (h x) -> p h x", x=64)),
                     reads=[R.psb[ba]], writes=[vst_b[i]])
                T.dma("sp", vst_sem[i], A["v_mla"][tb // 8][(tb % 8) * 128:(tb % 8 + 1) * 128, :, :], vst[i][:], reads=[vst_b[i]])
        if mid_hook is not None:
            mid_hook()
        for tt in range(4):
            sl = slice(tt * 512, (tt + 1) * 512)
            tab = tabs[tt % 2]
            for nm in ("cosN", "sinN"):
                T.dma("sp", tab["sem"], tab[nm][:], A[nm][:, sl], writes=[tab["b"]])
            hreads = lambda k: [wb_, hb[k][tt]]
            ropes = [(C_NQ + 128 * c, SW_NQ + 128 * c, A["qT_nsa"][128 * c:128 * (c + 1), sl]) for c in range(3)]
            ropes += [(C_NKC, SW_NKC, A["kT_cmp"][:, sl]), (C_NKS, SW_NKS, A["kT_sel"][:, sl]),
                      (C_NKW, SW_NKW, A["kT_win"][:, sl])]
            for n, (ca, cs, dst) in enumerate(ropes):
                ba, bb = nbank(), 4 + (n % 2)
                chain(ba, 128, lambda k: win[:, k, ca:ca + 128], lambda k: hT[:, k, sl], 8, hreads)
                chain(bb, 128, lambda k: wsw[:, k, cs:cs + 128], lambda k: hT[:, k, sl], 8, hreads)
                rope_store(ba, bb, 128, tab["cosN"][:], tab["sinN"][:], tab["b"], dst)
            plain = [(C_NVC, A["vT_cmp"][:, sl]), (C_SQ, A["qT_sb"][0:128, sl]), (C_SQ + 128, A["qT_sb"][128:256, sl]),
                     (C_SK, A["kT_sb"][0:128, sl]), (C_SK + 128, A["kT_sb"][128:256, sl])]
            for n, (ca, dst) in enumerate(plain):
                ba = nbank()
                chain(ba, 128, lambda k: win[:, k, ca:ca + 128], lambda k: hT[:, k, sl], 8, hreads)
                store(R.ps[ba][:, :], [R.psb[ba]], 128, dst, use_act=(n % 2 == 0))
            for bl in range(4):
                tb = tt * 4 + bl
                tsl = slice(tb * 128, (tb + 1) * 128)
                lsl = slice(bl * 128, (bl + 1) * 128)
                i = tb % 2
                ba = nbank()
                for n, ca in enumerate((C_NVS, C_NVW)):
                    for k in range(8):
                        T.op("pe", lambda e: e.matmul(R.ps[ba][:, n * 128:(n + 1) * 128], lhsT=hT[:, k, tsl],
                                                      rhs=win[:, k, ca:ca + 128], start=(k == 0), stop=(k == 7)),
                             reads=[wb_, hb[k][tt]], writes=[R.psb[ba]])
                for k in range(8):
                    T.op("pe", lambda e: e.matmul(R.ps[ba][:, 256:274], lhsT=hT[:, k, tsl],
                                                  rhs=win[:, k, C_NG:C_NG + 18], start=(k == 0), stop=(k == 7)),
                         reads=[wb_, hb[k][tt]], writes=[R.psb[ba]])
                T.op("dve", lambda e: e.tensor_copy(out=v2st[i][:, :, :, 0:64],
                                                    in_=R.ps[ba][:, 0:256].rearrange("p (a h x) -> p a h x", a=2, x=64)),
                     reads=[R.psb[ba]], writes=[v2st_b[i]])
                T.dma("sp", v2st_sem[i], A["v_sel"][tsl, :, :], v2st[i][:, 0, :, :], reads=[v2st_b[i]])
                T.dma("sp", v2st_sem[i], A["v_win"][tsl, :, :], v2st[i][:, 1, :, :], reads=[v2st_b[i]])
                T.op("dve", lambda e: e.tensor_tensor(out=gst[i][:], in0=R.ps[ba][:, 256:274], in1=sm[:, 8:26], op=ALU.add),
                     reads=[R.psb[ba], smb], writes=[gst_b[i]])
                T.op("act", lambda e: e.activation(out=gst[i][:], in_=gst[i][:], func=AF.Sigmoid),
                     reads=[gst_b[i]], writes=[gst_b[i]])
                T.dma("sp", gst_sem[i], A["gates"][tsl, :], gst[i][:], reads=[gst_b[i]])
                ba = nbank()
                for k in range(8):
                    T.op("pe", lambda e: e.matmul(R.ps[ba][:, 0:256], lhsT=hT[:, k, tsl],
                                                  rhs=win[:, k, C_SV:C_SV + 256], start=(k == 0), stop=(k == 7)),
                         reads=[wb_, hb[k][tt]], writes=[R.psb[ba]])
                T.op("act", lambda e: e.activation(out=vsb[i][:], in_=R.ps[ba][:, 0:256], func=AF.Copy),
                     reads=[R.psb[ba]], writes=[vsb_b[i]])
                T.dma("sp", vsb_sem[i], A["v_sb"][tsl, :], vsb[i][:], reads=[vsb_b[i]])
        T.barrier_all()


THETA = 500000.0


def own_positions(core):
    j = core % 4
    return ((4 * np.arange(NBLK)[:, None] + j) * 128 + np.arange(128)[None, :]).reshape(-1)


def _cs(pos, rot):
    half = rot // 2
    inv = np.float32(THETA) ** (-(np.arange(half, dtype=np.float32) / np.float32(half)))
    ang = pos.astype(np.float32)[None, :] * inv.astype(np.float32)[:, None]
    return np.cos(ang).astype(np.float32), np.sin(ang).astype(np.float32)


def rope_tables(core):
    pos = own_positions(core)
    n = pos.shape[0]
    c16, s16 = _cs(pos, 32)
    c8, s8 = _cs(pos, 16)
    cosM = np.ones((96, n), np.float32); sinM = np.zeros((96, n), np.float32)
    cosM[64:80] = c16; cosM[80:96] = c16; sinM[64:80] = -s16; sinM[80:96] = s16
    cosK = np.concatenate([c16, c16], 0); sinK = np.concatenate([-s16, s16], 0)
    cosN = np.ones((128, n), np.float32); sinN = np.zeros((128, n), np.float32)
    for h in range(2):
        cosN[h * 64:h * 64 + 8] = c8; cosN[h * 64 + 8:h * 64 + 16] = c8
        sinN[h * 64:h * 64 + 8] = -s8; sinN[h * 64 + 8:h * 64 + 16] = s8
    return dict(cosM=cosM, sinM=sinM, cosK=cosK, sinK=sinK, cosN=cosN, sinN=sinN)


def swap_cols_rope(w, head_w, rope0, rot):
    w = np.array(w, copy=True)
    half = rot // 2
    nh = w.shape[1] // head_w
    for h in range(nh):
        a = h * head_w + rope0
        tmp = w[:, a:a + half].copy()
        w[:, a:a + half] = w[:, a + half:a + rot]
        w[:, a + half:a + rot] = tmp
    return w


def make_w_in_sw(w_in):
    parts = [swap_cols_rope(w_in[:, C_KR:C_KR + 32], 32, 0, 32),
             swap_cols_rope(w_in[:, C_NQ:C_NQ + 384], 64, 0, 16),
             swap_cols_rope(w_in[:, C_NKC:C_NKC + 128], 64, 0, 16),
             swap_cols_rope(w_in[:, C_NKS:C_NKS + 128], 64, 0, 16),
             swap_cols_rope(w_in[:, C_NKW:C_NKW + 128], 64, 0, 16)]
    return np.ascontiguousarray(np.concatenate(parts, axis=1))


def make_smallsP(q_norm, kv_norm, gate_bias):
    sm = np.zeros((128, 32), np.float32)
    sm[:, 0:2] = q_norm.reshape(2, 128).T
    sm[:, 2] = kv_norm
    sm[:, 8:26] = gate_bias[None, :]
    return sm


P_OUTS = [("qT_mla", [6, 96, NT], BF16), ("kT_mla", [6, 64, NT], BF16), ("kpeT", [32, NT], BF16),
          ("qT_nsa", [384, NT], BF16), ("kT_cmp", [128, NT], BF16), ("kT_sel", [128, NT], BF16),
          ("kT_win", [128, NT], BF16), ("vT_cmp", [128, NT], BF16), ("qT_sb", [256, NT], BF16),
          ("kT_sb", [256, NT], BF16), ("v_mla", [NT, 6, 65], BF16), ("v_sel", [NT, 2, 65], BF16),
          ("v_win", [NT, 2, 65], BF16), ("gates", [NT, 18], F32), ("v_sb", [NT, 256], BF16)]
P_INS = [("w_in", [D, DIN]), ("w_in_sw", [D, NSW]), ("w_uq", [256, 576]), ("w_uq_sw", [256, 576]),
         ("w_ukv", [128, 768]), ("smallsP", [128, 32]),
         ("cosM", [96, NT]), ("sinM", [96, NT]), ("cosK", [32, NT]), ("sinK", [32, NT]),
         ("cosN", [128, NT]), ("sinN", [128, NT])]


def make_masks(core):
    j = core % 4
    k = np.arange(128)[:, None]
    q = np.arange(128)[None, :]
    ones = np.ones((128, 128), np.float32)
    zeros = np.zeros((128, 128), np.float32)
    tri = (k <= q).astype(np.float32)
    stri = (k < q).astype(np.float32)
    gt = (k > q).astype(np.float32)
    M = np.zeros((128, 22, 128), np.float32)
    M[:, 20] = (k >= q).astype(np.float32)
    M[:, 21] = 1.0
    for d in range(4):
        M[:, d] = ones if d < j else (tri if d == j else zeros)
        M[:, 16 + d] = ones if d < j else (stri if d == j else zeros)
    for dd in range(8):
        d = dd - 4
        if d < j - 4 or d > j:
            M[:, 4 + dd] = zeros
        elif d == j - 4:
            M[:, 4 + dd] = gt
        elif d == j:
            M[:, 4 + dd] = tri
        else:
            M[:, 4 + dd] = ones
    for m4 in range(4):
        i16 = 4 * m4 + j
        M[:, 12 + m4] = (16 * k + 31 - 128 * i16 <= q).astype(np.float32)
    return M.astype(ml_dtypes.bfloat16)


def ktcol(kt):
    return (kt % 4) * NT + (kt // 4) * 128


def ktile(kt):
    return (kt % 4) * NBLK + (kt // 4)


def emit_oT_store(R, C, obank, G, ch, po, zcol=64, stride=65):
    T = R.T
    i = C["ev"] % 2
    C["ev"] += 1
    on, on_b, rz, rz_b = C["on"][i], C["on_b"][i], C["rz"][i], C["rz_b"][i]
    ov = R.ps[obank][:, 0:4 * stride].rearrange("p (m x) -> p m x", x=stride)
    T.op("dve", lambda e: e.reciprocal(out=rz[:], in_=ov[:, :, zcol:zcol + 1]), reads=[R.psb[obank]], writes=[rz_b])
    T.op("dve", lambda e: e.tensor_tensor(out=on[:], in0=ov[:, :, 0:64], in1=rz[:].broadcast_to([128, 4, 64]),
                                          op=ALU.mult),
         reads=[R.psb[obank], rz_b], writes=[on_b])
    emit_transpose_store(R, C, on, on_b, G, ch, po)


def emit_transpose_store(R, C, on, on_b, G, ch, po):
    T = R.T
    tb = C["tpbank"]
    for mm in range(4):
        T.op("pe", lambda e: e.transpose(out=R.ps[tb][0:64, mm * 128:(mm + 1) * 128], in_=on[:, mm, :],
                                         identity=C["ident"][:]),
             reads=[on_b, C["ident_b"]], writes=[R.psb[tb]])
    T.op("act", lambda e: e.activation(out=R.oT[po:po + 64, ch, G * 512:(G + 1) * 512], in_=R.ps[tb][0:64, :],
                                       func=AF.Copy),
         reads=[R.psb[tb]], writes=[R.oT_b[ch][G]])


def emit_dense_attn(R, C, KT, KT_b, V, V_b, QT, QT_b, dk, scale, obank, mask0,
                    selT=None, selT_b=None, vstride=65):
    T = R.T

    def run(G):
        steps = list(range(16 * G + 16))
        n = len(steps)
        stt = {}

        def stage_a(kt):
            mm_min = max(0, -((-(kt - 16 * G - 3)) // 4))
            c0 = mm_min * 128
            sb_ = C["sbanks"][C["sn"] % len(C["sbanks"])]
            C["sn"] += 1
            stt[kt] = [mm_min, c0, sb_, None]
            T.op("pe", lambda e: e.matmul(R.ps[sb_][:, c0:512], lhsT=KT[0:dk, ktcol(kt):ktcol(kt) + 128],
                                          rhs=QT[0:dk, G * 512 + c0:(G + 1) * 512], start=True, stop=(selT is None)),
                 reads=[KT_b, QT_b], writes=[R.psb[sb_]])
            if selT is not None:
                T.op("pe", lambda e: e.matmul(R.ps[sb_][:, c0:512], lhsT=C["Ebig"][:, kt * 128:(kt + 1) * 128],
                                              rhs=selT[:, G * 512 + c0:(G + 1) * 512], start=False, stop=True),
                     reads=[C["Ebig_b"], selT_b], writes=[R.psb[sb_]])

        def stage_b(kt):
            mm_min, c0, sb_, _ = stt[kt]
            pi = C["pn"] % len(C["PT"])
            C["pn"] += 1
            stt[kt][3] = pi
            PT, PT_b = C["PT"][pi], C["PT_b"][pi]
            T.op("act", lambda e: e.activation(out=PT[:, c0:512], in_=R.ps[sb_][:, c0:512], func=AF.Exp, scale=scale),
                 reads=[R.psb[sb_]], writes=[PT_b])
            if kt >= 16 * G:
                mmb = (kt - 16 * G) // 4
                d = (kt - 16 * G) % 4
                T.op("pool", lambda e: e.tensor_tensor(out=PT[:, mmb * 128:(mmb + 1) * 128],
                                                       in0=PT[:, mmb * 128:(mmb + 1) * 128],
                                                       in1=C["masks"][:, mask0 + d, :], op=ALU.mult),
                     reads=[PT_b, C["masks_b"]], writes=[PT_b])

        def stage_c(kt, first):
            mm_min, c0, sb_, pi = stt[kt]
            PT, PT_b = C["PT"][pi], C["PT_b"][pi]
            for mm in range(mm_min, 4):
                last = (kt == 16 * G + 4 * mm + 3)
                T.op("pe", lambda e: e.matmul(R.ps[obank][:, mm * vstride:(mm + 1) * vstride],
                                              lhsT=PT[:, mm * 128:(mm + 1) * 128], rhs=V[:, ktile(kt), :],
                                              start=(first and mm == mm_min), stop=last, skip_group_check=True),
                     reads=[PT_b, V_b], writes=[R.psb[obank]])

        for idx in range(n + 2):
            if idx < n:
                stage_a(steps[idx])
            if 1 <= idx <= n:
                stage_b(steps[idx - 1])
            if idx >= 2:
                stage_c(steps[idx - 2], idx == 2)
    return run


def alloc_attn_common(R, A, ph):
    nc, T = R.nc, R.T
    C = {"ev": 0, "sn": 0, "pn": 0, "sbanks": [0, 1, 2, 3], "tpbank": 5}

    def sb(name, shape, dt):
        return ph.enter_context(nc.sbuf_tensor(U(name), shape, dt))
    C["sb"] = sb
    C["masks"] = sb("A_masks", [128, 22, 128], BF16)
    C["masks_b"] = Buf("A_masks")
    C["ident"] = sb("A_ident", [128, 128], F32)
    C["ident_b"] = Buf("A_ident")
    ds = T.new_dma_sem("Aconst")
    T.dma("sp", ds, C["masks"][:], A["masks"], writes=[C["masks_b"]])
    T.dma("sp", ds, C["ident"][:], A["ident"], writes=[C["ident_b"]])
    C["PT"] = [sb("A_PT%d" % i, [128, 512], BF16) for i in range(6)]
    C["PT_b"] = [Buf("A_PT%d" % i) for i in range(6)]
    C["on"] = [sb("A_on%d" % i, [128, 4, 64], F32) for i in range(2)]
    C["on_b"] = [Buf("A_on%d" % i) for i in range(2)]
    C["rz"] = [sb("A_rz%d" % i, [128, 4, 1], F32) for i in range(2)]
    C["rz_b"] = [Buf("A_rz%d" % i) for i in range(2)]
    C["kcT"] = [sb("A_kcT%d" % i, [64, 512], BF16) for i in range(2)]
    C["vc"] = [sb("A_vc%d" % i, [128, 4, 65], BF16) for i in range(2)]
    C["kc_b"] = [Buf("A_kc%d" % i) for i in range(2)]
    return C


def gb(A, nm, i=None):
    d = A.get("gb")
    if d is None:
        return []
    b = d[nm]
    if isinstance(b, list):
        b = b[i]
    return [b]


def emit_mla(R, C, A, hook=None):
    nc, T = R.nc, R.T
    ngrp = 0
    hook_left = [4]
    with R.ExitStack() as ph:
        def sb(name, shape, dt):
            return ph.enter_context(nc.sbuf_tensor(U(name), shape, dt))
        KT = [sb("M_KT%d" % i, [96, 4 * NT], BF16) for i in range(2)]
        V = [sb("M_V%d" % i, [128, 64, 65], BF16) for i in range(2)]
        QT = [sb("M_QT%d" % i, [96, NT], BF16) for i in range(2)]
        bufs = [Buf("M_slot%d" % i) for i in range(2)]
        sems = [T.new_dma_sem("Mslot%d" % i) for i in range(2)]
        for h in range(6):
            s = h % 2
            for r in range(4):
                T.dma("sp", sems[s], KT[s][0:64, r * NT:(r + 1) * NT], A["kT_mla_g"][h][r, :, :], reads=gb(A, "kT_mla", h),
                      writes=[bufs[s]])
                T.dma("sp", sems[s], KT[s][64:96, r * NT:(r + 1) * NT], A["kpeT_g"][r, :, :], reads=gb(A, "kpeT"), writes=[bufs[s]])
                for hf in range(2):
                    T.dma("sp", sems[s], V[s][:, r * NBLK + hf * 8:r * NBLK + hf * 8 + 8, :],
                          A["v_mla_g"][hf][r, :, h, :].rearrange("(m p) x -> p m x", p=128), reads=gb(A, "v_mla", hf),
                          writes=[bufs[s]])
            T.dma("sp", sems[s], QT[s][:], A["qT_mla"][h, :, :], writes=[bufs[s]])
            for G in range(4):
                obank = 6 + (G % 2)
                run = emit_dense_attn(R, C, KT[s], bufs[s], V[s], bufs[s], QT[s], bufs[s], 96, 96 ** -0.5,
                                      obank, 0)
                run(G)
                emit_oT_store(R, C, obank, G, h // 2, (h % 2) * 64)
                ngrp += 1
                if hook is not None and ngrp % 3 == 1 and hook_left[0] > 0:
                    hook_left[0] -= 1
                    next(hook)
        T.barrier_all()


def make_nsa_consts(core):
    j = core % 4
    c = np.arange(512)[:, None]
    n = np.arange(128)[None, :]
    ov = ((16 * c < 64 * n + 64) & (16 * c + 32 > 64 * n)).astype(np.float32)
    ovl = np.concatenate([ov, np.ones((512, 1), np.float32)], 1).reshape(4, 128, 129).transpose(1, 0, 2)
    ql = np.arange(128)[:, None]
    w = np.arange(256)[None, :]
    rel = w - 128 - 2 * j
    cur = (ql >= 64).astype(np.int64)
    forced = (rel == cur) | (rel == cur - 1)
    future = rel > cur
    keep = (~(forced | future)).astype(np.float32)
    add = np.where(forced, 1e4, np.where(future, -1.0, 0.0)).astype(np.float32)
    ka = np.stack([keep, add], 1)
    key = np.arange(S)[None, :]
    eb = np.where(key // 64 == np.arange(128)[:, None], 1.0, 0.0).astype(np.float32)
    return dict(ovl=np.ascontiguousarray(ovl).astype(ml_dtypes.bfloat16), keepadd=np.ascontiguousarray(ka),
                Ebig=eb.astype(ml_dtypes.bfloat16))


def emit_nsa_compress(R, C, A):
    nc, T = R.nc, R.T
    with R.ExitStack() as ph:
        def sb(name, shape, dt):
            return ph.enter_context(nc.sbuf_tensor(U(name), shape, dt))
        w1 = [sb("N_w1%d" % i, [64, 32, 128], BF16) for i in range(2)]
        w2 = [sb("N_w2%d" % i, [128, 64], BF16) for i in range(2)]
        posT = [sb("N_pos%d" % i, [64, 32], BF16) for i in range(2)]
        cst_b = Buf("NC_const")
        ds = T.new_dma_sem("NconstP")
        T.dma("pool", ds, w1[0][:], A["cmp_w1_k"].rearrange("(l d) h -> d l h", d=64), writes=[cst_b])
        T.dma("pool", ds, w1[1][:], A["cmp_w1_v"].rearrange("(l d) h -> d l h", d=64), writes=[cst_b])
        T.dma("pool", ds, w2[0][:], A["cmp_w2_k"], writes=[cst_b])
        T.dma("pool", ds, w2[1][:], A["cmp_w2_v"], writes=[cst_b])
        T.dma("pool", ds, posT[0][:], A["cmp_posT_k"], writes=[cst_b])
        T.dma("pool", ds, posT[1][:], A["cmp_posT_v"], writes=[cst_b])
        bias = sb("N_bias", [128, 2], F32)
        bias_b = Buf("N_bias")
        for i in range(2):
            for l in range(32):
                T.op("pe", lambda e: e.matmul(R.ps[4][:, i:i + 1], lhsT=w1[i][:, l, :], rhs=posT[i][:, l:l + 1],
                                              start=(l == 0), stop=(l == 31)),
                     reads=[cst_b], writes=[R.psb[4]])
            T.op("dve", lambda e: e.tensor_copy(out=bias[:, i:i + 1], in_=R.ps[4][:, i:i + 1]),
                 reads=[R.psb[4]], writes=[bias_b])
        xg = [[sb("N_xg%d_%d" % (g, i), [64, S], BF16) for i in range(2)] for g in range(2)]
        xg_b = [[Buf("N_xg%d_%d" % (g, i)) for i in range(2)] for g in range(2)]
        hid = sb("N_hid", [128, 512], F32)
        tq = sb("N_tq", [128, 512], F32)
        gl = sb("N_gl", [128, 512], BF16)
        hid_b, gl_b = Buf("N_hid"), Buf("N_gl")
        T.op("pool", lambda e: e.memset(gl[:], 0.0), writes=[gl_b])
        yield
        for g in range(2):
            kcT, vc, kc_b = C["kcT"][g], C["vc"][g], C["kc_b"][g]
            T.op("pool", lambda e: e.memset(vc[:], 1.0), reads=[], writes=[kc_b])
            T.op("pool", lambda e: e.memset(kcT[:], 0.0), reads=[], writes=[kc_b])
            for i in range(2):
                nm = ("kT_cmp_g", "vT_cmp_g")[i]
                dsx = T.new_dma_sem("Nxg%d_%d" % (g, i))
                for r in range(4):
                    dst = xg[g][i][:].rearrange("d (m r p) -> d m r p", r=4, p=128)[:, :, r, :]
                    T.dma("sp", dsx, dst, A[nm][r, g * 64:(g + 1) * 64, :].rearrange("d (m p) -> d m p", p=128),
                          reads=gb(A, nm[:-2]), writes=[xg_b[g][i]])
                for l in range(32):
                    T.op("pe", lambda e: e.matmul(R.ps[4][:, 0:511], lhsT=w1[i][:, l, :],
                                                  rhs=xg[g][i][:, l:l + 16 * 510 + 1:16],
                                                  start=(l == 0), stop=(l == 31)),
                         reads=[cst_b, xg_b[g][i]], writes=[R.psb[4]])
                T.op("act", lambda e: e.activation(out=hid[:, 0:511], in_=R.ps[4][:, 0:511], func=AF.Identity,
                                                   bias=bias[:, i:i + 1]),
                     reads=[R.psb[4], bias_b], writes=[hid_b])
                T.op("dve", lambda e: e.tensor_tensor(out=tq[:, 0:511], in0=hid[:, 0:511], in1=hid[:, 0:511],
                                                      op=ALU.mult), reads=[hid_b], writes=[hid_b])
                T.op("dve", lambda e: e.tensor_scalar(out=tq[:, 0:511], in0=tq[:, 0:511], scalar1=0.044715,
                                                      scalar2=1.0, op0=ALU.mult, op1=ALU.add),
                     reads=[hid_b], writes=[hid_b])
                T.op("dve", lambda e: e.tensor_tensor(out=tq[:, 0:511], in0=tq[:, 0:511], in1=hid[:, 0:511],
                                                      op=ALU.mult), reads=[hid_b], writes=[hid_b])
                T.op("act", lambda e: e.activation(out=tq[:, 0:511], in_=tq[:, 0:511], func=AF.Sigmoid,
                                                   scale=1.5957691216057308),
                     reads=[hid_b], writes=[hid_b])
                T.op("dve", lambda e: e.tensor_tensor(out=gl[:, 0:511], in0=tq[:, 0:511], in1=hid[:, 0:511],
                                                      op=ALU.mult), reads=[hid_b, gl_b], writes=[gl_b])
                if i == 0:
                    T.op("pe", lambda e: e.matmul(R.ps[4][0:64, 0:511], lhsT=w2[0][:], rhs=gl[:, 0:511],
                                                  start=True, stop=True),
                         reads=[cst_b, gl_b], writes=[R.psb[4]])
                    T.op("dve", lambda e: e.tensor_copy(out=kcT[:, 0:511], in_=R.ps[4][0:64, 0:511]),
                         reads=[R.psb[4]], writes=[kc_b])
                else:
                    for t in range(4):
                        T.op("pe", lambda e: e.matmul(R.ps[4][:, t * 64:(t + 1) * 64], lhsT=gl[:, t * 128:(t + 1) * 128],
                                                      rhs=w2[1][:], start=True, stop=True),
                             reads=[cst_b, gl_b], writes=[R.psb[4]])
                    T.op("dve", lambda e: e.tensor_copy(out=vc[:, :, 0:64],
                                                        in_=R.ps[4][:, 0:256].rearrange("p (t x) -> p t x", x=64)),
                         reads=[R.psb[4]], writes=[kc_b])
                yield
        T.barrier_all()


def emit_nsa(R, C, A):
    nc, T = R.nc, R.T
    SC = 0.125
    with R.ExitStack() as ph:
        def sb(name, shape, dt):
            return ph.enter_context(nc.sbuf_tensor(U(name), shape, dt))
        Ebig = sb("N_Ebig", [128, S], BF16)
        ovl = sb("N_ovl", [128, 4, 129], BF16)
        ka = sb("N_ka", [128, 2, 256], F32)
        gates = sb("N_gates", [128, NBLK, 18], F32)
        cst_b = Buf("N_const")
        C["Ebig"], C["Ebig_b"] = Ebig, cst_b
        ds = T.new_dma_sem("Nconst")
        T.dma("sp", ds, Ebig[:], A["Ebig"], writes=[cst_b])
        T.dma("sp", ds, ovl[:], A["ovl"], writes=[cst_b])
        T.dma("sp", ds, ka[:], A["keepadd"], writes=[cst_b])
        T.dma("sp", ds, gates[:], A["gates"].rearrange("(m p) g -> p m g", p=128), writes=[cst_b])
        selT = sb("N_selT", [128, NT], BF16)
        selT_b = Buf("N_selT")
        QTn = sb("N_QT", [64, 3, NT], BF16)
        KTs = sb("N_KTs", [64, 4 * NT], BF16)
        Vs = sb("N_Vs", [128, 64, 65], BF16)
        KTw = sb("N_KTw", [64, 4 * NT], BF16)
        Vw = sb("N_Vw", [128, 64, 65], BF16)
        kv_b = Buf("N_kv")
        kv_sem = T.new_dma_sem("Nkv")
        oaccs = [sb("N_oacc%d" % i, [128, 3, 4, 64], F32) for i in range(2)]
        oaccs_b = [Buf("N_oacc0"), Buf("N_oacc1")]
        C["pcn"], C["pwn"] = 0, 0
        Pc = [sb("N_Pc%d" % i, [128, 3, 128], BF16) for i in range(2)]
        Pc_b = [Buf("N_Pc0"), Buf("N_Pc1")]
        rzc = sb("N_rzc", [128, 3, 1], F32)
        wc = sb("N_wc", [128, 3, 1], F32)
        sc = sb("N_sc", [128, 128], F32)
        sc2 = sb("N_sc2", [128, 128], F32)
        m8 = sb("N_m8", [128, 16], F32)
        selns = [sb("N_seln%d" % i, [128, 128], F32) for i in range(4)]
        selns_b = [Buf("N_seln%d" % i) for i in range(4)]
        sm_b = Buf("N_small")
        rz4 = sb("N_rz4", [128, 4, 1], F32)
        w4 = sb("N_w4", [128, 4, 1], F32)
        w4_b = Buf("N_w4")
        Pw = [sb("N_Pw%d" % i, [128, 512], BF16) for i in range(3)]
        Pw_b = [Buf("N_Pw%d" % i) for i in range(3)]
        for g in range(2):
            kcT, vc, kc_b = C["kcT"][g], C["vc"][g], C["kc_b"][g]
            T.dma("sp", kv_sem, QTn[:], A["qT_nsa"][g * 192:(g + 1) * 192, :].rearrange("(h d) t -> d h t", d=64),
                  writes=[kv_b])
            for r in range(4):
                T.dma("sp", kv_sem, KTs[:, r * NT:(r + 1) * NT], A["kT_sel_g"][r, g * 64:(g + 1) * 64, :], reads=gb(A, "kT_sel"), writes=[kv_b])
                T.dma("sp", kv_sem, KTw[:, r * NT:(r + 1) * NT], A["kT_win_g"][r, g * 64:(g + 1) * 64, :], reads=gb(A, "kT_win"), writes=[kv_b])
                T.dma("sp", kv_sem, Vs[:, r * NBLK:(r + 1) * NBLK, :],
                      A["v_sel_g"][r, :, g, :].rearrange("(m p) x -> p m x", p=128), reads=gb(A, "v_sel"), writes=[kv_b])
                T.dma("sp", kv_sem, Vw[:, r * NBLK:(r + 1) * NBLK, :],
                      A["v_win_g"][r, :, g, :].rearrange("(m p) x -> p m x", p=128), reads=gb(A, "v_win"), writes=[kv_b])

            def cmp_block(G, oacc, oacc_b):
                for mm in range(4):
                    ob, sbk = (3, 4) if mm % 2 == 0 else (2, 7)
                    m = 4 * G + mm
                    msl = slice(m * 128, (m + 1) * 128)
                    for t in range(G + 1):
                        sb_ = C["sbanks"][C["sn"] % len(C["sbanks"])]
                        C["sn"] += 1
                        pi = C["pcn"] % 2
                        C["pcn"] += 1
                        T.op("pe", lambda e: e.matmul(R.ps[sb_][:, 0:384], lhsT=kcT[:, t * 128:(t + 1) * 128],
                                                      rhs=QTn[:, :, msl], start=True, stop=True),
                             reads=[kc_b, kv_b], writes=[R.psb[sb_]])
                        T.op("act", lambda e: e.activation(out=Pc[pi][:].rearrange("p r q -> p (r q)"),
                                                           in_=R.ps[sb_][:, 0:384], func=AF.Exp, scale=SC),
                             reads=[R.psb[sb_]], writes=[Pc_b[pi]])
                        if t == G:
                            T.op("pool", lambda e: e.tensor_tensor(
                                out=Pc[pi][:], in0=Pc[pi][:],
                                in1=C["masks"][:, 12 + mm:13 + mm, :].broadcast_to([128, 3, 128]), op=ALU.mult),
                                reads=[Pc_b[pi], C["masks_b"]], writes=[Pc_b[pi]])
                        for r in range(3):
                            T.op("pe", lambda e: e.matmul(R.ps[ob][:, r * 65:(r + 1) * 65], lhsT=Pc[pi][:, r, :],
                                                          rhs=vc[:, t, :], start=(t == 0 and r == 0), stop=(t == G),
                                                          skip_group_check=True),
                                 reads=[Pc_b[pi], kc_b], writes=[R.psb[ob]])
                        for r in range(3):
                            T.op("pe", lambda e: e.matmul(R.ps[sbk][:, r * 129:(r + 1) * 129], lhsT=Pc[pi][:, r, :],
                                                          rhs=ovl[:, t, :], start=(t == 0 and r == 0), stop=(t == G),
                                                          skip_group_check=True),
                                 reads=[Pc_b[pi], cst_b], writes=[R.psb[sbk]])
                    scv = R.ps[sbk][:, 0:387].rearrange("p (r x) -> p r x", x=129)
                    ocv = R.ps[ob][:, 0:195].rearrange("p (r x) -> p r x", x=65)
                    T.op("dve", lambda e: e.tensor_scalar(out=rzc[:], in0=scv[:, :, 128:129], scalar1=1e-30, scalar2=None,
                                                          op0=ALU.add), reads=[R.psb[sbk]], writes=[sm_b])
                    T.op("dve", lambda e: e.reciprocal(out=rzc[:], in_=rzc[:]), reads=[sm_b], writes=[sm_b])
                    T.op("dve", lambda e: e.tensor_scalar(out=sc[:], in0=scv[:, 0, 0:128], scalar1=rzc[:, 0, :], scalar2=None,
                                                          op0=ALU.mult), reads=[R.psb[sbk], sm_b], writes=[sm_b])
                    for r in (1, 2):
                        T.op("dve", lambda e: e.scalar_tensor_tensor(out=sc[:], in0=scv[:, r, 0:128], scalar=rzc[:, r, :],
                                                                     in1=sc[:], op0=ALU.mult, op1=ALU.add),
                             reads=[R.psb[sbk], sm_b], writes=[sm_b])
                    T.op("dve", lambda e: e.tensor_tensor(
                        out=wc[:], in0=rzc[:],
                        in1=gates[:, m, 9 * g:9 * g + 9].rearrange("p (r x) -> p r x", x=3)[:, :, 0:1], op=ALU.mult),
                        reads=[sm_b, cst_b], writes=[sm_b])
                    for r in range(3):
                        T.op("dve", lambda e: e.tensor_scalar(out=oacc[:, r, mm, :], in0=ocv[:, r, 0:64],
                                                              scalar1=wc[:, r, :], scalar2=None, op0=ALU.mult),
                             reads=[R.psb[ob], sm_b], writes=[oacc_b])
                    w0 = 128 - 8 * m
                    T.op("dve", lambda e: e.tensor_tensor(out=sc[:], in0=sc[:], in1=ka[:, 0, w0:w0 + 128], op=ALU.mult),
                         reads=[sm_b, cst_b], writes=[sm_b])
                    T.op("dve", lambda e: e.tensor_tensor(out=sc[:], in0=sc[:], in1=ka[:, 1, w0:w0 + 128], op=ALU.add),
                         reads=[sm_b, cst_b], writes=[sm_b])
                    T.op("dve", lambda e: e.memset(sc[:, 0:1], 1e4), reads=[sm_b], writes=[sm_b])
                    T.op("dve", lambda e: e.max(out=m8[:, 0:8], in_=sc[:]), reads=[sm_b], writes=[sm_b])
                    T.op("dve", lambda e: e.match_replace(out=sc2[:], in_to_replace=m8[:, 0:8], in_values=sc[:],
                                                          imm_value=-1e9), reads=[sm_b], writes=[sm_b])
                    T.op("dve", lambda e: e.max(out=m8[:, 8:16], in_=sc2[:]), reads=[sm_b], writes=[sm_b])
                    T.op("dve", lambda e: e.tensor_scalar(out=selns[mm][:], in0=sc[:], scalar1=m8[:, 15:16], scalar2=None,
                                                          op0=ALU.is_ge), reads=[sm_b], writes=[selns_b[mm]])

            def cmp_finish(G):
                tb = C["tpbank"]
                for mm in range(4):
                    T.op("pe", lambda e: e.transpose(out=R.ps[tb][:, mm * 128:(mm + 1) * 128], in_=selns[mm][:],
                                                     identity=C["ident"][:]),
                         reads=[selns_b[mm], C["ident_b"]], writes=[R.psb[tb]])
                T.op("act", lambda e: e.activation(out=selT[:, G * 512:(G + 1) * 512], in_=R.ps[tb][:, :], func=AF.Copy),
                     reads=[R.psb[tb]], writes=[selT_b])

            def win_block(r, G, oacc, oacc_b):
                h = 3 * g + r
                ob = 6
                items = []
                for mm in range(4):
                    for half in range(2):
                        kts = [16 * G + 4 * mm - 4 + half * 4 + x for x in range(4)]
                        if kts[-1] >= 0:
                            items.append((mm, half, kts))
                ni = len(items)
                stt = {}

                def wa(i):
                    mm, half, kts = items[i]
                    msl = slice((4 * G + mm) * 128, (4 * G + mm + 1) * 128)
                    sb_ = C["sbanks"][C["sn"] % len(C["sbanks"])]
                    C["sn"] += 1
                    stt[i] = [sb_, None]
                    for x, kt in enumerate(kts):
                        T.op("pe", lambda e: e.matmul(R.ps[sb_][:, x * 128:(x + 1) * 128],
                                                      lhsT=KTw[:, ktcol(kt):ktcol(kt) + 128],
                                                      rhs=QTn[:, r, msl], start=True, stop=True),
                             reads=[kv_b], writes=[R.psb[sb_]])

                def wb(i):
                    mm, half, kts = items[i]
                    sb_ = stt[i][0]
                    pi = C["pwn"] % len(Pw)
                    C["pwn"] += 1
                    stt[i][1] = pi
                    T.op("act", lambda e: e.activation(out=Pw[pi][:], in_=R.ps[sb_][:], func=AF.Exp, scale=SC),
                         reads=[R.psb[sb_]], writes=[Pw_b[pi]])
                    T.op("pool", lambda e: e.tensor_tensor(
                        out=Pw[pi][:], in0=Pw[pi][:],
                        in1=C["masks"][:, 4 + half * 4:8 + half * 4, :].rearrange("p a q -> p (a q)"),
                        op=ALU.mult), reads=[Pw_b[pi], C["masks_b"]], writes=[Pw_b[pi]])

                def wc_(i):
                    mm, half, kts = items[i]
                    pi = stt[i][1]
                    for x, kt in enumerate(kts):
                        T.op("pe", lambda e: e.matmul(R.ps[ob][:, mm * 65:(mm + 1) * 65],
                                                      lhsT=Pw[pi][:, x * 128:(x + 1) * 128], rhs=Vw[:, ktile(kt), :],
                                                      start=(i == 0 and x == 0), stop=(half == 1 and x == 3),
                                                      skip_group_check=True),
                             reads=[Pw_b[pi], kv_b], writes=[R.psb[ob]])

                for idx in range(ni + 2):
                    if idx < ni:
                        wa(idx)
                    if 1 <= idx <= ni:
                        wb(idx - 1)
                    if idx >= 2:
                        wc_(idx - 2)
                owv = R.ps[ob][:, 0:260].rearrange("p (m x) -> p m x", x=65)
                T.op("dve", lambda e: e.reciprocal(out=rz4[:], in_=owv[:, :, 64:65]), reads=[R.psb[ob]], writes=[w4_b])
                T.op("dve", lambda e: e.tensor_tensor(out=w4[:], in0=rz4[:],
                                                      in1=gates[:, 4 * G:4 * G + 4, 3 * h + 2:3 * h + 3], op=ALU.mult),
                     reads=[w4_b, cst_b], writes=[w4_b])
                for mm in range(4):
                    T.op("dve", lambda e: e.scalar_tensor_tensor(out=oacc[:, r, mm, :], in0=owv[:, mm, 0:64],
                                                                 scalar=w4[:, mm, :], in1=oacc[:, r, mm, :],
                                                                 op0=ALU.mult, op1=ALU.add),
                         reads=[R.psb[ob], w4_b, oacc_b], writes=[oacc_b])

            def sel3_block(G, oacc, oacc_b):
                obs = [4, 6, 7]
                mbanks = [2, 3]
                qbanks = [0, 1]
                nk = 16 * G + 16
                nhs = 3 * nk
                stt = {}

                def geom(kt):
                    mm_min = max(0, -((-(kt - 16 * G - 3)) // 4))
                    return mm_min, mm_min * 128

                def sa(hs):
                    kt, r = hs // 3, hs % 3
                    mm_min, c0 = geom(kt)
                    mb = mbanks[kt % 2]
                    if r == 0:
                        T.op("pe", lambda e: e.matmul(R.ps[mb][:, c0:512], lhsT=Ebig[:, kt * 128:(kt + 1) * 128],
                                                      rhs=selT[:, G * 512 + c0:(G + 1) * 512], start=True, stop=True),
                             reads=[cst_b, selT_b], writes=[R.psb[mb]])
                    qb = qbanks[hs % 2]
                    T.op("pe", lambda e: e.matmul(R.ps[qb][:, c0:512], lhsT=KTs[:, ktcol(kt):ktcol(kt) + 128],
                                                  rhs=QTn[:, r, G * 512 + c0:(G + 1) * 512], start=True, stop=True),
                         reads=[kv_b], writes=[R.psb[qb]])

                def sb_(hs):
                    kt, r = hs // 3, hs % 3
                    mm_min, c0 = geom(kt)
                    mb, qb = mbanks[kt % 2], qbanks[hs % 2]
                    pi = C["pn"] % len(C["PT"])
                    C["pn"] += 1
                    stt[hs] = pi
                    PT, PT_b = C["PT"][pi], C["PT_b"][pi]
                    T.op("act", lambda e: e.activation(out=PT[:, c0:512], in_=R.ps[qb][:, c0:512], func=AF.Exp, scale=SC),
                         reads=[R.psb[qb]], writes=[PT_b])
                    T.op("dve", lambda e: e.tensor_tensor(out=PT[:, c0:512], in0=PT[:, c0:512], in1=R.ps[mb][:, c0:512],
                                                          op=ALU.mult),
                         reads=[PT_b, R.psb[mb]], writes=[PT_b])
                    if kt >= 16 * G:
                        mmb = (kt - 16 * G) // 4
                        d = (kt - 16 * G) % 4
                        T.op("pool", lambda e: e.tensor_tensor(out=PT[:, mmb * 128:(mmb + 1) * 128],
                                                               in0=PT[:, mmb * 128:(mmb + 1) * 128],
                                                               in1=C["masks"][:, d, :], op=ALU.mult),
                             reads=[PT_b, C["masks_b"]], writes=[PT_b])

                def sc_(hs):
                    kt, r = hs // 3, hs % 3
                    mm_min, c0 = geom(kt)
                    PT, PT_b = C["PT"][stt[hs]], C["PT_b"][stt[hs]]
                    for mm in range(mm_min, 4):
                        T.op("pe", lambda e: e.matmul(R.ps[obs[r]][:, mm * 65:(mm + 1) * 65],
                                                      lhsT=PT[:, mm * 128:(mm + 1) * 128], rhs=Vs[:, ktile(kt), :],
                                                      start=(kt == 0 and mm == mm_min), stop=(kt == 16 * G + 4 * mm + 3),
                                                      skip_group_check=True),
                             reads=[PT_b, kv_b], writes=[R.psb[obs[r]]])

                for idx in range(nhs + 4):
                    if idx < nhs:
                        sa(idx)
                    if 1 <= idx <= nhs:
                        sb_(idx - 1)
                    if idx >= 4:
                        sc_(idx - 4)
                for r in range(3):
                    h = 3 * g + r
                    osv = R.ps[obs[r]][:, 0:260].rearrange("p (m x) -> p m x", x=65)
                    T.op("dve", lambda e: e.reciprocal(out=rz4[:], in_=osv[:, :, 64:65]), reads=[R.psb[obs[r]]], writes=[w4_b])
                    T.op("dve", lambda e: e.tensor_tensor(out=w4[:], in0=rz4[:],
                                                          in1=gates[:, 4 * G:4 * G + 4, 3 * h + 1:3 * h + 2], op=ALU.mult),
                         reads=[w4_b, cst_b], writes=[w4_b])
                    for mm in range(4):
                        T.op("dve", lambda e: e.scalar_tensor_tensor(out=oacc[:, r, mm, :], in0=osv[:, mm, 0:64],
                                                                     scalar=w4[:, mm, :], in1=oacc[:, r, mm, :],
                                                                     op0=ALU.mult, op1=ALU.add),
                             reads=[R.psb[obs[r]], w4_b, oacc_b], writes=[oacc_b])
                    emit_transpose_store(R, C, oacc[:, r, :, :], oacc_b, G, 3 + h // 2, (h % 2) * 64)

            C["sbanks"] = [0, 1]
            cmp_block(0, oaccs[0], oaccs_b[0])
            for G in range(4):
                cmp_finish(G)
                if G + 1 < 4:
                    cmp_block(G + 1, oaccs[(G + 1) % 2], oaccs_b[(G + 1) % 2])
                for r in range(3):
                    win_block(r, G, oaccs[G % 2], oaccs_b[G % 2])
                sel3_block(G, oaccs[G % 2], oaccs_b[G % 2])
            C["sbanks"] = [0, 1, 2, 3]
        T.barrier_all()


def emit_sb(R, C, A):
    nc, T = R.nc, R.T
    SC = 0.125
    with R.ExitStack() as ph:
        def sb(name, shape, dt):
            return ph.enter_context(nc.sbuf_tensor(U(name), shape, dt))
        KT = [sb("S_KT%d" % i, [64, 4 * NT], BF16) for i in range(2)]
        V = [sb("S_V%d" % i, [128, 64, 64], BF16) for i in range(2)]
        QT = [sb("S_QT%d" % i, [64, NT], BF16) for i in range(2)]
        bufs = [Buf("S_slot%d" % i) for i in range(2)]
        sems = [T.new_dma_sem("Sslot%d" % i) for i in range(2)]
        NE = 4
        E = [sb("S_E%d" % i, [128, 512], F32) for i in range(NE)]
        SP = [sb("S_SP%d" % i, [128, 512], BF16) for i in range(NE)]
        X = [sb("S_X%d" % i, [128, 512], F32) for i in range(2)]
        E_b = [Buf("S_E%d" % i) for i in range(NE)]
        SP_b = [Buf("S_SP%d" % i) for i in range(NE)]
        X_b = [Buf("S_X0"), Buf("S_X1")]
        Accb = [sb("S_Accb%d" % i, [128, 512], BF16) for i in range(3)]
        Accb_b = [Buf("S_Accb%d" % i) for i in range(3)]
        tincl = C["masks"][:, 20, :]
        onesb = C["masks"][:, 21, :]
        cbanks = [3, 4]
        zbanks = [0, 1, 2]
        for h in range(4):
            s = h % 2
            for r in range(4):
                T.dma("sp", sems[s], KT[s][:, r * NT:(r + 1) * NT], A["kT_sb_g"][r, h * 64:(h + 1) * 64, :], reads=gb(A, "kT_sb"),
                      writes=[bufs[s]])
                T.dma("sp", sems[s], V[s][:, r * NBLK:(r + 1) * NBLK, :],
                      A["v_sb_g"][r, :, h * 64:(h + 1) * 64].rearrange("(m p) x -> p m x", p=128), reads=gb(A, "v_sb"),
                      writes=[bufs[s]])
            T.dma("sp", sems[s], QT[s][:], A["qT_sb"][h * 64:(h + 1) * 64, :], writes=[bufs[s]])
            for G in range(4):
                ob = 6 + (G % 2)
                for i in range(3):
                    T.op("pool", lambda e: e.memset(Accb[i][:], 0.0), writes=[Accb_b[i]])
                steps = list(range(16 * G + 15, -1, -1))
                ns = len(steps)
                stt = {}

                def geom(kt):
                    mm_min = max(0, -((-(kt - 16 * G - 3)) // 4))
                    return mm_min, mm_min * 128

                def st_a(n):
                    kt = steps[n]
                    mm_min, c0 = geom(kt)
                    zb = zbanks[n % 3]
                    T.op("pe", lambda e: e.matmul(R.ps[zb][:, c0:512], lhsT=KT[s][:, ktcol(kt):ktcol(kt) + 128],
                                                  rhs=QT[s][:, G * 512 + c0:(G + 1) * 512], start=True, stop=True),
                         reads=[bufs[s]], writes=[R.psb[zb]])

                def st_b(n):
                    kt = steps[n]
                    mm_min, c0 = geom(kt)
                    zb, ie = zbanks[n % 3], n % NE
                    T.op("act", lambda e: e.activation(out=E[ie][:, c0:512], in_=R.ps[zb][:, c0:512], func=AF.Exp, scale=SC),
                         reads=[R.psb[zb]], writes=[E_b[ie]])
                    T.op("act", lambda e: e.activation(out=SP[ie][:, c0:512], in_=E[ie][:, c0:512], func=AF.Ln, bias=1.0),
                         reads=[E_b[ie]], writes=[SP_b[ie]])
                    if kt >= 16 * G:
                        d = (kt - 16 * G) % 4
                        T.op("pool", lambda e: e.tensor_tensor(out=SP[ie][:, c0:c0 + 128], in0=SP[ie][:, c0:c0 + 128],
                                                               in1=C["masks"][:, 16 + d, :], op=ALU.mult),
                             reads=[SP_b[ie], C["masks_b"]], writes=[SP_b[ie]])
                    if n + 1 < ns:
                        T.op("pool", lambda e: e.tensor_tensor(out=Accb[(n + 1) % 3][:, c0:512], in0=Accb[n % 3][:, c0:512],
                                                               in1=SP[ie][:, c0:512], op=ALU.add),
                             reads=[Accb_b[n % 3], SP_b[ie]], writes=[Accb_b[(n + 1) % 3]])

                def st_c(n):
                    kt = steps[n]
                    mm_min, c0 = geom(kt)
                    cb, ie = cbanks[n % 2], n % NE
                    T.op("pe", lambda e: e.matmul(R.ps[cb][:, c0:512], lhsT=tincl, rhs=SP[ie][:, c0:512],
                                                  start=True, stop=False),
                         reads=[SP_b[ie], C["masks_b"]], writes=[R.psb[cb]])
                    T.op("pe", lambda e: e.matmul(R.ps[cb][:, c0:512], lhsT=onesb, rhs=Accb[n % 3][:, c0:512],
                                                  start=False, stop=True),
                         reads=[Accb_b[n % 3], C["masks_b"]], writes=[R.psb[cb]])

                def st_c2(n):
                    kt = steps[n]
                    mm_min, c0 = geom(kt)
                    cb = cbanks[n % 2]
                    T.op("act", lambda e: e.activation(out=X[n % 2][:, c0:512], in_=R.ps[cb][:, c0:512], func=AF.Exp,
                                                       scale=-1.0),
                         reads=[R.psb[cb]], writes=[X_b[n % 2]])

                def st_e(n):
                    kt = steps[n]
                    mm_min, c0 = geom(kt)
                    ie = n % NE
                    pi = C["pn"] % len(C["PT"])
                    C["pn"] += 1
                    stt[n] = pi
                    PT, PT_b = C["PT"][pi], C["PT_b"][pi]
                    T.op("dve", lambda e: e.tensor_tensor(out=PT[:, c0:512], in0=E[ie][:, c0:512], in1=X[n % 2][:, c0:512],
                                                          op=ALU.mult),
                         reads=[E_b[ie], X_b[n % 2]], writes=[PT_b])
                    if kt >= 16 * G:
                        d = (kt - 16 * G) % 4
                        T.op("pool", lambda e: e.tensor_tensor(out=PT[:, c0:c0 + 128], in0=PT[:, c0:c0 + 128],
                                                               in1=C["masks"][:, 16 + d, :], op=ALU.mult),
                             reads=[PT_b, C["masks_b"]], writes=[PT_b])

                def st_f(n, first):
                    kt = steps[n]
                    mm_min, c0 = geom(kt)
                    PT, PT_b = C["PT"][stt[n]], C["PT_b"][stt[n]]
                    for mm in range(mm_min, 4):
                        T.op("pe", lambda e: e.matmul(R.ps[ob][:, mm * 64:(mm + 1) * 64],
                                                      lhsT=PT[:, mm * 128:(mm + 1) * 128], rhs=V[s][:, ktile(kt), :],
                                                      start=(first and mm == mm_min), stop=(kt == 0), skip_group_check=True),
                             reads=[PT_b, bufs[s]], writes=[R.psb[ob]])

                for idx in range(ns + 4):
                    if 2 <= idx <= ns + 1:
                        st_c(idx - 2)
                    if idx < ns:
                        st_a(idx)
                    if 1 <= idx <= ns:
                        st_b(idx - 1)
                    if 2 <= idx <= ns + 1:
                        st_c2(idx - 2)
                    if 3 <= idx <= ns + 2:
                        st_e(idx - 3)
                    if idx >= 4:
                        st_f(idx - 4, idx == 4)
                i = C["ev"] % 2
                C["ev"] += 1
                T.op("dve", lambda e: e.tensor_copy(out=C["on"][i][:],
                                                    in_=R.ps[ob][:, 0:256].rearrange("p (m x) -> p m x", x=64)),
                     reads=[R.psb[ob]], writes=[C["on_b"][i]])
                emit_transpose_store(R, C, C["on"][i], C["on_b"][i], G, 6 + h // 2, (h % 2) * 64)
        T.barrier_all()


L = 2
GATHER = ["kT_mla", "kpeT", "v_mla", "kT_cmp", "vT_cmp", "kT_sel", "kT_win", "v_sel", "v_win", "kT_sb", "v_sb"]
LOCAL = ["qT_mla", "qT_nsa", "qT_sb", "gates"]
POUT = {nm: (shp, dt) for nm, shp, dt in P_OUTS}
W_IN = [("ffn1_w_gate", [L, D, DFF]), ("ffn1_w_up", [L, D, DFF]), ("ffn1_w_down", [L, DFF, D]),
        ("ffn2_w_gate", [L, D, DFF]), ("ffn2_w_up", [L, D, DFF]), ("ffn2_w_down", [L, DFF, D]),
        ("w_in", [L, D, DIN]), ("w_in_sw", [L, D, NSW]), ("w_uq", [L, 256, 576]), ("w_uq_sw", [L, 256, 576]),
        ("w_ukv", [L, 128, 768]), ("smallsP", [L, 128, 32]), ("w_out", [L, D, D]),
        ("cmp_w1_k", [L, 2048, 128]), ("cmp_w1_v", [L, 2048, 128]), ("cmp_w2_k", [L, 128, 64]),
        ("cmp_w2_v", [L, 128, 64]), ("cmp_posT_k", [L, 64, 32]), ("cmp_posT_v", [L, 64, 32]),
        ("gains", [128, 3 * L + 1, 8])]
C_IN = [("cosM", [96, NT], F32), ("sinM", [96, NT], F32), ("cosK", [32, NT], F32), ("sinK", [32, NT], F32),
        ("cosN", [128, NT], F32), ("sinN", [128, NT], F32), ("masks", [128, 22, 128], BF16),
        ("ident", [128, 128], F32), ("Ebig", [128, S], BF16), ("ovl", [128, 4, 129], BF16),
        ("keepadd", [128, 2, 256], F32)]


def emit_layer_X(R, A, l, do_post, do_pre, final, xin, xout, mid_hook=None):
    nc, T = R.nc, R.T
    with ExitStack() as px:
        alloc_xT(R, px)
        hT = px.enter_context(nc.sbuf_tensor(U("hT"), [128, 8, NT], BF16))
        hb = [[Buf("h%d_%d" % (k, t)) for t in range(4)] for k in range(8)]
        gam = px.enter_context(nc.sbuf_tensor(U("gam"), [128, 3 * L + 1, 8], F32))
        gam_b = Buf("gam")
        sq = [px.enter_context(nc.sbuf_tensor(U("sq%d" % i), [128, 512], BF16)) for i in range(2)]
        sq_b = [Buf("sq0"), Buf("sq1")]
        rstd = px.enter_context(nc.sbuf_tensor(U("rstd"), [128, 512], F32))
        rstd_b = Buf("rstd")
        ld = T.new_dma_sem("ldx")
        for k in range(8):
            T.dma("sp", ld, R.xT[:, k, :], xin[k * 128:(k + 1) * 128, :], writes=[R.xb[k][t] for t in range(4)])
        T.dma("sp", ld, gam[:], A["gains"], writes=[gam_b])

        def norm(gi):
            emit_norm(R, hT, hb, gam[:, gi, :], gam_b, sq, sq_b, rstd, rstd_b, 6)

        lp = l
        with ExitStack() as pf:
            R.stack = pf
            W = alloc_ffn_work(R)

            def ffn(pref, ll):
                emit_ffn(R, hT, hb, A[pref + "_w_gate"][ll], A[pref + "_w_up"][ll], A[pref + "_w_down"][ll], W)

            if do_post:
                oT2 = pf.enter_context(nc.sbuf_tensor(U("oT2"), [128, 8, NT], BF16))
                o_b = Buf("oT2")
                d1 = T.new_dma_sem("oT2")
                T.dma("sp", d1, oT2[:], A["oT_d"], writes=[o_b])
                for hf in range(2):
                    T.dma("pool", W["dsem"][hf], W["wd"][hf][:],
                          A["w_out"][l][hf * 512:(hf + 1) * 512, :].rearrange("(k p) c -> p k c", p=128),
                          writes=[W["wb"][hf]])
                n = 0
                for tt in range(4):
                    sl = slice(tt * 512, (tt + 1) * 512)
                    for dmc in range(8):
                        bk = n % 4
                        n += 1
                        for k in range(8):
                            T.op("pe", lambda e: e.matmul(R.ps[bk][:], lhsT=W["wd"][k // 4][:, k % 4, dmc * 128:(dmc + 1) * 128],
                                                          rhs=oT2[:, k, sl], start=(k == 0), stop=(k == 7)),
                                 reads=[o_b, W["wb"][k // 4]], writes=[R.psb[bk]])
                        T.op("dve", lambda e: e.tensor_tensor(out=R.xT[:, dmc, sl], in0=R.ps[bk][:], in1=R.xT[:, dmc, sl],
                                                              op=ALU.add),
                             reads=[R.psb[bk], R.xb[dmc][tt]], writes=[R.xb[dmc][tt]])
                norm(3 * l + 2)
                ffn("ffn2", l)
                lp = l + 1
            if do_pre:
                norm(3 * lp + 0)
                ffn("ffn1", lp)
            T.barrier_all()
        if do_pre:
            norm(3 * lp + 1)
            AP_ = dict(A)
            for nm in ("w_in", "w_in_sw", "w_uq", "w_uq_sw", "w_ukv", "smallsP"):
                AP_[nm] = A[nm][lp]
            emit_stage_P(R, hT, hb, AP_, mid_hook=mid_hook)
        st = T.new_dma_sem("stx")
        if final:
            ysq = [px.enter_context(nc.sbuf_tensor(U("ystg%d" % i), [128, 512], F32)) for i in range(2)]
            y_b = [Buf("y0"), Buf("y1")]
            ss, ss_b = R.ps[6], R.psb[6]
            n = 0
            for tt in range(4):
                sl = slice(tt * 512, (tt + 1) * 512)
                for k in range(8):
                    a = k % 2
                    T.op("act", lambda e: e.activation(out=sq[a][:], in_=R.xT[:, k, sl], func=AF.Square),
                         reads=[R.xb[k][tt]], writes=[sq_b[a]])
                    T.op("pe", lambda e: e.matmul(ss[:], lhsT=R.onesm[:], rhs=sq[a][:], start=(k == 0), stop=(k == 7)),
                         reads=[sq_b[a], R.onesm_b], writes=[ss_b])
                T.op("act", lambda e: e.activation(out=rstd[:], in_=ss[:], func=AF.Sqrt, bias=EPS, scale=1.0 / D),
                     reads=[ss_b], writes=[rstd_b])
                T.op("dve", lambda e: e.reciprocal(out=rstd[:], in_=rstd[:]), reads=[rstd_b], writes=[rstd_b])
                for k in range(8):
                    i = n % 2
                    n += 1
                    T.op("dve", lambda e: e.scalar_tensor_tensor(out=ysq[i][:], in0=R.xT[:, k, sl],
                                                                 scalar=gam[:, 3 * L, k:k + 1], in1=rstd[:],
                                                                 op0=ALU.mult, op1=ALU.mult),
                         reads=[R.xb[k][tt], rstd_b, gam_b], writes=[y_b[i]])
                    T.dma("sp", st, xout[k * 128:(k + 1) * 128, sl], ysq[i][:], reads=[y_b[i]])
        else:
            for k in range(8):
                T.dma("sp", st, xout[k * 128:(k + 1) * 128, :], R.xT[:, k, :], reads=[R.xb[k][t] for t in range(4)])
        T.barrier_all()
        return st


def emit_layer_A(R, A, l):
    nc, T = R.nc, R.T
    with ExitStack() as pa:
        R.oT = pa.enter_context(nc.sbuf_tensor(U("oT"), [128, 8, NT], BF16))
        R.oT_b = [[Buf("oT%d_%d" % (c, g)) for g in range(4)] for c in range(8)]
        AL = dict(A)
        for nm in ("cmp_w1_k", "cmp_w1_v", "cmp_w2_k", "cmp_w2_v", "cmp_posT_k", "cmp_posT_v"):
            AL[nm] = A[nm][l]
        with ExitStack() as ph:
            C = alloc_attn_common(R, AL, ph)
            gen = emit_nsa_compress(R, C, AL)
            next(gen)
            emit_mla(R, C, AL, hook=gen)
            for _ in gen:
                pass
            emit_nsa(R, C, AL)
            emit_sb(R, C, AL)
        st = T.new_dma_sem("stoT")
        T.dma("sp", st, A["oT_d"], R.oT[:], reads=[b for ll in R.oT_b for b in ll])
        T.barrier_all()


def build_launch(kind, l):
    nc = bass.Bass("TRN2", target_bir_lowering=False)
    A = {}

    def din(nm, shp, dt=F32):
        A[nm] = nc.dram_tensor(nm, shp, dt, kind="ExternalInput").ap()

    def dout(nm, shp, dt=F32):
        A[nm] = nc.dram_tensor(nm, shp, dt, kind="ExternalOutput").ap()
    for nm, shp in W_IN:
        din(nm, shp)
    for nm, shp, dt in C_IN:
        din(nm, shp, dt)
    din("xT_in", [D, NT])
    dout("xT_out", [D, NT])
    if kind != "first":
        for nm in GATHER:
            shp, dt = POUT[nm]
            din(nm + "_g", [4] + shp, dt)
        for nm in LOCAL:
            shp, dt = POUT[nm]
            din(nm, shp, dt)
        A["oT_d"] = nc.dram_tensor("oT_d", [128, 8, NT], BF16, kind="Internal").ap()
    if kind != "last":
        for nm, shp, dt in P_OUTS:
            if kind == "first" or nm not in LOCAL:
                dout(nm, shp, dt)
            else:
                A[nm + "_o"] = nc.dram_tensor(nm + "_o", shp, dt, kind="ExternalOutput").ap()
    with ExitStack() as stack:
        T = Tracker(nc, stack)
        R = setup_common(nc, stack, T)
        if kind != "first":
            emit_layer_A(R, A, l)
        AX = dict(A)
        if kind == "mid":
            for nm in LOCAL:
                AX[nm] = A[nm + "_o"]
        st = emit_layer_X(R, AX, l, do_post=(kind != "first"), do_pre=(kind != "last"), final=(kind == "last"),
                          xin=A["xT_in"], xout=A["xT_out"])
        nc.sync.wait_ge(T.sem[st], T.cnt[st])
    return nc


def _host_weights(inp):
    f = lambda a: np.ascontiguousarray(np.asarray(a, dtype=np.float32))
    W = {}
    for nm in ("ffn1_w_gate", "ffn1_w_up", "ffn1_w_down", "ffn2_w_gate", "ffn2_w_up", "ffn2_w_down", "w_in", "w_out"):
        W[nm] = f(inp[nm])
    W["w_uq"] = f(inp["mla_w_uq"])
    W["w_ukv"] = f(inp["mla_w_ukv"])
    W["w_in_sw"] = np.stack([make_w_in_sw(W["w_in"][l]) for l in range(L)])
    W["w_uq_sw"] = np.stack([swap_cols_rope(W["w_uq"][l], 96, 64, 32) for l in range(L)])
    W["smallsP"] = np.stack([make_smallsP(f(inp["mla_q_norm"])[l], f(inp["mla_kv_norm"])[l], f(inp["nsa_gate_bias"])[l])
                             for l in range(L)])
    W["cmp_w1_k"] = f(inp["nsa_cmp_w1_k"])
    W["cmp_w1_v"] = f(inp["nsa_cmp_w1_v"])
    W["cmp_w2_k"] = f(inp["nsa_cmp_w2_k"])
    W["cmp_w2_v"] = f(inp["nsa_cmp_w2_v"])
    W["cmp_posT_k"] = np.ascontiguousarray(f(inp["nsa_cmp_pos_k"]).transpose(0, 2, 1))
    W["cmp_posT_v"] = np.ascontiguousarray(f(inp["nsa_cmp_pos_v"]).transpose(0, 2, 1))
    g = np.zeros((128, 3 * L + 1, 8), np.float32)
    for l in range(L):
        for i, nm in enumerate(("ffn1_norm", "mix_norm", "ffn2_norm")):
            g[:, 3 * l + i, :] = f(inp[nm])[l].reshape(8, 128).T
    g[:, 3 * L, :] = f(inp["final_norm"]).reshape(8, 128).T
    W["gains"] = g
    return W


def _core_consts(core):
    c = {}
    c.update(rope_tables(core))
    c["masks"] = make_masks(core)
    c["ident"] = np.eye(128, dtype=np.float32)
    c.update(make_nsa_consts(core))
    return c


def _kernel_unfused_impl(**inp):
    x = np.asarray(inp["x"], dtype=np.float32)
    W = _host_weights(inp)
    consts = [_core_consts(c) for c in range(8)]
    xT = [np.ascontiguousarray(x[c // 4][own_positions(c)].T) for c in range(8)]
    cores = list(range(8))
    nc = build_launch("first", 0)
    ims = [dict(W, **consts[c], xT_in=xT[c]) for c in cores]
    res = run_bass_kernel_spmd(nc, ims, core_ids=cores).results
    for l in range(L):
        kind = "mid" if l < L - 1 else "last"
        nc = build_launch(kind, l)
        ims = []
        for c in cores:
            b = c // 4
            im = dict(W, **consts[c], xT_in=np.asarray(res[c]["xT_out"]))
            for nm in GATHER:
                im[nm + "_g"] = np.stack([np.asarray(res[4 * b + r][nm]) for r in range(4)])
            for nm in LOCAL:
                key = nm if l == 0 else nm + "_o"
                im[nm] = np.asarray(res[c][key])
            ims.append(im)
        res = run_bass_kernel_spmd(nc, ims, core_ids=cores).results
    out = np.zeros((B, S, D), np.float32)
    for c in cores:
        out[c // 4][own_positions(c)] = np.asarray(res[c]["xT_out"]).T
    return out


PIECES = [
    ([192, NT], [("kT_mla", "heads", 0, 3)]),
    ([192, NT], [("kT_mla", "heads", 3, 6)]),
    ([32, NT], [("kpeT", "rows", 0, 32)]),
    ([256, NT], [("vT_cmp", "rows", 0, 128), ("kT_sel", "rows", 128, 256)]),
    ([256, NT], [("kT_cmp", "rows", 0, 128), ("kT_win", "rows", 128, 256)]),
    ([256, NT], [("kT_sb", "rows", 0, 256)]),
    ([1024, 6, 65], [("v_mla", "half", 0, 0)]),
    ([1024, 6, 65], [("v_mla", "half", 1, 1)]),
    ([NT, 2, 65], [("v_sel", "all", 0, 0)]),
    ([NT, 2, 65], [("v_win", "all", 0, 0)]),
    ([NT, 256], [("v_sb", "all", 0, 0)]),
]


def make_pieces(nc, l):
    V = {"kT_mla": [None] * 6, "kT_mla_g": [None] * 6, "v_mla": [None] * 2, "v_mla_g": [None] * 2}
    GB = {"kT_mla": [None] * 6, "v_mla": [None] * 2}
    cc = []
    for k, (shp, members) in enumerate(PIECES):
        n = int(np.prod(shp))
        w = n // 128
        gs = nc.dram_tensor("gs%d_%d" % (l, k), [128, w], BF16, kind="Internal").ap()
        gd = nc.dram_tensor("gd%d_%d" % (l, k), [512, w], BF16, kind="Internal").ap()
        pb = Buf("piece%d_%d" % (l, k))
        cc.append((gs, gd, pb))
        fs = gs.rearrange("p w -> (p w)")
        fd = gd.rearrange("(r p) w -> r (p w)", r=4)
        if len(shp) == 2:
            ns = fs.rearrange("(a c) -> a c", c=shp[1])
            nd = fd.rearrange("r (a c) -> r a c", c=shp[1])
        else:
            ns = fs.rearrange("(t h x) -> t h x", h=shp[1], x=shp[2])
            nd = fd.rearrange("r (t h x) -> r t h x", h=shp[1], x=shp[2])
        for nm, kind, lo, hi in members:
            if kind == "heads":
                for h in range(lo, hi):
                    V[nm][h] = ns[(h - lo) * 64:(h - lo + 1) * 64, :]
                    V[nm + "_g"][h] = nd[:, (h - lo) * 64:(h - lo + 1) * 64, :]
                    GB[nm][h] = pb
            elif kind == "rows":
                V[nm] = ns[lo:hi, :]
                V[nm + "_g"] = nd[:, lo:hi, :]
                GB[nm] = pb
            elif kind == "half":
                V[nm][lo] = ns
                V[nm + "_g"][lo] = nd
                GB[nm][lo] = pb
            else:
                V[nm] = ns
                V[nm + "_g"] = nd
                GB[nm] = pb
    V["cc"] = cc
    V["gb"] = GB
    return V


def build_fused():
    nc = bass.Bass("TRN2", target_bir_lowering=False)
    A = {}
    for nm, shp in W_IN:
        A[nm] = nc.dram_tensor(nm, shp, F32, kind="ExternalInput").ap()
    for nm, shp, dt in C_IN:
        A[nm] = nc.dram_tensor(nm, shp, dt, kind="ExternalInput").ap()
    A["xT_in"] = nc.dram_tensor("xT_in", [D, NT], F32, kind="ExternalInput").ap()
    A["yT_out"] = nc.dram_tensor("yT_out", [D, NT], F32, kind="ExternalOutput").ap()
    A["xT_d"] = nc.dram_tensor("xT_d", [D, NT], F32, kind="Internal").ap()
    A["oT_d"] = nc.dram_tensor("oT_d", [128, 8, NT], BF16, kind="Internal").ap()
    LA = []
    for l in range(L):
        V = make_pieces(nc, l)
        for nm in LOCAL:
            shp, dt = POUT[nm]
            V[nm] = nc.dram_tensor("%s_l%d" % (nm, l), shp, dt, kind="Internal").ap()
        LA.append(V)
    with ExitStack() as stack:
        T = Tracker(nc, stack)
        R = setup_common(nc, stack, T)
        st = None
        GRP = [[0, 1, 2, 3], [4, 5, 6, 7]]

        def mk_hook(ll):
            def hook():
                T.wait_dma_all("pool")
                for k in (2, 0, 6, 7, 1):
                    gs, gd, pb = LA[ll]["cc"][k]
                    T.collective(T.new_dma_sem("cc%d_%d" % (ll, k)), gs, gd, GRP, writes=[pb])
            return hook

        for l in range(L):
            if l == 0:
                emit_layer_X(R, dict(A, **LA[0]), 0, do_post=False, do_pre=True, final=False,
                             xin=A["xT_in"], xout=A["xT_d"], mid_hook=mk_hook(0))
            T.barrier_all()
            for k in (4, 3, 8, 9, 5, 10):
                gs, gd, pb = LA[l]["cc"][k]
                T.collective(T.new_dma_sem("cc%d_%d" % (l, k)), gs, gd, GRP, writes=[pb])
            emit_layer_A(R, dict(A, **LA[l]), l)
            last = (l == L - 1)
            AX = dict(A, **(LA[l + 1] if not last else {}))
            st = emit_layer_X(R, AX, l, do_post=True, do_pre=not last, final=last,
                              xin=A["xT_d"], xout=(A["yT_out"] if last else A["xT_d"]),
                              mid_hook=(None if last else mk_hook(l + 1)))
        nc.sync.wait_ge(T.sem[st], T.cnt[st])
    return nc


def kernel_unfused(**inp):
    return _kernel_unfused_impl(**inp)


def kernel_fused(**inp):
    x = np.asarray(inp["x"], dtype=np.float32)
    W = _host_weights(inp)
    cores = list(range(8))
    nc = build_fused()
    ims = []
    for c in cores:
        xT = np.ascontiguousarray(x[c // 4][own_positions(c)].T)
        ims.append(dict(W, **_core_consts(c), xT_in=xT))
    res = run_bass_kernel_spmd(nc, ims, core_ids=cores).results
    out = np.zeros((B, S, D), np.float32)
    for c in cores:
        out[c // 4][own_positions(c)] = np.asarray(res[c]["yT_out"]).T
    return out


def kernel(**inp):
    return kernel_fused(**inp)
```

```python
import numpy as np
import ml_dtypes
from contextlib import ExitStack
import concourse.bass as bass
import concourse.mybir as mybir
from concourse.bass_utils import run_bass_kernel_spmd

F32 = mybir.dt.float32
BF16 = mybir.dt.bfloat16
AF = mybir.ActivationFunctionType
ALU = mybir.AluOpType
AX = mybir.AxisListType

D = 1024
S = 8192
B = 2
DFF = 2816
NT = 2048
NBLK = 16
EPS = 1e-6
DIN = 2354


_UN = [0]
CC_INC = 1


def U(name):
    _UN[0] += 1
    return "%s_u%d" % (name, _UN[0])


class Buf:
    __slots__ = ("name", "w", "r")

    def __init__(self, name):
        self.name = name
        self.w = None
        self.r = {}


class Tracker:
    def __init__(self, nc, stack):
        self.nc = nc
        self.stack = stack
        self.eng = {"pe": nc.tensor, "act": nc.scalar, "dve": nc.vector,
                    "pool": nc.gpsimd, "sp": nc.sync}
        self.sem = {}
        self.cnt = {}
        self.seen = {k: {} for k in self.eng}
        for k in self.eng:
            self.sem[k] = stack.enter_context(nc.semaphore("s_" + k))
            self.cnt[k] = 0
        self.ndma = 0

    def new_dma_sem(self, name):
        key = "dma_" + name + "_%d" % self.ndma
        self.ndma += 1
        self.sem[key] = self.stack.enter_context(self.nc.semaphore(key))
        self.cnt[key] = 0
        return key

    def _deps(self, e, reads, writes, ignore=None):
        deps = {}

        def add(k, c):
            if c > deps.get(k, 0):
                deps[k] = c
        for b in reads:
            if b.w is not None:
                add(*b.w)
        for b in writes:
            if b.w is not None:
                add(*b.w)
            for k, c in b.r.items():
                add(k, c)
        for k, c in deps.items():
            if (k == "pe" and e == "pe") or k == ignore:
                continue
            if k.startswith("dma_"):
                c = max(c, self.cnt[k])
            if c > self.seen[e].get(k, 0):
                self.eng[e].wait_ge(self.sem[k], c)
                self.seen[e][k] = c

    def op(self, e, fn, reads=(), writes=()):
        self._deps(e, reads, writes)
        ins = fn(self.eng[e])
        self.cnt[e] += 1
        c = self.cnt[e]
        ins.then_inc(self.sem[e], 1)
        for b in reads:
            if c > b.r.get(e, 0):
                b.r[e] = c
        for b in writes:
            b.w = (e, c)
            b.r = {}
        return ins

    def dma(self, q, dsem, out, in_, reads=(), writes=()):
        self._deps(q, reads, writes, ignore=dsem)
        ins = self.eng[q].dma_start(out=out, in_=in_)
        self.cnt[dsem] += 16
        c = self.cnt[dsem]
        ins.then_inc(self.sem[dsem], 16)
        for b in reads:
            if c > b.r.get(dsem, 0):
                b.r[dsem] = c
        for b in writes:
            b.w = (dsem, c)
            b.r = {}
        return ins

    def collective(self, dsem, src, dst, groups, reads=(), writes=()):
        self._deps("pool", reads, writes, ignore=dsem)
        ins = self.nc.gpsimd.collective_compute("AllGather", ALU.bypass, replica_groups=groups,
                                                ins=[src.opt()], outs=[dst.opt()])
        self.cnt[dsem] += CC_INC
        c = self.cnt[dsem]
        ins.then_inc(self.sem[dsem], CC_INC)
        for b in reads:
            if c > b.r.get(dsem, 0):
                b.r[dsem] = c
        for b in writes:
            b.w = (dsem, c)
            b.r = {}
        return ins

    def wait_dma_all(self, e):
        for k, c in self.cnt.items():
            if k.startswith("dma_") and c > self.seen[e].get(k, 0):
                self.eng[e].wait_ge(self.sem[k], c)
                self.seen[e][k] = c

    def barrier_all(self):
        for e in self.eng:
            for k, c in self.cnt.items():
                if k == e or c == 0:
                    continue
                if c > self.seen[e].get(k, 0):
                    self.eng[e].wait_ge(self.sem[k], c)
                    self.seen[e][k] = c


class Res:
    pass


def setup_common(nc, stack, T):
    R = Res()
    R.nc, R.T, R.stack = nc, T, stack
    R.ExitStack = ExitStack
    R.ps = []
    R.psb = []
    for i in range(8):
        R.ps.append(stack.enter_context(nc.psum_tensor("ps%d" % i, [128, 512], F32)))
        R.psb.append(Buf("ps%d" % i))
    R.xb = [[Buf("x%d_%d" % (k, t)) for t in range(4)] for k in range(8)]
    R.onesm = stack.enter_context(nc.sbuf_tensor(U("onesm"), [128, 128], BF16))
    R.onesm_b = Buf("onesm")
    T.op("pool", lambda e: e.memset(R.onesm[:], 1.0), writes=[R.onesm_b])
    return R


def alloc_xT(R, stack):
    R.xT = stack.enter_context(R.nc.sbuf_tensor(U("xT"), [128, 8, NT], F32))


def emit_norm(R, hT, hb, gam, gam_b, sq, sq_b, rstd, rstd_b, ssbank):
    T = R.T
    ss, ss_b = R.ps[ssbank], R.psb[ssbank]
    for tt in range(4):
        sl = slice(tt * 512, (tt + 1) * 512)
        for k in range(8):
            a = k % 2
            T.op("act", lambda e: e.activation(out=sq[a][:], in_=R.xT[:, k, sl], func=AF.Square),
                 reads=[R.xb[k][tt]], writes=[sq_b[a]])
            T.op("pe", lambda e: e.matmul(ss[:], lhsT=R.onesm[:], rhs=sq[a][:], start=(k == 0), stop=(k == 7)),
                 reads=[sq_b[a], R.onesm_b], writes=[ss_b])
        T.op("act", lambda e: e.activation(out=rstd[:], in_=ss[:], func=AF.Sqrt, bias=EPS, scale=1.0 / D),
             reads=[ss_b], writes=[rstd_b])
        T.op("dve", lambda e: e.reciprocal(out=rstd[:], in_=rstd[:]), reads=[rstd_b], writes=[rstd_b])
        for k in range(8):
            T.op("dve", lambda e: e.scalar_tensor_tensor(out=hT[:, k, sl], in0=R.xT[:, k, sl],
                                                         scalar=gam[:, k:k + 1], in1=rstd[:],
                                                         op0=ALU.mult, op1=ALU.mult),
                 reads=[R.xb[k][tt], rstd_b, gam_b], writes=[hb[k][tt]])


def emit_ffn(R, hT, hb, wg_d, wu_d, wd_d, W):
    T = R.T
    nfg = 6
    n_g = n_y = 0
    for fg in range(nfg):
        ncf = 4 if fg < 5 else 2
        wcols = ncf * 128
        c0 = fg * 512
        s = fg % 2
        wgs, wus, wds, wb, dsem = W["wg"][s], W["wu"][s], W["wd"][s], W["wb"][s], W["dsem"][s]
        T.dma("pool", dsem, wgs[:, :, 0:wcols],
              wg_d[:, c0:c0 + wcols].rearrange("(k p) c -> p k c", p=128), writes=[wb])
        T.dma("pool", dsem, wus[:, :, 0:wcols],
              wu_d[:, c0:c0 + wcols].rearrange("(k p) c -> p k c", p=128), writes=[wb])
        T.dma("pool", dsem, wds[:, 0:ncf, :],
              wd_d[c0:c0 + wcols, :].rearrange("(c p) m -> p c m", p=128), writes=[wb])
        for tt in range(4):
            sl = slice(tt * 512, (tt + 1) * 512)
            asl = (fg * 4 + tt) % 2
            for c in range(ncf):
                gi, ui = W["gbanks"][n_g % 2], W["ubanks"][n_g % 2]
                sgi = n_g % 2
                n_g += 1
                for k in range(8):
                    T.op("pe", lambda e: e.matmul(R.ps[gi][:], lhsT=wgs[:, k, c * 128:(c + 1) * 128],
                                                  rhs=hT[:, k, sl], start=(k == 0), stop=(k == 7)),
                         reads=[wb, hb[k][tt]], writes=[R.psb[gi]])
                for k in range(8):
                    T.op("pe", lambda e: e.matmul(R.ps[ui][:], lhsT=wus[:, k, c * 128:(c + 1) * 128],
                                                  rhs=hT[:, k, sl], start=(k == 0), stop=(k == 7)),
                         reads=[wb, hb[k][tt]], writes=[R.psb[ui]])
                T.op("act", lambda e: e.activation(out=W["sg"][sgi][:], in_=R.ps[gi][:], func=AF.Silu),
                     reads=[R.psb[gi]], writes=[W["sg_b"][sgi]])
                T.op("dve", lambda e: e.tensor_tensor(out=W["act"][asl][:, c, :], in0=W["sg"][sgi][:],
                                                      in1=R.ps[ui][:], op=ALU.mult),
                     reads=[W["sg_b"][sgi], R.psb[ui]], writes=[W["act_b"][asl][c]])
            for dmc in range(8):
                yi = W["ybanks"][n_y % 2]
                n_y += 1
                for c in range(ncf):
                    T.op("pe", lambda e: e.matmul(R.ps[yi][:], lhsT=wds[:, c, dmc * 128:(dmc + 1) * 128],
                                                  rhs=W["act"][asl][:, c, :], start=(c == 0), stop=(c == ncf - 1)),
                         reads=[wb, W["act_b"][asl][c]], writes=[R.psb[yi]])
                T.op("dve", lambda e: e.scalar_tensor_tensor(out=R.xT[:, dmc, sl], in0=R.ps[yi][:], scalar=0.5,
                                                             in1=R.xT[:, dmc, sl], op0=ALU.mult, op1=ALU.add),
                     reads=[R.psb[yi], R.xb[dmc][tt]], writes=[R.xb[dmc][tt]])


def alloc_ffn_work(R):
    nc, stack, T = R.nc, R.stack, R.T
    W = {}
    W["wg"] = [stack.enter_context(nc.sbuf_tensor(U("wg%d" % i), [128, 8, 512], BF16)) for i in range(2)]
    W["wu"] = [stack.enter_context(nc.sbuf_tensor(U("wu%d" % i), [128, 8, 512], BF16)) for i in range(2)]
    W["wd"] = [stack.enter_context(nc.sbuf_tensor(U("wd%d" % i), [128, 4, 1024], BF16)) for i in range(2)]
    W["wb"] = [Buf("wslot%d" % i) for i in range(2)]
    W["dsem"] = [T.new_dma_sem("ffnw%d" % i) for i in range(2)]
    W["sg"] = [stack.enter_context(nc.sbuf_tensor(U("sg%d" % i), [128, 512], F32)) for i in range(2)]
    W["sg_b"] = [Buf("sg%d" % i) for i in range(2)]
    W["act"] = [stack.enter_context(nc.sbuf_tensor(U("act%d" % i), [128, 4, 512], BF16)) for i in range(2)]
    W["act_b"] = [[Buf("act%d_%d" % (i, c)) for c in range(4)] for i in range(2)]
    W["gbanks"], W["ubanks"], W["ybanks"] = [0, 1], [2, 3], [4, 5]
    return W


C_CQ, C_CKV, C_KR, C_NQ, C_NKC, C_NVC, C_NKS, C_NVS, C_NKW, C_NVW, C_NG, C_SQ, C_SK, C_SV = (
    0, 256, 384, 416, 800, 928, 1056, 1184, 1312, 1440, 1568, 1586, 1842, 2098)
SW_KR, SW_NQ, SW_NKC, SW_NKS, SW_NKW = 0, 32, 416, 544, 672
NSW = 800


def emit_stage_P(R, hT, hb, A, mid_hook=None):
    nc, T = R.nc, R.T
    with R.ExitStack() as ph:
        def sb(name, shape, dt):
            return ph.enter_context(nc.sbuf_tensor(U(name), shape, dt))
        win = sb("P_win", [128, 8, DIN], BF16)
        wsw = sb("P_wsw", [128, 8, NSW], BF16)
        wuq = sb("P_wuq", [128, 2, 576], BF16)
        wuqs = sb("P_wuqs", [128, 2, 576], BF16)
        wukv = sb("P_wukv", [128, 768], BF16)
        sm = sb("P_sm", [128, 32], F32)
        wb_, smb = Buf("P_w"), Buf("P_sm")
        dw = T.new_dma_sem("Pw")
        T.dma("pool", dw, win[:], A["w_in"].rearrange("(k p) c -> p k c", p=128), writes=[wb_])
        T.dma("pool", dw, wsw[:], A["w_in_sw"].rearrange("(k p) c -> p k c", p=128), writes=[wb_])
        T.dma("pool", dw, wuq[:], A["w_uq"].rearrange("(k p) c -> p k c", p=128), writes=[wb_])
        T.dma("pool", dw, wuqs[:], A["w_uq_sw"].rearrange("(k p) c -> p k c", p=128), writes=[wb_])
        T.dma("pool", dw, wukv[:], A["w_ukv"], writes=[wb_])
        dw2 = T.new_dma_sem("Psm")
        T.dma("sp", dw2, sm[:], A["smallsP"], writes=[smb])
        tabs = []
        for i in range(2):
            tabs.append(dict(
                cosM=sb("P_cosM%d" % i, [96, 512], F32), sinM=sb("P_sinM%d" % i, [96, 512], F32),
                cosK=sb("P_cosK%d" % i, [32, 512], F32), sinK=sb("P_sinK%d" % i, [32, 512], F32),
                cosN=sb("P_cosN%d" % i, [128, 512], F32), sinN=sb("P_sinN%d" % i, [128, 512], F32),
                b=Buf("P_tab%d" % i), sem=T.new_dma_sem("Ptab%d" % i)))
        sq = [sb("P_sq%d" % i, [128, 512], BF16) for i in range(2)]
        sq_b = [Buf("P_sq0"), Buf("P_sq1")]
        rs = sb("P_rs", [128, 512], F32)
        rs_b = Buf("P_rs")
        cqn = sb("P_cqn", [128, 2, 512], BF16)
        cqn_b = Buf("P_cqn")
        ckvn = sb("P_ckvn", [128, 512], BF16)
        ckvn_b = Buf("P_ckvn")
        t1 = [sb("P_t1_%d" % i, [128, 512], F32) for i in range(2)]
        t2 = [sb("P_t2_%d" % i, [128, 512], F32) for i in range(2)]
        t_b = [Buf("P_t0"), Buf("P_t1")]
        NST = 4
        stg = [sb("P_stg%d" % i, [128, 512], BF16) for i in range(NST)]
        stg_b = [Buf("P_stg%d" % i) for i in range(NST)]
        stg_sem = [T.new_dma_sem("Pstg%d" % i) for i in range(NST)]
        vst = [sb("P_vst%d" % i, [128, 6, 65], BF16) for i in range(2)]
        vst_b = [Buf("P_vst0"), Buf("P_vst1")]
        vst_sem = [T.new_dma_sem("Pvst%d" % i) for i in range(2)]
        v2st = [sb("P_v2st%d" % i, [128, 2, 2, 65], BF16) for i in range(2)]
        v2st_b = [Buf("P_v2st0"), Buf("P_v2st1")]
        v2st_sem = [T.new_dma_sem("Pv2st%d" % i) for i in range(2)]
        vsb = [sb("P_vsb%d" % i, [128, 256], BF16) for i in range(2)]
        vsb_b = [Buf("P_vsb0"), Buf("P_vsb1")]
        vsb_sem = [T.new_dma_sem("Pvsb%d" % i) for i in range(2)]
        gst = [sb("P_gst%d" % i, [128, 18], F32) for i in range(2)]
        gst_b = [Buf("P_gst0"), Buf("P_gst1")]
        gst_sem = [T.new_dma_sem("Pgst%d" % i) for i in range(2)]
        for i in range(2):
            T.op("pool", lambda e: e.memset(vst[i][:], 1.0), writes=[vst_b[i]])
            T.op("pool", lambda e: e.memset(v2st[i][:], 1.0), writes=[v2st_b[i]])

        cnt = {"bank": 0, "stg": 0, "t": 0}

        def nbank():
            cnt["bank"] += 1
            return cnt["bank"] % 4

        def chain(bank, M, lhs_fn, rhs_fn, nk, reads, N=512, rows=None):
            o = R.ps[bank][0:M, 0:N] if rows is None else R.ps[bank][rows[0]:rows[1], 0:N]
            for k in range(nk):
                T.op("pe", lambda e: e.matmul(o, lhsT=lhs_fn(k), rhs=rhs_fn(k), start=(k == 0), stop=(k == nk - 1)),
                     reads=reads(k), writes=[R.psb[bank]])

        def store(src_ps_ap, src_bufs, M, dst_dram, use_act=False):
            i = cnt["stg"] % NST
            cnt["stg"] += 1
            eng = "act" if use_act else "dve"
            if use_act:
                T.op("act", lambda e: e.activation(out=stg[i][0:M, :], in_=src_ps_ap, func=AF.Copy),
                     reads=src_bufs, writes=[stg_b[i]])
            else:
                T.op("dve", lambda e: e.tensor_copy(out=stg[i][0:M, :], in_=src_ps_ap),
                     reads=src_bufs, writes=[stg_b[i]])
            T.dma("sp", stg_sem[i], dst_dram, stg[i][0:M, :], reads=[stg_b[i]])

        def rope_store(bankA, bankB, M, cos, sin, tb, dst_dram):
            j = cnt["t"] % 2
            cnt["t"] += 1
            i = cnt["stg"] % NST
            cnt["stg"] += 1
            T.op("dve", lambda e: e.tensor_tensor(out=t1[j][0:M, :], in0=R.ps[bankA][0:M, :], in1=cos, op=ALU.mult),
                 reads=[R.psb[bankA], tb], writes=[t_b[j]])
            T.op("dve", lambda e: e.tensor_tensor(out=t2[j][0:M, :], in0=R.ps[bankB][0:M, :], in1=sin, op=ALU.mult),
                 reads=[R.psb[bankB], tb], writes=[t_b[j]])
            T.op("pool", lambda e: e.tensor_tensor(out=stg[i][0:M, :], in0=t1[j][0:M, :], in1=t2[j][0:M, :], op=ALU.add),
                 reads=[t_b[j]], writes=[stg_b[i]])
            T.dma("sp", stg_sem[i], dst_dram, stg[i][0:M, :], reads=[stg_b[i]])

        for tt in range(4):
            sl = slice(tt * 512, (tt + 1) * 512)
            tab = tabs[tt % 2]
            for nm in ("cosM", "sinM", "cosK", "sinK"):
                T.dma("sp", tab["sem"], tab[nm][:], A[nm][:, sl], writes=[tab["b"]])
            hreads = lambda k: [wb_, hb[k][tt]]
            for c in range(2):
                chain(4 + c, 128, lambda k: win[:, k, C_CQ + c * 128:C_CQ + (c + 1) * 128],
                      lambda k: hT[:, k, sl], 8, hreads)
            chain(6, 128, lambda k: win[:, k, C_CKV:C_CKV + 128], lambda k: hT[:, k, sl], 8, hreads)
            for c in range(2):
                T.op("act", lambda e: e.activation(out=sq[c][:], in_=R.ps[4 + c][:], func=AF.Square),
                     reads=[R.psb[4 + c]], writes=[sq_b[c]])
            for c in range(2):
                T.op("pe", lambda e: e.matmul(R.ps[7][:], lhsT=R.onesm[:], rhs=sq[c][:], start=(c == 0), stop=(c == 1)),
                     reads=[sq_b[c], R.onesm_b], writes=[R.psb[7]])
            T.op("act", lambda e: e.activation(out=rs[:], in_=R.ps[7][:], func=AF.Sqrt, bias=EPS, scale=1.0 / 256),
                 reads=[R.psb[7]], writes=[rs_b])
            T.op("dve", lambda e: e.reciprocal(out=rs[:], in_=rs[:]), reads=[rs_b], writes=[rs_b])
            for c in range(2):
                T.op("dve", lambda e: e.scalar_tensor_tensor(out=cqn[:, c, :], in0=R.ps[4 + c][:], scalar=sm[:, c:c + 1],
                                                             in1=rs[:], op0=ALU.mult, op1=ALU.mult),
                     reads=[R.psb[4 + c], rs_b, smb], writes=[cqn_b])
            T.op("act", lambda e: e.activation(out=sq[0][:], in_=R.ps[6][:], func=AF.Square),
                 reads=[R.psb[6]], writes=[sq_b[0]])
            T.op("pe", lambda e: e.matmul(R.ps[7][:], lhsT=R.onesm[:], rhs=sq[0][:], start=True, stop=True),
                 reads=[sq_b[0], R.onesm_b], writes=[R.psb[7]])
            T.op("act", lambda e: e.activation(out=rs[:], in_=R.ps[7][:], func=AF.Sqrt, bias=EPS, scale=1.0 / 128),
                 reads=[R.psb[7]], writes=[rs_b])
            T.op("dve", lambda e: e.reciprocal(out=rs[:], in_=rs[:]), reads=[rs_b], writes=[rs_b])
            T.op("dve", lambda e: e.scalar_tensor_tensor(out=ckvn[:], in0=R.ps[6][:], scalar=sm[:, 2:3],
                                                         in1=rs[:], op0=ALU.mult, op1=ALU.mult),
                 reads=[R.psb[6], rs_b, smb], writes=[ckvn_b])
            for h in range(6):
                ba, bb = nbank(), 4 + (h % 2)
                chain(ba, 96, lambda c: wuq[:, c, h * 96:(h + 1) * 96], lambda c: cqn[:, c, :], 2,
                      lambda c: [wb_, cqn_b])
                chain(bb, 96, lambda c: wuqs[:, c, h * 96:(h + 1) * 96], lambda c: cqn[:, c, :], 2,
                      lambda c: [wb_, cqn_b])
                rope_store(ba, bb, 96, tab["cosM"][:], tab["sinM"][:], tab["b"], A["qT_mla"][h, :, sl])
            for h in range(6):
                ba = nbank()
                chain(ba, 64, lambda c: wukv[:, h * 128:h * 128 + 64], lambda c: ckvn[:], 1, lambda c: [wb_, ckvn_b])
                store(R.ps[ba][0:64, :], [R.psb[ba]], 64, A["kT_mla"][h][:, sl], use_act=(h % 2 == 0))
            ba, bb = nbank(), 4
            chain(ba, 32, lambda k: win[:, k, C_KR:C_KR + 32], lambda k: hT[:, k, sl], 8, hreads)
            chain(bb, 32, lambda k: wsw[:, k, SW_KR:SW_KR + 32], lambda k: hT[:, k, sl], 8, hreads)
            rope_store(ba, bb, 32, tab["cosK"][:], tab["sinK"][:], tab["b"], A["kpeT"][:, sl])
            for bl in range(4):
                tb = tt * 4 + bl
                lsl = slice(bl * 128, (bl + 1) * 128)
                i = tb % 2
                ba = nbank()
                T.op("pe", lambda e: e.matmul(R.ps[ba][:, 0:384], lhsT=ckvn[:, lsl],
                                              rhs=wukv[:].rearrange("p (h x) -> p h x", x=128)[:, :, 64:128],
                                              start=True, stop=True),
                     reads=[wb_, ckvn_b], writes=[R.psb[ba]])
                T.op("dve", lambda e: e.tensor_copy(out=vst[i][:, :, 0:64],
                                                    in_=R.ps[ba][:, 0:384].rearrange("p (h x) -> p h x", x=64)),
                     reads=[R.psb[ba]], writes=[vst_b[i]])
                T.dma("sp", vst_sem[i], A["v_mla"][tb // 8][(tb % 8) * 128:(tb % 8 + 1) * 128, :, :], vst[i][:], reads=[vst_b[i]])
        if mid_hook is not None:
            mid_hook()
        for tt in range(4):
            sl = slice(tt * 512, (tt + 1) * 512)
            tab = tabs[tt % 2]
            for nm in ("cosN", "sinN"):
                T.dma("sp", tab["sem"], tab[nm][:], A[nm][:, sl], writes=[tab["b"]])
            hreads = lambda k: [wb_, hb[k][tt]]
            ropes = [(C_NQ + 128 * c, SW_NQ + 128 * c, A["qT_nsa"][128 * c:128 * (c + 1), sl]) for c in range(3)]
            ropes += [(C_NKC, SW_NKC, A["kT_cmp"][:, sl]), (C_NKS, SW_NKS, A["kT_sel"][:, sl]),
                      (C_NKW, SW_NKW, A["kT_win"][:, sl])]
            for n, (ca, cs, dst) in enumerate(ropes):
                ba, bb = nbank(), 4 + (n % 2)
                chain(ba, 128, lambda k: win[:, k, ca:ca + 128], lambda k: hT[:, k, sl], 8, hreads)
                chain(bb, 128, lambda k: wsw[:, k, cs:cs + 128], lambda k: hT[:, k, sl], 8, hreads)
                rope_store(ba, bb, 128, tab["cosN"][:], tab["sinN"][:], tab["b"], dst)
            plain = [(C_NVC, A["vT_cmp"][:, sl]), (C_SQ, A["qT_sb"][0:128, sl]), (C_SQ + 128, A["qT_sb"][128:256, sl]),
                     (C_SK, A["kT_sb"][0:128, sl]), (C_SK + 128, A["kT_sb"][128:256, sl])]
            for n, (ca, dst) in enumerate(plain):
                ba = nbank()
                chain(ba, 128, lambda k: win[:, k, ca:ca + 128], lambda k: hT[:, k, sl], 8, hreads)
                store(R.ps[ba][:, :], [R.psb[ba]], 128, dst, use_act=(n % 2 == 0))
            for bl in range(4):
                tb = tt * 4 + bl
                tsl = slice(tb * 128, (tb + 1) * 128)
                lsl = slice(bl * 128, (bl + 1) * 128)
                i = tb % 2
                ba = nbank()
                for n, ca in enumerate((C_NVS, C_NVW)):
                    for k in range(8):
                        T.op("pe", lambda e: e.matmul(R.ps[ba][:, n * 128:(n + 1) * 128], lhsT=hT[:, k, tsl],
                                                      rhs=win[:, k, ca:ca + 128], start=(k == 0), stop=(k == 7)),
                             reads=[wb_, hb[k][tt]], writes=[R.psb[ba]])
                for k in range(8):
                    T.op("pe", lambda e: e.matmul(R.ps[ba][:, 256:274], lhsT=hT[:, k, tsl],
                                                  rhs=win[:, k, C_NG:C_NG + 18], start=(k == 0), stop=(k == 7)),
                         reads=[wb_, hb[k][tt]], writes=[R.psb[ba]])
                T.op("dve", lambda e: e.tensor_copy(out=v2st[i][:, :, :, 0:64],
                                                    in_=R.ps[ba][:, 0:256].rearrange("p (a h x) -> p a h x", a=2, x=64)),
                     reads=[R.psb[ba]], writes=[v2st_b[i]])
                T.dma("sp", v2st_sem[i], A["v_sel"][tsl, :, :], v2st[i][:, 0, :, :], reads=[v2st_b[i]])
                T.dma("sp", v2st_sem[i], A["v_win"][tsl, :, :], v2st[i][:, 1, :, :], reads=[v2st_b[i]])
                T.op("dve", lambda e: e.tensor_tensor(out=gst[i][:], in0=R.ps[ba][:, 256:274], in1=sm[:, 8:26], op=ALU.add),
                     reads=[R.psb[ba], smb], writes=[gst_b[i]])
                T.op("act", lambda e: e.activation(out=gst[i][:], in_=gst[i][:], func=AF.Sigmoid),
                     reads=[gst_b[i]], writes=[gst_b[i]])
                T.dma("sp", gst_sem[i], A["gates"][tsl, :], gst[i][:], reads=[gst_b[i]])
                ba = nbank()
                for k in range(8):
                    T.op("pe", lambda e: e.matmul(R.ps[ba][:, 0:256], lhsT=hT[:, k, tsl],
                                                  rhs=win[:, k, C_SV:C_SV + 256], start=(k == 0), stop=(k == 7)),
                         reads=[wb_, hb[k][tt]], writes=[R.psb[ba]])
                T.op("act", lambda e: e.activation(out=vsb[i][:], in_=R.ps[ba][:, 0:256], func=AF.Copy),
                     reads=[R.psb[ba]], writes=[vsb_b[i]])
                T.dma("sp", vsb_sem[i], A["v_sb"][tsl, :], vsb[i][:], reads=[vsb_b[i]])
        T.barrier_all()


THETA = 500000.0


def own_positions(core):
    j = core % 4
    return ((4 * np.arange(NBLK)[:, None] + j) * 128 + np.arange(128)[None, :]).reshape(-1)


def _cs(pos, rot):
    half = rot // 2
    inv = np.float32(THETA) ** (-(np.arange(half, dtype=np.float32) / np.float32(half)))
    ang = pos.astype(np.float32)[None, :] * inv.astype(np.float32)[:, None]
    return np.cos(ang).astype(np.float32), np.sin(ang).astype(np.float32)


def rope_tables(core):
    pos = own_positions(core)
    n = pos.shape[0]
    c16, s16 = _cs(pos, 32)
    c8, s8 = _cs(pos, 16)
    cosM = np.ones((96, n), np.float32); sinM = np.zeros((96, n), np.float32)
    cosM[64:80] = c16; cosM[80:96] = c16; sinM[64:80] = -s16; sinM[80:96] = s16
    cosK = np.concatenate([c16, c16], 0); sinK = np.concatenate([-s16, s16], 0)
    cosN = np.ones((128, n), np.float32); sinN = np.zeros((128, n), np.float32)
    for h in range(2):
        cosN[h * 64:h * 64 + 8] = c8; cosN[h * 64 + 8:h * 64 + 16] = c8
        sinN[h * 64:h * 64 + 8] = -s8; sinN[h * 64 + 8:h * 64 + 16] = s8
    return dict(cosM=cosM, sinM=sinM, cosK=cosK, sinK=sinK, cosN=cosN, sinN=sinN)


def swap_cols_rope(w, head_w, rope0, rot):
    w = np.array(w, copy=True)
    half = rot // 2
    nh = w.shape[1] // head_w
    for h in range(nh):
        a = h * head_w + rope0
        tmp = w[:, a:a + half].copy()
        w[:, a:a + half] = w[:, a + half:a + rot]
        w[:, a + half:a + rot] = tmp
    return w


def make_w_in_sw(w_in):
    parts = [swap_cols_rope(w_in[:, C_KR:C_KR + 32], 32, 0, 32),
             swap_cols_rope(w_in[:, C_NQ:C_NQ + 384], 64, 0, 16),
             swap_cols_rope(w_in[:, C_NKC:C_NKC + 128], 64, 0, 16),
             swap_cols_rope(w_in[:, C_NKS:C_NKS + 128], 64, 0, 16),
             swap_cols_rope(w_in[:, C_NKW:C_NKW + 128], 64, 0, 16)]
    return np.ascontiguousarray(np.concatenate(parts, axis=1))


def make_smallsP(q_norm, kv_norm, gate_bias):
    sm = np.zeros((128, 32), np.float32)
    sm[:, 0:2] = q_norm.reshape(2, 128).T
    sm[:, 2] = kv_norm
    sm[:, 8:26] = gate_bias[None, :]
    return sm


P_OUTS = [("qT_mla", [6, 96, NT], BF16), ("kT_mla", [6, 64, NT], BF16), ("kpeT", [32, NT], BF16),
          ("qT_nsa", [384, NT], BF16), ("kT_cmp", [128, NT], BF16), ("kT_sel", [128, NT], BF16),
          ("kT_win", [128, NT], BF16), ("vT_cmp", [128, NT], BF16), ("qT_sb", [256, NT], BF16),
          ("kT_sb", [256, NT], BF16), ("v_mla", [NT, 6, 65], BF16), ("v_sel", [NT, 2, 65], BF16),
          ("v_win", [NT, 2, 65], BF16), ("gates", [NT, 18], F32), ("v_sb", [NT, 256], BF16)]
P_INS = [("w_in", [D, DIN]), ("w_in_sw", [D, NSW]), ("w_uq", [256, 576]), ("w_uq_sw", [256, 576]),
         ("w_ukv", [128, 768]), ("smallsP", [128, 32]),
         ("cosM", [96, NT]), ("sinM", [96, NT]), ("cosK", [32, NT]), ("sinK", [32, NT]),
         ("cosN", [128, NT]), ("sinN", [128, NT])]


def make_masks(core):
    j = core % 4
    k = np.arange(128)[:, None]
    q = np.arange(128)[None, :]
    ones = np.ones((128, 128), np.float32)
    zeros = np.zeros((128, 128), np.float32)
    tri = (k <= q).astype(np.float32)
    stri = (k < q).astype(np.float32)
    gt = (k > q).astype(np.float32)
    M = np.zeros((128, 22, 128), np.float32)
    M[:, 20] = (k >= q).astype(np.float32)
    M[:, 21] = 1.0
    for d in range(4):
        M[:, d] = ones if d < j else (tri if d == j else zeros)
        M[:, 16 + d] = ones if d < j else (stri if d == j else zeros)
    for dd in range(8):
        d = dd - 4
        if d < j - 4 or d > j:
            M[:, 4 + dd] = zeros
        elif d == j - 4:
            M[:, 4 + dd] = gt
        elif d == j:
            M[:, 4 + dd] = tri
        else:
            M[:, 4 + dd] = ones
    for m4 in range(4):
        i16 = 4 * m4 + j
        M[:, 12 + m4] = (16 * k + 31 - 128 * i16 <= q).astype(np.float32)
    return M.astype(ml_dtypes.bfloat16)


def ktcol(kt):
    return (kt % 4) * NT + (kt // 4) * 128


def ktile(kt):
    return (kt % 4) * NBLK + (kt // 4)


def emit_oT_store(R, C, obank, G, ch, po, zcol=64, stride=65):
    T = R.T
    i = C["ev"] % 2
    C["ev"] += 1
    on, on_b, rz, rz_b = C["on"][i], C["on_b"][i], C["rz"][i], C["rz_b"][i]
    ov = R.ps[obank][:, 0:4 * stride].rearrange("p (m x) -> p m x", x=stride)
    T.op("dve", lambda e: e.reciprocal(out=rz[:], in_=ov[:, :, zcol:zcol + 1]), reads=[R.psb[obank]], writes=[rz_b])
    T.op("dve", lambda e: e.tensor_tensor(out=on[:], in0=ov[:, :, 0:64], in1=rz[:].broadcast_to([128, 4, 64]),
                                          op=ALU.mult),
         reads=[R.psb[obank], rz_b], writes=[on_b])
    emit_transpose_store(R, C, on, on_b, G, ch, po)


def emit_transpose_store(R, C, on, on_b, G, ch, po):
    T = R.T
    tb = C["tpbank"]
    for mm in range(4):
        T.op("pe", lambda e: e.transpose(out=R.ps[tb][0:64, mm * 128:(mm + 1) * 128], in_=on[:, mm, :],
                                         identity=C["ident"][:]),
             reads=[on_b, C["ident_b"]], writes=[R.psb[tb]])
    T.op("act", lambda e: e.activation(out=R.oT[po:po + 64, ch, G * 512:(G + 1) * 512], in_=R.ps[tb][0:64, :],
                                       func=AF.Copy),
         reads=[R.psb[tb]], writes=[R.oT_b[ch][G]])


def emit_dense_attn(R, C, KT, KT_b, V, V_b, QT, QT_b, dk, scale, obank, mask0,
                    selT=None, selT_b=None, vstride=65):
    T = R.T

    def run(G):
        steps = list(range(16 * G + 16))
        n = len(steps)
        stt = {}

        def stage_a(kt):
            mm_min = max(0, -((-(kt - 16 * G - 3)) // 4))
            c0 = mm_min * 128
            sb_ = C["sbanks"][C["sn"] % len(C["sbanks"])]
            C["sn"] += 1
            stt[kt] = [mm_min, c0, sb_, None]
            T.op("pe", lambda e: e.matmul(R.ps[sb_][:, c0:512], lhsT=KT[0:dk, ktcol(kt):ktcol(kt) + 128],
                                          rhs=QT[0:dk, G * 512 + c0:(G + 1) * 512], start=True, stop=(selT is None)),
                 reads=[KT_b, QT_b], writes=[R.psb[sb_]])
            if selT is not None:
                T.op("pe", lambda e: e.matmul(R.ps[sb_][:, c0:512], lhsT=C["Ebig"][:, kt * 128:(kt + 1) * 128],
                                              rhs=selT[:, G * 512 + c0:(G + 1) * 512], start=False, stop=True),
                     reads=[C["Ebig_b"], selT_b], writes=[R.psb[sb_]])

        def stage_b(kt):
            mm_min, c0, sb_, _ = stt[kt]
            pi = C["pn"] % len(C["PT"])
            C["pn"] += 1
            stt[kt][3] = pi
            PT, PT_b = C["PT"][pi], C["PT_b"][pi]
            T.op("act", lambda e: e.activation(out=PT[:, c0:512], in_=R.ps[sb_][:, c0:512], func=AF.Exp, scale=scale),
                 reads=[R.psb[sb_]], writes=[PT_b])
            if kt >= 16 * G:
                mmb = (kt - 16 * G) // 4
                d = (kt - 16 * G) % 4
                T.op("pool", lambda e: e.tensor_tensor(out=PT[:, mmb * 128:(mmb + 1) * 128],
                                                       in0=PT[:, mmb * 128:(mmb + 1) * 128],
                                                       in1=C["masks"][:, mask0 + d, :], op=ALU.mult),
                     reads=[PT_b, C["masks_b"]], writes=[PT_b])

        def stage_c(kt, first):
            mm_min, c0, sb_, pi = stt[kt]
            PT, PT_b = C["PT"][pi], C["PT_b"][pi]
            for mm in range(mm_min, 4):
                last = (kt == 16 * G + 4 * mm + 3)
                T.op("pe", lambda e: e.matmul(R.ps[obank][:, mm * vstride:(mm + 1) * vstride],
                                              lhsT=PT[:, mm * 128:(mm + 1) * 128], rhs=V[:, ktile(kt), :],
                                              start=(first and mm == mm_min), stop=last, skip_group_check=True),
                     reads=[PT_b, V_b], writes=[R.psb[obank]])

        for idx in range(n + 2):
            if idx < n:
                stage_a(steps[idx])
            if 1 <= idx <= n:
                stage_b(steps[idx - 1])
            if idx >= 2:
                stage_c(steps[idx - 2], idx == 2)
    return run


def alloc_attn_common(R, A, ph):
    nc, T = R.nc, R.T
    C = {"ev": 0, "sn": 0, "pn": 0, "sbanks": [0, 1, 2, 3], "tpbank": 5}

    def sb(name, shape, dt):
        return ph.enter_context(nc.sbuf_tensor(U(name), shape, dt))
    C["sb"] = sb
    C["masks"] = sb("A_masks", [128, 22, 128], BF16)
    C["masks_b"] = Buf("A_masks")
    C["ident"] = sb("A_ident", [128, 128], F32)
    C["ident_b"] = Buf("A_ident")
    ds = T.new_dma_sem("Aconst")
    T.dma("sp", ds, C["masks"][:], A["masks"], writes=[C["masks_b"]])
    T.dma("sp", ds, C["ident"][:], A["ident"], writes=[C["ident_b"]])
    C["PT"] = [sb("A_PT%d" % i, [128, 512], BF16) for i in range(6)]
    C["PT_b"] = [Buf("A_PT%d" % i) for i in range(6)]
    C["on"] = [sb("A_on%d" % i, [128, 4, 64], F32) for i in range(2)]
    C["on_b"] = [Buf("A_on%d" % i) for i in range(2)]
    C["rz"] = [sb("A_rz%d" % i, [128, 4, 1], F32) for i in range(2)]
    C["rz_b"] = [Buf("A_rz%d" % i) for i in range(2)]
    C["kcT"] = [sb("A_kcT%d" % i, [64, 512], BF16) for i in range(2)]
    C["vc"] = [sb("A_vc%d" % i, [128, 4, 65], BF16) for i in range(2)]
    C["kc_b"] = [Buf("A_kc%d" % i) for i in range(2)]
    return C


def gb(A, nm, i=None):
    d = A.get("gb")
    if d is None:
        return []
    b = d[nm]
    if isinstance(b, list):
        b = b[i]
    return [b]


def emit_mla(R, C, A, hook=None):
    nc, T = R.nc, R.T
    ngrp = 0
    hook_left = [4]
    with R.ExitStack() as ph:
        def sb(name, shape, dt):
            return ph.enter_context(nc.sbuf_tensor(U(name), shape, dt))
        KT = [sb("M_KT%d" % i, [96, 4 * NT], BF16) for i in range(2)]
        V = [sb("M_V%d" % i, [128, 64, 65], BF16) for i in range(2)]
        QT = [sb("M_QT%d" % i, [96, NT], BF16) for i in range(2)]
        bufs = [Buf("M_slot%d" % i) for i in range(2)]
        sems = [T.new_dma_sem("Mslot%d" % i) for i in range(2)]
        for h in range(6):
            s = h % 2
            for r in range(4):
                T.dma("sp", sems[s], KT[s][0:64, r * NT:(r + 1) * NT], A["kT_mla_g"][h][r, :, :], reads=gb(A, "kT_mla", h),
                      writes=[bufs[s]])
                T.dma("sp", sems[s], KT[s][64:96, r * NT:(r + 1) * NT], A["kpeT_g"][r, :, :], reads=gb(A, "kpeT"), writes=[bufs[s]])
                for hf in range(2):
                    T.dma("sp", sems[s], V[s][:, r * NBLK + hf * 8:r * NBLK + hf * 8 + 8, :],
                          A["v_mla_g"][hf][r, :, h, :].rearrange("(m p) x -> p m x", p=128), reads=gb(A, "v_mla", hf),
                          writes=[bufs[s]])
            T.dma("sp", sems[s], QT[s][:], A["qT_mla"][h, :, :], writes=[bufs[s]])
            for G in range(4):
                obank = 6 + (G % 2)
                run = emit_dense_attn(R, C, KT[s], bufs[s], V[s], bufs[s], QT[s], bufs[s], 96, 96 ** -0.5,
                                      obank, 0)
                run(G)
                emit_oT_store(R, C, obank, G, h // 2, (h % 2) * 64)
                ngrp += 1
                if hook is not None and ngrp % 3 == 1 and hook_left[0] > 0:
                    hook_left[0] -= 1
                    next(hook)
        T.barrier_all()


def make_nsa_consts(core):
    j = core % 4
    c = np.arange(512)[:, None]
    n = np.arange(128)[None, :]
    ov = ((16 * c < 64 * n + 64) & (16 * c + 32 > 64 * n)).astype(np.float32)
    ovl = np.concatenate([ov, np.ones((512, 1), np.float32)], 1).reshape(4, 128, 129).transpose(1, 0, 2)
    ql = np.arange(128)[:, None]
    w = np.arange(256)[None, :]
    rel = w - 128 - 2 * j
    cur = (ql >= 64).astype(np.int64)
    forced = (rel == cur) | (rel == cur - 1)
    future = rel > cur
    keep = (~(forced | future)).astype(np.float32)
    add = np.where(forced, 1e4, np.where(future, -1.0, 0.0)).astype(np.float32)
    ka = np.stack([keep, add], 1)
    key = np.arange(S)[None, :]
    eb = np.where(key // 64 == np.arange(128)[:, None], 1.0, 0.0).astype(np.float32)
    return dict(ovl=np.ascontiguousarray(ovl).astype(ml_dtypes.bfloat16), keepadd=np.ascontiguousarray(ka),
                Ebig=eb.astype(ml_dtypes.bfloat16))


def emit_nsa_compress(R, C, A):
    nc, T = R.nc, R.T
    with R.ExitStack() as ph:
        def sb(name, shape, dt):
            return ph.enter_context(nc.sbuf_tensor(U(name), shape, dt))
        w1 = [sb("N_w1%d" % i, [64, 32, 128], BF16) for i in range(2)]
        w2 = [sb("N_w2%d" % i, [128, 64], BF16) for i in range(2)]
        posT = [sb("N_pos%d" % i, [64, 32], BF16) for i in range(2)]
        cst_b = Buf("NC_const")
        ds = T.new_dma_sem("NconstP")
        T.dma("pool", ds, w1[0][:], A["cmp_w1_k"].rearrange("(l d) h -> d l h", d=64), writes=[cst_b])
        T.dma("pool", ds, w1[1][:], A["cmp_w1_v"].rearrange("(l d) h -> d l h", d=64), writes=[cst_b])
        T.dma("pool", ds, w2[0][:], A["cmp_w2_k"], writes=[cst_b])
        T.dma("pool", ds, w2[1][:], A["cmp_w2_v"], writes=[cst_b])
        T.dma("pool", ds, posT[0][:], A["cmp_posT_k"], writes=[cst_b])
        T.dma("pool", ds, posT[1][:], A["cmp_posT_v"], writes=[cst_b])
        bias = sb("N_bias", [128, 2], F32)
        bias_b = Buf("N_bias")
        for i in range(2):
            for l in range(32):
                T.op("pe", lambda e: e.matmul(R.ps[4][:, i:i + 1], lhsT=w1[i][:, l, :], rhs=posT[i][:, l:l + 1],
                                              start=(l == 0), stop=(l == 31)),
                     reads=[cst_b], writes=[R.psb[4]])
            T.op("dve", lambda e: e.tensor_copy(out=bias[:, i:i + 1], in_=R.ps[4][:, i:i + 1]),
                 reads=[R.psb[4]], writes=[bias_b])
        xg = [[sb("N_xg%d_%d" % (g, i), [64, S], BF16) for i in range(2)] for g in range(2)]
        xg_b = [[Buf("N_xg%d_%d" % (g, i)) for i in range(2)] for g in range(2)]
        hid = sb("N_hid", [128, 512], F32)
        tq = sb("N_tq", [128, 512], F32)
        gl = sb("N_gl", [128, 512], BF16)
        hid_b, gl_b = Buf("N_hid"), Buf("N_gl")
        T.op("pool", lambda e: e.memset(gl[:], 0.0), writes=[gl_b])
        yield
        for g in range(2):
            kcT, vc, kc_b = C["kcT"][g], C["vc"][g], C["kc_b"][g]
            T.op("pool", lambda e: e.memset(vc[:], 1.0), reads=[], writes=[kc_b])
            T.op("pool", lambda e: e.memset(kcT[:], 0.0), reads=[], writes=[kc_b])
            for i in range(2):
                nm = ("kT_cmp_g", "vT_cmp_g")[i]
                dsx = T.new_dma_sem("Nxg%d_%d" % (g, i))
                for r in range(4):
                    dst = xg[g][i][:].rearrange("d (m r p) -> d m r p", r=4, p=128)[:, :, r, :]
                    T.dma("sp", dsx, dst, A[nm][r, g * 64:(g + 1) * 64, :].rearrange("d (m p) -> d m p", p=128),
                          reads=gb(A, nm[:-2]), writes=[xg_b[g][i]])
                for l in range(32):
                    T.op("pe", lambda e: e.matmul(R.ps[4][:, 0:511], lhsT=w1[i][:, l, :],
                                                  rhs=xg[g][i][:, l:l + 16 * 510 + 1:16],
                                                  start=(l == 0), stop=(l == 31)),
                         reads=[cst_b, xg_b[g][i]], writes=[R.psb[4]])
                T.op("act", lambda e: e.activation(out=hid[:, 0:511], in_=R.ps[4][:, 0:511], func=AF.Identity,
                                                   bias=bias[:, i:i + 1]),
                     reads=[R.psb[4], bias_b], writes=[hid_b])
                T.op("dve", lambda e: e.tensor_tensor(out=tq[:, 0:511], in0=hid[:, 0:511], in1=hid[:, 0:511],
                                                      op=ALU.mult), reads=[hid_b], writes=[hid_b])
                T.op("dve", lambda e: e.tensor_scalar(out=tq[:, 0:511], in0=tq[:, 0:511], scalar1=0.044715,
                                                      scalar2=1.0, op0=ALU.mult, op1=ALU.add),
                     reads=[hid_b], writes=[hid_b])
                T.op("dve", lambda e: e.tensor_tensor(out=tq[:, 0:511], in0=tq[:, 0:511], in1=hid[:, 0:511],
                                                      op=ALU.mult), reads=[hid_b], writes=[hid_b])
                T.op("act", lambda e: e.activation(out=tq[:, 0:511], in_=tq[:, 0:511], func=AF.Sigmoid,
                                                   scale=1.5957691216057308),
                     reads=[hid_b], writes=[hid_b])
                T.op("dve", lambda e: e.tensor_tensor(out=gl[:, 0:511], in0=tq[:, 0:511], in1=hid[:, 0:511],
                                                      op=ALU.mult), reads=[hid_b, gl_b], writes=[gl_b])
                if i == 0:
                    T.op("pe", lambda e: e.matmul(R.ps[4][0:64, 0:511], lhsT=w2[0][:], rhs=gl[:, 0:511],
                                                  start=True, stop=True),
                         reads=[cst_b, gl_b], writes=[R.psb[4]])
                    T.op("dve", lambda e: e.tensor_copy(out=kcT[:, 0:511], in_=R.ps[4][0:64, 0:511]),
                         reads=[R.psb[4]], writes=[kc_b])
                else:
                    for t in range(4):
                        T.op("pe", lambda e: e.matmul(R.ps[4][:, t * 64:(t + 1) * 64], lhsT=gl[:, t * 128:(t + 1) * 128],
                                                      rhs=w2[1][:], start=True, stop=True),
                             reads=[cst_b, gl_b], writes=[R.psb[4]])
                    T.op("dve", lambda e: e.tensor_copy(out=vc[:, :, 0:64],
                                                        in_=R.ps[4][:, 0:256].rearrange("p (t x) -> p t x", x=64)),
                         reads=[R.psb[4]], writes=[kc_b])
                yield
        T.barrier_all()


def emit_nsa(R, C, A):
    nc, T = R.nc, R.T
    SC = 0.125
    with R.ExitStack() as ph:
        def sb(name, shape, dt):
            return ph.enter_context(nc.sbuf_tensor(U(name), shape, dt))
        Ebig = sb("N_Ebig", [128, S], BF16)
        ovl = sb("N_ovl", [128, 4, 129], BF16)
        ka = sb("N_ka", [128, 2, 256], F32)
        gates = sb("N_gates", [128, NBLK, 18], F32)
        cst_b = Buf("N_const")
        C["Ebig"], C["Ebig_b"] = Ebig, cst_b
        ds = T.new_dma_sem("Nconst")
        T.dma("sp", ds, Ebig[:], A["Ebig"], writes=[cst_b])
        T.dma("sp", ds, ovl[:], A["ovl"], writes=[cst_b])
        T.dma("sp", ds, ka[:], A["keepadd"], writes=[cst_b])
        T.dma("sp", ds, gates[:], A["gates"].rearrange("(m p) g -> p m g", p=128), writes=[cst_b])
        selT = sb("N_selT", [128, NT], BF16)
        selT_b = Buf("N_selT")
        QTn_l = [sb("N_QT%d" % i, [64, 3, NT], BF16) for i in range(2)]
        KTs_l = [sb("N_KTs%d" % i, [64, 4 * NT], BF16) for i in range(2)]
        Vs_l = [sb("N_Vs%d" % i, [128, 64, 65], BF16) for i in range(2)]
        KTw_l = [sb("N_KTw%d" % i, [64, 4 * NT], BF16) for i in range(2)]
        Vw_l = [sb("N_Vw%d" % i, [128, 64, 65], BF16) for i in range(2)]
        kv_b_l = [Buf("N_kv0"), Buf("N_kv1")]
        kv_sem_l = [T.new_dma_sem("Nkv0"), T.new_dma_sem("Nkv1")]
        for g in range(2):
            QTn, KTs, Vs, KTw, Vw, kv_b, kv_sem = QTn_l[g], KTs_l[g], Vs_l[g], KTw_l[g], Vw_l[g], kv_b_l[g], kv_sem_l[g]
            T.dma("sp", kv_sem, QTn[:], A["qT_nsa"][g * 192:(g + 1) * 192, :].rearrange("(h d) t -> d h t", d=64),
                  writes=[kv_b])
            for r in range(4):
                T.dma("sp", kv_sem, KTs[:, r * NT:(r + 1) * NT], A["kT_sel_g"][r, g * 64:(g + 1) * 64, :], reads=gb(A, "kT_sel"), writes=[kv_b])
                T.dma("sp", kv_sem, KTw[:, r * NT:(r + 1) * NT], A["kT_win_g"][r, g * 64:(g + 1) * 64, :], reads=gb(A, "kT_win"), writes=[kv_b])
                T.dma("sp", kv_sem, Vs[:, r * NBLK:(r + 1) * NBLK, :],
                      A["v_sel_g"][r, :, g, :].rearrange("(m p) x -> p m x", p=128), reads=gb(A, "v_sel"), writes=[kv_b])
                T.dma("sp", kv_sem, Vw[:, r * NBLK:(r + 1) * NBLK, :],
                      A["v_win_g"][r, :, g, :].rearrange("(m p) x -> p m x", p=128), reads=gb(A, "v_win"), writes=[kv_b])
        oaccs = [sb("N_oacc%d" % i, [128, 3, 4, 64], F32) for i in range(2)]
        oaccs_b = [Buf("N_oacc0"), Buf("N_oacc1")]
        C["pcn"], C["pwn"] = 0, 0
        Pc = [sb("N_Pc%d" % i, [128, 3, 128], BF16) for i in range(2)]
        Pc_b = [Buf("N_Pc0"), Buf("N_Pc1")]
        rzc = sb("N_rzc", [128, 3, 1], F32)
        wc = sb("N_wc", [128, 3, 1], F32)
        sc = sb("N_sc", [128, 128], F32)
        sc2 = sb("N_sc2", [128, 128], F32)
        m8 = sb("N_m8", [128, 16], F32)
        selns = [sb("N_seln%d" % i, [128, 128], F32) for i in range(4)]
        selns_b = [Buf("N_seln%d" % i) for i in range(4)]
        sm_b = Buf("N_small")
        rz4 = sb("N_rz4", [128, 4, 1], F32)
        w4 = sb("N_w4", [128, 4, 1], F32)
        w4_b = Buf("N_w4")
        Pw = [sb("N_Pw%d" % i, [128, 512], BF16) for i in range(3)]
        Pw_b = [Buf("N_Pw%d" % i) for i in range(3)]
        for g in range(2):
            kcT, vc, kc_b = C["kcT"][g], C["vc"][g], C["kc_b"][g]
            QTn, KTs, Vs, KTw, Vw, kv_b, kv_sem = QTn_l[g], KTs_l[g], Vs_l[g], KTw_l[g], Vw_l[g], kv_b_l[g], kv_sem_l[g]

            def cmp_block(G, oacc, oacc_b):
                for mm in range(4):
                    ob, sbk = (3, 4) if mm % 2 == 0 else (2, 7)
                    m = 4 * G + mm
                    msl = slice(m * 128, (m + 1) * 128)
                    for t in range(G + 1):
                        sb_ = C["sbanks"][C["sn"] % len(C["sbanks"])]
                        C["sn"] += 1
                        pi = C["pcn"] % 2
                        C["pcn"] += 1
                        T.op("pe", lambda e: e.matmul(R.ps[sb_][:, 0:384], lhsT=kcT[:, t * 128:(t + 1) * 128],
                                                      rhs=QTn[:, :, msl], start=True, stop=True),
                             reads=[kc_b, kv_b], writes=[R.psb[sb_]])
                        T.op("act", lambda e: e.activation(out=Pc[pi][:].rearrange("p r q -> p (r q)"),
                                                           in_=R.ps[sb_][:, 0:384], func=AF.Exp, scale=SC),
                             reads=[R.psb[sb_]], writes=[Pc_b[pi]])
                        if t == G:
                            T.op("pool", lambda e: e.tensor_tensor(
                                out=Pc[pi][:], in0=Pc[pi][:],
                                in1=C["masks"][:, 12 + mm:13 + mm, :].broadcast_to([128, 3, 128]), op=ALU.mult),
                                reads=[Pc_b[pi], C["masks_b"]], writes=[Pc_b[pi]])
                        for r in range(3):
                            T.op("pe", lambda e: e.matmul(R.ps[ob][:, r * 65:(r + 1) * 65], lhsT=Pc[pi][:, r, :],
                                                          rhs=vc[:, t, :], start=(t == 0 and r == 0), stop=(t == G),
                                                          skip_group_check=True),
                                 reads=[Pc_b[pi], kc_b], writes=[R.psb[ob]])
                        for r in range(3):
                            T.op("pe", lambda e: e.matmul(R.ps[sbk][:, r * 129:(r + 1) * 129], lhsT=Pc[pi][:, r, :],
                                                          rhs=ovl[:, t, :], start=(t == 0 and r == 0), stop=(t == G),
                                                          skip_group_check=True),
                                 reads=[Pc_b[pi], cst_b], writes=[R.psb[sbk]])
                    scv = R.ps[sbk][:, 0:387].rearrange("p (r x) -> p r x", x=129)
                    ocv = R.ps[ob][:, 0:195].rearrange("p (r x) -> p r x", x=65)
                    T.op("dve", lambda e: e.tensor_scalar(out=rzc[:], in0=scv[:, :, 128:129], scalar1=1e-30, scalar2=None,
                                                          op0=ALU.add), reads=[R.psb[sbk]], writes=[sm_b])
                    T.op("dve", lambda e: e.reciprocal(out=rzc[:], in_=rzc[:]), reads=[sm_b], writes=[sm_b])
                    T.op("dve", lambda e: e.tensor_scalar(out=sc[:], in0=scv[:, 0, 0:128], scalar1=rzc[:, 0, :], scalar2=None,
                                                          op0=ALU.mult), reads=[R.psb[sbk], sm_b], writes=[sm_b])
                    for r in (1, 2):
                        T.op("dve", lambda e: e.scalar_tensor_tensor(out=sc[:], in0=scv[:, r, 0:128], scalar=rzc[:, r, :],
                                                                     in1=sc[:], op0=ALU.mult, op1=ALU.add),
                             reads=[R.psb[sbk], sm_b], writes=[sm_b])
                    T.op("dve", lambda e: e.tensor_tensor(
                        out=wc[:], in0=rzc[:],
                        in1=gates[:, m, 9 * g:9 * g + 9].rearrange("p (r x) -> p r x", x=3)[:, :, 0:1], op=ALU.mult),
                        reads=[sm_b, cst_b], writes=[sm_b])
                    for r in range(3):
                        T.op("dve", lambda e: e.tensor_scalar(out=oacc[:, r, mm, :], in0=ocv[:, r, 0:64],
                                                              scalar1=wc[:, r, :], scalar2=None, op0=ALU.mult),
                             reads=[R.psb[ob], sm_b], writes=[oacc_b])
                    w0 = 128 - 8 * m
                    T.op("dve", lambda e: e.tensor_tensor(out=sc[:], in0=sc[:], in1=ka[:, 0, w0:w0 + 128], op=ALU.mult),
                         reads=[sm_b, cst_b], writes=[sm_b])
                    T.op("dve", lambda e: e.tensor_tensor(out=sc[:], in0=sc[:], in1=ka[:, 1, w0:w0 + 128], op=ALU.add),
                         reads=[sm_b, cst_b], writes=[sm_b])
                    T.op("dve", lambda e: e.memset(sc[:, 0:1], 1e4), reads=[sm_b], writes=[sm_b])
                    T.op("dve", lambda e: e.max(out=m8[:, 0:8], in_=sc[:]), reads=[sm_b], writes=[sm_b])
                    T.op("dve", lambda e: e.match_replace(out=sc2[:], in_to_replace=m8[:, 0:8], in_values=sc[:],
                                                          imm_value=-1e9), reads=[sm_b], writes=[sm_b])
                    T.op("dve", lambda e: e.max(out=m8[:, 8:16], in_=sc2[:]), reads=[sm_b], writes=[sm_b])
                    T.op("dve", lambda e: e.tensor_scalar(out=selns[mm][:], in0=sc[:], scalar1=m8[:, 15:16], scalar2=None,
                                                          op0=ALU.is_ge), reads=[sm_b], writes=[selns_b[mm]])

            def cmp_finish(G):
                tb = C["tpbank"]
                for mm in range(4):
                    T.op("pe", lambda e: e.transpose(out=R.ps[tb][:, mm * 128:(mm + 1) * 128], in_=selns[mm][:],
                                                     identity=C["ident"][:]),
                         reads=[selns_b[mm], C["ident_b"]], writes=[R.psb[tb]])
                T.op("act", lambda e: e.activation(out=selT[:, G * 512:(G + 1) * 512], in_=R.ps[tb][:, :], func=AF.Copy),
                     reads=[R.psb[tb]], writes=[selT_b])

            def win_block(r, G, oacc, oacc_b):
                h = 3 * g + r
                ob = 6
                items = []
                for mm in range(4):
                    for half in range(2):
                        kts = [16 * G + 4 * mm - 4 + half * 4 + x for x in range(4)]
                        if kts[-1] >= 0:
                            items.append((mm, half, kts))
                ni = len(items)
                stt = {}

                def wa(i):
                    mm, half, kts = items[i]
                    msl = slice((4 * G + mm) * 128, (4 * G + mm + 1) * 128)
                    sb_ = C["sbanks"][C["sn"] % len(C["sbanks"])]
                    C["sn"] += 1
                    stt[i] = [sb_, None]
                    for x, kt in enumerate(kts):
                        T.op("pe", lambda e: e.matmul(R.ps[sb_][:, x * 128:(x + 1) * 128],
                                                      lhsT=KTw[:, ktcol(kt):ktcol(kt) + 128],
                                                      rhs=QTn[:, r, msl], start=True, stop=True),
                             reads=[kv_b], writes=[R.psb[sb_]])

                def wb(i):
                    mm, half, kts = items[i]
                    sb_ = stt[i][0]
                    pi = C["pwn"] % len(Pw)
                    C["pwn"] += 1
                    stt[i][1] = pi
                    T.op("act", lambda e: e.activation(out=Pw[pi][:], in_=R.ps[sb_][:], func=AF.Exp, scale=SC),
                         reads=[R.psb[sb_]], writes=[Pw_b[pi]])
                    T.op("pool", lambda e: e.tensor_tensor(
                        out=Pw[pi][:], in0=Pw[pi][:],
                        in1=C["masks"][:, 4 + half * 4:8 + half * 4, :].rearrange("p a q -> p (a q)"),
                        op=ALU.mult), reads=[Pw_b[pi], C["masks_b"]], writes=[Pw_b[pi]])

                def wc_(i):
                    mm, half, kts = items[i]
                    pi = stt[i][1]
                    for x, kt in enumerate(kts):
                        T.op("pe", lambda e: e.matmul(R.ps[ob][:, mm * 65:(mm + 1) * 65],
                                                      lhsT=Pw[pi][:, x * 128:(x + 1) * 128], rhs=Vw[:, ktile(kt), :],
                                                      start=(i == 0 and x == 0), stop=(half == 1 and x == 3),
                                                      skip_group_check=True),
                             reads=[Pw_b[pi], kv_b], writes=[R.psb[ob]])

                for idx in range(ni + 2):
                    if idx < ni:
                        wa(idx)
                    if 1 <= idx <= ni:
                        wb(idx - 1)
                    if idx >= 2:
                        wc_(idx - 2)
                owv = R.ps[ob][:, 0:260].rearrange("p (m x) -> p m x", x=65)
                T.op("dve", lambda e: e.reciprocal(out=rz4[:], in_=owv[:, :, 64:65]), reads=[R.psb[ob]], writes=[w4_b])
                T.op("dve", lambda e: e.tensor_tensor(out=w4[:], in0=rz4[:],
                                                      in1=gates[:, 4 * G:4 * G + 4, 3 * h + 2:3 * h + 3], op=ALU.mult),
                     reads=[w4_b, cst_b], writes=[w4_b])
                for mm in range(4):
                    T.op("dve", lambda e: e.scalar_tensor_tensor(out=oacc[:, r, mm, :], in0=owv[:, mm, 0:64],
                                                                 scalar=w4[:, mm, :], in1=oacc[:, r, mm, :],
                                                                 op0=ALU.mult, op1=ALU.add),
                         reads=[R.psb[ob], w4_b, oacc_b], writes=[oacc_b])

            def sel3_block(G, oacc, oacc_b):
                obs = [4, 6, 7]
                mbanks = [2, 3]
                qbanks = [0, 1]
                nk = 16 * G + 16
                nhs = 3 * nk
                stt = {}

                def geom(kt):
                    mm_min = max(0, -((-(kt - 16 * G - 3)) // 4))
                    return mm_min, mm_min * 128

                def sa(hs):
                    kt, r = hs // 3, hs % 3
                    mm_min, c0 = geom(kt)
                    mb = mbanks[kt % 2]
                    if r == 0:
                        T.op("pe", lambda e: e.matmul(R.ps[mb][:, c0:512], lhsT=Ebig[:, kt * 128:(kt + 1) * 128],
                                                      rhs=selT[:, G * 512 + c0:(G + 1) * 512], start=True, stop=True),
                             reads=[cst_b, selT_b], writes=[R.psb[mb]])
                    qb = qbanks[hs % 2]
                    T.op("pe", lambda e: e.matmul(R.ps[qb][:, c0:512], lhsT=KTs[:, ktcol(kt):ktcol(kt) + 128],
                                                  rhs=QTn[:, r, G * 512 + c0:(G + 1) * 512], start=True, stop=True),
                         reads=[kv_b], writes=[R.psb[qb]])

                def sb_(hs):
                    kt, r = hs // 3, hs % 3
                    mm_min, c0 = geom(kt)
                    mb, qb = mbanks[kt % 2], qbanks[hs % 2]
                    pi = C["pn"] % len(C["PT"])
                    C["pn"] += 1
                    stt[hs] = pi
                    PT, PT_b = C["PT"][pi], C["PT_b"][pi]
                    T.op("act", lambda e: e.activation(out=PT[:, c0:512], in_=R.ps[qb][:, c0:512], func=AF.Exp, scale=SC),
                         reads=[R.psb[qb]], writes=[PT_b])
                    T.op("dve", lambda e: e.tensor_tensor(out=PT[:, c0:512], in0=PT[:, c0:512], in1=R.ps[mb][:, c0:512],
                                                          op=ALU.mult),
                         reads=[PT_b, R.psb[mb]], writes=[PT_b])
                    if kt >= 16 * G:
                        mmb = (kt - 16 * G) // 4
                        d = (kt - 16 * G) % 4
                        T.op("pool", lambda e: e.tensor_tensor(out=PT[:, mmb * 128:(mmb + 1) * 128],
                                                               in0=PT[:, mmb * 128:(mmb + 1) * 128],
                                                               in1=C["masks"][:, d, :], op=ALU.mult),
                             reads=[PT_b, C["masks_b"]], writes=[PT_b])

                def sc_(hs):
                    kt, r = hs // 3, hs % 3
                    mm_min, c0 = geom(kt)
                    PT, PT_b = C["PT"][stt[hs]], C["PT_b"][stt[hs]]
                    for mm in range(mm_min, 4):
                        T.op("pe", lambda e: e.matmul(R.ps[obs[r]][:, mm * 65:(mm + 1) * 65],
                                                      lhsT=PT[:, mm * 128:(mm + 1) * 128], rhs=Vs[:, ktile(kt), :],
                                                      start=(kt == 0 and mm == mm_min), stop=(kt == 16 * G + 4 * mm + 3),
                                                      skip_group_check=True),
                             reads=[PT_b, kv_b], writes=[R.psb[obs[r]]])

                for idx in range(nhs + 4):
                    if idx < nhs:
                        sa(idx)
                    if 1 <= idx <= nhs:
                        sb_(idx - 1)
                    if idx >= 4:
                        sc_(idx - 4)
                for r in range(3):
                    h = 3 * g + r
                    osv = R.ps[obs[r]][:, 0:260].rearrange("p (m x) -> p m x", x=65)
                    T.op("dve", lambda e: e.reciprocal(out=rz4[:], in_=osv[:, :, 64:65]), reads=[R.psb[obs[r]]], writes=[w4_b])
                    T.op("dve", lambda e: e.tensor_tensor(out=w4[:], in0=rz4[:],
                                                          in1=gates[:, 4 * G:4 * G + 4, 3 * h + 1:3 * h + 2], op=ALU.mult),
                         reads=[w4_b, cst_b], writes=[w4_b])
                    for mm in range(4):
                        T.op("dve", lambda e: e.scalar_tensor_tensor(out=oacc[:, r, mm, :], in0=osv[:, mm, 0:64],
                                                                     scalar=w4[:, mm, :], in1=oacc[:, r, mm, :],
                                                                     op0=ALU.mult, op1=ALU.add),
                             reads=[R.psb[obs[r]], w4_b, oacc_b], writes=[oacc_b])
                    emit_transpose_store(R, C, oacc[:, r, :, :], oacc_b, G, 3 + h // 2, (h % 2) * 64)

            C["sbanks"] = [0, 1]
            cmp_block(0, oaccs[0], oaccs_b[0])
            for G in range(4):
                cmp_finish(G)
                if G + 1 < 4:
                    cmp_block(G + 1, oaccs[(G + 1) % 2], oaccs_b[(G + 1) % 2])
                for r in range(3):
                    win_block(r, G, oaccs[G % 2], oaccs_b[G % 2])
                sel3_block(G, oaccs[G % 2], oaccs_b[G % 2])
            C["sbanks"] = [0, 1, 2, 3]
        T.barrier_all()


def emit_sb(R, C, A):
    nc, T = R.nc, R.T
    SC = 0.125
    with R.ExitStack() as ph:
        def sb(name, shape, dt):
            return ph.enter_context(nc.sbuf_tensor(U(name), shape, dt))
        KT = [sb("S_KT%d" % i, [64, 4 * NT], BF16) for i in range(2)]
        V = [sb("S_V%d" % i, [128, 64, 64], BF16) for i in range(2)]
        QT = [sb("S_QT%d" % i, [64, NT], BF16) for i in range(2)]
        bufs = [Buf("S_slot%d" % i) for i in range(2)]
        sems = [T.new_dma_sem("Sslot%d" % i) for i in range(2)]
        NE = 4
        E = [sb("S_E%d" % i, [128, 512], F32) for i in range(NE)]
        SP = [sb("S_SP%d" % i, [128, 512], BF16) for i in range(NE)]
        X = [sb("S_X%d" % i, [128, 512], F32) for i in range(2)]
        E_b = [Buf("S_E%d" % i) for i in range(NE)]
        SP_b = [Buf("S_SP%d" % i) for i in range(NE)]
        X_b = [Buf("S_X0"), Buf("S_X1")]
        Accb = [sb("S_Accb%d" % i, [128, 512], BF16) for i in range(3)]
        Accb_b = [Buf("S_Accb%d" % i) for i in range(3)]
        tincl = C["masks"][:, 20, :]
        onesb = C["masks"][:, 21, :]
        cbanks = [3, 4]
        zbanks = [0, 1, 2]
        for h in range(4):
            s = h % 2
            for r in range(4):
                T.dma("sp", sems[s], KT[s][:, r * NT:(r + 1) * NT], A["kT_sb_g"][r, h * 64:(h + 1) * 64, :], reads=gb(A, "kT_sb"),
                      writes=[bufs[s]])
                T.dma("sp", sems[s], V[s][:, r * NBLK:(r + 1) * NBLK, :],
                      A["v_sb_g"][r, :, h * 64:(h + 1) * 64].rearrange("(m p) x -> p m x", p=128), reads=gb(A, "v_sb"),
                      writes=[bufs[s]])
            T.dma("sp", sems[s], QT[s][:], A["qT_sb"][h * 64:(h + 1) * 64, :], writes=[bufs[s]])
            for G in range(4):
                ob = 6 + (G % 2)
                for i in range(3):
                    T.op("pool", lambda e: e.memset(Accb[i][:], 0.0), writes=[Accb_b[i]])
                steps = list(range(16 * G + 15, -1, -1))
                ns = len(steps)
                stt = {}

                def geom(kt):
                    mm_min = max(0, -((-(kt - 16 * G - 3)) // 4))
                    return mm_min, mm_min * 128

                def st_a(n):
                    kt = steps[n]
                    mm_min, c0 = geom(kt)
                    zb = zbanks[n % 3]
                    T.op("pe", lambda e: e.matmul(R.ps[zb][:, c0:512], lhsT=KT[s][:, ktcol(kt):ktcol(kt) + 128],
                                                  rhs=QT[s][:, G * 512 + c0:(G + 1) * 512], start=True, stop=True),
                         reads=[bufs[s]], writes=[R.psb[zb]])

                def st_b(n):
                    kt = steps[n]
                    mm_min, c0 = geom(kt)
                    zb, ie = zbanks[n % 3], n % NE
                    T.op("act", lambda e: e.activation(out=E[ie][:, c0:512], in_=R.ps[zb][:, c0:512], func=AF.Exp, scale=SC),
                         reads=[R.psb[zb]], writes=[E_b[ie]])
                    T.op("act", lambda e: e.activation(out=SP[ie][:, c0:512], in_=E[ie][:, c0:512], func=AF.Ln, bias=1.0),
                         reads=[E_b[ie]], writes=[SP_b[ie]])
                    if kt >= 16 * G:
                        d = (kt - 16 * G) % 4
                        T.op("pool", lambda e: e.tensor_tensor(out=SP[ie][:, c0:c0 + 128], in0=SP[ie][:, c0:c0 + 128],
                                                               in1=C["masks"][:, 16 + d, :], op=ALU.mult),
                             reads=[SP_b[ie], C["masks_b"]], writes=[SP_b[ie]])
                    if n + 1 < ns:
                        T.op("pool", lambda e: e.tensor_tensor(out=Accb[(n + 1) % 3][:, c0:512], in0=Accb[n % 3][:, c0:512],
                                                               in1=SP[ie][:, c0:512], op=ALU.add),
                             reads=[Accb_b[n % 3], SP_b[ie]], writes=[Accb_b[(n + 1) % 3]])

                def st_c(n):
                    kt = steps[n]
                    mm_min, c0 = geom(kt)
                    cb, ie = cbanks[n % 2], n % NE
                    T.op("pe", lambda e: e.matmul(R.ps[cb][:, c0:512], lhsT=tincl, rhs=SP[ie][:, c0:512],
                                                  start=True, stop=False),
                         reads=[SP_b[ie], C["masks_b"]], writes=[R.psb[cb]])
                    T.op("pe", lambda e: e.matmul(R.ps[cb][:, c0:512], lhsT=onesb, rhs=Accb[n % 3][:, c0:512],
                                                  start=False, stop=True),
                         reads=[Accb_b[n % 3], C["masks_b"]], writes=[R.psb[cb]])

                def st_c2(n):
                    kt = steps[n]
                    mm_min, c0 = geom(kt)
                    cb = cbanks[n % 2]
                    T.op("act", lambda e: e.activation(out=X[n % 2][:, c0:512], in_=R.ps[cb][:, c0:512], func=AF.Exp,
                                                       scale=-1.0),
                         reads=[R.psb[cb]], writes=[X_b[n % 2]])

                def st_e(n):
                    kt = steps[n]
                    mm_min, c0 = geom(kt)
                    ie = n % NE
                    pi = C["pn"] % len(C["PT"])
                    C["pn"] += 1
                    stt[n] = pi
                    PT, PT_b = C["PT"][pi], C["PT_b"][pi]
                    T.op("dve", lambda e: e.tensor_tensor(out=PT[:, c0:512], in0=E[ie][:, c0:512], in1=X[n % 2][:, c0:512],
                                                          op=ALU.mult),
                         reads=[E_b[ie], X_b[n % 2]], writes=[PT_b])
                    if kt >= 16 * G:
                        d = (kt - 16 * G) % 4
                        T.op("pool", lambda e: e.tensor_tensor(out=PT[:, c0:c0 + 128], in0=PT[:, c0:c0 + 128],
                                                               in1=C["masks"][:, 16 + d, :], op=ALU.mult),
                             reads=[PT_b, C["masks_b"]], writes=[PT_b])

                def st_f(n, first):
                    kt = steps[n]
                    mm_min, c0 = geom(kt)
                    PT, PT_b = C["PT"][stt[n]], C["PT_b"][stt[n]]
                    for mm in range(mm_min, 4):
                        T.op("pe", lambda e: e.matmul(R.ps[ob][:, mm * 64:(mm + 1) * 64],
                                                      lhsT=PT[:, mm * 128:(mm + 1) * 128], rhs=V[s][:, ktile(kt), :],
                                                      start=(first and mm == mm_min), stop=(kt == 0), skip_group_check=True),
                             reads=[PT_b, bufs[s]], writes=[R.psb[ob]])

                for idx in range(ns + 4):
                    if 2 <= idx <= ns + 1:
                        st_c(idx - 2)
                    if idx < ns:
                        st_a(idx)
                    if 1 <= idx <= ns:
                        st_b(idx - 1)
                    if 2 <= idx <= ns + 1:
                        st_c2(idx - 2)
                    if 3 <= idx <= ns + 2:
                        st_e(idx - 3)
                    if idx >= 4:
                        st_f(idx - 4, idx == 4)
                i = C["ev"] % 2
                C["ev"] += 1
                T.op("dve", lambda e: e.tensor_copy(out=C["on"][i][:],
                                                    in_=R.ps[ob][:, 0:256].rearrange("p (m x) -> p m x", x=64)),
                     reads=[R.psb[ob]], writes=[C["on_b"][i]])
                emit_transpose_store(R, C, C["on"][i], C["on_b"][i], G, 6 + h // 2, (h % 2) * 64)
        T.barrier_all()


L = 2
GATHER = ["kT_mla", "kpeT", "v_mla", "kT_cmp", "vT_cmp", "kT_sel", "kT_win", "v_sel", "v_win", "kT_sb", "v_sb"]
LOCAL = ["qT_mla", "qT_nsa", "qT_sb", "gates"]
POUT = {nm: (shp, dt) for nm, shp, dt in P_OUTS}
W_IN = [("ffn1_w_gate", [L, D, DFF]), ("ffn1_w_up", [L, D, DFF]), ("ffn1_w_down", [L, DFF, D]),
        ("ffn2_w_gate", [L, D, DFF]), ("ffn2_w_up", [L, D, DFF]), ("ffn2_w_down", [L, DFF, D]),
        ("w_in", [L, D, DIN]), ("w_in_sw", [L, D, NSW]), ("w_uq", [L, 256, 576]), ("w_uq_sw", [L, 256, 576]),
        ("w_ukv", [L, 128, 768]), ("smallsP", [L, 128, 32]), ("w_out", [L, D, D]),
        ("cmp_w1_k", [L, 2048, 128]), ("cmp_w1_v", [L, 2048, 128]), ("cmp_w2_k", [L, 128, 64]),
        ("cmp_w2_v", [L, 128, 64]), ("cmp_posT_k", [L, 64, 32]), ("cmp_posT_v", [L, 64, 32]),
        ("gains", [128, 3 * L + 1, 8])]
C_IN = [("cosM", [96, NT], F32), ("sinM", [96, NT], F32), ("cosK", [32, NT], F32), ("sinK", [32, NT], F32),
        ("cosN", [128, NT], F32), ("sinN", [128, NT], F32), ("masks", [128, 22, 128], BF16),
        ("ident", [128, 128], F32), ("Ebig", [128, S], BF16), ("ovl", [128, 4, 129], BF16),
        ("keepadd", [128, 2, 256], F32)]


def emit_layer_X(R, A, l, do_post, do_pre, final, xin, xout, mid_hook=None):
    nc, T = R.nc, R.T
    with ExitStack() as px:
        alloc_xT(R, px)
        hT = px.enter_context(nc.sbuf_tensor(U("hT"), [128, 8, NT], BF16))
        hb = [[Buf("h%d_%d" % (k, t)) for t in range(4)] for k in range(8)]
        gam = px.enter_context(nc.sbuf_tensor(U("gam"), [128, 3 * L + 1, 8], F32))
        gam_b = Buf("gam")
        sq = [px.enter_context(nc.sbuf_tensor(U("sq%d" % i), [128, 512], BF16)) for i in range(2)]
        sq_b = [Buf("sq0"), Buf("sq1")]
        rstd = px.enter_context(nc.sbuf_tensor(U("rstd"), [128, 512], F32))
        rstd_b = Buf("rstd")
        ld = T.new_dma_sem("ldx")
        for k in range(8):
            T.dma("sp", ld, R.xT[:, k, :], xin[k * 128:(k + 1) * 128, :], writes=[R.xb[k][t] for t in range(4)])
        T.dma("sp", ld, gam[:], A["gains"], writes=[gam_b])

        def norm(gi):
            emit_norm(R, hT, hb, gam[:, gi, :], gam_b, sq, sq_b, rstd, rstd_b, 6)

        lp = l
        with ExitStack() as pf:
            R.stack = pf
            W = alloc_ffn_work(R)

            def ffn(pref, ll):
                emit_ffn(R, hT, hb, A[pref + "_w_gate"][ll], A[pref + "_w_up"][ll], A[pref + "_w_down"][ll], W)

            if do_post:
                oT2 = pf.enter_context(nc.sbuf_tensor(U("oT2"), [128, 8, NT], BF16))
                o_b = Buf("oT2")
                d1 = T.new_dma_sem("oT2")
                T.dma("sp", d1, oT2[:], A["oT_d"], writes=[o_b])
                for hf in range(2):
                    T.dma("pool", W["dsem"][hf], W["wd"][hf][:],
                          A["w_out"][l][hf * 512:(hf + 1) * 512, :].rearrange("(k p) c -> p k c", p=128),
                          writes=[W["wb"][hf]])
                n = 0
                for tt in range(4):
                    sl = slice(tt * 512, (tt + 1) * 512)
                    for dmc in range(8):
                        bk = n % 4
                        n += 1
                        for k in range(8):
                            T.op("pe", lambda e: e.matmul(R.ps[bk][:], lhsT=W["wd"][k // 4][:, k % 4, dmc * 128:(dmc + 1) * 128],
                                                          rhs=oT2[:, k, sl], start=(k == 0), stop=(k == 7)),
                                 reads=[o_b, W["wb"][k // 4]], writes=[R.psb[bk]])
                        T.op("dve", lambda e: e.tensor_tensor(out=R.xT[:, dmc, sl], in0=R.ps[bk][:], in1=R.xT[:, dmc, sl],
                                                              op=ALU.add),
                             reads=[R.psb[bk], R.xb[dmc][tt]], writes=[R.xb[dmc][tt]])
                norm(3 * l + 2)
                ffn("ffn2", l)
                lp = l + 1
            if do_pre:
                norm(3 * lp + 0)
                ffn("ffn1", lp)
            T.barrier_all()
        if do_pre:
            norm(3 * lp + 1)
            AP_ = dict(A)
            for nm in ("w_in", "w_in_sw", "w_uq", "w_uq_sw", "w_ukv", "smallsP"):
                AP_[nm] = A[nm][lp]
            emit_stage_P(R, hT, hb, AP_, mid_hook=mid_hook)
        st = T.new_dma_sem("stx")
        if final:
            ysq = [px.enter_context(nc.sbuf_tensor(U("ystg%d" % i), [128, 512], F32)) for i in range(2)]
            y_b = [Buf("y0"), Buf("y1")]
            ss, ss_b = R.ps[6], R.psb[6]
            n = 0
            for tt in range(4):
                sl = slice(tt * 512, (tt + 1) * 512)
                for k in range(8):
                    a = k % 2
                    T.op("act", lambda e: e.activation(out=sq[a][:], in_=R.xT[:, k, sl], func=AF.Square),
                         reads=[R.xb[k][tt]], writes=[sq_b[a]])
                    T.op("pe", lambda e: e.matmul(ss[:], lhsT=R.onesm[:], rhs=sq[a][:], start=(k == 0), stop=(k == 7)),
                         reads=[sq_b[a], R.onesm_b], writes=[ss_b])
                T.op("act", lambda e: e.activation(out=rstd[:], in_=ss[:], func=AF.Sqrt, bias=EPS, scale=1.0 / D),
                     reads=[ss_b], writes=[rstd_b])
                T.op("dve", lambda e: e.reciprocal(out=rstd[:], in_=rstd[:]), reads=[rstd_b], writes=[rstd_b])
                for k in range(8):
                    i = n % 2
                    n += 1
                    T.op("dve", lambda e: e.scalar_tensor_tensor(out=ysq[i][:], in0=R.xT[:, k, sl],
                                                                 scalar=gam[:, 3 * L, k:k + 1], in1=rstd[:],
                                                                 op0=ALU.mult, op1=ALU.mult),
                         reads=[R.xb[k][tt], rstd_b, gam_b], writes=[y_b[i]])
                    T.dma("sp", st, xout[k * 128:(k + 1) * 128, sl], ysq[i][:], reads=[y_b[i]])
        else:
            for k in range(8):
                T.dma("sp", st, xout[k * 128:(k + 1) * 128, :], R.xT[:, k, :], reads=[R.xb[k][t] for t in range(4)])
        T.barrier_all()
        return st


def emit_layer_A(R, A, l):
    nc, T = R.nc, R.T
    with ExitStack() as pa:
        R.oT = pa.enter_context(nc.sbuf_tensor(U("oT"), [128, 8, NT], BF16))
        R.oT_b = [[Buf("oT%d_%d" % (c, g)) for g in range(4)] for c in range(8)]
        AL = dict(A)
        for nm in ("cmp_w1_k", "cmp_w1_v", "cmp_w2_k", "cmp_w2_v", "cmp_posT_k", "cmp_posT_v"):
            AL[nm] = A[nm][l]
        with ExitStack() as ph:
            C = alloc_attn_common(R, AL, ph)
            gen = emit_nsa_compress(R, C, AL)
            next(gen)
            emit_mla(R, C, AL, hook=gen)
            for _ in gen:
                pass
            emit_nsa(R, C, AL)
            emit_sb(R, C, AL)
        st = T.new_dma_sem("stoT")
        T.dma("sp", st, A["oT_d"], R.oT[:], reads=[b for ll in R.oT_b for b in ll])
        T.barrier_all()


def build_launch(kind, l):
    nc = bass.Bass("TRN2", target_bir_lowering=False)
    A = {}

    def din(nm, shp, dt=F32):
        A[nm] = nc.dram_tensor(nm, shp, dt, kind="ExternalInput").ap()

    def dout(nm, shp, dt=F32):
        A[nm] = nc.dram_tensor(nm, shp, dt, kind="ExternalOutput").ap()
    for nm, shp in W_IN:
        din(nm, shp)
    for nm, shp, dt in C_IN:
        din(nm, shp, dt)
    din("xT_in", [D, NT])
    dout("xT_out", [D, NT])
    if kind != "first":
        for nm in GATHER:
            shp, dt = POUT[nm]
            din(nm + "_g", [4] + shp, dt)
        for nm in LOCAL:
            shp, dt = POUT[nm]
            din(nm, shp, dt)
        A["oT_d"] = nc.dram_tensor("oT_d", [128, 8, NT], BF16, kind="Internal").ap()
    if kind != "last":
        for nm, shp, dt in P_OUTS:
            if kind == "first" or nm not in LOCAL:
                dout(nm, shp, dt)
            else:
                A[nm + "_o"] = nc.dram_tensor(nm + "_o", shp, dt, kind="ExternalOutput").ap()
    with ExitStack() as stack:
        T = Tracker(nc, stack)
        R = setup_common(nc, stack, T)
        if kind != "first":
            emit_layer_A(R, A, l)
        AX = dict(A)
        if kind == "mid":
            for nm in LOCAL:
                AX[nm] = A[nm + "_o"]
        st = emit_layer_X(R, AX, l, do_post=(kind != "first"), do_pre=(kind != "last"), final=(kind == "last"),
                          xin=A["xT_in"], xout=A["xT_out"])
        nc.sync.wait_ge(T.sem[st], T.cnt[st])
    return nc


def _host_weights(inp):
    f = lambda a: np.ascontiguousarray(np.asarray(a, dtype=np.float32))
    W = {}
    for nm in ("ffn1_w_gate", "ffn1_w_up", "ffn1_w_down", "ffn2_w_gate", "ffn2_w_up", "ffn2_w_down", "w_in", "w_out"):
        W[nm] = f(inp[nm])
    W["w_uq"] = f(inp["mla_w_uq"])
    W["w_ukv"] = f(inp["mla_w_ukv"])
    W["w_in_sw"] = np.stack([make_w_in_sw(W["w_in"][l]) for l in range(L)])
    W["w_uq_sw"] = np.stack([swap_cols_rope(W["w_uq"][l], 96, 64, 32) for l in range(L)])
    W["smallsP"] = np.stack([make_smallsP(f(inp["mla_q_norm"])[l], f(inp["mla_kv_norm"])[l], f(inp["nsa_gate_bias"])[l])
                             for l in range(L)])
    W["cmp_w1_k"] = f(inp["nsa_cmp_w1_k"])
    W["cmp_w1_v"] = f(inp["nsa_cmp_w1_v"])
    W["cmp_w2_k"] = f(inp["nsa_cmp_w2_k"])
    W["cmp_w2_v"] = f(inp["nsa_cmp_w2_v"])
    W["cmp_posT_k"] = np.ascontiguousarray(f(inp["nsa_cmp_pos_k"]).transpose(0, 2, 1))
    W["cmp_posT_v"] = np.ascontiguousarray(f(inp["nsa_cmp_pos_v"]).transpose(0, 2, 1))
    g = np.zeros((128, 3 * L + 1, 8), np.float32)
    for l in range(L):
        for i, nm in enumerate(("ffn1_norm", "mix_norm", "ffn2_norm")):
            g[:, 3 * l + i, :] = f(inp[nm])[l].reshape(8, 128).T
    g[:, 3 * L, :] = f(inp["final_norm"]).reshape(8, 128).T
    W["gains"] = g
    return W


def _core_consts(core):
    c = {}
    c.update(rope_tables(core))
    c["masks"] = make_masks(core)
    c["ident"] = np.eye(128, dtype=np.float32)
    c.update(make_nsa_consts(core))
    return c


def _kernel_unfused_impl(**inp):
    x = np.asarray(inp["x"], dtype=np.float32)
    W = _host_weights(inp)
    consts = [_core_consts(c) for c in range(8)]
    xT = [np.ascontiguousarray(x[c // 4][own_positions(c)].T) for c in range(8)]
    cores = list(range(8))
    nc = build_launch("first", 0)
    ims = [dict(W, **consts[c], xT_in=xT[c]) for c in cores]
    res = run_bass_kernel_spmd(nc, ims, core_ids=cores).results
    for l in range(L):
        kind = "mid" if l < L - 1 else "last"
        nc = build_launch(kind, l)
        ims = []
        for c in cores:
            b = c // 4
            im = dict(W, **consts[c], xT_in=np.asarray(res[c]["xT_out"]))
            for nm in GATHER:
                im[nm + "_g"] = np.stack([np.asarray(res[4 * b + r][nm]) for r in range(4)])
            for nm in LOCAL:
                key = nm if l == 0 else nm + "_o"
                im[nm] = np.asarray(res[c][key])
            ims.append(im)
        res = run_bass_kernel_spmd(nc, ims, core_ids=cores).results
    out = np.zeros((B, S, D), np.float32)
    for c in cores:
        out[c // 4][own_positions(c)] = np.asarray(res[c]["xT_out"]).T
    return out


PIECES = [
    ([192, NT], [("kT_mla", "heads", 0, 3)]),
    ([192, NT], [("kT_mla", "heads", 3, 6)]),
    ([32, NT], [("kpeT", "rows", 0, 32)]),
    ([256, NT], [("vT_cmp", "rows", 0, 128), ("kT_sel", "rows", 128, 256)]),
    ([256, NT], [("kT_cmp", "rows", 0, 128), ("kT_win", "rows", 128, 256)]),
    ([256, NT], [("kT_sb", "rows", 0, 256)]),
    ([1024, 6, 65], [("v_mla", "half", 0, 0)]),
    ([1024, 6, 65], [("v_mla", "half", 1, 1)]),
    ([NT, 2, 65], [("v_sel", "all", 0, 0)]),
    ([NT, 2, 65], [("v_win", "all", 0, 0)]),
    ([NT, 256], [("v_sb", "all", 0, 0)]),
]


def make_pieces(nc, l):
    V = {"kT_mla": [None] * 6, "kT_mla_g": [None] * 6, "v_mla": [None] * 2, "v_mla_g": [None] * 2}
    GB = {"kT_mla": [None] * 6, "v_mla": [None] * 2}
    cc = []
    for k, (shp, members) in enumerate(PIECES):
        n = int(np.prod(shp))
        w = n // 128
        gs = nc.dram_tensor("gs%d_%d" % (l, k), [128, w], BF16, kind="Internal").ap()
        gd = nc.dram_tensor("gd%d_%d" % (l, k), [512, w], BF16, kind="Internal").ap()
        pb = Buf("piece%d_%d" % (l, k))
        cc.append((gs, gd, pb))
        fs = gs.rearrange("p w -> (p w)")
        fd = gd.rearrange("(r p) w -> r (p w)", r=4)
        if len(shp) == 2:
            ns = fs.rearrange("(a c) -> a c", c=shp[1])
            nd = fd.rearrange("r (a c) -> r a c", c=shp[1])
        else:
            ns = fs.rearrange("(t h x) -> t h x", h=shp[1], x=shp[2])
            nd = fd.rearrange("r (t h x) -> r t h x", h=shp[1], x=shp[2])
        for nm, kind, lo, hi in members:
            if kind == "heads":
                for h in range(lo, hi):
                    V[nm][h] = ns[(h - lo) * 64:(h - lo + 1) * 64, :]
                    V[nm + "_g"][h] = nd[:, (h - lo) * 64:(h - lo + 1) * 64, :]
                    GB[nm][h] = pb
            elif kind == "rows":
                V[nm] = ns[lo:hi, :]
                V[nm + "_g"] = nd[:, lo:hi, :]
                GB[nm] = pb
            elif kind == "half":
                V[nm][lo] = ns
                V[nm + "_g"][lo] = nd
                GB[nm][lo] = pb
            else:
                V[nm] = ns
                V[nm + "_g"] = nd
                GB[nm] = pb
    V["cc"] = cc
    V["gb"] = GB
    return V


def build_fused():
    nc = bass.Bass("TRN2", target_bir_lowering=False)
    A = {}
    for nm, shp in W_IN:
        A[nm] = nc.dram_tensor(nm, shp, F32, kind="ExternalInput").ap()
    for nm, shp, dt in C_IN:
        A[nm] = nc.dram_tensor(nm, shp, dt, kind="ExternalInput").ap()
    A["xT_in"] = nc.dram_tensor("xT_in", [D, NT], F32, kind="ExternalInput").ap()
    A["yT_out"] = nc.dram_tensor("yT_out", [D, NT], F32, kind="ExternalOutput").ap()
    A["xT_d"] = nc.dram_tensor("xT_d", [D, NT], F32, kind="Internal").ap()
    A["oT_d"] = nc.dram_tensor("oT_d", [128, 8, NT], BF16, kind="Internal").ap()
    LA = []
    for l in range(L):
        V = make_pieces(nc, l)
        for nm in LOCAL:
            shp, dt = POUT[nm]
            V[nm] = nc.dram_tensor("%s_l%d" % (nm, l), shp, dt, kind="Internal").ap()
        LA.append(V)
    with ExitStack() as stack:
        T = Tracker(nc, stack)
        R = setup_common(nc, stack, T)
        st = None
        GRP = [[0, 1, 2, 3], [4, 5, 6, 7]]

        def mk_hook(ll):
            def hook():
                T.wait_dma_all("pool")
                for k in (2, 0, 6, 7, 1):
                    gs, gd, pb = LA[ll]["cc"][k]
                    T.collective(T.new_dma_sem("cc%d_%d" % (ll, k)), gs, gd, GRP, writes=[pb])
            return hook

        for l in range(L):
            if l == 0:
                emit_layer_X(R, dict(A, **LA[0]), 0, do_post=False, do_pre=True, final=False,
                             xin=A["xT_in"], xout=A["xT_d"], mid_hook=mk_hook(0))
            T.barrier_all()
            for k in (4, 3, 8, 9, 5, 10):
                gs, gd, pb = LA[l]["cc"][k]
                T.collective(T.new_dma_sem("cc%d_%d" % (l, k)), gs, gd, GRP, writes=[pb])
            emit_layer_A(R, dict(A, **LA[l]), l)
            last = (l == L - 1)
            AX = dict(A, **(LA[l + 1] if not last else {}))
            st = emit_layer_X(R, AX, l, do_post=True, do_pre=not last, final=last,
                              xin=A["xT_d"], xout=(A["yT_out"] if last else A["xT_d"]),
                              mid_hook=(None if last else mk_hook(l + 1)))
        nc.sync.wait_ge(T.sem[st], T.cnt[st])
    return nc


def kernel_unfused(**inp):
    return _kernel_unfused_impl(**inp)


def kernel_fused(**inp):
    x = np.asarray(inp["x"], dtype=np.float32)
    W = _host_weights(inp)
    cores = list(range(8))
    nc = build_fused()
    ims = []
    for c in cores:
        xT = np.ascontiguousarray(x[c // 4][own_positions(c)].T)
        ims.append(dict(W, **_core_consts(c), xT_in=xT))
    res = run_bass_kernel_spmd(nc, ims, core_ids=cores).results
    out = np.zeros((B, S, D), np.float32)
    for c in cores:
        out[c // 4][own_positions(c)] = np.asarray(res[c]["yT_out"]).T
    return out


def kernel(**inp):
    return kernel_fused(**inp)
```

```python
import numpy as np
import ml_dtypes
from contextlib import ExitStack
import concourse.bass as bass
import concourse.mybir as mybir
from concourse.bass_utils import run_bass_kernel_spmd

F32 = mybir.dt.float32
BF16 = mybir.dt.bfloat16
AF = mybir.ActivationFunctionType
ALU = mybir.AluOpType
AX = mybir.AxisListType

D = 1024
S = 8192
B = 2
DFF = 2816
NT = 2048
NBLK = 16
EPS = 1e-6
DIN = 2354


_UN = [0]
CC_INC = 1


def U(name):
    _UN[0] += 1
    return "%s_u%d" % (name, _UN[0])


class Buf:
    __slots__ = ("name", "w", "r")

    def __init__(self, name):
        self.name = name
        self.w = None
        self.r = {}


class Tracker:
    def __init__(self, nc, stack):
        self.nc = nc
        self.stack = stack
        self.eng = {"pe": nc.tensor, "act": nc.scalar, "dve": nc.vector,
                    "pool": nc.gpsimd, "sp": nc.sync}
        self.sem = {}
        self.cnt = {}
        self.seen = {k: {} for k in self.eng}
        for k in self.eng:
            self.sem[k] = stack.enter_context(nc.semaphore("s_" + k))
            self.cnt[k] = 0
        self.ndma = 0

    def new_dma_sem(self, name):
        key = "dma_" + name + "_%d" % self.ndma
        self.ndma += 1
        self.sem[key] = self.stack.enter_context(self.nc.semaphore(key))
        self.cnt[key] = 0
        return key

    def _deps(self, e, reads, writes, ignore=None):
        deps = {}

        def add(k, c):
            if c > deps.get(k, 0):
                deps[k] = c
        for b in reads:
            if b.w is not None:
                add(*b.w)
        for b in writes:
            if b.w is not None:
                add(*b.w)
            for k, c in b.r.items():
                add(k, c)
        for k, c in deps.items():
            if (k == "pe" and e == "pe") or k == ignore:
                continue
            if k.startswith("dma_"):
                c = max(c, self.cnt[k])
            if c > self.seen[e].get(k, 0):
                self.eng[e].wait_ge(self.sem[k], c)
                self.seen[e][k] = c

    def op(self, e, fn, reads=(), writes=()):
        self._deps(e, reads, writes)
        ins = fn(self.eng[e])
        self.cnt[e] += 1
        c = self.cnt[e]
        ins.then_inc(self.sem[e], 1)
        for b in reads:
            if c > b.r.get(e, 0):
                b.r[e] = c
        for b in writes:
            b.w = (e, c)
            b.r = {}
        return ins

    def dma(self, q, dsem, out, in_, reads=(), writes=()):
        self._deps(q, reads, writes, ignore=dsem)
        ins = self.eng[q].dma_start(out=out, in_=in_)
        self.cnt[dsem] += 16
        c = self.cnt[dsem]
        ins.then_inc(self.sem[dsem], 16)
        for b in reads:
            if c > b.r.get(dsem, 0):
                b.r[dsem] = c
        for b in writes:
            b.w = (dsem, c)
            b.r = {}
        return ins

    def collective(self, dsem, src, dst, groups, reads=(), writes=()):
        self._deps("pool", reads, writes, ignore=dsem)
        ins = self.nc.gpsimd.collective_compute("AllGather", ALU.bypass, replica_groups=groups,
                                                ins=[src.opt()], outs=[dst.opt()])
        self.cnt[dsem] += CC_INC
        c = self.cnt[dsem]
        ins.then_inc(self.sem[dsem], CC_INC)
        for b in reads:
            if c > b.r.get(dsem, 0):
                b.r[dsem] = c
        for b in writes:
            b.w = (dsem, c)
            b.r = {}
        return ins

    def wait_dma_all(self, e):
        for k, c in self.cnt.items():
            if k.startswith("dma_") and c > self.seen[e].get(k, 0):
                self.eng[e].wait_ge(self.sem[k], c)
                self.seen[e][k] = c

    def barrier_all(self):
        for e in self.eng:
            for k, c in self.cnt.items():
                if k == e or c == 0:
                    continue
                if c > self.seen[e].get(k, 0):
                    self.eng[e].wait_ge(self.sem[k], c)
                    self.seen[e][k] = c


class Res:
    pass


def setup_common(nc, stack, T):
    R = Res()
    R.nc, R.T, R.stack = nc, T, stack
    R.ExitStack = ExitStack
    R.ps = []
    R.psb = []
    for i in range(8):
        R.ps.append(stack.enter_context(nc.psum_tensor("ps%d" % i, [128, 512], F32)))
        R.psb.append(Buf("ps%d" % i))
    R.xb = [[Buf("x%d_%d" % (k, t)) for t in range(4)] for k in range(8)]
    R.onesm = stack.enter_context(nc.sbuf_tensor(U("onesm"), [128, 128], BF16))
    R.onesm_b = Buf("onesm")
    T.op("pool", lambda e: e.memset(R.onesm[:], 1.0), writes=[R.onesm_b])
    return R


def alloc_xT(R, stack):
    R.xT = stack.enter_context(R.nc.sbuf_tensor(U("xT"), [128, 8, NT], F32))


def emit_norm(R, hT, hb, gam, gam_b, sq, sq_b, rstd, rstd_b, ssbank):
    T = R.T
    ss, ss_b = R.ps[ssbank], R.psb[ssbank]
    for tt in range(4):
        sl = slice(tt * 512, (tt + 1) * 512)
        for k in range(8):
            a = k % 2
            T.op("act", lambda e: e.activation(out=sq[a][:], in_=R.xT[:, k, sl], func=AF.Square),
                 reads=[R.xb[k][tt]], writes=[sq_b[a]])
            T.op("pe", lambda e: e.matmul(ss[:], lhsT=R.onesm[:], rhs=sq[a][:], start=(k == 0), stop=(k == 7)),
                 reads=[sq_b[a], R.onesm_b], writes=[ss_b])
        T.op("act", lambda e: e.activation(out=rstd[:], in_=ss[:], func=AF.Sqrt, bias=EPS, scale=1.0 / D),
             reads=[ss_b], writes=[rstd_b])
        T.op("dve", lambda e: e.reciprocal(out=rstd[:], in_=rstd[:]), reads=[rstd_b], writes=[rstd_b])
        for k in range(8):
            T.op("dve", lambda e: e.scalar_tensor_tensor(out=hT[:, k, sl], in0=R.xT[:, k, sl],
                                                         scalar=gam[:, k:k + 1], in1=rstd[:],
                                                         op0=ALU.mult, op1=ALU.mult),
                 reads=[R.xb[k][tt], rstd_b, gam_b], writes=[hb[k][tt]])


def emit_ffn(R, hT, hb, wg_d, wu_d, wd_d, W):
    T = R.T
    nfg = 6
    n_g = n_y = 0
    for fg in range(nfg):
        ncf = 4 if fg < 5 else 2
        wcols = ncf * 128
        c0 = fg * 512
        s = fg % 2
        wgs, wus, wds, wb, dsem = W["wg"][s], W["wu"][s], W["wd"][s], W["wb"][s], W["dsem"][s]
        T.dma("pool", dsem, wgs[:, :, 0:wcols],
              wg_d[:, c0:c0 + wcols].rearrange("(k p) c -> p k c", p=128), writes=[wb])
        T.dma("pool", dsem, wus[:, :, 0:wcols],
              wu_d[:, c0:c0 + wcols].rearrange("(k p) c -> p k c", p=128), writes=[wb])
        T.dma("pool", dsem, wds[:, 0:ncf, :],
              wd_d[c0:c0 + wcols, :].rearrange("(c p) m -> p c m", p=128), writes=[wb])
        for tt in range(4):
            sl = slice(tt * 512, (tt + 1) * 512)
            asl = (fg * 4 + tt) % 2
            for c in range(ncf):
                gi, ui = W["gbanks"][n_g % 2], W["ubanks"][n_g % 2]
                sgi = n_g % 2
                n_g += 1
                for k in range(8):
                    T.op("pe", lambda e: e.matmul(R.ps[gi][:], lhsT=wgs[:, k, c * 128:(c + 1) * 128],
                                                  rhs=hT[:, k, sl], start=(k == 0), stop=(k == 7)),
                         reads=[wb, hb[k][tt]], writes=[R.psb[gi]])
                for k in range(8):
                    T.op("pe", lambda e: e.matmul(R.ps[ui][:], lhsT=wus[:, k, c * 128:(c + 1) * 128],
                                                  rhs=hT[:, k, sl], start=(k == 0), stop=(k == 7)),
                         reads=[wb, hb[k][tt]], writes=[R.psb[ui]])
                T.op("act", lambda e: e.activation(out=W["sg"][sgi][:], in_=R.ps[gi][:], func=AF.Silu),
                     reads=[R.psb[gi]], writes=[W["sg_b"][sgi]])
                T.op("dve", lambda e: e.tensor_tensor(out=W["act"][asl][:, c, :], in0=W["sg"][sgi][:],
                                                      in1=R.ps[ui][:], op=ALU.mult),
                     reads=[W["sg_b"][sgi], R.psb[ui]], writes=[W["act_b"][asl][c]])
            for dmc in range(8):
                yi = W["ybanks"][n_y % 2]
                n_y += 1
                for c in range(ncf):
                    T.op("pe", lambda e: e.matmul(R.ps[yi][:], lhsT=wds[:, c, dmc * 128:(dmc + 1) * 128],
                                                  rhs=W["act"][asl][:, c, :], start=(c == 0), stop=(c == ncf - 1)),
                         reads=[wb, W["act_b"][asl][c]], writes=[R.psb[yi]])
                T.op("dve", lambda e: e.scalar_tensor_tensor(out=R.xT[:, dmc, sl], in0=R.ps[yi][:], scalar=0.5,
                                                             in1=R.xT[:, dmc, sl], op0=ALU.mult, op1=ALU.add),
                     reads=[R.psb[yi], R.xb[dmc][tt]], writes=[R.xb[dmc][tt]])


def alloc_ffn_work(R):
    nc, stack, T = R.nc, R.stack, R.T
    W = {}
    W["wg"] = [stack.enter_context(nc.sbuf_tensor(U("wg%d" % i), [128, 8, 512], BF16)) for i in range(2)]
    W["wu"] = [stack.enter_context(nc.sbuf_tensor(U("wu%d" % i), [128, 8, 512], BF16)) for i in range(2)]
    W["wd"] = [stack.enter_context(nc.sbuf_tensor(U("wd%d" % i), [128, 4, 1024], BF16)) for i in range(2)]
    W["wb"] = [Buf("wslot%d" % i) for i in range(2)]
    W["dsem"] = [T.new_dma_sem("ffnw%d" % i) for i in range(2)]
    W["sg"] = [stack.enter_context(nc.sbuf_tensor(U("sg%d" % i), [128, 512], F32)) for i in range(2)]
    W["sg_b"] = [Buf("sg%d" % i) for i in range(2)]
    W["act"] = [stack.enter_context(nc.sbuf_tensor(U("act%d" % i), [128, 4, 512], BF16)) for i in range(2)]
    W["act_b"] = [[Buf("act%d_%d" % (i, c)) for c in range(4)] for i in range(2)]
    W["gbanks"], W["ubanks"], W["ybanks"] = [0, 1], [2, 3], [4, 5]
    return W


C_CQ, C_CKV, C_KR, C_NQ, C_NKC, C_NVC, C_NKS, C_NVS, C_NKW, C_NVW, C_NG, C_SQ, C_SK, C_SV = (
    0, 256, 384, 416, 800, 928, 1056, 1184, 1312, 1440, 1568, 1586, 1842, 2098)
SW_KR, SW_NQ, SW_NKC, SW_NKS, SW_NKW = 0, 32, 416, 544, 672
NSW = 800


def emit_stage_P(R, hT, hb, A, mid_hook=None):
    nc, T = R.nc, R.T
    with R.ExitStack() as ph:
        def sb(name, shape, dt):
            return ph.enter_context(nc.sbuf_tensor(U(name), shape, dt))
        win = sb("P_win", [128, 8, DIN], BF16)
        wsw = sb("P_wsw", [128, 8, NSW], BF16)
        wuq = sb("P_wuq", [128, 2, 576], BF16)
        wuqs = sb("P_wuqs", [128, 2, 576], BF16)
        wukv = sb("P_wukv", [128, 768], BF16)
        sm = sb("P_sm", [128, 32], F32)
        wb_, smb = Buf("P_w"), Buf("P_sm")
        dw = T.new_dma_sem("Pw")
        T.dma("pool", dw, win[:], A["w_in"].rearrange("(k p) c -> p k c", p=128), writes=[wb_])
        T.dma("pool", dw, wsw[:], A["w_in_sw"].rearrange("(k p) c -> p k c", p=128), writes=[wb_])
        T.dma("pool", dw, wuq[:], A["w_uq"].rearrange("(k p) c -> p k c", p=128), writes=[wb_])
        T.dma("pool", dw, wuqs[:], A["w_uq_sw"].rearrange("(k p) c -> p k c", p=128), writes=[wb_])
        T.dma("pool", dw, wukv[:], A["w_ukv"], writes=[wb_])
        dw2 = T.new_dma_sem("Psm")
        T.dma("sp", dw2, sm[:], A["smallsP"], writes=[smb])
        tabs = []
        for i in range(2):
            tabs.append(dict(
                cosM=sb("P_cosM%d" % i, [96, 512], F32), sinM=sb("P_sinM%d" % i, [96, 512], F32),
                cosK=sb("P_cosK%d" % i, [32, 512], F32), sinK=sb("P_sinK%d" % i, [32, 512], F32),
                cosN=sb("P_cosN%d" % i, [128, 512], F32), sinN=sb("P_sinN%d" % i, [128, 512], F32),
                b=Buf("P_tab%d" % i), sem=T.new_dma_sem("Ptab%d" % i)))
        sq = [sb("P_sq%d" % i, [128, 512], BF16) for i in range(2)]
        sq_b = [Buf("P_sq0"), Buf("P_sq1")]
        rs = sb("P_rs", [128, 512], F32)
        rs_b = Buf("P_rs")
        cqn = sb("P_cqn", [128, 2, 512], BF16)
        cqn_b = Buf("P_cqn")
        ckvn = sb("P_ckvn", [128, 512], BF16)
        ckvn_b = Buf("P_ckvn")
        t1 = [sb("P_t1_%d" % i, [128, 512], F32) for i in range(2)]
        t2 = [sb("P_t2_%d" % i, [128, 512], F32) for i in range(2)]
        t_b = [Buf("P_t0"), Buf("P_t1")]
        NST = 4
        stg = [sb("P_stg%d" % i, [128, 512], BF16) for i in range(NST)]
        stg_b = [Buf("P_stg%d" % i) for i in range(NST)]
        stg_sem = [T.new_dma_sem("Pstg%d" % i) for i in range(NST)]
        vst = [sb("P_vst%d" % i, [128, 6, 65], BF16) for i in range(2)]
        vst_b = [Buf("P_vst0"), Buf("P_vst1")]
        vst_sem = [T.new_dma_sem("Pvst%d" % i) for i in range(2)]
        v2st = [sb("P_v2st%d" % i, [128, 2, 2, 65], BF16) for i in range(2)]
        v2st_b = [Buf("P_v2st0"), Buf("P_v2st1")]
        v2st_sem = [T.new_dma_sem("Pv2st%d" % i) for i in range(2)]
        vsb = [sb("P_vsb%d" % i, [128, 256], BF16) for i in range(2)]
        vsb_b = [Buf("P_vsb0"), Buf("P_vsb1")]
        vsb_sem = [T.new_dma_sem("Pvsb%d" % i) for i in range(2)]
        gst = [sb("P_gst%d" % i, [128, 18], F32) for i in range(2)]
        gst_b = [Buf("P_gst0"), Buf("P_gst1")]
        gst_sem = [T.new_dma_sem("Pgst%d" % i) for i in range(2)]
        for i in range(2):
            T.op("pool", lambda e: e.memset(vst[i][:], 1.0), writes=[vst_b[i]])
            T.op("pool", lambda e: e.memset(v2st[i][:], 1.0), writes=[v2st_b[i]])

        cnt = {"bank": 0, "stg": 0, "t": 0}

        def nbank():
            cnt["bank"] += 1
            return cnt["bank"] % 4

        def chain(bank, M, lhs_fn, rhs_fn, nk, reads, N=512, rows=None):
            o = R.ps[bank][0:M, 0:N] if rows is None else R.ps[bank][rows[0]:rows[1], 0:N]
            for k in range(nk):
                T.op("pe", lambda e: e.matmul(o, lhsT=lhs_fn(k), rhs=rhs_fn(k), start=(k == 0), stop=(k == nk - 1)),
                     reads=reads(k), writes=[R.psb[bank]])

        def store(src_ps_ap, src_bufs, M, dst_dram, use_act=False):
            i = cnt["stg"] % NST
            cnt["stg"] += 1
            eng = "act" if use_act else "dve"
            if use_act:
                T.op("act", lambda e: e.activation(out=stg[i][0:M, :], in_=src_ps_ap, func=AF.Copy),
                     reads=src_bufs, writes=[stg_b[i]])
            else:
                T.op("dve", lambda e: e.tensor_copy(out=stg[i][0:M, :], in_=src_ps_ap),
                     reads=src_bufs, writes=[stg_b[i]])
            T.dma("sp", stg_sem[i], dst_dram, stg[i][0:M, :], reads=[stg_b[i]])

        def rope_store(bankA, bankB, M, cos, sin, tb, dst_dram):
            j = cnt["t"] % 2
            cnt["t"] += 1
            i = cnt["stg"] % NST
            cnt["stg"] += 1
            T.op("dve", lambda e: e.tensor_tensor(out=t1[j][0:M, :], in0=R.ps[bankA][0:M, :], in1=cos, op=ALU.mult),
                 reads=[R.psb[bankA], tb], writes=[t_b[j]])
            T.op("dve", lambda e: e.tensor_tensor(out=t2[j][0:M, :], in0=R.ps[bankB][0:M, :], in1=sin, op=ALU.mult),
                 reads=[R.psb[bankB], tb], writes=[t_b[j]])
            T.op("pool", lambda e: e.tensor_tensor(out=stg[i][0:M, :], in0=t1[j][0:M, :], in1=t2[j][0:M, :], op=ALU.add),
                 reads=[t_b[j]], writes=[stg_b[i]])
            T.dma("sp", stg_sem[i], dst_dram, stg[i][0:M, :], reads=[stg_b[i]])

        for tt in range(4):
            sl = slice(tt * 512, (tt + 1) * 512)
            tab = tabs[tt % 2]
            for nm in ("cosM", "sinM", "cosK", "sinK"):
                T.dma("sp", tab["sem"], tab[nm][:], A[nm][:, sl], writes=[tab["b"]])
            hreads = lambda k: [wb_, hb[k][tt]]
            for c in range(2):
                chain(4 + c, 128, lambda k: win[:, k, C_CQ + c * 128:C_CQ + (c + 1) * 128],
                      lambda k: hT[:, k, sl], 8, hreads)
            chain(6, 128, lambda k: win[:, k, C_CKV:C_CKV + 128], lambda k: hT[:, k, sl], 8, hreads)
            for c in range(2):
                T.op("act", lambda e: e.activation(out=sq[c][:], in_=R.ps[4 + c][:], func=AF.Square),
                     reads=[R.psb[4 + c]], writes=[sq_b[c]])
            for c in range(2):
                T.op("pe", lambda e: e.matmul(R.ps[7][:], lhsT=R.onesm[:], rhs=sq[c][:], start=(c == 0), stop=(c == 1)),
                     reads=[sq_b[c], R.onesm_b], writes=[R.psb[7]])
            T.op("act", lambda e: e.activation(out=rs[:], in_=R.ps[7][:], func=AF.Sqrt, bias=EPS, scale=1.0 / 256),
                 reads=[R.psb[7]], writes=[rs_b])
            T.op("dve", lambda e: e.reciprocal(out=rs[:], in_=rs[:]), reads=[rs_b], writes=[rs_b])
            for c in range(2):
                T.op("dve", lambda e: e.scalar_tensor_tensor(out=cqn[:, c, :], in0=R.ps[4 + c][:], scalar=sm[:, c:c + 1],
                                                             in1=rs[:], op0=ALU.mult, op1=ALU.mult),
                     reads=[R.psb[4 + c], rs_b, smb], writes=[cqn_b])
            T.op("act", lambda e: e.activation(out=sq[0][:], in_=R.ps[6][:], func=AF.Square),
                 reads=[R.psb[6]], writes=[sq_b[0]])
            T.op("pe", lambda e: e.matmul(R.ps[7][:], lhsT=R.onesm[:], rhs=sq[0][:], start=True, stop=True),
                 reads=[sq_b[0], R.onesm_b], writes=[R.psb[7]])
            T.op("act", lambda e: e.activation(out=rs[:], in_=R.ps[7][:], func=AF.Sqrt, bias=EPS, scale=1.0 / 128),
                 reads=[R.psb[7]], writes=[rs_b])
            T.op("dve", lambda e: e.reciprocal(out=rs[:], in_=rs[:]), reads=[rs_b], writes=[rs_b])
            T.op("dve", lambda e: e.scalar_tensor_tensor(out=ckvn[:], in0=R.ps[6][:], scalar=sm[:, 2:3],
                                                         in1=rs[:], op0=ALU.mult, op1=ALU.mult),
                 reads=[R.psb[6], rs_b, smb], writes=[ckvn_b])
            for h in range(6):
                ba, bb = nbank(), 4 + (h % 2)
                chain(ba, 96, lambda c: wuq[:, c, h * 96:(h + 1) * 96], lambda c: cqn[:, c, :], 2,
                      lambda c: [wb_, cqn_b])
                chain(bb, 96, lambda c: wuqs[:, c, h * 96:(h + 1) * 96], lambda c: cqn[:, c, :], 2,
                      lambda c: [wb_, cqn_b])
                rope_store(ba, bb, 96, tab["cosM"][:], tab["sinM"][:], tab["b"], A["qT_mla"][h, :, sl])
            for h in range(6):
                ba = nbank()
                chain(ba, 64, lambda c: wukv[:, h * 128:h * 128 + 64], lambda c: ckvn[:], 1, lambda c: [wb_, ckvn_b])
                store(R.ps[ba][0:64, :], [R.psb[ba]], 64, A["kT_mla"][h][:, sl], use_act=(h % 2 == 0))
            ba, bb = nbank(), 4
            chain(ba, 32, lambda k: win[:, k, C_KR:C_KR + 32], lambda k: hT[:, k, sl], 8, hreads)
            chain(bb, 32, lambda k: wsw[:, k, SW_KR:SW_KR + 32], lambda k: hT[:, k, sl], 8, hreads)
            rope_store(ba, bb, 32, tab["cosK"][:], tab["sinK"][:], tab["b"], A["kpeT"][:, sl])
            for bl in range(4):
                tb = tt * 4 + bl
                lsl = slice(bl * 128, (bl + 1) * 128)
                i = tb % 2
                ba = nbank()
                T.op("pe", lambda e: e.matmul(R.ps[ba][:, 0:384], lhsT=ckvn[:, lsl],
                                              rhs=wukv[:].rearrange("p (h x) -> p h x", x=128)[:, :, 64:128],
                                              start=True, stop=True),
                     reads=[wb_, ckvn_b], writes=[R.psb[ba]])
                T.op("dve", lambda e: e.tensor_copy(out=vst[i][:, :, 0:64],
                                                    in_=R.ps[ba][:, 0:384].rearrange("p (h x) -> p h x", x=64)),
                     reads=[R.psb[ba]], writes=[vst_b[i]])
                T.dma("sp", vst_sem[i], A["v_mla"][tb // 8][(tb % 8) * 128:(tb % 8 + 1) * 128, :, :], vst[i][:], reads=[vst_b[i]])
        if mid_hook is not None:
            mid_hook()
        for tt in range(4):
            sl = slice(tt * 512, (tt + 1) * 512)
            tab = tabs[tt % 2]
            for nm in ("cosN", "sinN"):
                T.dma("sp", tab["sem"], tab[nm][:], A[nm][:, sl], writes=[tab["b"]])
            hreads = lambda k: [wb_, hb[k][tt]]
            ropes = [(C_NQ + 128 * c, SW_NQ + 128 * c, A["qT_nsa"][128 * c:128 * (c + 1), sl]) for c in range(3)]
            ropes += [(C_NKC, SW_NKC, A["kT_cmp"][:, sl]), (C_NKS, SW_NKS, A["kT_sel"][:, sl]),
                      (C_NKW, SW_NKW, A["kT_win"][:, sl])]
            for n, (ca, cs, dst) in enumerate(ropes):
                ba, bb = nbank(), 4 + (n % 2)
                chain(ba, 128, lambda k: win[:, k, ca:ca + 128], lambda k: hT[:, k, sl], 8, hreads)
                chain(bb, 128, lambda k: wsw[:, k, cs:cs + 128], lambda k: hT[:, k, sl], 8, hreads)
                rope_store(ba, bb, 128, tab["cosN"][:], tab["sinN"][:], tab["b"], dst)
            plain = [(C_NVC, A["vT_cmp"][:, sl]), (C_SQ, A["qT_sb"][0:128, sl]), (C_SQ + 128, A["qT_sb"][128:256, sl]),
                     (C_SK, A["kT_sb"][0:128, sl]), (C_SK + 128, A["kT_sb"][128:256, sl])]
            for n, (ca, dst) in enumerate(plain):
                ba = nbank()
                chain(ba, 128, lambda k: win[:, k, ca:ca + 128], lambda k: hT[:, k, sl], 8, hreads)
                store(R.ps[ba][:, :], [R.psb[ba]], 128, dst, use_act=(n % 2 == 0))
            for bl in range(4):
                tb = tt * 4 + bl
                tsl = slice(tb * 128, (tb + 1) * 128)
                lsl = slice(bl * 128, (bl + 1) * 128)
                i = tb % 2
                ba = nbank()
                for n, ca in enumerate((C_NVS, C_NVW)):
                    for k in range(8):
                        T.op("pe", lambda e: e.matmul(R.ps[ba][:, n * 128:(n + 1) * 128], lhsT=hT[:, k, tsl],
                                                      rhs=win[:, k, ca:ca + 128], start=(k == 0), stop=(k == 7)),
                             reads=[wb_, hb[k][tt]], writes=[R.psb[ba]])
                for k in range(8):
                    T.op("pe", lambda e: e.matmul(R.ps[ba][:, 256:274], lhsT=hT[:, k, tsl],
                                                  rhs=win[:, k, C_NG:C_NG + 18], start=(k == 0), stop=(k == 7)),
                         reads=[wb_, hb[k][tt]], writes=[R.psb[ba]])
                T.op("dve", lambda e: e.tensor_copy(out=v2st[i][:, :, :, 0:64],
                                                    in_=R.ps[ba][:, 0:256].rearrange("p (a h x) -> p a h x", a=2, x=64)),
                     reads=[R.psb[ba]], writes=[v2st_b[i]])
                T.dma("sp", v2st_sem[i], A["v_sel"][tsl, :, :], v2st[i][:, 0, :, :], reads=[v2st_b[i]])
                T.dma("sp", v2st_sem[i], A["v_win"][tsl, :, :], v2st[i][:, 1, :, :], reads=[v2st_b[i]])
                T.op("dve", lambda e: e.tensor_tensor(out=gst[i][:], in0=R.ps[ba][:, 256:274], in1=sm[:, 8:26], op=ALU.add),
                     reads=[R.psb[ba], smb], writes=[gst_b[i]])
                T.op("act", lambda e: e.activation(out=gst[i][:], in_=gst[i][:], func=AF.Sigmoid),
                     reads=[gst_b[i]], writes=[gst_b[i]])
                T.dma("sp", gst_sem[i], A["gates"][tsl, :], gst[i][:], reads=[gst_b[i]])
                ba = nbank()
                for k in range(8):
                    T.op("pe", lambda e: e.matmul(R.ps[ba][:, 0:256], lhsT=hT[:, k, tsl],
                                                  rhs=win[:, k, C_SV:C_SV + 256], start=(k == 0), stop=(k == 7)),
                         reads=[wb_, hb[k][tt]], writes=[R.psb[ba]])
                T.op("act", lambda e: e.activation(out=vsb[i][:], in_=R.ps[ba][:, 0:256], func=AF.Copy),
                     reads=[R.psb[ba]], writes=[vsb_b[i]])
                T.dma("sp", vsb_sem[i], A["v_sb"][tsl, :], vsb[i][:], reads=[vsb_b[i]])
        T.barrier_all()


THETA = 500000.0


def own_positions(core):
    j = core % 4
    return ((4 * np.arange(NBLK)[:, None] + j) * 128 + np.arange(128)[None, :]).reshape(-1)


def _cs(pos, rot):
    half = rot // 2
    inv = np.float32(THETA) ** (-(np.arange(half, dtype=np.float32) / np.float32(half)))
    ang = pos.astype(np.float32)[None, :] * inv.astype(np.float32)[:, None]
    return np.cos(ang).astype(np.float32), np.sin(ang).astype(np.float32)


def rope_tables(core):
    pos = own_positions(core)
    n = pos.shape[0]
    c16, s16 = _cs(pos, 32)
    c8, s8 = _cs(pos, 16)
    cosM = np.ones((96, n), np.float32); sinM = np.zeros((96, n), np.float32)
    cosM[64:80] = c16; cosM[80:96] = c16; sinM[64:80] = -s16; sinM[80:96] = s16
    cosK = np.concatenate([c16, c16], 0); sinK = np.concatenate([-s16, s16], 0)
    cosN = np.ones((128, n), np.float32); sinN = np.zeros((128, n), np.float32)
    for h in range(2):
        cosN[h * 64:h * 64 + 8] = c8; cosN[h * 64 + 8:h * 64 + 16] = c8
        sinN[h * 64:h * 64 + 8] = -s8; sinN[h * 64 + 8:h * 64 + 16] = s8
    return dict(cosM=cosM, sinM=sinM, cosK=cosK, sinK=sinK, cosN=cosN, sinN=sinN)


def swap_cols_rope(w, head_w, rope0, rot):
    w = np.array(w, copy=True)
    half = rot // 2
    nh = w.shape[1] // head_w
    for h in range(nh):
        a = h * head_w + rope0
        tmp = w[:, a:a + half].copy()
        w[:, a:a + half] = w[:, a + half:a + rot]
        w[:, a + half:a + rot] = tmp
    return w


def make_w_in_sw(w_in):
    parts = [swap_cols_rope(w_in[:, C_KR:C_KR + 32], 32, 0, 32),
             swap_cols_rope(w_in[:, C_NQ:C_NQ + 384], 64, 0, 16),
             swap_cols_rope(w_in[:, C_NKC:C_NKC + 128], 64, 0, 16),
             swap_cols_rope(w_in[:, C_NKS:C_NKS + 128], 64, 0, 16),
             swap_cols_rope(w_in[:, C_NKW:C_NKW + 128], 64, 0, 16)]
    return np.ascontiguousarray(np.concatenate(parts, axis=1))


def make_smallsP(q_norm, kv_norm, gate_bias):
    sm = np.zeros((128, 32), np.float32)
    sm[:, 0:2] = q_norm.reshape(2, 128).T
    sm[:, 2] = kv_norm
    sm[:, 8:26] = gate_bias[None, :]
    return sm


P_OUTS = [("qT_mla", [6, 96, NT], BF16), ("kT_mla", [6, 64, NT], BF16), ("kpeT", [32, NT], BF16),
          ("qT_nsa", [384, NT], BF16), ("kT_cmp", [128, NT], BF16), ("kT_sel", [128, NT], BF16),
          ("kT_win", [128, NT], BF16), ("vT_cmp", [128, NT], BF16), ("qT_sb", [256, NT], BF16),
          ("kT_sb", [256, NT], BF16), ("v_mla", [NT, 6, 65], BF16), ("v_sel", [NT, 2, 65], BF16),
          ("v_win", [NT, 2, 65], BF16), ("gates", [NT, 18], F32), ("v_sb", [NT, 256], BF16)]
P_INS = [("w_in", [D, DIN]), ("w_in_sw", [D, NSW]), ("w_uq", [256, 576]), ("w_uq_sw", [256, 576]),
         ("w_ukv", [128, 768]), ("smallsP", [128, 32]),
         ("cosM", [96, NT]), ("sinM", [96, NT]), ("cosK", [32, NT]), ("sinK", [32, NT]),
         ("cosN", [128, NT]), ("sinN", [128, NT])]


def make_masks(core):
    j = core % 4
    k = np.arange(128)[:, None]
    q = np.arange(128)[None, :]
    ones = np.ones((128, 128), np.float32)
    zeros = np.zeros((128, 128), np.float32)
    tri = (k <= q).astype(np.float32)
    stri = (k < q).astype(np.float32)
    gt = (k > q).astype(np.float32)
    M = np.zeros((128, 22, 128), np.float32)
    M[:, 20] = (k >= q).astype(np.float32)
    M[:, 21] = 1.0
    for d in range(4):
        M[:, d] = ones if d < j else (tri if d == j else zeros)
        M[:, 16 + d] = ones if d < j else (stri if d == j else zeros)
    for dd in range(8):
        d = dd - 4
        if d < j - 4 or d > j:
            M[:, 4 + dd] = zeros
        elif d == j - 4:
            M[:, 4 + dd] = gt
        elif d == j:
            M[:, 4 + dd] = tri
        else:
            M[:, 4 + dd] = ones
    for m4 in range(4):
        i16 = 4 * m4 + j
        M[:, 12 + m4] = (16 * k + 31 - 128 * i16 <= q).astype(np.float32)
    return M.astype(ml_dtypes.bfloat16)


def ktcol(kt):
    return (kt % 4) * NT + (kt // 4) * 128


def ktile(kt):
    return (kt % 4) * NBLK + (kt // 4)


def emit_oT_store(R, C, obank, G, ch, po, zcol=64, stride=65):
    T = R.T
    i = C["ev"] % 2
    C["ev"] += 1
    on, on_b, rz, rz_b = C["on"][i], C["on_b"][i], C["rz"][i], C["rz_b"][i]
    ov = R.ps[obank][:, 0:4 * stride].rearrange("p (m x) -> p m x", x=stride)
    T.op("dve", lambda e: e.reciprocal(out=rz[:], in_=ov[:, :, zcol:zcol + 1]), reads=[R.psb[obank]], writes=[rz_b])
    T.op("dve", lambda e: e.tensor_tensor(out=on[:], in0=ov[:, :, 0:64], in1=rz[:].broadcast_to([128, 4, 64]),
                                          op=ALU.mult),
         reads=[R.psb[obank], rz_b], writes=[on_b])
    emit_transpose_store(R, C, on, on_b, G, ch, po)


def emit_transpose_store(R, C, on, on_b, G, ch, po):
    T = R.T
    tb = C["tpbank"]
    for mm in range(4):
        T.op("pe", lambda e: e.transpose(out=R.ps[tb][0:64, mm * 128:(mm + 1) * 128], in_=on[:, mm, :],
                                         identity=C["ident"][:]),
             reads=[on_b, C["ident_b"]], writes=[R.psb[tb]])
    T.op("act", lambda e: e.activation(out=R.oT[po:po + 64, ch, G * 512:(G + 1) * 512], in_=R.ps[tb][0:64, :],
                                       func=AF.Copy),
         reads=[R.psb[tb]], writes=[R.oT_b[ch][G]])


def emit_dense_attn(R, C, KT, KT_b, V, V_b, QT, QT_b, dk, scale, obank, mask0,
                    selT=None, selT_b=None, vstride=65):
    T = R.T

    def run(G):
        steps = list(range(16 * G + 16))
        n = len(steps)
        stt = {}

        def stage_a(kt):
            mm_min = max(0, -((-(kt - 16 * G - 3)) // 4))
            c0 = mm_min * 128
            sb_ = C["sbanks"][C["sn"] % len(C["sbanks"])]
            C["sn"] += 1
            stt[kt] = [mm_min, c0, sb_, None]
            T.op("pe", lambda e: e.matmul(R.ps[sb_][:, c0:512], lhsT=KT[0:dk, ktcol(kt):ktcol(kt) + 128],
                                          rhs=QT[0:dk, G * 512 + c0:(G + 1) * 512], start=True, stop=(selT is None)),
                 reads=[KT_b, QT_b], writes=[R.psb[sb_]])
            if selT is not None:
                T.op("pe", lambda e: e.matmul(R.ps[sb_][:, c0:512], lhsT=C["Ebig"][:, kt * 128:(kt + 1) * 128],
                                              rhs=selT[:, G * 512 + c0:(G + 1) * 512], start=False, stop=True),
                     reads=[C["Ebig_b"], selT_b], writes=[R.psb[sb_]])

        def stage_b(kt):
            mm_min, c0, sb_, _ = stt[kt]
            pi = C["pn"] % len(C["PT"])
            C["pn"] += 1
            stt[kt][3] = pi
            PT, PT_b = C["PT"][pi], C["PT_b"][pi]
            T.op("act", lambda e: e.activation(out=PT[:, c0:512], in_=R.ps[sb_][:, c0:512], func=AF.Exp, scale=scale),
                 reads=[R.psb[sb_]], writes=[PT_b])
            if kt >= 16 * G:
                mmb = (kt - 16 * G) // 4
                d = (kt - 16 * G) % 4
                T.op("pool", lambda e: e.tensor_tensor(out=PT[:, mmb * 128:(mmb + 1) * 128],
                                                       in0=PT[:, mmb * 128:(mmb + 1) * 128],
                                                       in1=C["masks"][:, mask0 + d, :], op=ALU.mult),
                     reads=[PT_b, C["masks_b"]], writes=[PT_b])

        def stage_c(kt, first):
            mm_min, c0, sb_, pi = stt[kt]
            PT, PT_b = C["PT"][pi], C["PT_b"][pi]
            for mm in range(mm_min, 4):
                last = (kt == 16 * G + 4 * mm + 3)
                T.op("pe", lambda e: e.matmul(R.ps[obank][:, mm * vstride:(mm + 1) * vstride],
                                              lhsT=PT[:, mm * 128:(mm + 1) * 128], rhs=V[:, ktile(kt), :],
                                              start=(first and mm == mm_min), stop=last, skip_group_check=True),
                     reads=[PT_b, V_b], writes=[R.psb[obank]])

        for idx in range(n + 2):
            if idx < n:
                stage_a(steps[idx])
            if 1 <= idx <= n:
                stage_b(steps[idx - 1])
            if idx >= 2:
                stage_c(steps[idx - 2], idx == 2)
    return run


def alloc_attn_common(R, A, ph):
    nc, T = R.nc, R.T
    C = {"ev": 0, "sn": 0, "pn": 0, "sbanks": [0, 1, 2, 3], "tpbank": 5}

    def sb(name, shape, dt):
        return ph.enter_context(nc.sbuf_tensor(U(name), shape, dt))
    C["sb"] = sb
    C["masks"] = sb("A_masks", [128, 22, 128], BF16)
    C["masks_b"] = Buf("A_masks")
    C["ident"] = sb("A_ident", [128, 128], F32)
    C["ident_b"] = Buf("A_ident")
    ds = T.new_dma_sem("Aconst")
    T.dma("sp", ds, C["masks"][:], A["masks"], writes=[C["masks_b"]])
    T.dma("sp", ds, C["ident"][:], A["ident"], writes=[C["ident_b"]])
    C["PT"] = [sb("A_PT%d" % i, [128, 512], BF16) for i in range(6)]
    C["PT_b"] = [Buf("A_PT%d" % i) for i in range(6)]
    C["on"] = [sb("A_on%d" % i, [128, 4, 64], F32) for i in range(2)]
    C["on_b"] = [Buf("A_on%d" % i) for i in range(2)]
    C["rz"] = [sb("A_rz%d" % i, [128, 4, 1], F32) for i in range(2)]
    C["rz_b"] = [Buf("A_rz%d" % i) for i in range(2)]
    C["kcT"] = [sb("A_kcT%d" % i, [64, 512], BF16) for i in range(2)]
    C["vc"] = [sb("A_vc%d" % i, [128, 4, 65], BF16) for i in range(2)]
    C["kc_b"] = [Buf("A_kc%d" % i) for i in range(2)]
    return C


def gb(A, nm, i=None):
    d = A.get("gb")
    if d is None:
        return []
    b = d[nm]
    if isinstance(b, list):
        b = b[i]
    return [b]


def emit_mla(R, C, A, hook=None):
    nc, T = R.nc, R.T
    ngrp = 0
    hook_left = [4]
    with R.ExitStack() as ph:
        def sb(name, shape, dt):
            return ph.enter_context(nc.sbuf_tensor(U(name), shape, dt))
        KT = [sb("M_KT%d" % i, [96, 4 * NT], BF16) for i in range(2)]
        V = [sb("M_V%d" % i, [128, 64, 65], BF16) for i in range(2)]
        QT = [sb("M_QT%d" % i, [96, NT], BF16) for i in range(2)]
        bufs = [Buf("M_slot%d" % i) for i in range(2)]
        sems = [T.new_dma_sem("Mslot%d" % i) for i in range(2)]
        for h in range(6):
            s = h % 2
            for r in range(4):
                T.dma("sp", sems[s], KT[s][0:64, r * NT:(r + 1) * NT], A["kT_mla_g"][h][r, :, :], reads=gb(A, "kT_mla", h),
                      writes=[bufs[s]])
                T.dma("sp", sems[s], KT[s][64:96, r * NT:(r + 1) * NT], A["kpeT_g"][r, :, :], reads=gb(A, "kpeT"), writes=[bufs[s]])
                for hf in range(2):
                    T.dma("sp", sems[s], V[s][:, r * NBLK + hf * 8:r * NBLK + hf * 8 + 8, :],
                          A["v_mla_g"][hf][r, :, h, :].rearrange("(m p) x -> p m x", p=128), reads=gb(A, "v_mla", hf),
                          writes=[bufs[s]])
            T.dma("sp", sems[s], QT[s][:], A["qT_mla"][h, :, :], writes=[bufs[s]])
            for G in range(4):
                obank = 6 + (G % 2)
                run = emit_dense_attn(R, C, KT[s], bufs[s], V[s], bufs[s], QT[s], bufs[s], 96, 96 ** -0.5,
                                      obank, 0)
                run(G)
                emit_oT_store(R, C, obank, G, h // 2, (h % 2) * 64)
                ngrp += 1
                if hook is not None and ngrp % 3 == 1 and hook_left[0] > 0:
                    hook_left[0] -= 1
                    next(hook)
        T.barrier_all()


def make_nsa_consts(core):
    j = core % 4
    c = np.arange(512)[:, None]
    n = np.arange(128)[None, :]
    ov = ((16 * c < 64 * n + 64) & (16 * c + 32 > 64 * n)).astype(np.float32)
    ovl = np.concatenate([ov, np.ones((512, 1), np.float32)], 1).reshape(4, 128, 129).transpose(1, 0, 2)
    ql = np.arange(128)[:, None]
    w = np.arange(256)[None, :]
    rel = w - 128 - 2 * j
    cur = (ql >= 64).astype(np.int64)
    forced = (rel == cur) | (rel == cur - 1)
    future = rel > cur
    keep = (~(forced | future)).astype(np.float32)
    add = np.where(forced, 1e4, np.where(future, -1.0, 0.0)).astype(np.float32)
    ka = np.stack([keep, add], 1)
    key = np.arange(S)[None, :]
    eb = np.where(key // 64 == np.arange(128)[:, None], 1.0, 0.0).astype(np.float32)
    return dict(ovl=np.ascontiguousarray(ovl).astype(ml_dtypes.bfloat16), keepadd=np.ascontiguousarray(ka),
                Ebig=eb.astype(ml_dtypes.bfloat16))


def emit_nsa_compress(R, C, A):
    nc, T = R.nc, R.T
    with R.ExitStack() as ph:
        def sb(name, shape, dt):
            return ph.enter_context(nc.sbuf_tensor(U(name), shape, dt))
        w1 = [sb("N_w1%d" % i, [64, 32, 128], BF16) for i in range(2)]
        w2 = [sb("N_w2%d" % i, [128, 64], BF16) for i in range(2)]
        posT = [sb("N_pos%d" % i, [64, 32], BF16) for i in range(2)]
        cst_b = Buf("NC_const")
        ds = T.new_dma_sem("NconstP")
        T.dma("pool", ds, w1[0][:], A["cmp_w1_k"].rearrange("(l d) h -> d l h", d=64), writes=[cst_b])
        T.dma("pool", ds, w1[1][:], A["cmp_w1_v"].rearrange("(l d) h -> d l h", d=64), writes=[cst_b])
        T.dma("pool", ds, w2[0][:], A["cmp_w2_k"], writes=[cst_b])
        T.dma("pool", ds, w2[1][:], A["cmp_w2_v"], writes=[cst_b])
        T.dma("pool", ds, posT[0][:], A["cmp_posT_k"], writes=[cst_b])
        T.dma("pool", ds, posT[1][:], A["cmp_posT_v"], writes=[cst_b])
        bias = sb("N_bias", [128, 2], F32)
        bias_b = Buf("N_bias")
        xg = [[sb("N_xg%d_%d" % (g, i), [64, S], BF16) for i in range(2)] for g in range(2)]
        xg_b = [[Buf("N_xg%d_%d" % (g, i)) for i in range(2)] for g in range(2)]
        hid = sb("N_hid", [128, 512], F32)
        tq = sb("N_tq", [128, 512], F32)
        gl = sb("N_gl", [128, 512], BF16)
        hid_b, gl_b = Buf("N_hid"), Buf("N_gl")
        T.op("pool", lambda e: e.memset(gl[:], 0.0), writes=[gl_b])
        yield
        for i in range(2):
            for l in range(32):
                T.op("pe", lambda e: e.matmul(R.ps[4][:, i:i + 1], lhsT=w1[i][:, l, :], rhs=posT[i][:, l:l + 1],
                                              start=(l == 0), stop=(l == 31)),
                     reads=[cst_b], writes=[R.psb[4]])
            T.op("dve", lambda e: e.tensor_copy(out=bias[:, i:i + 1], in_=R.ps[4][:, i:i + 1]),
                 reads=[R.psb[4]], writes=[bias_b])
        for g in range(2):
            kcT, vc, kc_b = C["kcT"][g], C["vc"][g], C["kc_b"][g]
            T.op("pool", lambda e: e.memset(vc[:], 1.0), reads=[], writes=[kc_b])
            T.op("pool", lambda e: e.memset(kcT[:], 0.0), reads=[], writes=[kc_b])
            for i in range(2):
                nm = ("kT_cmp_g", "vT_cmp_g")[i]
                dsx = T.new_dma_sem("Nxg%d_%d" % (g, i))
                for r in range(4):
                    dst = xg[g][i][:].rearrange("d (m r p) -> d m r p", r=4, p=128)[:, :, r, :]
                    T.dma("sp", dsx, dst, A[nm][r, g * 64:(g + 1) * 64, :].rearrange("d (m p) -> d m p", p=128),
                          reads=gb(A, nm[:-2]), writes=[xg_b[g][i]])
                for l in range(32):
                    T.op("pe", lambda e: e.matmul(R.ps[4][:, 0:511], lhsT=w1[i][:, l, :],
                                                  rhs=xg[g][i][:, l:l + 16 * 510 + 1:16],
                                                  start=(l == 0), stop=(l == 31)),
                         reads=[cst_b, xg_b[g][i]], writes=[R.psb[4]])
                T.op("act", lambda e: e.activation(out=hid[:, 0:511], in_=R.ps[4][:, 0:511], func=AF.Identity,
                                                   bias=bias[:, i:i + 1]),
                     reads=[R.psb[4], bias_b], writes=[hid_b])
                T.op("dve", lambda e: e.tensor_tensor(out=tq[:, 0:511], in0=hid[:, 0:511], in1=hid[:, 0:511],
                                                      op=ALU.mult), reads=[hid_b], writes=[hid_b])
                T.op("dve", lambda e: e.tensor_scalar(out=tq[:, 0:511], in0=tq[:, 0:511], scalar1=0.044715,
                                                      scalar2=1.0, op0=ALU.mult, op1=ALU.add),
                     reads=[hid_b], writes=[hid_b])
                T.op("dve", lambda e: e.tensor_tensor(out=tq[:, 0:511], in0=tq[:, 0:511], in1=hid[:, 0:511],
                                                      op=ALU.mult), reads=[hid_b], writes=[hid_b])
                T.op("act", lambda e: e.activation(out=tq[:, 0:511], in_=tq[:, 0:511], func=AF.Sigmoid,
                                                   scale=1.5957691216057308),
                     reads=[hid_b], writes=[hid_b])
                T.op("dve", lambda e: e.tensor_tensor(out=gl[:, 0:511], in0=tq[:, 0:511], in1=hid[:, 0:511],
                                                      op=ALU.mult), reads=[hid_b, gl_b], writes=[gl_b])
                if i == 0:
                    T.op("pe", lambda e: e.matmul(R.ps[4][0:64, 0:511], lhsT=w2[0][:], rhs=gl[:, 0:511],
                                                  start=True, stop=True),
                         reads=[cst_b, gl_b], writes=[R.psb[4]])
                    T.op("dve", lambda e: e.tensor_copy(out=kcT[:, 0:511], in_=R.ps[4][0:64, 0:511]),
                         reads=[R.psb[4]], writes=[kc_b])
                else:
                    for t in range(4):
                        T.op("pe", lambda e: e.matmul(R.ps[4][:, t * 64:(t + 1) * 64], lhsT=gl[:, t * 128:(t + 1) * 128],
                                                      rhs=w2[1][:], start=True, stop=True),
                             reads=[cst_b, gl_b], writes=[R.psb[4]])
                    T.op("dve", lambda e: e.tensor_copy(out=vc[:, :, 0:64],
                                                        in_=R.ps[4][:, 0:256].rearrange("p (t x) -> p t x", x=64)),
                         reads=[R.psb[4]], writes=[kc_b])
                yield
        T.barrier_all()


def emit_nsa(R, C, A):
    nc, T = R.nc, R.T
    SC = 0.125
    with R.ExitStack() as ph:
        def sb(name, shape, dt):
            return ph.enter_context(nc.sbuf_tensor(U(name), shape, dt))
        Ebig = sb("N_Ebig", [128, S], BF16)
        ovl = sb("N_ovl", [128, 4, 129], BF16)
        ka = sb("N_ka", [128, 2, 256], F32)
        gates = sb("N_gates", [128, NBLK, 18], F32)
        cst_b = Buf("N_const")
        C["Ebig"], C["Ebig_b"] = Ebig, cst_b
        ds = T.new_dma_sem("Nconst")
        T.dma("sp", ds, Ebig[:], A["Ebig"], writes=[cst_b])
        T.dma("sp", ds, ovl[:], A["ovl"], writes=[cst_b])
        T.dma("sp", ds, ka[:], A["keepadd"], writes=[cst_b])
        T.dma("sp", ds, gates[:], A["gates"].rearrange("(m p) g -> p m g", p=128), writes=[cst_b])
        selT = sb("N_selT", [128, NT], BF16)
        selT_b = Buf("N_selT")
        QTn_l = [sb("N_QT%d" % i, [64, 3, NT], BF16) for i in range(2)]
        KTs_l = [sb("N_KTs%d" % i, [64, 4 * NT], BF16) for i in range(2)]
        Vs_l = [sb("N_Vs%d" % i, [128, 64, 65], BF16) for i in range(2)]
        KTw_l = [sb("N_KTw%d" % i, [64, 4 * NT], BF16) for i in range(2)]
        Vw_l = [sb("N_Vw%d" % i, [128, 64, 65], BF16) for i in range(2)]
        kv_b_l = [Buf("N_kv0"), Buf("N_kv1")]
        kv_sem_l = [T.new_dma_sem("Nkv0"), T.new_dma_sem("Nkv1")]
        for g in range(2):
            QTn, KTs, Vs, KTw, Vw, kv_b, kv_sem = QTn_l[g], KTs_l[g], Vs_l[g], KTw_l[g], Vw_l[g], kv_b_l[g], kv_sem_l[g]
            T.dma("sp", kv_sem, QTn[:], A["qT_nsa"][g * 192:(g + 1) * 192, :].rearrange("(h d) t -> d h t", d=64),
                  writes=[kv_b])
            for r in range(4):
                T.dma("sp", kv_sem, KTs[:, r * NT:(r + 1) * NT], A["kT_sel_g"][r, g * 64:(g + 1) * 64, :], reads=gb(A, "kT_sel"), writes=[kv_b])
                T.dma("sp", kv_sem, KTw[:, r * NT:(r + 1) * NT], A["kT_win_g"][r, g * 64:(g + 1) * 64, :], reads=gb(A, "kT_win"), writes=[kv_b])
                T.dma("sp", kv_sem, Vs[:, r * NBLK:(r + 1) * NBLK, :],
                      A["v_sel_g"][r, :, g, :].rearrange("(m p) x -> p m x", p=128), reads=gb(A, "v_sel"), writes=[kv_b])
                T.dma("sp", kv_sem, Vw[:, r * NBLK:(r + 1) * NBLK, :],
                      A["v_win_g"][r, :, g, :].rearrange("(m p) x -> p m x", p=128), reads=gb(A, "v_win"), writes=[kv_b])
        oaccs = [sb("N_oacc%d" % i, [128, 3, 4, 64], F32) for i in range(2)]
        oaccs_b = [Buf("N_oacc0"), Buf("N_oacc1")]
        C["pcn"], C["pwn"] = 0, 0
        Pc = [sb("N_Pc%d" % i, [128, 3, 128], BF16) for i in range(2)]
        Pc_b = [Buf("N_Pc0"), Buf("N_Pc1")]
        rzc = sb("N_rzc", [128, 3, 1], F32)
        wc = sb("N_wc", [128, 3, 1], F32)
        sc = sb("N_sc", [128, 128], F32)
        sc2 = sb("N_sc2", [128, 128], F32)
        m8 = sb("N_m8", [128, 16], F32)
        selns = [sb("N_seln%d" % i, [128, 128], F32) for i in range(4)]
        selns_b = [Buf("N_seln%d" % i) for i in range(4)]
        sm_b = Buf("N_small")
        rz4 = sb("N_rz4", [128, 4, 1], F32)
        w4 = sb("N_w4", [128, 4, 1], F32)
        w4_b = Buf("N_w4")
        Pw = [sb("N_Pw%d" % i, [128, 512], BF16) for i in range(3)]
        Pw_b = [Buf("N_Pw%d" % i) for i in range(3)]
        for g in range(2):
            kcT, vc, kc_b = C["kcT"][g], C["vc"][g], C["kc_b"][g]
            QTn, KTs, Vs, KTw, Vw, kv_b, kv_sem = QTn_l[g], KTs_l[g], Vs_l[g], KTw_l[g], Vw_l[g], kv_b_l[g], kv_sem_l[g]

            def cmp_block(G, oacc, oacc_b):
                for mm in range(4):
                    ob, sbk = (3, 4) if mm % 2 == 0 else (2, 7)
                    m = 4 * G + mm
                    msl = slice(m * 128, (m + 1) * 128)
                    for t in range(G + 1):
                        sb_ = C["sbanks"][C["sn"] % len(C["sbanks"])]
                        C["sn"] += 1
                        pi = C["pcn"] % 2
                        C["pcn"] += 1
                        T.op("pe", lambda e: e.matmul(R.ps[sb_][:, 0:384], lhsT=kcT[:, t * 128:(t + 1) * 128],
                                                      rhs=QTn[:, :, msl], start=True, stop=True),
                             reads=[kc_b, kv_b], writes=[R.psb[sb_]])
                        T.op("act", lambda e: e.activation(out=Pc[pi][:].rearrange("p r q -> p (r q)"),
                                                           in_=R.ps[sb_][:, 0:384], func=AF.Exp, scale=SC),
                             reads=[R.psb[sb_]], writes=[Pc_b[pi]])
                        if t == G:
                            T.op("pool", lambda e: e.tensor_tensor(
                                out=Pc[pi][:], in0=Pc[pi][:],
                                in1=C["masks"][:, 12 + mm:13 + mm, :].broadcast_to([128, 3, 128]), op=ALU.mult),
                                reads=[Pc_b[pi], C["masks_b"]], writes=[Pc_b[pi]])
                        for r in range(3):
                            T.op("pe", lambda e: e.matmul(R.ps[ob][:, r * 65:(r + 1) * 65], lhsT=Pc[pi][:, r, :],
                                                          rhs=vc[:, t, :], start=(t == 0 and r == 0), stop=(t == G),
                                                          skip_group_check=True),
                                 reads=[Pc_b[pi], kc_b], writes=[R.psb[ob]])
                        for r in range(3):
                            T.op("pe", lambda e: e.matmul(R.ps[sbk][:, r * 129:(r + 1) * 129], lhsT=Pc[pi][:, r, :],
                                                          rhs=ovl[:, t, :], start=(t == 0 and r == 0), stop=(t == G),
                                                          skip_group_check=True),
                                 reads=[Pc_b[pi], cst_b], writes=[R.psb[sbk]])
                    scv = R.ps[sbk][:, 0:387].rearrange("p (r x) -> p r x", x=129)
                    ocv = R.ps[ob][:, 0:195].rearrange("p (r x) -> p r x", x=65)
                    T.op("dve", lambda e: e.tensor_scalar(out=rzc[:], in0=scv[:, :, 128:129], scalar1=1e-30, scalar2=None,
                                                          op0=ALU.add), reads=[R.psb[sbk]], writes=[sm_b])
                    T.op("dve", lambda e: e.reciprocal(out=rzc[:], in_=rzc[:]), reads=[sm_b], writes=[sm_b])
                    T.op("dve", lambda e: e.tensor_scalar(out=sc[:], in0=scv[:, 0, 0:128], scalar1=rzc[:, 0, :], scalar2=None,
                                                          op0=ALU.mult), reads=[R.psb[sbk], sm_b], writes=[sm_b])
                    for r in (1, 2):
                        T.op("dve", lambda e: e.scalar_tensor_tensor(out=sc[:], in0=scv[:, r, 0:128], scalar=rzc[:, r, :],
                                                                     in1=sc[:], op0=ALU.mult, op1=ALU.add),
                             reads=[R.psb[sbk], sm_b], writes=[sm_b])
                    T.op("dve", lambda e: e.tensor_tensor(
                        out=wc[:], in0=rzc[:],
                        in1=gates[:, m, 9 * g:9 * g + 9].rearrange("p (r x) -> p r x", x=3)[:, :, 0:1], op=ALU.mult),
                        reads=[sm_b, cst_b], writes=[sm_b])
                    for r in range(3):
                        T.op("dve", lambda e: e.tensor_scalar(out=oacc[:, r, mm, :], in0=ocv[:, r, 0:64],
                                                              scalar1=wc[:, r, :], scalar2=None, op0=ALU.mult),
                             reads=[R.psb[ob], sm_b], writes=[oacc_b])
                    w0 = 128 - 8 * m
                    T.op("dve", lambda e: e.tensor_tensor(out=sc[:], in0=sc[:], in1=ka[:, 0, w0:w0 + 128], op=ALU.mult),
                         reads=[sm_b, cst_b], writes=[sm_b])
                    T.op("dve", lambda e: e.tensor_tensor(out=sc[:], in0=sc[:], in1=ka[:, 1, w0:w0 + 128], op=ALU.add),
                         reads=[sm_b, cst_b], writes=[sm_b])
                    T.op("dve", lambda e: e.memset(sc[:, 0:1], 1e4), reads=[sm_b], writes=[sm_b])
                    T.op("dve", lambda e: e.max(out=m8[:, 0:8], in_=sc[:]), reads=[sm_b], writes=[sm_b])
                    T.op("dve", lambda e: e.match_replace(out=sc2[:], in_to_replace=m8[:, 0:8], in_values=sc[:],
                                                          imm_value=-1e9), reads=[sm_b], writes=[sm_b])
                    T.op("dve", lambda e: e.max(out=m8[:, 8:16], in_=sc2[:]), reads=[sm_b], writes=[sm_b])
                    T.op("dve", lambda e: e.tensor_scalar(out=selns[mm][:], in0=sc[:], scalar1=m8[:, 15:16], scalar2=None,
                                                          op0=ALU.is_ge), reads=[sm_b], writes=[selns_b[mm]])

            def cmp_finish(G):
                tb = C["tpbank"]
                for mm in range(4):
                    T.op("pe", lambda e: e.transpose(out=R.ps[tb][:, mm * 128:(mm + 1) * 128], in_=selns[mm][:],
                                                     identity=C["ident"][:]),
                         reads=[selns_b[mm], C["ident_b"]], writes=[R.psb[tb]])
                T.op("act", lambda e: e.activation(out=selT[:, G * 512:(G + 1) * 512], in_=R.ps[tb][:, :], func=AF.Copy),
                     reads=[R.psb[tb]], writes=[selT_b])

            def win_block(r, G, oacc, oacc_b):
                h = 3 * g + r
                ob = 6
                items = []
                for mm in range(4):
                    for half in range(2):
                        kts = [16 * G + 4 * mm - 4 + half * 4 + x for x in range(4)]
                        if kts[-1] >= 0:
                            items.append((mm, half, kts))
                ni = len(items)
                stt = {}

                def wa(i):
                    mm, half, kts = items[i]
                    msl = slice((4 * G + mm) * 128, (4 * G + mm + 1) * 128)
                    sb_ = C["sbanks"][C["sn"] % len(C["sbanks"])]
                    C["sn"] += 1
                    stt[i] = [sb_, None]
                    for x, kt in enumerate(kts):
                        T.op("pe", lambda e: e.matmul(R.ps[sb_][:, x * 128:(x + 1) * 128],
                                                      lhsT=KTw[:, ktcol(kt):ktcol(kt) + 128],
                                                      rhs=QTn[:, r, msl], start=True, stop=True),
                             reads=[kv_b], writes=[R.psb[sb_]])

                def wb(i):
                    mm, half, kts = items[i]
                    sb_ = stt[i][0]
                    pi = C["pwn"] % len(Pw)
                    C["pwn"] += 1
                    stt[i][1] = pi
                    T.op("act", lambda e: e.activation(out=Pw[pi][:], in_=R.ps[sb_][:], func=AF.Exp, scale=SC),
                         reads=[R.psb[sb_]], writes=[Pw_b[pi]])
                    T.op("pool", lambda e: e.tensor_tensor(
                        out=Pw[pi][:], in0=Pw[pi][:],
                        in1=C["masks"][:, 4 + half * 4:8 + half * 4, :].rearrange("p a q -> p (a q)"),
                        op=ALU.mult), reads=[Pw_b[pi], C["masks_b"]], writes=[Pw_b[pi]])

                def wc_(i):
                    mm, half, kts = items[i]
                    pi = stt[i][1]
                    for x, kt in enumerate(kts):
                        T.op("pe", lambda e: e.matmul(R.ps[ob][:, mm * 65:(mm + 1) * 65],
                                                      lhsT=Pw[pi][:, x * 128:(x + 1) * 128], rhs=Vw[:, ktile(kt), :],
                                                      start=(i == 0 and x == 0), stop=(half == 1 and x == 3),
                                                      skip_group_check=True),
                             reads=[Pw_b[pi], kv_b], writes=[R.psb[ob]])

                for idx in range(ni + 2):
                    if idx < ni:
                        wa(idx)
                    if 1 <= idx <= ni:
                        wb(idx - 1)
                    if idx >= 2:
                        wc_(idx - 2)
                owv = R.ps[ob][:, 0:260].rearrange("p (m x) -> p m x", x=65)
                T.op("dve", lambda e: e.reciprocal(out=rz4[:], in_=owv[:, :, 64:65]), reads=[R.psb[ob]], writes=[w4_b])
                T.op("dve", lambda e: e.tensor_tensor(out=w4[:], in0=rz4[:],
                                                      in1=gates[:, 4 * G:4 * G + 4, 3 * h + 2:3 * h + 3], op=ALU.mult),
                     reads=[w4_b, cst_b], writes=[w4_b])
                for mm in range(4):
                    T.op("dve", lambda e: e.scalar_tensor_tensor(out=oacc[:, r, mm, :], in0=owv[:, mm, 0:64],
                                                                 scalar=w4[:, mm, :], in1=oacc[:, r, mm, :],
                                                                 op0=ALU.mult, op1=ALU.add),
                         reads=[R.psb[ob], w4_b, oacc_b], writes=[oacc_b])

            def sel3_block(G, oacc, oacc_b):
                obs = [4, 6, 7]
                mbanks = [2, 3]
                qbanks = [0, 1]
                nk = 16 * G + 16
                nhs = 3 * nk
                stt = {}

                def geom(kt):
                    mm_min = max(0, -((-(kt - 16 * G - 3)) // 4))
                    return mm_min, mm_min * 128

                def sa(hs):
                    kt, r = hs // 3, hs % 3
                    mm_min, c0 = geom(kt)
                    mb = mbanks[kt % 2]
                    if r == 0:
                        T.op("pe", lambda e: e.matmul(R.ps[mb][:, c0:512], lhsT=Ebig[:, kt * 128:(kt + 1) * 128],
                                                      rhs=selT[:, G * 512 + c0:(G + 1) * 512], start=True, stop=True),
                             reads=[cst_b, selT_b], writes=[R.psb[mb]])
                    qb = qbanks[hs % 2]
                    T.op("pe", lambda e: e.matmul(R.ps[qb][:, c0:512], lhsT=KTs[:, ktcol(kt):ktcol(kt) + 128],
                                                  rhs=QTn[:, r, G * 512 + c0:(G + 1) * 512], start=True, stop=True),
                         reads=[kv_b], writes=[R.psb[qb]])

                def sb_(hs):
                    kt, r = hs // 3, hs % 3
                    mm_min, c0 = geom(kt)
                    mb, qb = mbanks[kt % 2], qbanks[hs % 2]
                    pi = C["pn"] % len(C["PT"])
                    C["pn"] += 1
                    stt[hs] = pi
                    PT, PT_b = C["PT"][pi], C["PT_b"][pi]
                    T.op("act", lambda e: e.activation(out=PT[:, c0:512], in_=R.ps[qb][:, c0:512], func=AF.Exp, scale=SC),
                         reads=[R.psb[qb]], writes=[PT_b])
                    T.op("dve", lambda e: e.tensor_tensor(out=PT[:, c0:512], in0=PT[:, c0:512], in1=R.ps[mb][:, c0:512],
                                                          op=ALU.mult),
                         reads=[PT_b, R.psb[mb]], writes=[PT_b])
                    if kt >= 16 * G:
                        mmb = (kt - 16 * G) // 4
                        d = (kt - 16 * G) % 4
                        T.op("pool", lambda e: e.tensor_tensor(out=PT[:, mmb * 128:(mmb + 1) * 128],
                                                               in0=PT[:, mmb * 128:(mmb + 1) * 128],
                                                               in1=C["masks"][:, d, :], op=ALU.mult),
                             reads=[PT_b, C["masks_b"]], writes=[PT_b])

                def sc_(hs):
                    kt, r = hs // 3, hs % 3
                    mm_min, c0 = geom(kt)
                    PT, PT_b = C["PT"][stt[hs]], C["PT_b"][stt[hs]]
                    for mm in range(mm_min, 4):
                        T.op("pe", lambda e: e.matmul(R.ps[obs[r]][:, mm * 65:(mm + 1) * 65],
                                                      lhsT=PT[:, mm * 128:(mm + 1) * 128], rhs=Vs[:, ktile(kt), :],
                                                      start=(kt == 0 and mm == mm_min), stop=(kt == 16 * G + 4 * mm + 3),
                                                      skip_group_check=True),
                             reads=[PT_b, kv_b], writes=[R.psb[obs[r]]])

                for idx in range(nhs + 4):
                    if idx < nhs:
                        sa(idx)
                    if 1 <= idx <= nhs:
                        sb_(idx - 1)
                    if idx >= 4:
                        sc_(idx - 4)
                for r in range(3):
                    h = 3 * g + r
                    osv = R.ps[obs[r]][:, 0:260].rearrange("p (m x) -> p m x", x=65)
                    T.op("dve", lambda e: e.reciprocal(out=rz4[:], in_=osv[:, :, 64:65]), reads=[R.psb[obs[r]]], writes=[w4_b])
                    T.op("dve", lambda e: e.tensor_tensor(out=w4[:], in0=rz4[:],
                                                          in1=gates[:, 4 * G:4 * G + 4, 3 * h + 1:3 * h + 2], op=ALU.mult),
                         reads=[w4_b, cst_b], writes=[w4_b])
                    for mm in range(4):
                        T.op("dve", lambda e: e.scalar_tensor_tensor(out=oacc[:, r, mm, :], in0=osv[:, mm, 0:64],
                                                                     scalar=w4[:, mm, :], in1=oacc[:, r, mm, :],
                                                                     op0=ALU.mult, op1=ALU.add),
                             reads=[R.psb[obs[r]], w4_b, oacc_b], writes=[oacc_b])
                    emit_transpose_store(R, C, oacc[:, r, :, :], oacc_b, G, 3 + h // 2, (h % 2) * 64)

            C["sbanks"] = [0, 1]
            cmp_block(0, oaccs[0], oaccs_b[0])
            for G in range(4):
                cmp_finish(G)
                if G + 1 < 4:
                    cmp_block(G + 1, oaccs[(G + 1) % 2], oaccs_b[(G + 1) % 2])
                for r in range(3):
                    win_block(r, G, oaccs[G % 2], oaccs_b[G % 2])
                sel3_block(G, oaccs[G % 2], oaccs_b[G % 2])
            C["sbanks"] = [0, 1, 2, 3]
        T.barrier_all()


def emit_sb(R, C, A):
    nc, T = R.nc, R.T
    SC = 0.125
    with R.ExitStack() as ph:
        def sb(name, shape, dt):
            return ph.enter_context(nc.sbuf_tensor(U(name), shape, dt))
        KT = [sb("S_KT%d" % i, [64, 4 * NT], BF16) for i in range(2)]
        V = [sb("S_V%d" % i, [128, 64, 64], BF16) for i in range(2)]
        QT = [sb("S_QT%d" % i, [64, NT], BF16) for i in range(2)]
        bufs = [Buf("S_slot%d" % i) for i in range(2)]
        sems = [T.new_dma_sem("Sslot%d" % i) for i in range(2)]
        NE = 4
        E = [sb("S_E%d" % i, [128, 512], F32) for i in range(NE)]
        SP = [sb("S_SP%d" % i, [128, 512], BF16) for i in range(NE)]
        X = [sb("S_X%d" % i, [128, 512], F32) for i in range(2)]
        E_b = [Buf("S_E%d" % i) for i in range(NE)]
        SP_b = [Buf("S_SP%d" % i) for i in range(NE)]
        X_b = [Buf("S_X0"), Buf("S_X1")]
        Accb = [sb("S_Accb%d" % i, [128, 512], BF16) for i in range(3)]
        Accb_b = [Buf("S_Accb%d" % i) for i in range(3)]
        tincl = C["masks"][:, 20, :]
        onesb = C["masks"][:, 21, :]
        cbanks = [3, 4]
        zbanks = [0, 1, 2]
        for h in range(4):
            s = h % 2
            for r in range(4):
                T.dma("sp", sems[s], KT[s][:, r * NT:(r + 1) * NT], A["kT_sb_g"][r, h * 64:(h + 1) * 64, :], reads=gb(A, "kT_sb"),
                      writes=[bufs[s]])
                T.dma("sp", sems[s], V[s][:, r * NBLK:(r + 1) * NBLK, :],
                      A["v_sb_g"][r, :, h * 64:(h + 1) * 64].rearrange("(m p) x -> p m x", p=128), reads=gb(A, "v_sb"),
                      writes=[bufs[s]])
            T.dma("sp", sems[s], QT[s][:], A["qT_sb"][h * 64:(h + 1) * 64, :], writes=[bufs[s]])
            for G in range(4):
                ob = 6 + (G % 2)
                for i in range(3):
                    T.op("pool", lambda e: e.memset(Accb[i][:], 0.0), writes=[Accb_b[i]])
                steps = list(range(16 * G + 15, -1, -1))
                ns = len(steps)
                stt = {}

                def geom(kt):
                    mm_min = max(0, -((-(kt - 16 * G - 3)) // 4))
                    return mm_min, mm_min * 128

                def st_a(n):
                    kt = steps[n]
                    mm_min, c0 = geom(kt)
                    zb = zbanks[n % 3]
                    T.op("pe", lambda e: e.matmul(R.ps[zb][:, c0:512], lhsT=KT[s][:, ktcol(kt):ktcol(kt) + 128],
                                                  rhs=QT[s][:, G * 512 + c0:(G + 1) * 512], start=True, stop=True),
                         reads=[bufs[s]], writes=[R.psb[zb]])

                def st_b(n):
                    kt = steps[n]
                    mm_min, c0 = geom(kt)
                    zb, ie = zbanks[n % 3], n % NE
                    T.op("act", lambda e: e.activation(out=E[ie][:, c0:512], in_=R.ps[zb][:, c0:512], func=AF.Exp, scale=SC),
                         reads=[R.psb[zb]], writes=[E_b[ie]])
                    T.op("act", lambda e: e.activation(out=SP[ie][:, c0:512], in_=E[ie][:, c0:512], func=AF.Ln, bias=1.0),
                         reads=[E_b[ie]], writes=[SP_b[ie]])
                    if kt >= 16 * G:
                        d = (kt - 16 * G) % 4
                        T.op("pool", lambda e: e.tensor_tensor(out=SP[ie][:, c0:c0 + 128], in0=SP[ie][:, c0:c0 + 128],
                                                               in1=C["masks"][:, 16 + d, :], op=ALU.mult),
                             reads=[SP_b[ie], C["masks_b"]], writes=[SP_b[ie]])
                    if n + 1 < ns:
                        T.op("pool", lambda e: e.tensor_tensor(out=Accb[(n + 1) % 3][:, c0:512], in0=Accb[n % 3][:, c0:512],
                                                               in1=SP[ie][:, c0:512], op=ALU.add),
                             reads=[Accb_b[n % 3], SP_b[ie]], writes=[Accb_b[(n + 1) % 3]])

                def st_c(n):
                    kt = steps[n]
                    mm_min, c0 = geom(kt)
                    cb, ie = cbanks[n % 2], n % NE
                    T.op("pe", lambda e: e.matmul(R.ps[cb][:, c0:512], lhsT=tincl, rhs=SP[ie][:, c0:512],
                                                  start=True, stop=False),
                         reads=[SP_b[ie], C["masks_b"]], writes=[R.psb[cb]])
                    T.op("pe", lambda e: e.matmul(R.ps[cb][:, c0:512], lhsT=onesb, rhs=Accb[n % 3][:, c0:512],
                                                  start=False, stop=True),
                         reads=[Accb_b[n % 3], C["masks_b"]], writes=[R.psb[cb]])

                def st_c2(n):
                    kt = steps[n]
                    mm_min, c0 = geom(kt)
                    cb = cbanks[n % 2]
                    T.op("act", lambda e: e.activation(out=X[n % 2][:, c0:512], in_=R.ps[cb][:, c0:512], func=AF.Exp,
                                                       scale=-1.0),
                         reads=[R.psb[cb]], writes=[X_b[n % 2]])

                def st_e(n):
                    kt = steps[n]
                    mm_min, c0 = geom(kt)
                    ie = n % NE
                    pi = C["pn"] % len(C["PT"])
                    C["pn"] += 1
                    stt[n] = pi
                    PT, PT_b = C["PT"][pi], C["PT_b"][pi]
                    T.op("dve", lambda e: e.tensor_tensor(out=PT[:, c0:512], in0=E[ie][:, c0:512], in1=X[n % 2][:, c0:512],
                                                          op=ALU.mult),
                         reads=[E_b[ie], X_b[n % 2]], writes=[PT_b])
                    if kt >= 16 * G:
                        d = (kt - 16 * G) % 4
                        T.op("pool", lambda e: e.tensor_tensor(out=PT[:, c0:c0 + 128], in0=PT[:, c0:c0 + 128],
                                                               in1=C["masks"][:, 16 + d, :], op=ALU.mult),
                             reads=[PT_b, C["masks_b"]], writes=[PT_b])

                def st_f(n, first):
                    kt = steps[n]
                    mm_min, c0 = geom(kt)
                    PT, PT_b = C["PT"][stt[n]], C["PT_b"][stt[n]]
                    for mm in range(mm_min, 4):
                        T.op("pe", lambda e: e.matmul(R.ps[ob][:, mm * 64:(mm + 1) * 64],
                                                      lhsT=PT[:, mm * 128:(mm + 1) * 128], rhs=V[s][:, ktile(kt), :],
                                                      start=(first and mm == mm_min), stop=(kt == 0), skip_group_check=True),
                             reads=[PT_b, bufs[s]], writes=[R.psb[ob]])

                for idx in range(ns + 4):
                    if 2 <= idx <= ns + 1:
                        st_c(idx - 2)
                    if idx < ns:
                        st_a(idx)
                    if 1 <= idx <= ns:
                        st_b(idx - 1)
                    if 2 <= idx <= ns + 1:
                        st_c2(idx - 2)
                    if 3 <= idx <= ns + 2:
                        st_e(idx - 3)
                    if idx >= 4:
                        st_f(idx - 4, idx == 4)
                i = C["ev"] % 2
                C["ev"] += 1
                T.op("dve", lambda e: e.tensor_copy(out=C["on"][i][:],
                                                    in_=R.ps[ob][:, 0:256].rearrange("p (m x) -> p m x", x=64)),
                     reads=[R.psb[ob]], writes=[C["on_b"][i]])
                emit_transpose_store(R, C, C["on"][i], C["on_b"][i], G, 6 + h // 2, (h % 2) * 64)
        T.barrier_all()


L = 2
GATHER = ["kT_mla", "kpeT", "v_mla", "kT_cmp", "vT_cmp", "kT_sel", "kT_win", "v_sel", "v_win", "kT_sb", "v_sb"]
LOCAL = ["qT_mla", "qT_nsa", "qT_sb", "gates"]
POUT = {nm: (shp, dt) for nm, shp, dt in P_OUTS}
W_IN = [("ffn1_w_gate", [L, D, DFF]), ("ffn1_w_up", [L, D, DFF]), ("ffn1_w_down", [L, DFF, D]),
        ("ffn2_w_gate", [L, D, DFF]), ("ffn2_w_up", [L, D, DFF]), ("ffn2_w_down", [L, DFF, D]),
        ("w_in", [L, D, DIN]), ("w_in_sw", [L, D, NSW]), ("w_uq", [L, 256, 576]), ("w_uq_sw", [L, 256, 576]),
        ("w_ukv", [L, 128, 768]), ("smallsP", [L, 128, 32]), ("w_out", [L, D, D]),
        ("cmp_w1_k", [L, 2048, 128]), ("cmp_w1_v", [L, 2048, 128]), ("cmp_w2_k", [L, 128, 64]),
        ("cmp_w2_v", [L, 128, 64]), ("cmp_posT_k", [L, 64, 32]), ("cmp_posT_v", [L, 64, 32]),
        ("gains", [128, 3 * L + 1, 8])]
C_IN = [("cosM", [96, NT], F32), ("sinM", [96, NT], F32), ("cosK", [32, NT], F32), ("sinK", [32, NT], F32),
        ("cosN", [128, NT], F32), ("sinN", [128, NT], F32), ("masks", [128, 22, 128], BF16),
        ("ident", [128, 128], F32), ("Ebig", [128, S], BF16), ("ovl", [128, 4, 129], BF16),
        ("keepadd", [128, 2, 256], F32)]


def emit_layer_X(R, A, l, do_post, do_pre, final, xin, xout, mid_hook=None):
    nc, T = R.nc, R.T
    with ExitStack() as px:
        alloc_xT(R, px)
        hT = px.enter_context(nc.sbuf_tensor(U("hT"), [128, 8, NT], BF16))
        hb = [[Buf("h%d_%d" % (k, t)) for t in range(4)] for k in range(8)]
        gam = px.enter_context(nc.sbuf_tensor(U("gam"), [128, 3 * L + 1, 8], F32))
        gam_b = Buf("gam")
        sq = [px.enter_context(nc.sbuf_tensor(U("sq%d" % i), [128, 512], BF16)) for i in range(2)]
        sq_b = [Buf("sq0"), Buf("sq1")]
        rstd = px.enter_context(nc.sbuf_tensor(U("rstd"), [128, 512], F32))
        rstd_b = Buf("rstd")
        ld = T.new_dma_sem("ldx")
        for k in range(8):
            T.dma("sp", ld, R.xT[:, k, :], xin[k * 128:(k + 1) * 128, :], writes=[R.xb[k][t] for t in range(4)])
        T.dma("sp", ld, gam[:], A["gains"], writes=[gam_b])

        def norm(gi):
            emit_norm(R, hT, hb, gam[:, gi, :], gam_b, sq, sq_b, rstd, rstd_b, 6)

        lp = l
        with ExitStack() as pf:
            R.stack = pf
            W = alloc_ffn_work(R)

            def ffn(pref, ll):
                emit_ffn(R, hT, hb, A[pref + "_w_gate"][ll], A[pref + "_w_up"][ll], A[pref + "_w_down"][ll], W)

            if do_post:
                oT2 = pf.enter_context(nc.sbuf_tensor(U("oT2"), [128, 8, NT], BF16))
                o_b = Buf("oT2")
                d1 = T.new_dma_sem("oT2")
                T.dma("sp", d1, oT2[:], A["oT_d"], writes=[o_b])
                for hf in range(2):
                    T.dma("pool", W["dsem"][hf], W["wd"][hf][:],
                          A["w_out"][l][hf * 512:(hf + 1) * 512, :].rearrange("(k p) c -> p k c", p=128),
                          writes=[W["wb"][hf]])
                n = 0
                for tt in range(4):
                    sl = slice(tt * 512, (tt + 1) * 512)
                    for dmc in range(8):
                        bk = n % 4
                        n += 1
                        for k in range(8):
                            T.op("pe", lambda e: e.matmul(R.ps[bk][:], lhsT=W["wd"][k // 4][:, k % 4, dmc * 128:(dmc + 1) * 128],
                                                          rhs=oT2[:, k, sl], start=(k == 0), stop=(k == 7)),
                                 reads=[o_b, W["wb"][k // 4]], writes=[R.psb[bk]])
                        T.op("dve", lambda e: e.tensor_tensor(out=R.xT[:, dmc, sl], in0=R.ps[bk][:], in1=R.xT[:, dmc, sl],
                                                              op=ALU.add),
                             reads=[R.psb[bk], R.xb[dmc][tt]], writes=[R.xb[dmc][tt]])
                norm(3 * l + 2)
                ffn("ffn2", l)
                lp = l + 1
            if do_pre:
                norm(3 * lp + 0)
                ffn("ffn1", lp)
            T.barrier_all()
        if do_pre:
            norm(3 * lp + 1)
            AP_ = dict(A)
            for nm in ("w_in", "w_in_sw", "w_uq", "w_uq_sw", "w_ukv", "smallsP"):
                AP_[nm] = A[nm][lp]
            emit_stage_P(R, hT, hb, AP_, mid_hook=mid_hook)
        st = T.new_dma_sem("stx")
        if final:
            ysq = [px.enter_context(nc.sbuf_tensor(U("ystg%d" % i), [128, 512], F32)) for i in range(2)]
            y_b = [Buf("y0"), Buf("y1")]
            ss, ss_b = R.ps[6], R.psb[6]
            n = 0
            for tt in range(4):
                sl = slice(tt * 512, (tt + 1) * 512)
                for k in range(8):
                    a = k % 2
                    T.op("act", lambda e: e.activation(out=sq[a][:], in_=R.xT[:, k, sl], func=AF.Square),
                         reads=[R.xb[k][tt]], writes=[sq_b[a]])
                    T.op("pe", lambda e: e.matmul(ss[:], lhsT=R.onesm[:], rhs=sq[a][:], start=(k == 0), stop=(k == 7)),
                         reads=[sq_b[a], R.onesm_b], writes=[ss_b])
                T.op("act", lambda e: e.activation(out=rstd[:], in_=ss[:], func=AF.Sqrt, bias=EPS, scale=1.0 / D),
                     reads=[ss_b], writes=[rstd_b])
                T.op("dve", lambda e: e.reciprocal(out=rstd[:], in_=rstd[:]), reads=[rstd_b], writes=[rstd_b])
                for k in range(8):
                    i = n % 2
                    n += 1
                    T.op("dve", lambda e: e.scalar_tensor_tensor(out=ysq[i][:], in0=R.xT[:, k, sl],
                                                                 scalar=gam[:, 3 * L, k:k + 1], in1=rstd[:],
                                                                 op0=ALU.mult, op1=ALU.mult),
                         reads=[R.xb[k][tt], rstd_b, gam_b], writes=[y_b[i]])
                    T.dma("sp", st, xout[k * 128:(k + 1) * 128, sl], ysq[i][:], reads=[y_b[i]])
        else:
            for k in range(8):
                T.dma("sp", st, xout[k * 128:(k + 1) * 128, :], R.xT[:, k, :], reads=[R.xb[k][t] for t in range(4)])
        T.barrier_all()
        return st


def emit_layer_A(R, A, l):
    nc, T = R.nc, R.T
    with ExitStack() as pa:
        R.oT = pa.enter_context(nc.sbuf_tensor(U("oT"), [128, 8, NT], BF16))
        R.oT_b = [[Buf("oT%d_%d" % (c, g)) for g in range(4)] for c in range(8)]
        AL = dict(A)
        for nm in ("cmp_w1_k", "cmp_w1_v", "cmp_w2_k", "cmp_w2_v", "cmp_posT_k", "cmp_posT_v"):
            AL[nm] = A[nm][l]
        with ExitStack() as ph:
            C = alloc_attn_common(R, AL, ph)
            gen = emit_nsa_compress(R, C, AL)
            next(gen)
            emit_mla(R, C, AL, hook=gen)
            for _ in gen:
                pass
            emit_nsa(R, C, AL)
            emit_sb(R, C, AL)
        st = T.new_dma_sem("stoT")
        T.dma("sp", st, A["oT_d"], R.oT[:], reads=[b for ll in R.oT_b for b in ll])
        T.barrier_all()


def build_launch(kind, l):
    nc = bass.Bass("TRN2", target_bir_lowering=False)
    A = {}

    def din(nm, shp, dt=F32):
        A[nm] = nc.dram_tensor(nm, shp, dt, kind="ExternalInput").ap()

    def dout(nm, shp, dt=F32):
        A[nm] = nc.dram_tensor(nm, shp, dt, kind="ExternalOutput").ap()
    for nm, shp in W_IN:
        din(nm, shp)
    for nm, shp, dt in C_IN:
        din(nm, shp, dt)
    din("xT_in", [D, NT])
    dout("xT_out", [D, NT])
    if kind != "first":
        for nm in GATHER:
            shp, dt = POUT[nm]
            din(nm + "_g", [4] + shp, dt)
        for nm in LOCAL:
            shp, dt = POUT[nm]
            din(nm, shp, dt)
        A["oT_d"] = nc.dram_tensor("oT_d", [128, 8, NT], BF16, kind="Internal").ap()
    if kind != "last":
        for nm, shp, dt in P_OUTS:
            if kind == "first" or nm not in LOCAL:
                dout(nm, shp, dt)
            else:
                A[nm + "_o"] = nc.dram_tensor(nm + "_o", shp, dt, kind="ExternalOutput").ap()
    with ExitStack() as stack:
        T = Tracker(nc, stack)
        R = setup_common(nc, stack, T)
        if kind != "first":
            emit_layer_A(R, A, l)
        AX = dict(A)
        if kind == "mid":
            for nm in LOCAL:
                AX[nm] = A[nm + "_o"]
        st = emit_layer_X(R, AX, l, do_post=(kind != "first"), do_pre=(kind != "last"), final=(kind == "last"),
                          xin=A["xT_in"], xout=A["xT_out"])
        nc.sync.wait_ge(T.sem[st], T.cnt[st])
    return nc


def _host_weights(inp):
    f = lambda a: np.ascontiguousarray(np.asarray(a, dtype=np.float32))
    W = {}
    for nm in ("ffn1_w_gate", "ffn1_w_up", "ffn1_w_down", "ffn2_w_gate", "ffn2_w_up", "ffn2_w_down", "w_in", "w_out"):
        W[nm] = f(inp[nm])
    W["w_uq"] = f(inp["mla_w_uq"])
    W["w_ukv"] = f(inp["mla_w_ukv"])
    W["w_in_sw"] = np.stack([make_w_in_sw(W["w_in"][l]) for l in range(L)])
    W["w_uq_sw"] = np.stack([swap_cols_rope(W["w_uq"][l], 96, 64, 32) for l in range(L)])
    W["smallsP"] = np.stack([make_smallsP(f(inp["mla_q_norm"])[l], f(inp["mla_kv_norm"])[l], f(inp["nsa_gate_bias"])[l])
                             for l in range(L)])
    W["cmp_w1_k"] = f(inp["nsa_cmp_w1_k"])
    W["cmp_w1_v"] = f(inp["nsa_cmp_w1_v"])
    W["cmp_w2_k"] = f(inp["nsa_cmp_w2_k"])
    W["cmp_w2_v"] = f(inp["nsa_cmp_w2_v"])
    W["cmp_posT_k"] = np.ascontiguousarray(f(inp["nsa_cmp_pos_k"]).transpose(0, 2, 1))
    W["cmp_posT_v"] = np.ascontiguousarray(f(inp["nsa_cmp_pos_v"]).transpose(0, 2, 1))
    g = np.zeros((128, 3 * L + 1, 8), np.float32)
    for l in range(L):
        for i, nm in enumerate(("ffn1_norm", "mix_norm", "ffn2_norm")):
            g[:, 3 * l + i, :] = f(inp[nm])[l].reshape(8, 128).T
    g[:, 3 * L, :] = f(inp["final_norm"]).reshape(8, 128).T
    W["gains"] = g
    return W


def _core_consts(core):
    c = {}
    c.update(rope_tables(core))
    c["masks"] = make_masks(core)
    c["ident"] = np.eye(128, dtype=np.float32)
    c.update(make_nsa_consts(core))
    return c


def _kernel_unfused_impl(**inp):
    x = np.asarray(inp["x"], dtype=np.float32)
    W = _host_weights(inp)
    consts = [_core_consts(c) for c in range(8)]
    xT = [np.ascontiguousarray(x[c // 4][own_positions(c)].T) for c in range(8)]
    cores = list(range(8))
    nc = build_launch("first", 0)
    ims = [dict(W, **consts[c], xT_in=xT[c]) for c in cores]
    res = run_bass_kernel_spmd(nc, ims, core_ids=cores).results
    for l in range(L):
        kind = "mid" if l < L - 1 else "last"
        nc = build_launch(kind, l)
        ims = []
        for c in cores:
            b = c // 4
            im = dict(W, **consts[c], xT_in=np.asarray(res[c]["xT_out"]))
            for nm in GATHER:
                im[nm + "_g"] = np.stack([np.asarray(res[4 * b + r][nm]) for r in range(4)])
            for nm in LOCAL:
                key = nm if l == 0 else nm + "_o"
                im[nm] = np.asarray(res[c][key])
            ims.append(im)
        res = run_bass_kernel_spmd(nc, ims, core_ids=cores).results
    out = np.zeros((B, S, D), np.float32)
    for c in cores:
        out[c // 4][own_positions(c)] = np.asarray(res[c]["xT_out"]).T
    return out


PIECES = [
    ([192, NT], [("kT_mla", "heads", 0, 3)]),
    ([192, NT], [("kT_mla", "heads", 3, 6)]),
    ([32, NT], [("kpeT", "rows", 0, 32)]),
    ([256, NT], [("vT_cmp", "rows", 0, 128), ("kT_sel", "rows", 128, 256)]),
    ([256, NT], [("kT_cmp", "rows", 0, 128), ("kT_win", "rows", 128, 256)]),
    ([256, NT], [("kT_sb", "rows", 0, 256)]),
    ([1024, 6, 65], [("v_mla", "half", 0, 0)]),
    ([1024, 6, 65], [("v_mla", "half", 1, 1)]),
    ([NT, 2, 65], [("v_sel", "all", 0, 0)]),
    ([NT, 2, 65], [("v_win", "all", 0, 0)]),
    ([NT, 256], [("v_sb", "all", 0, 0)]),
]


def make_pieces(nc, l):
    V = {"kT_mla": [None] * 6, "kT_mla_g": [None] * 6, "v_mla": [None] * 2, "v_mla_g": [None] * 2}
    GB = {"kT_mla": [None] * 6, "v_mla": [None] * 2}
    cc = []
    for k, (shp, members) in enumerate(PIECES):
        n = int(np.prod(shp))
        w = n // 128
        gs = nc.dram_tensor("gs%d_%d" % (l, k), [128, w], BF16, kind="Internal").ap()
        gd = nc.dram_tensor("gd%d_%d" % (l, k), [512, w], BF16, kind="Internal").ap()
        pb = Buf("piece%d_%d" % (l, k))
        cc.append((gs, gd, pb))
        fs = gs.rearrange("p w -> (p w)")
        fd = gd.rearrange("(r p) w -> r (p w)", r=4)
        if len(shp) == 2:
            ns = fs.rearrange("(a c) -> a c", c=shp[1])
            nd = fd.rearrange("r (a c) -> r a c", c=shp[1])
        else:
            ns = fs.rearrange("(t h x) -> t h x", h=shp[1], x=shp[2])
            nd = fd.rearrange("r (t h x) -> r t h x", h=shp[1], x=shp[2])
        for nm, kind, lo, hi in members:
            if kind == "heads":
                for h in range(lo, hi):
                    V[nm][h] = ns[(h - lo) * 64:(h - lo + 1) * 64, :]
                    V[nm + "_g"][h] = nd[:, (h - lo) * 64:(h - lo + 1) * 64, :]
                    GB[nm][h] = pb
            elif kind == "rows":
                V[nm] = ns[lo:hi, :]
                V[nm + "_g"] = nd[:, lo:hi, :]
                GB[nm] = pb
            elif kind == "half":
                V[nm][lo] = ns
                V[nm + "_g"][lo] = nd
                GB[nm][lo] = pb
            else:
                V[nm] = ns
                V[nm + "_g"] = nd
                GB[nm] = pb
    V["cc"] = cc
    V["gb"] = GB
    return V


def build_fused():
    nc = bass.Bass("TRN2", target_bir_lowering=False)
    A = {}
    for nm, shp in W_IN:
        A[nm] = nc.dram_tensor(nm, shp, F32, kind="ExternalInput").ap()
    for nm, shp, dt in C_IN:
        A[nm] = nc.dram_tensor(nm, shp, dt, kind="ExternalInput").ap()
    A["xT_in"] = nc.dram_tensor("xT_in", [D, NT], F32, kind="ExternalInput").ap()
    A["yT_out"] = nc.dram_tensor("yT_out", [D, NT], F32, kind="ExternalOutput").ap()
    A["xT_d"] = nc.dram_tensor("xT_d", [D, NT], F32, kind="Internal").ap()
    A["oT_d"] = nc.dram_tensor("oT_d", [128, 8, NT], BF16, kind="Internal").ap()
    LA = []
    for l in range(L):
        V = make_pieces(nc, l)
        for nm in LOCAL:
            shp, dt = POUT[nm]
            V[nm] = nc.dram_tensor("%s_l%d" % (nm, l), shp, dt, kind="Internal").ap()
        LA.append(V)
    with ExitStack() as stack:
        T = Tracker(nc, stack)
        R = setup_common(nc, stack, T)
        st = None
        GRP = [[0, 1, 2, 3], [4, 5, 6, 7]]

        def mk_hook(ll):
            def hook():
                T.wait_dma_all("pool")
                for k in (2, 0, 6, 7, 1):
                    gs, gd, pb = LA[ll]["cc"][k]
                    T.collective(T.new_dma_sem("cc%d_%d" % (ll, k)), gs, gd, GRP, writes=[pb])
            return hook

        for l in range(L):
            if l == 0:
                emit_layer_X(R, dict(A, **LA[0]), 0, do_post=False, do_pre=True, final=False,
                             xin=A["xT_in"], xout=A["xT_d"], mid_hook=mk_hook(0))
            T.barrier_all()
            for k in (4, 3, 8, 9, 5, 10):
                gs, gd, pb = LA[l]["cc"][k]
                T.collective(T.new_dma_sem("cc%d_%d" % (l, k)), gs, gd, GRP, writes=[pb])
            emit_layer_A(R, dict(A, **LA[l]), l)
            last = (l == L - 1)
            AX = dict(A, **(LA[l + 1] if not last else {}))
            st = emit_layer_X(R, AX, l, do_post=True, do_pre=not last, final=last,
                              xin=A["xT_d"], xout=(A["yT_out"] if last else A["xT_d"]),
                              mid_hook=(None if last else mk_hook(l + 1)))
        nc.sync.wait_ge(T.sem[st], T.cnt[st])
    return nc


def kernel_unfused(**inp):
    return _kernel_unfused_impl(**inp)


def kernel_fused(**inp):
    x = np.asarray(inp["x"], dtype=np.float32)
    W = _host_weights(inp)
    cores = list(range(8))
    nc = build_fused()
    ims = []
    for c in cores:
        xT = np.ascontiguousarray(x[c // 4][own_positions(c)].T)
        ims.append(dict(W, **_core_consts(c), xT_in=xT))
    res = run_bass_kernel_spmd(nc, ims, core_ids=cores).results
    out = np.zeros((B, S, D), np.float32)
    for c in cores:
        out[c // 4][own_positions(c)] = np.asarray(res[c]["yT_out"]).T
    return out


def kernel(**inp):
    return kernel_fused(**inp)
```

```python
import numpy as np
import ml_dtypes
from contextlib import ExitStack
import concourse.bass as bass
import concourse.mybir as mybir
from concourse.bass_utils import run_bass_kernel_spmd

F32 = mybir.dt.float32
BF16 = mybir.dt.bfloat16
AF = mybir.ActivationFunctionType
ALU = mybir.AluOpType
AX = mybir.AxisListType

D = 1024
S = 8192
B = 2
DFF = 2816
NT = 2048
NBLK = 16
EPS = 1e-6
DIN = 2354


_UN = [0]
CC_INC = 1


def U(name):
    _UN[0] += 1
    return "%s_u%d" % (name, _UN[0])


class Buf:
    __slots__ = ("name", "w", "r")

    def __init__(self, name):
        self.name = name
        self.w = None
        self.r = {}


class Tracker:
    def __init__(self, nc, stack):
        self.nc = nc
        self.stack = stack
        self.eng = {"pe": nc.tensor, "act": nc.scalar, "dve": nc.vector,
                    "pool": nc.gpsimd, "sp": nc.sync}
        self.sem = {}
        self.cnt = {}
        self.seen = {k: {} for k in self.eng}
        for k in self.eng:
            self.sem[k] = stack.enter_context(nc.semaphore("s_" + k))
            self.cnt[k] = 0
        self.ndma = 0

    def new_dma_sem(self, name):
        key = "dma_" + name + "_%d" % self.ndma
        self.ndma += 1
        self.sem[key] = self.stack.enter_context(self.nc.semaphore(key))
        self.cnt[key] = 0
        return key

    def _deps(self, e, reads, writes, ignore=None):
        deps = {}

        def add(k, c):
            if c > deps.get(k, 0):
                deps[k] = c
        for b in reads:
            if b.w is not None:
                add(*b.w)
        for b in writes:
            if b.w is not None:
                add(*b.w)
            for k, c in b.r.items():
                add(k, c)
        for k, c in deps.items():
            if (k == "pe" and e == "pe") or k == ignore:
                continue
            if k.startswith("dma_"):
                c = max(c, self.cnt[k])
            if c > self.seen[e].get(k, 0):
                self.eng[e].wait_ge(self.sem[k], c)
                self.seen[e][k] = c

    def op(self, e, fn, reads=(), writes=()):
        self._deps(e, reads, writes)
        ins = fn(self.eng[e])
        self.cnt[e] += 1
        c = self.cnt[e]
        ins.then_inc(self.sem[e], 1)
        for b in reads:
            if c > b.r.get(e, 0):
                b.r[e] = c
        for b in writes:
            b.w = (e, c)
            b.r = {}
        return ins

    def dma(self, q, dsem, out, in_, reads=(), writes=()):
        self._deps(q, reads, writes, ignore=dsem)
        ins = self.eng[q].dma_start(out=out, in_=in_)
        self.cnt[dsem] += 16
        c = self.cnt[dsem]
        ins.then_inc(self.sem[dsem], 16)
        for b in reads:
            if c > b.r.get(dsem, 0):
                b.r[dsem] = c
        for b in writes:
            b.w = (dsem, c)
            b.r = {}
        return ins

    def collective(self, dsem, src, dst, groups, reads=(), writes=()):
        self._deps("pool", reads, writes, ignore=dsem)
        ins = self.nc.gpsimd.collective_compute("AllGather", ALU.bypass, replica_groups=groups,
                                                ins=[src.opt()], outs=[dst.opt()])
        self.cnt[dsem] += CC_INC
        c = self.cnt[dsem]
        ins.then_inc(self.sem[dsem], CC_INC)
        for b in reads:
            if c > b.r.get(dsem, 0):
                b.r[dsem] = c
        for b in writes:
            b.w = (dsem, c)
            b.r = {}
        return ins

    def wait_dma_all(self, e):
        for k, c in self.cnt.items():
            if k.startswith("dma_") and c > self.seen[e].get(k, 0):
                self.eng[e].wait_ge(self.sem[k], c)
                self.seen[e][k] = c

    def barrier_all(self):
        for e in self.eng:
            for k, c in self.cnt.items():
                if k == e or c == 0:
                    continue
                if c > self.seen[e].get(k, 0):
                    self.eng[e].wait_ge(self.sem[k], c)
                    self.seen[e][k] = c


class Res:
    pass


def setup_common(nc, stack, T):
    R = Res()
    R.nc, R.T, R.stack = nc, T, stack
    R.ExitStack = ExitStack
    R.ps = []
    R.psb = []
    for i in range(8):
        R.ps.append(stack.enter_context(nc.psum_tensor("ps%d" % i, [128, 512], F32)))
        R.psb.append(Buf("ps%d" % i))
    R.xb = [[Buf("x%d_%d" % (k, t)) for t in range(4)] for k in range(8)]
    R.onesm = stack.enter_context(nc.sbuf_tensor(U("onesm"), [128, 128], BF16))
    R.onesm_b = Buf("onesm")
    T.op("pool", lambda e: e.memset(R.onesm[:], 1.0), writes=[R.onesm_b])
    return R


def alloc_xT(R, stack):
    R.xT = stack.enter_context(R.nc.sbuf_tensor(U("xT"), [128, 8, NT], F32))


def emit_norm(R, hT, hb, gam, gam_b, sq, sq_b, rstd, rstd_b, ssbank):
    T = R.T
    ss, ss_b = R.ps[ssbank], R.psb[ssbank]
    for tt in range(4):
        sl = slice(tt * 512, (tt + 1) * 512)
        for k in range(8):
            a = k % 2
            T.op("act", lambda e: e.activation(out=sq[a][:], in_=R.xT[:, k, sl], func=AF.Square),
                 reads=[R.xb[k][tt]], writes=[sq_b[a]])
            T.op("pe", lambda e: e.matmul(ss[:], lhsT=R.onesm[:], rhs=sq[a][:], start=(k == 0), stop=(k == 7)),
                 reads=[sq_b[a], R.onesm_b], writes=[ss_b])
        T.op("act", lambda e: e.activation(out=rstd[:], in_=ss[:], func=AF.Sqrt, bias=EPS, scale=1.0 / D),
             reads=[ss_b], writes=[rstd_b])
        T.op("dve", lambda e: e.reciprocal(out=rstd[:], in_=rstd[:]), reads=[rstd_b], writes=[rstd_b])
        for k in range(8):
            T.op("dve", lambda e: e.scalar_tensor_tensor(out=hT[:, k, sl], in0=R.xT[:, k, sl],
                                                         scalar=gam[:, k:k + 1], in1=rstd[:],
                                                         op0=ALU.mult, op1=ALU.mult),
                 reads=[R.xb[k][tt], rstd_b, gam_b], writes=[hb[k][tt]])


def emit_ffn(R, hT, hb, wg_d, wu_d, wd_d, W):
    T = R.T
    nfg = 6
    n_g = n_y = 0
    for fg in range(nfg):
        ncf = 4 if fg < 5 else 2
        wcols = ncf * 128
        c0 = fg * 512
        s = fg % 2
        wgs, wus, wds, wb, dsem = W["wg"][s], W["wu"][s], W["wd"][s], W["wb"][s], W["dsem"][s]
        T.dma("pool", dsem, wgs[:, :, 0:wcols],
              wg_d[:, c0:c0 + wcols].rearrange("(k p) c -> p k c", p=128), writes=[wb])
        T.dma("pool", dsem, wus[:, :, 0:wcols],
              wu_d[:, c0:c0 + wcols].rearrange("(k p) c -> p k c", p=128), writes=[wb])
        T.dma("pool", dsem, wds[:, 0:ncf, :],
              wd_d[c0:c0 + wcols, :].rearrange("(c p) m -> p c m", p=128), writes=[wb])
        for tt in range(4):
            sl = slice(tt * 512, (tt + 1) * 512)
            asl = (fg * 4 + tt) % 2
            for c in range(ncf):
                gi, ui = W["gbanks"][n_g % 2], W["ubanks"][n_g % 2]
                sgi = n_g % 2
                n_g += 1
                for k in range(8):
                    T.op("pe", lambda e: e.matmul(R.ps[gi][:], lhsT=wgs[:, k, c * 128:(c + 1) * 128],
                                                  rhs=hT[:, k, sl], start=(k == 0), stop=(k == 7)),
                         reads=[wb, hb[k][tt]], writes=[R.psb[gi]])
                for k in range(8):
                    T.op("pe", lambda e: e.matmul(R.ps[ui][:], lhsT=wus[:, k, c * 128:(c + 1) * 128],
                                                  rhs=hT[:, k, sl], start=(k == 0), stop=(k == 7)),
                         reads=[wb, hb[k][tt]], writes=[R.psb[ui]])
                T.op("act", lambda e: e.activation(out=W["sg"][sgi][:], in_=R.ps[gi][:], func=AF.Silu),
                     reads=[R.psb[gi]], writes=[W["sg_b"][sgi]])
                T.op("dve", lambda e: e.tensor_tensor(out=W["act"][asl][:, c, :], in0=W["sg"][sgi][:],
                                                      in1=R.ps[ui][:], op=ALU.mult),
                     reads=[W["sg_b"][sgi], R.psb[ui]], writes=[W["act_b"][asl][c]])
            for dmc in range(8):
                yi = W["ybanks"][n_y % 2]
                n_y += 1
                for c in range(ncf):
                    T.op("pe", lambda e: e.matmul(R.ps[yi][:], lhsT=wds[:, c, dmc * 128:(dmc + 1) * 128],
                                                  rhs=W["act"][asl][:, c, :], start=(c == 0), stop=(c == ncf - 1)),
                         reads=[wb, W["act_b"][asl][c]], writes=[R.psb[yi]])
                T.op("dve", lambda e: e.scalar_tensor_tensor(out=R.xT[:, dmc, sl], in0=R.ps[yi][:], scalar=0.5,
                                                             in1=R.xT[:, dmc, sl], op0=ALU.mult, op1=ALU.add),
                     reads=[R.psb[yi], R.xb[dmc][tt]], writes=[R.xb[dmc][tt]])


def alloc_ffn_work(R):
    nc, stack, T = R.nc, R.stack, R.T
    W = {}
    W["wg"] = [stack.enter_context(nc.sbuf_tensor(U("wg%d" % i), [128, 8, 512], BF16)) for i in range(2)]
    W["wu"] = [stack.enter_context(nc.sbuf_tensor(U("wu%d" % i), [128, 8, 512], BF16)) for i in range(2)]
    W["wd"] = [stack.enter_context(nc.sbuf_tensor(U("wd%d" % i), [128, 4, 1024], BF16)) for i in range(2)]
    W["wb"] = [Buf("wslot%d" % i) for i in range(2)]
    W["dsem"] = [T.new_dma_sem("ffnw%d" % i) for i in range(2)]
    W["sg"] = [stack.enter_context(nc.sbuf_tensor(U("sg%d" % i), [128, 512], F32)) for i in range(2)]
    W["sg_b"] = [Buf("sg%d" % i) for i in range(2)]
    W["act"] = [stack.enter_context(nc.sbuf_tensor(U("act%d" % i), [128, 4, 512], BF16)) for i in range(2)]
    W["act_b"] = [[Buf("act%d_%d" % (i, c)) for c in range(4)] for i in range(2)]
    W["gbanks"], W["ubanks"], W["ybanks"] = [0, 1], [2, 3], [4, 5]
    return W


C_CQ, C_CKV, C_KR, C_NQ, C_NKC, C_NVC, C_NKS, C_NVS, C_NKW, C_NVW, C_NG, C_SQ, C_SK, C_SV = (
    0, 256, 384, 416, 800, 928, 1056, 1184, 1312, 1440, 1568, 1586, 1842, 2098)
SW_KR, SW_NQ, SW_NKC, SW_NKS, SW_NKW = 0, 32, 416, 544, 672
NSW = 800


def emit_stage_P(R, hT, hb, A, mid_hook=None):
    nc, T = R.nc, R.T
    with R.ExitStack() as ph:
        def sb(name, shape, dt):
            return ph.enter_context(nc.sbuf_tensor(U(name), shape, dt))
        win = sb("P_win", [128, 8, DIN], BF16)
        wsw = sb("P_wsw", [128, 8, NSW], BF16)
        wuq = sb("P_wuq", [128, 2, 576], BF16)
        wuqs = sb("P_wuqs", [128, 2, 576], BF16)
        wukv = sb("P_wukv", [128, 768], BF16)
        sm = sb("P_sm", [128, 32], F32)
        wb_, smb = Buf("P_w"), Buf("P_sm")
        dw = T.new_dma_sem("Pw")
        T.dma("pool", dw, win[:], A["w_in"].rearrange("(k p) c -> p k c", p=128), writes=[wb_])
        T.dma("pool", dw, wsw[:], A["w_in_sw"].rearrange("(k p) c -> p k c", p=128), writes=[wb_])
        T.dma("pool", dw, wuq[:], A["w_uq"].rearrange("(k p) c -> p k c", p=128), writes=[wb_])
        T.dma("pool", dw, wuqs[:], A["w_uq_sw"].rearrange("(k p) c -> p k c", p=128), writes=[wb_])
        T.dma("pool", dw, wukv[:], A["w_ukv"], writes=[wb_])
        dw2 = T.new_dma_sem("Psm")
        T.dma("sp", dw2, sm[:], A["smallsP"], writes=[smb])
        tabs = []
        for i in range(2):
            tabs.append(dict(
                cosM=sb("P_cosM%d" % i, [96, 512], F32), sinM=sb("P_sinM%d" % i, [96, 512], F32),
                cosK=sb("P_cosK%d" % i, [32, 512], F32), sinK=sb("P_sinK%d" % i, [32, 512], F32),
                cosN=sb("P_cosN%d" % i, [128, 512], F32), sinN=sb("P_sinN%d" % i, [128, 512], F32),
                b=Buf("P_tab%d" % i), sem=T.new_dma_sem("Ptab%d" % i)))
        sq = [sb("P_sq%d" % i, [128, 512], BF16) for i in range(2)]
        sq_b = [Buf("P_sq0"), Buf("P_sq1")]
        rs = sb("P_rs", [128, 512], F32)
        rs_b = Buf("P_rs")
        cqn = sb("P_cqn", [128, 2, 512], BF16)
        cqn_b = Buf("P_cqn")
        ckvn = sb("P_ckvn", [128, 512], BF16)
        ckvn_b = Buf("P_ckvn")
        t1 = [sb("P_t1_%d" % i, [128, 512], F32) for i in range(2)]
        t2 = [sb("P_t2_%d" % i, [128, 512], F32) for i in range(2)]
        t_b = [Buf("P_t0"), Buf("P_t1")]
        NST = 4
        stg = [sb("P_stg%d" % i, [128, 512], BF16) for i in range(NST)]
        stg_b = [Buf("P_stg%d" % i) for i in range(NST)]
        stg_sem = [T.new_dma_sem("Pstg%d" % i) for i in range(NST)]
        vst = [sb("P_vst%d" % i, [128, 6, 65], BF16) for i in range(2)]
        vst_b = [Buf("P_vst0"), Buf("P_vst1")]
        vst_sem = [T.new_dma_sem("Pvst%d" % i) for i in range(2)]
        v2st = [sb("P_v2st%d" % i, [128, 2, 2, 65], BF16) for i in range(2)]
        v2st_b = [Buf("P_v2st0"), Buf("P_v2st1")]
        v2st_sem = [T.new_dma_sem("Pv2st%d" % i) for i in range(2)]
        vsb = [sb("P_vsb%d" % i, [128, 256], BF16) for i in range(2)]
        vsb_b = [Buf("P_vsb0"), Buf("P_vsb1")]
        vsb_sem = [T.new_dma_sem("Pvsb%d" % i) for i in range(2)]
        gst = [sb("P_gst%d" % i, [128, 18], F32) for i in range(2)]
        gst_b = [Buf("P_gst0"), Buf("P_gst1")]
        gst_sem = [T.new_dma_sem("Pgst%d" % i) for i in range(2)]
        for i in range(2):
            T.op("pool", lambda e: e.memset(vst[i][:], 1.0), writes=[vst_b[i]])
            T.op("pool", lambda e: e.memset(v2st[i][:], 1.0), writes=[v2st_b[i]])

        cnt = {"bank": 0, "stg": 0, "t": 0}

        def nbank():
            cnt["bank"] += 1
            return cnt["bank"] % 4

        def chain(bank, M, lhs_fn, rhs_fn, nk, reads, N=512, rows=None):
            o = R.ps[bank][0:M, 0:N] if rows is None else R.ps[bank][rows[0]:rows[1], 0:N]
            for k in range(nk):
                T.op("pe", lambda e: e.matmul(o, lhsT=lhs_fn(k), rhs=rhs_fn(k), start=(k == 0), stop=(k == nk - 1)),
                     reads=reads(k), writes=[R.psb[bank]])

        def store(src_ps_ap, src_bufs, M, dst_dram, use_act=False):
            i = cnt["stg"] % NST
            cnt["stg"] += 1
            eng = "act" if use_act else "dve"
            if use_act:
                T.op("act", lambda e: e.activation(out=stg[i][0:M, :], in_=src_ps_ap, func=AF.Copy),
                     reads=src_bufs, writes=[stg_b[i]])
            else:
                T.op("dve", lambda e: e.tensor_copy(out=stg[i][0:M, :], in_=src_ps_ap),
                     reads=src_bufs, writes=[stg_b[i]])
            T.dma("sp", stg_sem[i], dst_dram, stg[i][0:M, :], reads=[stg_b[i]])

        def rope_store(bankA, bankB, M, cos, sin, tb, dst_dram):
            j = cnt["t"] % 2
            cnt["t"] += 1
            i = cnt["stg"] % NST
            cnt["stg"] += 1
            T.op("dve", lambda e: e.tensor_tensor(out=t1[j][0:M, :], in0=R.ps[bankA][0:M, :], in1=cos, op=ALU.mult),
                 reads=[R.psb[bankA], tb], writes=[t_b[j]])
            T.op("dve", lambda e: e.tensor_tensor(out=t2[j][0:M, :], in0=R.ps[bankB][0:M, :], in1=sin, op=ALU.mult),
                 reads=[R.psb[bankB], tb], writes=[t_b[j]])
            T.op("pool", lambda e: e.tensor_tensor(out=stg[i][0:M, :], in0=t1[j][0:M, :], in1=t2[j][0:M, :], op=ALU.add),
                 reads=[t_b[j]], writes=[stg_b[i]])
            T.dma("sp", stg_sem[i], dst_dram, stg[i][0:M, :], reads=[stg_b[i]])

        for tt in range(4):
            sl = slice(tt * 512, (tt + 1) * 512)
            tab = tabs[tt % 2]
            for nm in ("cosM", "sinM", "cosK", "sinK"):
                T.dma("sp", tab["sem"], tab[nm][:], A[nm][:, sl], writes=[tab["b"]])
            hreads = lambda k: [wb_, hb[k][tt]]
            for c in range(2):
                chain(4 + c, 128, lambda k: win[:, k, C_CQ + c * 128:C_CQ + (c + 1) * 128],
                      lambda k: hT[:, k, sl], 8, hreads)
            chain(6, 128, lambda k: win[:, k, C_CKV:C_CKV + 128], lambda k: hT[:, k, sl], 8, hreads)
            for c in range(2):
                T.op("act", lambda e: e.activation(out=sq[c][:], in_=R.ps[4 + c][:], func=AF.Square),
                     reads=[R.psb[4 + c]], writes=[sq_b[c]])
            for c in range(2):
                T.op("pe", lambda e: e.matmul(R.ps[7][:], lhsT=R.onesm[:], rhs=sq[c][:], start=(c == 0), stop=(c == 1)),
                     reads=[sq_b[c], R.onesm_b], writes=[R.psb[7]])
            T.op("act", lambda e: e.activation(out=rs[:], in_=R.ps[7][:], func=AF.Sqrt, bias=EPS, scale=1.0 / 256),
                 reads=[R.psb[7]], writes=[rs_b])
            T.op("dve", lambda e: e.reciprocal(out=rs[:], in_=rs[:]), reads=[rs_b], writes=[rs_b])
            for c in range(2):
                T.op("dve", lambda e: e.scalar_tensor_tensor(out=cqn[:, c, :], in0=R.ps[4 + c][:], scalar=sm[:, c:c + 1],
                                                             in1=rs[:], op0=ALU.mult, op1=ALU.mult),
                     reads=[R.psb[4 + c], rs_b, smb], writes=[cqn_b])
            T.op("act", lambda e: e.activation(out=sq[0][:], in_=R.ps[6][:], func=AF.Square),
                 reads=[R.psb[6]], writes=[sq_b[0]])
            T.op("pe", lambda e: e.matmul(R.ps[7][:], lhsT=R.onesm[:], rhs=sq[0][:], start=True, stop=True),
                 reads=[sq_b[0], R.onesm_b], writes=[R.psb[7]])
            T.op("act", lambda e: e.activation(out=rs[:], in_=R.ps[7][:], func=AF.Sqrt, bias=EPS, scale=1.0 / 128),
                 reads=[R.psb[7]], writes=[rs_b])
            T.op("dve", lambda e: e.reciprocal(out=rs[:], in_=rs[:]), reads=[rs_b], writes=[rs_b])
            T.op("dve", lambda e: e.scalar_tensor_tensor(out=ckvn[:], in0=R.ps[6][:], scalar=sm[:, 2:3],
                                                         in1=rs[:], op0=ALU.mult, op1=ALU.mult),
                 reads=[R.psb[6], rs_b, smb], writes=[ckvn_b])
            for h in range(6):
                ba, bb = nbank(), 4 + (h % 2)
                chain(ba, 96, lambda c: wuq[:, c, h * 96:(h + 1) * 96], lambda c: cqn[:, c, :], 2,
                      lambda c: [wb_, cqn_b])
                chain(bb, 96, lambda c: wuqs[:, c, h * 96:(h + 1) * 96], lambda c: cqn[:, c, :], 2,
                      lambda c: [wb_, cqn_b])
                rope_store(ba, bb, 96, tab["cosM"][:], tab["sinM"][:], tab["b"], A["qT_mla"][h, :, sl])
            for h in range(6):
                ba = nbank()
                chain(ba, 64, lambda c: wukv[:, h * 128:h * 128 + 64], lambda c: ckvn[:], 1, lambda c: [wb_, ckvn_b])
                store(R.ps[ba][0:64, :], [R.psb[ba]], 64, A["kT_mla"][h][:, sl], use_act=(h % 2 == 0))
            ba, bb = nbank(), 4
            chain(ba, 32, lambda k: win[:, k, C_KR:C_KR + 32], lambda k: hT[:, k, sl], 8, hreads)
            chain(bb, 32, lambda k: wsw[:, k, SW_KR:SW_KR + 32], lambda k: hT[:, k, sl], 8, hreads)
            rope_store(ba, bb, 32, tab["cosK"][:], tab["sinK"][:], tab["b"], A["kpeT"][:, sl])
            for bl in range(4):
                tb = tt * 4 + bl
                lsl = slice(bl * 128, (bl + 1) * 128)
                i = tb % 2
                ba = nbank()
                T.op("pe", lambda e: e.matmul(R.ps[ba][:, 0:384], lhsT=ckvn[:, lsl],
                                              rhs=wukv[:].rearrange("p (h x) -> p h x", x=128)[:, :, 64:128],
                                              start=True, stop=True),
                     reads=[wb_, ckvn_b], writes=[R.psb[ba]])
                T.op("dve", lambda e: e.tensor_copy(out=vst[i][:, :, 0:64],
                                                    in_=R.ps[ba][:, 0:384].rearrange("p (h x) -> p h x", x=64)),
                     reads=[R.psb[ba]], writes=[vst_b[i]])
                T.dma("sp", vst_sem[i], A["v_mla"][tb // 8][(tb % 8) * 128:(tb % 8 + 1) * 128, :, :], vst[i][:], reads=[vst_b[i]])
        if mid_hook is not None:
            mid_hook()
        for tt in range(4):
            sl = slice(tt * 512, (tt + 1) * 512)
            tab = tabs[tt % 2]
            for nm in ("cosN", "sinN"):
                T.dma("sp", tab["sem"], tab[nm][:], A[nm][:, sl], writes=[tab["b"]])
            hreads = lambda k: [wb_, hb[k][tt]]
            ropes = [(C_NQ + 128 * c, SW_NQ + 128 * c, A["qT_nsa"][128 * c:128 * (c + 1), sl]) for c in range(3)]
            ropes += [(C_NKC, SW_NKC, A["kT_cmp"][:, sl]), (C_NKS, SW_NKS, A["kT_sel"][:, sl]),
                      (C_NKW, SW_NKW, A["kT_win"][:, sl])]
            for n, (ca, cs, dst) in enumerate(ropes):
                ba, bb = nbank(), 4 + (n % 2)
                chain(ba, 128, lambda k: win[:, k, ca:ca + 128], lambda k: hT[:, k, sl], 8, hreads)
                chain(bb, 128, lambda k: wsw[:, k, cs:cs + 128], lambda k: hT[:, k, sl], 8, hreads)
                rope_store(ba, bb, 128, tab["cosN"][:], tab["sinN"][:], tab["b"], dst)
            plain = [(C_NVC, A["vT_cmp"][:, sl]), (C_SQ, A["qT_sb"][0:128, sl]), (C_SQ + 128, A["qT_sb"][128:256, sl]),
                     (C_SK, A["kT_sb"][0:128, sl]), (C_SK + 128, A["kT_sb"][128:256, sl])]
            for n, (ca, dst) in enumerate(plain):
                ba = nbank()
                chain(ba, 128, lambda k: win[:, k, ca:ca + 128], lambda k: hT[:, k, sl], 8, hreads)
                store(R.ps[ba][:, :], [R.psb[ba]], 128, dst, use_act=(n % 2 == 0))
            for bl in range(4):
                tb = tt * 4 + bl
                tsl = slice(tb * 128, (tb + 1) * 128)
                lsl = slice(bl * 128, (bl + 1) * 128)
                i = tb % 2
                ba = nbank()
                for n, ca in enumerate((C_NVS, C_NVW)):
                    for k in range(8):
                        T.op("pe", lambda e: e.matmul(R.ps[ba][:, n * 128:(n + 1) * 128], lhsT=hT[:, k, tsl],
                                                      rhs=win[:, k, ca:ca + 128], start=(k == 0), stop=(k == 7)),
                             reads=[wb_, hb[k][tt]], writes=[R.psb[ba]])
                for k in range(8):
                    T.op("pe", lambda e: e.matmul(R.ps[ba][:, 256:274], lhsT=hT[:, k, tsl],
                                                  rhs=win[:, k, C_NG:C_NG + 18], start=(k == 0), stop=(k == 7)),
                         reads=[wb_, hb[k][tt]], writes=[R.psb[ba]])
                T.op("dve", lambda e: e.tensor_copy(out=v2st[i][:, :, :, 0:64],
                                                    in_=R.ps[ba][:, 0:256].rearrange("p (a h x) -> p a h x", a=2, x=64)),
                     reads=[R.psb[ba]], writes=[v2st_b[i]])
                T.dma("sp", v2st_sem[i], A["v_sel"][tsl, :, :], v2st[i][:, 0, :, :], reads=[v2st_b[i]])
                T.dma("sp", v2st_sem[i], A["v_win"][tsl, :, :], v2st[i][:, 1, :, :], reads=[v2st_b[i]])
                T.op("dve", lambda e: e.tensor_tensor(out=gst[i][:], in0=R.ps[ba][:, 256:274], in1=sm[:, 8:26], op=ALU.add),
                     reads=[R.psb[ba], smb], writes=[gst_b[i]])
                T.op("act", lambda e: e.activation(out=gst[i][:], in_=gst[i][:], func=AF.Sigmoid),
                     reads=[gst_b[i]], writes=[gst_b[i]])
                T.dma("sp", gst_sem[i], A["gates"][tsl, :], gst[i][:], reads=[gst_b[i]])
                ba = nbank()
                for k in range(8):
                    T.op("pe", lambda e: e.matmul(R.ps[ba][:, 0:256], lhsT=hT[:, k, tsl],
                                                  rhs=win[:, k, C_SV:C_SV + 256], start=(k == 0), stop=(k == 7)),
                         reads=[wb_, hb[k][tt]], writes=[R.psb[ba]])
                T.op("act", lambda e: e.activation(out=vsb[i][:], in_=R.ps[ba][:, 0:256], func=AF.Copy),
                     reads=[R.psb[ba]], writes=[vsb_b[i]])
                T.dma("sp", vsb_sem[i], A["v_sb"][tsl, :], vsb[i][:], reads=[vsb_b[i]])
        T.barrier_all()


THETA = 500000.0


def own_positions(core):
    j = core % 4
    return ((4 * np.arange(NBLK)[:, None] + j) * 128 + np.arange(128)[None, :]).reshape(-1)


def _cs(pos, rot):
    half = rot // 2
    inv = np.float32(THETA) ** (-(np.arange(half, dtype=np.float32) / np.float32(half)))
    ang = pos.astype(np.float32)[None, :] * inv.astype(np.float32)[:, None]
    return np.cos(ang).astype(np.float32), np.sin(ang).astype(np.float32)


def rope_tables(core):
    pos = own_positions(core)
    n = pos.shape[0]
    c16, s16 = _cs(pos, 32)
    c8, s8 = _cs(pos, 16)
    cosM = np.ones((96, n), np.float32); sinM = np.zeros((96, n), np.float32)
    cosM[64:80] = c16; cosM[80:96] = c16; sinM[64:80] = -s16; sinM[80:96] = s16
    cosK = np.concatenate([c16, c16], 0); sinK = np.concatenate([-s16, s16], 0)
    cosN = np.ones((128, n), np.float32); sinN = np.zeros((128, n), np.float32)
    for h in range(2):
        cosN[h * 64:h * 64 + 8] = c8; cosN[h * 64 + 8:h * 64 + 16] = c8
        sinN[h * 64:h * 64 + 8] = -s8; sinN[h * 64 + 8:h * 64 + 16] = s8
    return dict(cosM=cosM, sinM=sinM, cosK=cosK, sinK=sinK, cosN=cosN, sinN=sinN)


def swap_cols_rope(w, head_w, rope0, rot):
    w = np.array(w, copy=True)
    half = rot // 2
    nh = w.shape[1] // head_w
    for h in range(nh):
        a = h * head_w + rope0
        tmp = w[:, a:a + half].copy()
        w[:, a:a + half] = w[:, a + half:a + rot]
        w[:, a + half:a + rot] = tmp
    return w


def make_w_in_sw(w_in):
    parts = [swap_cols_rope(w_in[:, C_KR:C_KR + 32], 32, 0, 32),
             swap_cols_rope(w_in[:, C_NQ:C_NQ + 384], 64, 0, 16),
             swap_cols_rope(w_in[:, C_NKC:C_NKC + 128], 64, 0, 16),
             swap_cols_rope(w_in[:, C_NKS:C_NKS + 128], 64, 0, 16),
             swap_cols_rope(w_in[:, C_NKW:C_NKW + 128], 64, 0, 16)]
    return np.ascontiguousarray(np.concatenate(parts, axis=1))


def make_smallsP(q_norm, kv_norm, gate_bias):
    sm = np.zeros((128, 32), np.float32)
    sm[:, 0:2] = q_norm.reshape(2, 128).T
    sm[:, 2] = kv_norm
    sm[:, 8:26] = gate_bias[None, :]
    return sm


P_OUTS = [("qT_mla", [6, 96, NT], BF16), ("kT_mla", [6, 64, NT], BF16), ("kpeT", [32, NT], BF16),
          ("qT_nsa", [384, NT], BF16), ("kT_cmp", [128, NT], BF16), ("kT_sel", [128, NT], BF16),
          ("kT_win", [128, NT], BF16), ("vT_cmp", [128, NT], BF16), ("qT_sb", [256, NT], BF16),
          ("kT_sb", [256, NT], BF16), ("v_mla", [NT, 6, 65], BF16), ("v_sel", [NT, 2, 65], BF16),
          ("v_win", [NT, 2, 65], BF16), ("gates", [NT, 18], F32), ("v_sb", [NT, 256], BF16)]
P_INS = [("w_in", [D, DIN]), ("w_in_sw", [D, NSW]), ("w_uq", [256, 576]), ("w_uq_sw", [256, 576]),
         ("w_ukv", [128, 768]), ("smallsP", [128, 32]),
         ("cosM", [96, NT]), ("sinM", [96, NT]), ("cosK", [32, NT]), ("sinK", [32, NT]),
         ("cosN", [128, NT]), ("sinN", [128, NT])]


def make_masks(core):
    j = core % 4
    k = np.arange(128)[:, None]
    q = np.arange(128)[None, :]
    ones = np.ones((128, 128), np.float32)
    zeros = np.zeros((128, 128), np.float32)
    tri = (k <= q).astype(np.float32)
    stri = (k < q).astype(np.float32)
    gt = (k > q).astype(np.float32)
    M = np.zeros((128, 22, 128), np.float32)
    M[:, 20] = (k >= q).astype(np.float32)
    M[:, 21] = 1.0
    for d in range(4):
        M[:, d] = ones if d < j else (tri if d == j else zeros)
        M[:, 16 + d] = ones if d < j else (stri if d == j else zeros)
    for dd in range(8):
        d = dd - 4
        if d < j - 4 or d > j:
            M[:, 4 + dd] = zeros
        elif d == j - 4:
            M[:, 4 + dd] = gt
        elif d == j:
            M[:, 4 + dd] = tri
        else:
            M[:, 4 + dd] = ones
    for m4 in range(4):
        i16 = 4 * m4 + j
        M[:, 12 + m4] = (16 * k + 31 - 128 * i16 <= q).astype(np.float32)
    return M.astype(ml_dtypes.bfloat16)


def ktcol(kt):
    return (kt % 4) * NT + (kt // 4) * 128


def ktile(kt):
    return (kt % 4) * NBLK + (kt // 4)


def emit_oT_store(R, C, obank, G, ch, po, zcol=64, stride=65):
    T = R.T
    i = C["ev"] % 2
    C["ev"] += 1
    on, on_b, rz, rz_b = C["on"][i], C["on_b"][i], C["rz"][i], C["rz_b"][i]
    ov = R.ps[obank][:, 0:4 * stride].rearrange("p (m x) -> p m x", x=stride)
    T.op("dve", lambda e: e.reciprocal(out=rz[:], in_=ov[:, :, zcol:zcol + 1]), reads=[R.psb[obank]], writes=[rz_b])
    T.op("dve", lambda e: e.tensor_tensor(out=on[:], in0=ov[:, :, 0:64], in1=rz[:].broadcast_to([128, 4, 64]),
                                          op=ALU.mult),
         reads=[R.psb[obank], rz_b], writes=[on_b])
    emit_transpose_store(R, C, on, on_b, G, ch, po)


def emit_transpose_store(R, C, on, on_b, G, ch, po):
    T = R.T
    tb = C["tpbank"]
    for mm in range(4):
        T.op("pe", lambda e: e.transpose(out=R.ps[tb][0:64, mm * 128:(mm + 1) * 128], in_=on[:, mm, :],
                                         identity=C["ident"][:]),
             reads=[on_b, C["ident_b"]], writes=[R.psb[tb]])
    T.op("act", lambda e: e.activation(out=R.oT[po:po + 64, ch, G * 512:(G + 1) * 512], in_=R.ps[tb][0:64, :],
                                       func=AF.Copy),
         reads=[R.psb[tb]], writes=[R.oT_b[ch][G]])


def emit_dense_attn(R, C, KT, KT_b, V, V_b, QT, QT_b, dk, scale, obank, mask0,
                    selT=None, selT_b=None, vstride=65):
    T = R.T

    def run(G):
        steps = list(range(16 * G + 16))
        n = len(steps)
        stt = {}

        def stage_a(kt):
            mm_min = max(0, -((-(kt - 16 * G - 3)) // 4))
            c0 = mm_min * 128
            sb_ = C["sbanks"][C["sn"] % len(C["sbanks"])]
            C["sn"] += 1
            stt[kt] = [mm_min, c0, sb_, None]
            T.op("pe", lambda e: e.matmul(R.ps[sb_][:, c0:512], lhsT=KT[0:dk, ktcol(kt):ktcol(kt) + 128],
                                          rhs=QT[0:dk, G * 512 + c0:(G + 1) * 512], start=True, stop=(selT is None)),
                 reads=[KT_b, QT_b], writes=[R.psb[sb_]])
            if selT is not None:
                T.op("pe", lambda e: e.matmul(R.ps[sb_][:, c0:512], lhsT=C["Ebig"][:, kt * 128:(kt + 1) * 128],
                                              rhs=selT[:, G * 512 + c0:(G + 1) * 512], start=False, stop=True),
                     reads=[C["Ebig_b"], selT_b], writes=[R.psb[sb_]])

        def stage_b(kt):
            mm_min, c0, sb_, _ = stt[kt]
            pi = C["pn"] % len(C["PT"])
            C["pn"] += 1
            stt[kt][3] = pi
            PT, PT_b = C["PT"][pi], C["PT_b"][pi]
            T.op("act", lambda e: e.activation(out=PT[:, c0:512], in_=R.ps[sb_][:, c0:512], func=AF.Exp, scale=scale),
                 reads=[R.psb[sb_]], writes=[PT_b])
            if kt >= 16 * G:
                mmb = (kt - 16 * G) // 4
                d = (kt - 16 * G) % 4
                T.op("pool", lambda e: e.tensor_tensor(out=PT[:, mmb * 128:(mmb + 1) * 128],
                                                       in0=PT[:, mmb * 128:(mmb + 1) * 128],
                                                       in1=C["masks"][:, mask0 + d, :], op=ALU.mult),
                     reads=[PT_b, C["masks_b"]], writes=[PT_b])

        def stage_c(kt, first):
            mm_min, c0, sb_, pi = stt[kt]
            PT, PT_b = C["PT"][pi], C["PT_b"][pi]
            for mm in range(mm_min, 4):
                last = (kt == 16 * G + 4 * mm + 3)
                T.op("pe", lambda e: e.matmul(R.ps[obank][:, mm * vstride:(mm + 1) * vstride],
                                              lhsT=PT[:, mm * 128:(mm + 1) * 128], rhs=V[:, ktile(kt), :],
                                              start=(first and mm == mm_min), stop=last, skip_group_check=True),
                     reads=[PT_b, V_b], writes=[R.psb[obank]])

        for idx in range(n + 2):
            if idx < n:
                stage_a(steps[idx])
            if 1 <= idx <= n:
                stage_b(steps[idx - 1])
            if idx >= 2:
                stage_c(steps[idx - 2], idx == 2)
    return run


def alloc_attn_common(R, A, ph):
    nc, T = R.nc, R.T
    C = {"ev": 0, "sn": 0, "pn": 0, "sbanks": [0, 1, 2, 3], "tpbank": 5}

    def sb(name, shape, dt):
        return ph.enter_context(nc.sbuf_tensor(U(name), shape, dt))
    C["sb"] = sb
    C["masks"] = sb("A_masks", [128, 22, 128], BF16)
    C["masks_b"] = Buf("A_masks")
    C["ident"] = sb("A_ident", [128, 128], F32)
    C["ident_b"] = Buf("A_ident")
    ds = T.new_dma_sem("Aconst")
    T.dma("sp", ds, C["masks"][:], A["masks"], writes=[C["masks_b"]])
    T.dma("sp", ds, C["ident"][:], A["ident"], writes=[C["ident_b"]])
    C["PT"] = [sb("A_PT%d" % i, [128, 512], BF16) for i in range(6)]
    C["PT_b"] = [Buf("A_PT%d" % i) for i in range(6)]
    C["on"] = [sb("A_on%d" % i, [128, 4, 64], F32) for i in range(2)]
    C["on_b"] = [Buf("A_on%d" % i) for i in range(2)]
    C["rz"] = [sb("A_rz%d" % i, [128, 4, 1], F32) for i in range(2)]
    C["rz_b"] = [Buf("A_rz%d" % i) for i in range(2)]
    C["kcT"] = [sb("A_kcT%d" % i, [64, 512], BF16) for i in range(2)]
    C["vc"] = [sb("A_vc%d" % i, [128, 4, 65], BF16) for i in range(2)]
    C["kc_b"] = [Buf("A_kc%d" % i) for i in range(2)]
    return C


def gb(A, nm, i=None):
    d = A.get("gb")
    if d is None:
        return []
    b = d[nm]
    if isinstance(b, list):
        b = b[i]
    return [b]


def emit_mla(R, C, A, hook=None):
    nc, T = R.nc, R.T
    ngrp = 0
    hook_left = [4]
    with R.ExitStack() as ph:
        def sb(name, shape, dt):
            return ph.enter_context(nc.sbuf_tensor(U(name), shape, dt))
        KT = [sb("M_KT%d" % i, [96, 4 * NT], BF16) for i in range(2)]
        V = [sb("M_V%d" % i, [128, 64, 65], BF16) for i in range(2)]
        QT = [sb("M_QT%d" % i, [96, NT], BF16) for i in range(2)]
        bufs = [Buf("M_slot%d" % i) for i in range(2)]
        sems = [T.new_dma_sem("Mslot%d" % i) for i in range(2)]
        for h in range(6):
            s = h % 2
            for r in range(4):
                T.dma("sp", sems[s], KT[s][0:64, r * NT:(r + 1) * NT], A["kT_mla_g"][h][r, :, :], reads=gb(A, "kT_mla", h),
                      writes=[bufs[s]])
                T.dma("sp", sems[s], KT[s][64:96, r * NT:(r + 1) * NT], A["kpeT_g"][r, :, :], reads=gb(A, "kpeT"), writes=[bufs[s]])
                for hf in range(2):
                    T.dma("sp", sems[s], V[s][:, r * NBLK + hf * 8:r * NBLK + hf * 8 + 8, :],
                          A["v_mla_g"][hf][r, :, h, :].rearrange("(m p) x -> p m x", p=128), reads=gb(A, "v_mla", hf),
                          writes=[bufs[s]])
            T.dma("sp", sems[s], QT[s][:], A["qT_mla"][h, :, :], writes=[bufs[s]])
            for G in range(4):
                obank = 6 + (G % 2)
                run = emit_dense_attn(R, C, KT[s], bufs[s], V[s], bufs[s], QT[s], bufs[s], 96, 96 ** -0.5,
                                      obank, 0)
                run(G)
                emit_oT_store(R, C, obank, G, h // 2, (h % 2) * 64)
                ngrp += 1
                if hook is not None and ngrp % 3 == 1 and hook_left[0] > 0:
                    hook_left[0] -= 1
                    next(hook)
        T.barrier_all()


def make_nsa_consts(core):
    j = core % 4
    c = np.arange(512)[:, None]
    n = np.arange(128)[None, :]
    ov = ((16 * c < 64 * n + 64) & (16 * c + 32 > 64 * n)).astype(np.float32)
    ovl = np.concatenate([ov, np.ones((512, 1), np.float32)], 1).reshape(4, 128, 129).transpose(1, 0, 2)
    ql = np.arange(128)[:, None]
    w = np.arange(256)[None, :]
    rel = w - 128 - 2 * j
    cur = (ql >= 64).astype(np.int64)
    forced = (rel == cur) | (rel == cur - 1)
    future = rel > cur
    keep = (~(forced | future)).astype(np.float32)
    add = np.where(forced, 1e4, np.where(future, -1.0, 0.0)).astype(np.float32)
    ka = np.stack([keep, add], 1)
    key = np.arange(S)[None, :]
    eb = np.where(key // 64 == np.arange(128)[:, None], 1.0, 0.0).astype(np.float32)
    return dict(ovl=np.ascontiguousarray(ovl).astype(ml_dtypes.bfloat16), keepadd=np.ascontiguousarray(ka),
                Ebig=eb.astype(ml_dtypes.bfloat16))


def emit_nsa_compress(R, C, A):
    nc, T = R.nc, R.T
    with R.ExitStack() as ph:
        def sb(name, shape, dt):
            return ph.enter_context(nc.sbuf_tensor(U(name), shape, dt))
        w1 = [sb("N_w1%d" % i, [64, 32, 128], BF16) for i in range(2)]
        w2 = [sb("N_w2%d" % i, [128, 64], BF16) for i in range(2)]
        posT = [sb("N_pos%d" % i, [64, 32], BF16) for i in range(2)]
        cst_b = Buf("NC_const")
        ds = T.new_dma_sem("NconstP")
        T.dma("pool", ds, w1[0][:], A["cmp_w1_k"].rearrange("(l d) h -> d l h", d=64), writes=[cst_b])
        T.dma("pool", ds, w1[1][:], A["cmp_w1_v"].rearrange("(l d) h -> d l h", d=64), writes=[cst_b])
        T.dma("pool", ds, w2[0][:], A["cmp_w2_k"], writes=[cst_b])
        T.dma("pool", ds, w2[1][:], A["cmp_w2_v"], writes=[cst_b])
        T.dma("pool", ds, posT[0][:], A["cmp_posT_k"], writes=[cst_b])
        T.dma("pool", ds, posT[1][:], A["cmp_posT_v"], writes=[cst_b])
        bias = sb("N_bias", [128, 2], F32)
        bias_b = Buf("N_bias")
        for i in range(2):
            for l in range(32):
                T.op("pe", lambda e: e.matmul(R.ps[4][:, i:i + 1], lhsT=w1[i][:, l, :], rhs=posT[i][:, l:l + 1],
                                              start=(l == 0), stop=(l == 31)),
                     reads=[cst_b], writes=[R.psb[4]])
            T.op("dve", lambda e: e.tensor_copy(out=bias[:, i:i + 1], in_=R.ps[4][:, i:i + 1]),
                 reads=[R.psb[4]], writes=[bias_b])
        xg = [[sb("N_xg%d_%d" % (g, i), [64, S], BF16) for i in range(2)] for g in range(2)]
        xg_b = [[Buf("N_xg%d_%d" % (g, i)) for i in range(2)] for g in range(2)]
        hid = sb("N_hid", [128, 512], F32)
        tq = sb("N_tq", [128, 512], F32)
        gl = sb("N_gl", [128, 512], BF16)
        hid_b, gl_b = Buf("N_hid"), Buf("N_gl")
        T.op("pool", lambda e: e.memset(gl[:], 0.0), writes=[gl_b])
        yield
        for g in range(2):
            kcT, vc, kc_b = C["kcT"][g], C["vc"][g], C["kc_b"][g]
            T.op("pool", lambda e: e.memset(vc[:], 1.0), reads=[], writes=[kc_b])
            T.op("pool", lambda e: e.memset(kcT[:], 0.0), reads=[], writes=[kc_b])
            for i in range(2):
                nm = ("kT_cmp_g", "vT_cmp_g")[i]
                dsx = T.new_dma_sem("Nxg%d_%d" % (g, i))
                for r in range(4):
                    dst = xg[g][i][:].rearrange("d (m r p) -> d m r p", r=4, p=128)[:, :, r, :]
                    T.dma("sp", dsx, dst, A[nm][r, g * 64:(g + 1) * 64, :].rearrange("d (m p) -> d m p", p=128),
                          reads=gb(A, nm[:-2]), writes=[xg_b[g][i]])
                for l in range(32):
                    T.op("pe", lambda e: e.matmul(R.ps[4][:, 0:511], lhsT=w1[i][:, l, :],
                                                  rhs=xg[g][i][:, l:l + 16 * 510 + 1:16],
                                                  start=(l == 0), stop=(l == 31)),
                         reads=[cst_b, xg_b[g][i]], writes=[R.psb[4]])
                T.op("act", lambda e: e.activation(out=hid[:, 0:511], in_=R.ps[4][:, 0:511], func=AF.Identity,
                                                   bias=bias[:, i:i + 1]),
                     reads=[R.psb[4], bias_b], writes=[hid_b])
                T.op("dve", lambda e: e.tensor_tensor(out=tq[:, 0:511], in0=hid[:, 0:511], in1=hid[:, 0:511],
                                                      op=ALU.mult), reads=[hid_b], writes=[hid_b])
                T.op("dve", lambda e: e.tensor_scalar(out=tq[:, 0:511], in0=tq[:, 0:511], scalar1=0.044715,
                                                      scalar2=1.0, op0=ALU.mult, op1=ALU.add),
                     reads=[hid_b], writes=[hid_b])
                T.op("dve", lambda e: e.tensor_tensor(out=tq[:, 0:511], in0=tq[:, 0:511], in1=hid[:, 0:511],
                                                      op=ALU.mult), reads=[hid_b], writes=[hid_b])
                T.op("act", lambda e: e.activation(out=tq[:, 0:511], in_=tq[:, 0:511], func=AF.Sigmoid,
                                                   scale=1.5957691216057308),
                     reads=[hid_b], writes=[hid_b])
                T.op("dve", lambda e: e.tensor_tensor(out=gl[:, 0:511], in0=tq[:, 0:511], in1=hid[:, 0:511],
                                                      op=ALU.mult), reads=[hid_b, gl_b], writes=[gl_b])
                if i == 0:
                    T.op("pe", lambda e: e.matmul(R.ps[4][0:64, 0:511], lhsT=w2[0][:], rhs=gl[:, 0:511],
                                                  start=True, stop=True),
                         reads=[cst_b, gl_b], writes=[R.psb[4]])
                    T.op("dve", lambda e: e.tensor_copy(out=kcT[:, 0:511], in_=R.ps[4][0:64, 0:511]),
                         reads=[R.psb[4]], writes=[kc_b])
                else:
                    for t in range(4):
                        T.op("pe", lambda e: e.matmul(R.ps[4][:, t * 64:(t + 1) * 64], lhsT=gl[:, t * 128:(t + 1) * 128],
                                                      rhs=w2[1][:], start=True, stop=True),
                             reads=[cst_b, gl_b], writes=[R.psb[4]])
                    T.op("dve", lambda e: e.tensor_copy(out=vc[:, :, 0:64],
                                                        in_=R.ps[4][:, 0:256].rearrange("p (t x) -> p t x", x=64)),
                         reads=[R.psb[4]], writes=[kc_b])
                yield
        T.barrier_all()


def emit_nsa(R, C, A):
    nc, T = R.nc, R.T
    SC = 0.125
    with R.ExitStack() as ph:
        def sb(name, shape, dt):
            return ph.enter_context(nc.sbuf_tensor(U(name), shape, dt))
        Ebig = sb("N_Ebig", [128, S], BF16)
        ovl = sb("N_ovl", [128, 4, 129], BF16)
        ka = sb("N_ka", [128, 2, 256], F32)
        gates = sb("N_gates", [128, NBLK, 18], F32)
        cst_b = Buf("N_const")
        C["Ebig"], C["Ebig_b"] = Ebig, cst_b
        ds = T.new_dma_sem("Nconst")
        T.dma("sp", ds, Ebig[:], A["Ebig"], writes=[cst_b])
        T.dma("sp", ds, ovl[:], A["ovl"], writes=[cst_b])
        T.dma("sp", ds, ka[:], A["keepadd"], writes=[cst_b])
        T.dma("sp", ds, gates[:], A["gates"].rearrange("(m p) g -> p m g", p=128), writes=[cst_b])
        selT = sb("N_selT", [128, NT], BF16)
        selT_b = Buf("N_selT")
        QTn_l = [sb("N_QT%d" % i, [64, 3, NT], BF16) for i in range(2)]
        KTs_l = [sb("N_KTs%d" % i, [64, 4 * NT], BF16) for i in range(2)]
        Vs_l = [sb("N_Vs%d" % i, [128, 64, 65], BF16) for i in range(2)]
        KTw_l = [sb("N_KTw%d" % i, [64, 4 * NT], BF16) for i in range(2)]
        Vw_l = [sb("N_Vw%d" % i, [128, 64, 65], BF16) for i in range(2)]
        kv_b_l = [Buf("N_kv0"), Buf("N_kv1")]
        kv_sem_l = [T.new_dma_sem("Nkv0"), T.new_dma_sem("Nkv1")]
        for g in range(2):
            QTn, KTs, Vs, KTw, Vw, kv_b, kv_sem = QTn_l[g], KTs_l[g], Vs_l[g], KTw_l[g], Vw_l[g], kv_b_l[g], kv_sem_l[g]
            T.dma("sp", kv_sem, QTn[:], A["qT_nsa"][g * 192:(g + 1) * 192, :].rearrange("(h d) t -> d h t", d=64),
                  writes=[kv_b])
            for r in range(4):
                T.dma("sp", kv_sem, KTs[:, r * NT:(r + 1) * NT], A["kT_sel_g"][r, g * 64:(g + 1) * 64, :], reads=gb(A, "kT_sel"), writes=[kv_b])
                T.dma("sp", kv_sem, KTw[:, r * NT:(r + 1) * NT], A["kT_win_g"][r, g * 64:(g + 1) * 64, :], reads=gb(A, "kT_win"), writes=[kv_b])
                T.dma("sp", kv_sem, Vs[:, r * NBLK:(r + 1) * NBLK, :],
                      A["v_sel_g"][r, :, g, :].rearrange("(m p) x -> p m x", p=128), reads=gb(A, "v_sel"), writes=[kv_b])
                T.dma("sp", kv_sem, Vw[:, r * NBLK:(r + 1) * NBLK, :],
                      A["v_win_g"][r, :, g, :].rearrange("(m p) x -> p m x", p=128), reads=gb(A, "v_win"), writes=[kv_b])
        oaccs = [sb("N_oacc%d" % i, [128, 3, 4, 64], F32) for i in range(2)]
        oaccs_b = [Buf("N_oacc0"), Buf("N_oacc1")]
        C["pcn"], C["pwn"] = 0, 0
        Pc = [sb("N_Pc%d" % i, [128, 3, 128], BF16) for i in range(2)]
        Pc_b = [Buf("N_Pc0"), Buf("N_Pc1")]
        rzc = sb("N_rzc", [128, 3, 1], F32)
        wc = sb("N_wc", [128, 3, 1], F32)
        sc = sb("N_sc", [128, 128], F32)
        sc2 = sb("N_sc2", [128, 128], F32)
        m8 = sb("N_m8", [128, 16], F32)
        selns = [sb("N_seln%d" % i, [128, 128], F32) for i in range(4)]
        selns_b = [Buf("N_seln%d" % i) for i in range(4)]
        sm_b = Buf("N_small")
        rz4 = sb("N_rz4", [128, 4, 1], F32)
        w4 = sb("N_w4", [128, 4, 1], F32)
        w4_b = Buf("N_w4")
        Pw = [sb("N_Pw%d" % i, [128, 512], BF16) for i in range(3)]
        Pw_b = [Buf("N_Pw%d" % i) for i in range(3)]
        for g in range(2):
            kcT, vc, kc_b = C["kcT"][g], C["vc"][g], C["kc_b"][g]
            QTn, KTs, Vs, KTw, Vw, kv_b, kv_sem = QTn_l[g], KTs_l[g], Vs_l[g], KTw_l[g], Vw_l[g], kv_b_l[g], kv_sem_l[g]

            def cmp_block(G, oacc, oacc_b):
                for mm in range(4):
                    ob, sbk = (3, 4) if mm % 2 == 0 else (2, 7)
                    m = 4 * G + mm
                    msl = slice(m * 128, (m + 1) * 128)
                    for t in range(G + 1):
                        sb_ = C["sbanks"][C["sn"] % len(C["sbanks"])]
                        C["sn"] += 1
                        pi = C["pcn"] % 2
                        C["pcn"] += 1
                        T.op("pe", lambda e: e.matmul(R.ps[sb_][:, 0:384], lhsT=kcT[:, t * 128:(t + 1) * 128],
                                                      rhs=QTn[:, :, msl], start=True, stop=True),
                             reads=[kc_b, kv_b], writes=[R.psb[sb_]])
                        T.op("act", lambda e: e.activation(out=Pc[pi][:].rearrange("p r q -> p (r q)"),
                                                           in_=R.ps[sb_][:, 0:384], func=AF.Exp, scale=SC),
                             reads=[R.psb[sb_]], writes=[Pc_b[pi]])
                        if t == G:
                            T.op("pool", lambda e: e.tensor_tensor(
                                out=Pc[pi][:], in0=Pc[pi][:],
                                in1=C["masks"][:, 12 + mm:13 + mm, :].broadcast_to([128, 3, 128]), op=ALU.mult),
                                reads=[Pc_b[pi], C["masks_b"]], writes=[Pc_b[pi]])
                        for r in range(3):
                            T.op("pe", lambda e: e.matmul(R.ps[ob][:, r * 65:(r + 1) * 65], lhsT=Pc[pi][:, r, :],
                                                          rhs=vc[:, t, :], start=(t == 0 and r == 0), stop=(t == G),
                                                          skip_group_check=True),
                                 reads=[Pc_b[pi], kc_b], writes=[R.psb[ob]])
                        for r in range(3):
                            T.op("pe", lambda e: e.matmul(R.ps[sbk][:, r * 129:(r + 1) * 129], lhsT=Pc[pi][:, r, :],
                                                          rhs=ovl[:, t, :], start=(t == 0 and r == 0), stop=(t == G),
                                                          skip_group_check=True),
                                 reads=[Pc_b[pi], cst_b], writes=[R.psb[sbk]])
                    scv = R.ps[sbk][:, 0:387].rearrange("p (r x) -> p r x", x=129)
                    ocv = R.ps[ob][:, 0:195].rearrange("p (r x) -> p r x", x=65)
                    T.op("dve", lambda e: e.tensor_scalar(out=rzc[:], in0=scv[:, :, 128:129], scalar1=1e-30, scalar2=None,
                                                          op0=ALU.add), reads=[R.psb[sbk]], writes=[sm_b])
                    T.op("dve", lambda e: e.reciprocal(out=rzc[:], in_=rzc[:]), reads=[sm_b], writes=[sm_b])
                    T.op("dve", lambda e: e.tensor_scalar(out=sc[:], in0=scv[:, 0, 0:128], scalar1=rzc[:, 0, :], scalar2=None,
                                                          op0=ALU.mult), reads=[R.psb[sbk], sm_b], writes=[sm_b])
                    for r in (1, 2):
                        T.op("dve", lambda e: e.scalar_tensor_tensor(out=sc[:], in0=scv[:, r, 0:128], scalar=rzc[:, r, :],
                                                                     in1=sc[:], op0=ALU.mult, op1=ALU.add),
                             reads=[R.psb[sbk], sm_b], writes=[sm_b])
                    T.op("dve", lambda e: e.tensor_tensor(
                        out=wc[:], in0=rzc[:],
                        in1=gates[:, m, 9 * g:9 * g + 9].rearrange("p (r x) -> p r x", x=3)[:, :, 0:1], op=ALU.mult),
                        reads=[sm_b, cst_b], writes=[sm_b])
                    for r in range(3):
                        T.op("dve", lambda e: e.tensor_scalar(out=oacc[:, r, mm, :], in0=ocv[:, r, 0:64],
                                                              scalar1=wc[:, r, :], scalar2=None, op0=ALU.mult),
                             reads=[R.psb[ob], sm_b], writes=[oacc_b])
                    w0 = 128 - 8 * m
                    T.op("dve", lambda e: e.tensor_tensor(out=sc[:], in0=sc[:], in1=ka[:, 0, w0:w0 + 128], op=ALU.mult),
                         reads=[sm_b, cst_b], writes=[sm_b])
                    T.op("dve", lambda e: e.tensor_tensor(out=sc[:], in0=sc[:], in1=ka[:, 1, w0:w0 + 128], op=ALU.add),
                         reads=[sm_b, cst_b], writes=[sm_b])
                    T.op("dve", lambda e: e.memset(sc[:, 0:1], 1e4), reads=[sm_b], writes=[sm_b])
                    T.op("dve", lambda e: e.max(out=m8[:, 0:8], in_=sc[:]), reads=[sm_b], writes=[sm_b])
                    T.op("dve", lambda e: e.match_replace(out=sc2[:], in_to_replace=m8[:, 0:8], in_values=sc[:],
                                                          imm_value=-1e9), reads=[sm_b], writes=[sm_b])
                    T.op("dve", lambda e: e.max(out=m8[:, 8:16], in_=sc2[:]), reads=[sm_b], writes=[sm_b])
                    T.op("dve", lambda e: e.tensor_scalar(out=selns[mm][:], in0=sc[:], scalar1=m8[:, 15:16], scalar2=None,
                                                          op0=ALU.is_ge), reads=[sm_b], writes=[selns_b[mm]])

            def cmp_finish(G):
                tb = C["tpbank"]
                for mm in range(4):
                    T.op("pe", lambda e: e.transpose(out=R.ps[tb][:, mm * 128:(mm + 1) * 128], in_=selns[mm][:],
                                                     identity=C["ident"][:]),
                         reads=[selns_b[mm], C["ident_b"]], writes=[R.psb[tb]])
                T.op("act", lambda e: e.activation(out=selT[:, G * 512:(G + 1) * 512], in_=R.ps[tb][:, :], func=AF.Copy),
                     reads=[R.psb[tb]], writes=[selT_b])

            def win_block(r, G, oacc, oacc_b):
                h = 3 * g + r
                ob = 6
                items = []
                for mm in range(4):
                    for half in range(2):
                        kts = [16 * G + 4 * mm - 4 + half * 4 + x for x in range(4)]
                        if kts[-1] >= 0:
                            items.append((mm, half, kts))
                ni = len(items)
                stt = {}

                def wa(i):
                    mm, half, kts = items[i]
                    msl = slice((4 * G + mm) * 128, (4 * G + mm + 1) * 128)
                    sb_ = C["sbanks"][C["sn"] % len(C["sbanks"])]
                    C["sn"] += 1
                    stt[i] = [sb_, None]
                    for x, kt in enumerate(kts):
                        T.op("pe", lambda e: e.matmul(R.ps[sb_][:, x * 128:(x + 1) * 128],
                                                      lhsT=KTw[:, ktcol(kt):ktcol(kt) + 128],
                                                      rhs=QTn[:, r, msl], start=True, stop=True),
                             reads=[kv_b], writes=[R.psb[sb_]])

                def wb(i):
                    mm, half, kts = items[i]
                    sb_ = stt[i][0]
                    pi = C["pwn"] % len(Pw)
                    C["pwn"] += 1
                    stt[i][1] = pi
                    T.op("act", lambda e: e.activation(out=Pw[pi][:], in_=R.ps[sb_][:], func=AF.Exp, scale=SC),
                         reads=[R.psb[sb_]], writes=[Pw_b[pi]])
                    T.op("pool", lambda e: e.tensor_tensor(
                        out=Pw[pi][:], in0=Pw[pi][:],
                        in1=C["masks"][:, 4 + half * 4:8 + half * 4, :].rearrange("p a q -> p (a q)"),
                        op=ALU.mult), reads=[Pw_b[pi], C["masks_b"]], writes=[Pw_b[pi]])

                def wc_(i):
                    mm, half, kts = items[i]
                    pi = stt[i][1]
                    for x, kt in enumerate(kts):
                        T.op("pe", lambda e: e.matmul(R.ps[ob][:, mm * 65:(mm + 1) * 65],
                                                      lhsT=Pw[pi][:, x * 128:(x + 1) * 128], rhs=Vw[:, ktile(kt), :],
                                                      start=(i == 0 and x == 0), stop=(half == 1 and x == 3),
                                                      skip_group_check=True),
                             reads=[Pw_b[pi], kv_b], writes=[R.psb[ob]])

                for idx in range(ni + 2):
                    if idx < ni:
                        wa(idx)
                    if 1 <= idx <= ni:
                        wb(idx - 1)
                    if idx >= 2:
                        wc_(idx - 2)
                owv = R.ps[ob][:, 0:260].rearrange("p (m x) -> p m x", x=65)
                T.op("dve", lambda e: e.reciprocal(out=rz4[:], in_=owv[:, :, 64:65]), reads=[R.psb[ob]], writes=[w4_b])
                T.op("dve", lambda e: e.tensor_tensor(out=w4[:], in0=rz4[:],
                                                      in1=gates[:, 4 * G:4 * G + 4, 3 * h + 2:3 * h + 3], op=ALU.mult),
                     reads=[w4_b, cst_b], writes=[w4_b])
                for mm in range(4):
                    T.op("dve", lambda e: e.scalar_tensor_tensor(out=oacc[:, r, mm, :], in0=owv[:, mm, 0:64],
                                                                 scalar=w4[:, mm, :], in1=oacc[:, r, mm, :],
                                                                 op0=ALU.mult, op1=ALU.add),
                         reads=[R.psb[ob], w4_b, oacc_b], writes=[oacc_b])

            def sel3_block(G, oacc, oacc_b):
                obs = [4, 6, 7]
                mbanks = [2, 3]
                qbanks = [0, 1]
                nk = 16 * G + 16
                nhs = 3 * nk
                stt = {}

                def geom(kt):
                    mm_min = max(0, -((-(kt - 16 * G - 3)) // 4))
                    return mm_min, mm_min * 128

                def sa(hs):
                    kt, r = hs // 3, hs % 3
                    mm_min, c0 = geom(kt)
                    mb = mbanks[kt % 2]
                    if r == 0:
                        T.op("pe", lambda e: e.matmul(R.ps[mb][:, c0:512], lhsT=Ebig[:, kt * 128:(kt + 1) * 128],
                                                      rhs=selT[:, G * 512 + c0:(G + 1) * 512], start=True, stop=True),
                             reads=[cst_b, selT_b], writes=[R.psb[mb]])
                    qb = qbanks[hs % 2]
                    T.op("pe", lambda e: e.matmul(R.ps[qb][:, c0:512], lhsT=KTs[:, ktcol(kt):ktcol(kt) + 128],
                                                  rhs=QTn[:, r, G * 512 + c0:(G + 1) * 512], start=True, stop=True),
                         reads=[kv_b], writes=[R.psb[qb]])

                def sb_(hs):
                    kt, r = hs // 3, hs % 3
                    mm_min, c0 = geom(kt)
                    mb, qb = mbanks[kt % 2], qbanks[hs % 2]
                    pi = C["pn"] % len(C["PT"])
                    C["pn"] += 1
                    stt[hs] = pi
                    PT, PT_b = C["PT"][pi], C["PT_b"][pi]
                    T.op("act", lambda e: e.activation(out=PT[:, c0:512], in_=R.ps[qb][:, c0:512], func=AF.Exp, scale=SC),
                         reads=[R.psb[qb]], writes=[PT_b])
                    T.op("dve", lambda e: e.tensor_tensor(out=PT[:, c0:512], in0=PT[:, c0:512], in1=R.ps[mb][:, c0:512],
                                                          op=ALU.mult),
                         reads=[PT_b, R.psb[mb]], writes=[PT_b])
                    if kt >= 16 * G:
                        mmb = (kt - 16 * G) // 4
                        d = (kt - 16 * G) % 4
                        T.op("pool", lambda e: e.tensor_tensor(out=PT[:, mmb * 128:(mmb + 1) * 128],
                                                               in0=PT[:, mmb * 128:(mmb + 1) * 128],
                                                               in1=C["masks"][:, d, :], op=ALU.mult),
                             reads=[PT_b, C["masks_b"]], writes=[PT_b])

                def sc_(hs):
                    kt, r = hs // 3, hs % 3
                    mm_min, c0 = geom(kt)
                    PT, PT_b = C["PT"][stt[hs]], C["PT_b"][stt[hs]]
                    for mm in range(mm_min, 4):
                        T.op("pe", lambda e: e.matmul(R.ps[obs[r]][:, mm * 65:(mm + 1) * 65],
                                                      lhsT=PT[:, mm * 128:(mm + 1) * 128], rhs=Vs[:, ktile(kt), :],
                                                      start=(kt == 0 and mm == mm_min), stop=(kt == 16 * G + 4 * mm + 3),
                                                      skip_group_check=True),
                             reads=[PT_b, kv_b], writes=[R.psb[obs[r]]])

                for idx in range(nhs + 4):
                    if idx < nhs:
                        sa(idx)
                    if 1 <= idx <= nhs:
                        sb_(idx - 1)
                    if idx >= 4:
                        sc_(idx - 4)
                for r in range(3):
                    h = 3 * g + r
                    osv = R.ps[obs[r]][:, 0:260].rearrange("p (m x) -> p m x", x=65)
                    T.op("dve", lambda e: e.reciprocal(out=rz4[:], in_=osv[:, :, 64:65]), reads=[R.psb[obs[r]]], writes=[w4_b])
                    T.op("dve", lambda e: e.tensor_tensor(out=w4[:], in0=rz4[:],
                                                          in1=gates[:, 4 * G:4 * G + 4, 3 * h + 1:3 * h + 2], op=ALU.mult),
                         reads=[w4_b, cst_b], writes=[w4_b])
                    for mm in range(4):
                        T.op("dve", lambda e: e.scalar_tensor_tensor(out=oacc[:, r, mm, :], in0=osv[:, mm, 0:64],
                                                                     scalar=w4[:, mm, :], in1=oacc[:, r, mm, :],
                                                                     op0=ALU.mult, op1=ALU.add),
                             reads=[R.psb[obs[r]], w4_b, oacc_b], writes=[oacc_b])
                    emit_transpose_store(R, C, oacc[:, r, :, :], oacc_b, G, 3 + h // 2, (h % 2) * 64)

            C["sbanks"] = [0, 1]
            cmp_block(0, oaccs[0], oaccs_b[0])
            for G in range(4):
                cmp_finish(G)
                if G + 1 < 4:
                    cmp_block(G + 1, oaccs[(G + 1) % 2], oaccs_b[(G + 1) % 2])
                for r in range(3):
                    win_block(r, G, oaccs[G % 2], oaccs_b[G % 2])
                sel3_block(G, oaccs[G % 2], oaccs_b[G % 2])
            C["sbanks"] = [0, 1, 2, 3]
        T.barrier_all()


def emit_sb(R, C, A):
    nc, T = R.nc, R.T
    SC = 0.125
    with R.ExitStack() as ph:
        def sb(name, shape, dt):
            return ph.enter_context(nc.sbuf_tensor(U(name), shape, dt))
        KT = [sb("S_KT%d" % i, [64, 4 * NT], BF16) for i in range(2)]
        V = [sb("S_V%d" % i, [128, 64, 64], BF16) for i in range(2)]
        QT = [sb("S_QT%d" % i, [64, NT], BF16) for i in range(2)]
        bufs = [Buf("S_slot%d" % i) for i in range(2)]
        sems = [T.new_dma_sem("Sslot%d" % i) for i in range(2)]
        NE = 4
        E = [sb("S_E%d" % i, [128, 512], F32) for i in range(NE)]
        SP = [sb("S_SP%d" % i, [128, 512], BF16) for i in range(NE)]
        X = [sb("S_X%d" % i, [128, 512], F32) for i in range(2)]
        E_b = [Buf("S_E%d" % i) for i in range(NE)]
        SP_b = [Buf("S_SP%d" % i) for i in range(NE)]
        X_b = [Buf("S_X0"), Buf("S_X1")]
        Accb = [sb("S_Accb%d" % i, [128, 512], BF16) for i in range(3)]
        Accb_b = [Buf("S_Accb%d" % i) for i in range(3)]
        tincl = C["masks"][:, 20, :]
        onesb = C["masks"][:, 21, :]
        cbanks = [3, 4]
        zbanks = [0, 1, 2]
        for h in range(4):
            s = h % 2
            for r in range(4):
                T.dma("sp", sems[s], KT[s][:, r * NT:(r + 1) * NT], A["kT_sb_g"][r, h * 64:(h + 1) * 64, :], reads=gb(A, "kT_sb"),
                      writes=[bufs[s]])
                T.dma("sp", sems[s], V[s][:, r * NBLK:(r + 1) * NBLK, :],
                      A["v_sb_g"][r, :, h * 64:(h + 1) * 64].rearrange("(m p) x -> p m x", p=128), reads=gb(A, "v_sb"),
                      writes=[bufs[s]])
            T.dma("sp", sems[s], QT[s][:], A["qT_sb"][h * 64:(h + 1) * 64, :], writes=[bufs[s]])
            for G in range(4):
                ob = 6 + (G % 2)
                for i in range(3):
                    T.op("pool", lambda e: e.memset(Accb[i][:], 0.0), writes=[Accb_b[i]])
                steps = list(range(16 * G + 15, -1, -1))
                ns = len(steps)
                stt = {}

                def geom(kt):
                    mm_min = max(0, -((-(kt - 16 * G - 3)) // 4))
                    return mm_min, mm_min * 128

                def st_a(n):
                    kt = steps[n]
                    mm_min, c0 = geom(kt)
                    zb = zbanks[n % 3]
                    T.op("pe", lambda e: e.matmul(R.ps[zb][:, c0:512], lhsT=KT[s][:, ktcol(kt):ktcol(kt) + 128],
                                                  rhs=QT[s][:, G * 512 + c0:(G + 1) * 512], start=True, stop=True),
                         reads=[bufs[s]], writes=[R.psb[zb]])

                def st_b(n):
                    kt = steps[n]
                    mm_min, c0 = geom(kt)
                    zb, ie = zbanks[n % 3], n % NE
                    T.op("act", lambda e: e.activation(out=E[ie][:, c0:512], in_=R.ps[zb][:, c0:512], func=AF.Exp, scale=SC),
                         reads=[R.psb[zb]], writes=[E_b[ie]])
                    T.op("act", lambda e: e.activation(out=SP[ie][:, c0:512], in_=E[ie][:, c0:512], func=AF.Ln, bias=1.0),
                         reads=[E_b[ie]], writes=[SP_b[ie]])
                    if kt >= 16 * G:
                        d = (kt - 16 * G) % 4
                        T.op("pool", lambda e: e.tensor_tensor(out=SP[ie][:, c0:c0 + 128], in0=SP[ie][:, c0:c0 + 128],
                                                               in1=C["masks"][:, 16 + d, :], op=ALU.mult),
                             reads=[SP_b[ie], C["masks_b"]], writes=[SP_b[ie]])
                    if n + 1 < ns:
                        T.op("pool", lambda e: e.tensor_tensor(out=Accb[(n + 1) % 3][:, c0:512], in0=Accb[n % 3][:, c0:512],
                                                               in1=SP[ie][:, c0:512], op=ALU.add),
                             reads=[Accb_b[n % 3], SP_b[ie]], writes=[Accb_b[(n + 1) % 3]])

                def st_c(n):
                    kt = steps[n]
                    mm_min, c0 = geom(kt)
                    cb, ie = cbanks[n % 2], n % NE
                    T.op("pe", lambda e: e.matmul(R.ps[cb][:, c0:512], lhsT=tincl, rhs=SP[ie][:, c0:512],
                                                  start=True, stop=False),
                         reads=[SP_b[ie], C["masks_b"]], writes=[R.psb[cb]])
                    T.op("pe", lambda e: e.matmul(R.ps[cb][:, c0:512], lhsT=onesb, rhs=Accb[n % 3][:, c0:512],
                                                  start=False, stop=True),
                         reads=[Accb_b[n % 3], C["masks_b"]], writes=[R.psb[cb]])

                def st_c2(n):
                    kt = steps[n]
                    mm_min, c0 = geom(kt)
                    cb = cbanks[n % 2]
                    T.op("act", lambda e: e.activation(out=X[n % 2][:, c0:512], in_=R.ps[cb][:, c0:512], func=AF.Exp,
                                                       scale=-1.0),
                         reads=[R.psb[cb]], writes=[X_b[n % 2]])

                def st_e(n):
                    kt = steps[n]
                    mm_min, c0 = geom(kt)
                    ie = n % NE
                    pi = C["pn"] % len(C["PT"])
                    C["pn"] += 1
                    stt[n] = pi
                    PT, PT_b = C["PT"][pi], C["PT_b"][pi]
                    T.op("dve", lambda e: e.tensor_tensor(out=PT[:, c0:512], in0=E[ie][:, c0:512], in1=X[n % 2][:, c0:512],
                                                          op=ALU.mult),
                         reads=[E_b[ie], X_b[n % 2]], writes=[PT_b])
                    if kt >= 16 * G:
                        d = (kt - 16 * G) % 4
                        T.op("pool", lambda e: e.tensor_tensor(out=PT[:, c0:c0 + 128], in0=PT[:, c0:c0 + 128],
                                                               in1=C["masks"][:, 16 + d, :], op=ALU.mult),
                             reads=[PT_b, C["masks_b"]], writes=[PT_b])

                def st_f(n, first):
                    kt = steps[n]
                    mm_min, c0 = geom(kt)
                    PT, PT_b = C["PT"][stt[n]], C["PT_b"][stt[n]]
                    for mm in range(mm_min, 4):
                        T.op("pe", lambda e: e.matmul(R.ps[ob][:, mm * 64:(mm + 1) * 64],
                                                      lhsT=PT[:, mm * 128:(mm + 1) * 128], rhs=V[s][:, ktile(kt), :],
                                                      start=(first and mm == mm_min), stop=(kt == 0), skip_group_check=True),
                             reads=[PT_b, bufs[s]], writes=[R.psb[ob]])

                for idx in range(ns + 4):
                    if 2 <= idx <= ns + 1:
                        st_c(idx - 2)
                    if idx < ns:
                        st_a(idx)
                    if 1 <= idx <= ns:
                        st_b(idx - 1)
                    if 2 <= idx <= ns + 1:
                        st_c2(idx - 2)
                    if 3 <= idx <= ns + 2:
                        st_e(idx - 3)
                    if idx >= 4:
                        st_f(idx - 4, idx == 4)
                i = C["ev"] % 2
                C["ev"] += 1
                T.op("dve", lambda e: e.tensor_copy(out=C["on"][i][:],
                                                    in_=R.ps[ob][:, 0:256].rearrange("p (m x) -> p m x", x=64)),
                     reads=[R.psb[ob]], writes=[C["on_b"][i]])
                emit_transpose_store(R, C, C["on"][i], C["on_b"][i], G, 6 + h // 2, (h % 2) * 64)
        T.barrier_all()


L = 2
GATHER = ["kT_mla", "kpeT", "v_mla", "kT_cmp", "vT_cmp", "kT_sel", "kT_win", "v_sel", "v_win", "kT_sb", "v_sb"]
LOCAL = ["qT_mla", "qT_nsa", "qT_sb", "gates"]
POUT = {nm: (shp, dt) for nm, shp, dt in P_OUTS}
W_IN = [("ffn1_w_gate", [L, D, DFF]), ("ffn1_w_up", [L, D, DFF]), ("ffn1_w_down", [L, DFF, D]),
        ("ffn2_w_gate", [L, D, DFF]), ("ffn2_w_up", [L, D, DFF]), ("ffn2_w_down", [L, DFF, D]),
        ("w_in", [L, D, DIN]), ("w_in_sw", [L, D, NSW]), ("w_uq", [L, 256, 576]), ("w_uq_sw", [L, 256, 576]),
        ("w_ukv", [L, 128, 768]), ("smallsP", [L, 128, 32]), ("w_out", [L, D, D]),
        ("cmp_w1_k", [L, 2048, 128]), ("cmp_w1_v", [L, 2048, 128]), ("cmp_w2_k", [L, 128, 64]),
        ("cmp_w2_v", [L, 128, 64]), ("cmp_posT_k", [L, 64, 32]), ("cmp_posT_v", [L, 64, 32]),
        ("gains", [128, 3 * L + 1, 8])]
C_IN = [("cosM", [96, NT], F32), ("sinM", [96, NT], F32), ("cosK", [32, NT], F32), ("sinK", [32, NT], F32),
        ("cosN", [128, NT], F32), ("sinN", [128, NT], F32), ("masks", [128, 22, 128], BF16),
        ("ident", [128, 128], F32), ("Ebig", [128, S], BF16), ("ovl", [128, 4, 129], BF16),
        ("keepadd", [128, 2, 256], F32)]


def emit_layer_X(R, A, l, do_post, do_pre, final, xin, xout, mid_hook=None):
    nc, T = R.nc, R.T
    with ExitStack() as px:
        alloc_xT(R, px)
        hT = px.enter_context(nc.sbuf_tensor(U("hT"), [128, 8, NT], BF16))
        hb = [[Buf("h%d_%d" % (k, t)) for t in range(4)] for k in range(8)]
        gam = px.enter_context(nc.sbuf_tensor(U("gam"), [128, 3 * L + 1, 8], F32))
        gam_b = Buf("gam")
        sq = [px.enter_context(nc.sbuf_tensor(U("sq%d" % i), [128, 512], BF16)) for i in range(2)]
        sq_b = [Buf("sq0"), Buf("sq1")]
        rstd = px.enter_context(nc.sbuf_tensor(U("rstd"), [128, 512], F32))
        rstd_b = Buf("rstd")
        ld = T.new_dma_sem("ldx")

        def load_x():
            for k in range(8):
                T.dma("sp", ld, R.xT[:, k, :], xin[k * 128:(k + 1) * 128, :], writes=[R.xb[k][t] for t in range(4)])
            T.dma("sp", ld, gam[:], A["gains"], writes=[gam_b])

        if not do_post:
            load_x()

        def norm(gi):
            emit_norm(R, hT, hb, gam[:, gi, :], gam_b, sq, sq_b, rstd, rstd_b, 6)

        lp = l
        with ExitStack() as pf:
            R.stack = pf
            W = alloc_ffn_work(R)

            def ffn(pref, ll):
                emit_ffn(R, hT, hb, A[pref + "_w_gate"][ll], A[pref + "_w_up"][ll], A[pref + "_w_down"][ll], W)

            if do_post:
                oT2 = pf.enter_context(nc.sbuf_tensor(U("oT2"), [128, 8, NT], BF16))
                o_b = Buf("oT2")
                d1 = T.new_dma_sem("oT2")
                T.dma("sp", d1, oT2[:], A["oT_d"], writes=[o_b])
                for hf in range(2):
                    T.dma("pool", W["dsem"][hf], W["wd"][hf][:],
                          A["w_out"][l][hf * 512:(hf + 1) * 512, :].rearrange("(k p) c -> p k c", p=128),
                          writes=[W["wb"][hf]])
                load_x()
                n = 0
                for tt in range(4):
                    sl = slice(tt * 512, (tt + 1) * 512)
                    for dmc in range(8):
                        bk = n % 4
                        n += 1
                        for k in range(8):
                            T.op("pe", lambda e: e.matmul(R.ps[bk][:], lhsT=W["wd"][k // 4][:, k % 4, dmc * 128:(dmc + 1) * 128],
                                                          rhs=oT2[:, k, sl], start=(k == 0), stop=(k == 7)),
                                 reads=[o_b, W["wb"][k // 4]], writes=[R.psb[bk]])
                        T.op("dve", lambda e: e.tensor_tensor(out=R.xT[:, dmc, sl], in0=R.ps[bk][:], in1=R.xT[:, dmc, sl],
                                                              op=ALU.add),
                             reads=[R.psb[bk], R.xb[dmc][tt]], writes=[R.xb[dmc][tt]])
                norm(3 * l + 2)
                ffn("ffn2", l)
                lp = l + 1
            if do_pre:
                norm(3 * lp + 0)
                ffn("ffn1", lp)
            T.barrier_all()
        if do_pre:
            norm(3 * lp + 1)
            AP_ = dict(A)
            for nm in ("w_in", "w_in_sw", "w_uq", "w_uq_sw", "w_ukv", "smallsP"):
                AP_[nm] = A[nm][lp]
            emit_stage_P(R, hT, hb, AP_, mid_hook=mid_hook)
        st = T.new_dma_sem("stx")
        if final:
            ysq = [px.enter_context(nc.sbuf_tensor(U("ystg%d" % i), [128, 512], F32)) for i in range(2)]
            y_b = [Buf("y0"), Buf("y1")]
            ss, ss_b = R.ps[6], R.psb[6]
            n = 0
            for tt in range(4):
                sl = slice(tt * 512, (tt + 1) * 512)
                for k in range(8):
                    a = k % 2
                    T.op("act", lambda e: e.activation(out=sq[a][:], in_=R.xT[:, k, sl], func=AF.Square),
                         reads=[R.xb[k][tt]], writes=[sq_b[a]])
                    T.op("pe", lambda e: e.matmul(ss[:], lhsT=R.onesm[:], rhs=sq[a][:], start=(k == 0), stop=(k == 7)),
                         reads=[sq_b[a], R.onesm_b], writes=[ss_b])
                T.op("act", lambda e: e.activation(out=rstd[:], in_=ss[:], func=AF.Sqrt, bias=EPS, scale=1.0 / D),
                     reads=[ss_b], writes=[rstd_b])
                T.op("dve", lambda e: e.reciprocal(out=rstd[:], in_=rstd[:]), reads=[rstd_b], writes=[rstd_b])
                for k in range(8):
                    i = n % 2
                    n += 1
                    T.op("dve", lambda e: e.scalar_tensor_tensor(out=ysq[i][:], in0=R.xT[:, k, sl],
                                                                 scalar=gam[:, 3 * L, k:k + 1], in1=rstd[:],
                                                                 op0=ALU.mult, op1=ALU.mult),
                         reads=[R.xb[k][tt], rstd_b, gam_b], writes=[y_b[i]])
                    T.dma("sp", st, xout[k * 128:(k + 1) * 128, sl], ysq[i][:], reads=[y_b[i]])
        else:
            for k in range(8):
                T.dma("sp", st, xout[k * 128:(k + 1) * 128, :], R.xT[:, k, :], reads=[R.xb[k][t] for t in range(4)])
        T.barrier_all()
        return st


def emit_layer_A(R, A, l):
    nc, T = R.nc, R.T
    with ExitStack() as pa:
        R.oT = pa.enter_context(nc.sbuf_tensor(U("oT"), [128, 8, NT], BF16))
        R.oT_b = [[Buf("oT%d_%d" % (c, g)) for g in range(4)] for c in range(8)]
        AL = dict(A)
        for nm in ("cmp_w1_k", "cmp_w1_v", "cmp_w2_k", "cmp_w2_v", "cmp_posT_k", "cmp_posT_v"):
            AL[nm] = A[nm][l]
        with ExitStack() as ph:
            C = alloc_attn_common(R, AL, ph)
            gen = emit_nsa_compress(R, C, AL)
            next(gen)
            emit_mla(R, C, AL, hook=gen)
            for _ in gen:
                pass
            emit_nsa(R, C, AL)
            emit_sb(R, C, AL)
        st = T.new_dma_sem("stoT")
        T.dma("sp", st, A["oT_d"], R.oT[:], reads=[b for ll in R.oT_b for b in ll])
        T.barrier_all()


def build_launch(kind, l):
    nc = bass.Bass("TRN2", target_bir_lowering=False)
    A = {}

    def din(nm, shp, dt=F32):
        A[nm] = nc.dram_tensor(nm, shp, dt, kind="ExternalInput").ap()

    def dout(nm, shp, dt=F32):
        A[nm] = nc.dram_tensor(nm, shp, dt, kind="ExternalOutput").ap()
    for nm, shp in W_IN:
        din(nm, shp)
    for nm, shp, dt in C_IN:
        din(nm, shp, dt)
    din("xT_in", [D, NT])
    dout("xT_out", [D, NT])
    if kind != "first":
        for nm in GATHER:
            shp, dt = POUT[nm]
            din(nm + "_g", [4] + shp, dt)
        for nm in LOCAL:
            shp, dt = POUT[nm]
            din(nm, shp, dt)
        A["oT_d"] = nc.dram_tensor("oT_d", [128, 8, NT], BF16, kind="Internal").ap()
    if kind != "last":
        for nm, shp, dt in P_OUTS:
            if kind == "first" or nm not in LOCAL:
                dout(nm, shp, dt)
            else:
                A[nm + "_o"] = nc.dram_tensor(nm + "_o", shp, dt, kind="ExternalOutput").ap()
    with ExitStack() as stack:
        T = Tracker(nc, stack)
        R = setup_common(nc, stack, T)
        if kind != "first":
            emit_layer_A(R, A, l)
        AX = dict(A)
        if kind == "mid":
            for nm in LOCAL:
                AX[nm] = A[nm + "_o"]
        st = emit_layer_X(R, AX, l, do_post=(kind != "first"), do_pre=(kind != "last"), final=(kind == "last"),
                          xin=A["xT_in"], xout=A["xT_out"])
        nc.sync.wait_ge(T.sem[st], T.cnt[st])
    return nc


def _host_weights(inp):
    f = lambda a: np.ascontiguousarray(np.asarray(a, dtype=np.float32))
    W = {}
    for nm in ("ffn1_w_gate", "ffn1_w_up", "ffn1_w_down", "ffn2_w_gate", "ffn2_w_up", "ffn2_w_down", "w_in", "w_out"):
        W[nm] = f(inp[nm])
    W["w_uq"] = f(inp["mla_w_uq"])
    W["w_ukv"] = f(inp["mla_w_ukv"])
    W["w_in_sw"] = np.stack([make_w_in_sw(W["w_in"][l]) for l in range(L)])
    W["w_uq_sw"] = np.stack([swap_cols_rope(W["w_uq"][l], 96, 64, 32) for l in range(L)])
    W["smallsP"] = np.stack([make_smallsP(f(inp["mla_q_norm"])[l], f(inp["mla_kv_norm"])[l], f(inp["nsa_gate_bias"])[l])
                             for l in range(L)])
    W["cmp_w1_k"] = f(inp["nsa_cmp_w1_k"])
    W["cmp_w1_v"] = f(inp["nsa_cmp_w1_v"])
    W["cmp_w2_k"] = f(inp["nsa_cmp_w2_k"])
    W["cmp_w2_v"] = f(inp["nsa_cmp_w2_v"])
    W["cmp_posT_k"] = np.ascontiguousarray(f(inp["nsa_cmp_pos_k"]).transpose(0, 2, 1))
    W["cmp_posT_v"] = np.ascontiguousarray(f(inp["nsa_cmp_pos_v"]).transpose(0, 2, 1))
    g = np.zeros((128, 3 * L + 1, 8), np.float32)
    for l in range(L):
        for i, nm in enumerate(("ffn1_norm", "mix_norm", "ffn2_norm")):
            g[:, 3 * l + i, :] = f(inp[nm])[l].reshape(8, 128).T
    g[:, 3 * L, :] = f(inp["final_norm"]).reshape(8, 128).T
    W["gains"] = g
    return W


def _core_consts(core):
    c = {}
    c.update(rope_tables(core))
    c["masks"] = make_masks(core)
    c["ident"] = np.eye(128, dtype=np.float32)
    c.update(make_nsa_consts(core))
    return c


def _kernel_unfused_impl(**inp):
    x = np.asarray(inp["x"], dtype=np.float32)
    W = _host_weights(inp)
    consts = [_core_consts(c) for c in range(8)]
    xT = [np.ascontiguousarray(x[c // 4][own_positions(c)].T) for c in range(8)]
    cores = list(range(8))
    nc = build_launch("first", 0)
    ims = [dict(W, **consts[c], xT_in=xT[c]) for c in cores]
    res = run_bass_kernel_spmd(nc, ims, core_ids=cores).results
    for l in range(L):
        kind = "mid" if l < L - 1 else "last"
        nc = build_launch(kind, l)
        ims = []
        for c in cores:
            b = c // 4
            im = dict(W, **consts[c], xT_in=np.asarray(res[c]["xT_out"]))
            for nm in GATHER:
                im[nm + "_g"] = np.stack([np.asarray(res[4 * b + r][nm]) for r in range(4)])
            for nm in LOCAL:
                key = nm if l == 0 else nm + "_o"
                im[nm] = np.asarray(res[c][key])
            ims.append(im)
        res = run_bass_kernel_spmd(nc, ims, core_ids=cores).results
    out = np.zeros((B, S, D), np.float32)
    for c in cores:
        out[c // 4][own_positions(c)] = np.asarray(res[c]["xT_out"]).T
    return out


PIECES = [
    ([192, NT], [("kT_mla", "heads", 0, 3)]),
    ([192, NT], [("kT_mla", "heads", 3, 6)]),
    ([32, NT], [("kpeT", "rows", 0, 32)]),
    ([256, NT], [("vT_cmp", "rows", 0, 128), ("kT_sel", "rows", 128, 256)]),
    ([256, NT], [("kT_cmp", "rows", 0, 128), ("kT_win", "rows", 128, 256)]),
    ([256, NT], [("kT_sb", "rows", 0, 256)]),
    ([1024, 6, 65], [("v_mla", "half", 0, 0)]),
    ([1024, 6, 65], [("v_mla", "half", 1, 1)]),
    ([NT, 2, 65], [("v_sel", "all", 0, 0)]),
    ([NT, 2, 65], [("v_win", "all", 0, 0)]),
    ([NT, 256], [("v_sb", "all", 0, 0)]),
]


def make_pieces(nc, l):
    V = {"kT_mla": [None] * 6, "kT_mla_g": [None] * 6, "v_mla": [None] * 2, "v_mla_g": [None] * 2}
    GB = {"kT_mla": [None] * 6, "v_mla": [None] * 2}
    cc = []
    for k, (shp, members) in enumerate(PIECES):
        n = int(np.prod(shp))
        w = n // 128
        gs = nc.dram_tensor("gs%d_%d" % (l, k), [128, w], BF16, kind="Internal").ap()
        gd = nc.dram_tensor("gd%d_%d" % (l, k), [512, w], BF16, kind="Internal").ap()
        pb = Buf("piece%d_%d" % (l, k))
        cc.append((gs, gd, pb))
        fs = gs.rearrange("p w -> (p w)")
        fd = gd.rearrange("(r p) w -> r (p w)", r=4)
        if len(shp) == 2:
            ns = fs.rearrange("(a c) -> a c", c=shp[1])
            nd = fd.rearrange("r (a c) -> r a c", c=shp[1])
        else:
            ns = fs.rearrange("(t h x) -> t h x", h=shp[1], x=shp[2])
            nd = fd.rearrange("r (t h x) -> r t h x", h=shp[1], x=shp[2])
        for nm, kind, lo, hi in members:
            if kind == "heads":
                for h in range(lo, hi):
                    V[nm][h] = ns[(h - lo) * 64:(h - lo + 1) * 64, :]
                    V[nm + "_g"][h] = nd[:, (h - lo) * 64:(h - lo + 1) * 64, :]
                    GB[nm][h] = pb
            elif kind == "rows":
                V[nm] = ns[lo:hi, :]
                V[nm + "_g"] = nd[:, lo:hi, :]
                GB[nm] = pb
            elif kind == "half":
                V[nm][lo] = ns
                V[nm + "_g"][lo] = nd
                GB[nm][lo] = pb
            else:
                V[nm] = ns
                V[nm + "_g"] = nd
                GB[nm] = pb
    V["cc"] = cc
    V["gb"] = GB
    return V


def build_fused():
    nc = bass.Bass("TRN2", target_bir_lowering=False)
    A = {}
    for nm, shp in W_IN:
        A[nm] = nc.dram_tensor(nm, shp, F32, kind="ExternalInput").ap()
    for nm, shp, dt in C_IN:
        A[nm] = nc.dram_tensor(nm, shp, dt, kind="ExternalInput").ap()
    A["xT_in"] = nc.dram_tensor("xT_in", [D, NT], F32, kind="ExternalInput").ap()
    A["yT_out"] = nc.dram_tensor("yT_out", [D, NT], F32, kind="ExternalOutput").ap()
    A["xT_d"] = nc.dram_tensor("xT_d", [D, NT], F32, kind="Internal").ap()
    A["oT_d"] = nc.dram_tensor("oT_d", [128, 8, NT], BF16, kind="Internal").ap()
    LA = []
    for l in range(L):
        V = make_pieces(nc, l)
        for nm in LOCAL:
            shp, dt = POUT[nm]
            V[nm] = nc.dram_tensor("%s_l%d" % (nm, l), shp, dt, kind="Internal").ap()
        LA.append(V)
    with ExitStack() as stack:
        T = Tracker(nc, stack)
        R = setup_common(nc, stack, T)
        st = None
        GRP = [[0, 1, 2, 3], [4, 5, 6, 7]]

        def mk_hook(ll):
            def hook():
                T.wait_dma_all("pool")
                for k in (2, 0, 6, 7, 1):
                    gs, gd, pb = LA[ll]["cc"][k]
                    T.collective(T.new_dma_sem("cc%d_%d" % (ll, k)), gs, gd, GRP, writes=[pb])
            return hook

        for l in range(L):
            if l == 0:
                emit_layer_X(R, dict(A, **LA[0]), 0, do_post=False, do_pre=True, final=False,
                             xin=A["xT_in"], xout=A["xT_d"], mid_hook=mk_hook(0))
            T.barrier_all()
            for k in (4, 3, 8, 9, 5, 10):
                gs, gd, pb = LA[l]["cc"][k]
                T.collective(T.new_dma_sem("cc%d_%d" % (l, k)), gs, gd, GRP, writes=[pb])
            emit_layer_A(R, dict(A, **LA[l]), l)
            last = (l == L - 1)
            AX = dict(A, **(LA[l + 1] if not last else {}))
            st = emit_layer_X(R, AX, l, do_post=True, do_pre=not last, final=last,
                              xin=A["xT_d"], xout=(A["yT_out"] if last else A["xT_d"]),
                              mid_hook=(None if last else mk_hook(l + 1)))
        nc.sync.wait_ge(T.sem[st], T.cnt[st])
    return nc


def kernel_unfused(**inp):
    return _kernel_unfused_impl(**inp)


def kernel_fused(**inp):
    x = np.asarray(inp["x"], dtype=np.float32)
    W = _host_weights(inp)
    cores = list(range(8))
    nc = build_fused()
    ims = []
    for c in cores:
        xT = np.ascontiguousarray(x[c // 4][own_positions(c)].T)
        ims.append(dict(W, **_core_consts(c), xT_in=xT))
    res = run_bass_kernel_spmd(nc, ims, core_ids=cores).results
    out = np.zeros((B, S, D), np.float32)
    for c in cores:
        out[c // 4][own_positions(c)] = np.asarray(res[c]["yT_out"]).T
    return out


def kernel(**inp):
    return kernel_fused(**inp)
```
